# Optimizing a Trainium2 kernel written in Bass

```python
import math
import jax, jax.numpy as jnp
from jax import lax
import numpy as np

D_MODEL = 1024
BATCH = 2
SEQ = 16384
DEPTH = 2

HEAD_DIM = 64
A_HEADS = 6
A_WIDTH = A_HEADS * HEAD_DIM
DILATED_PATTERNS = ((128, 1), (512, 4), (2048, 16))
B_HEADS = 6
B_NOPE = 64
B_ROPE = 32
B_VDIM = 64
Q_RANK = 192
KV_RANK = 128
B_WIDTH = B_HEADS * B_VDIM
ROPE_THETA = 10000.0
QBLOCK = 128
C_GROUPS = 4
C_GROUP_DIM = 64
C_WIDTH = C_GROUPS * C_GROUP_DIM
CHUNK = 128
MIX_WIDTH = A_WIDTH + B_WIDTH + C_WIDTH
D_FF = 4 * D_MODEL
ALPHA = (2 * DEPTH) ** 0.25
BETA = (8 * DEPTH) ** -0.25
EPS = 1e-5
NEG_INF = -1e30
OFF_CQ = 3 * A_WIDTH
OFF_CKV = OFF_CQ + Q_RANK
OFF_KPE = OFF_CKV + KV_RANK
OFF_U = OFF_KPE + B_ROPE
OFF_V = OFF_U + C_WIDTH
P_IN = OFF_V + C_WIDTH

kernel_name = 'hybrid_dilated_mla_sgu_deepnorm_encoder'


def _layer_norm(x, g, b):
    xf = x.astype(jnp.float32)
    mu = xf.mean(-1, keepdims=True)
    var = jnp.square(xf - mu).mean(-1, keepdims=True)
    return ((xf - mu) * lax.rsqrt(var + EPS)).astype(x.dtype) * g + b


def _rms(x):
    xf = x.astype(jnp.float32)
    return (xf * lax.rsqrt(jnp.mean(xf * xf, -1, keepdims=True) + EPS)).astype(x.dtype)


def _rope(t, cos, sin):
    t1, t2 = jnp.split(t, 2, axis=-1)
    return jnp.concatenate([t1 * cos - t2 * sin, t1 * sin + t2 * cos], axis=-1)


def _dilated_band_attention(q, k, v, window, dil, slopes):
    B, S, H, Dh = q.shape
    band = window // (2 * dil)
    L = S // dil
    nb = -(-L // band)
    Lp = nb * band

    def to_sub(t):
        return t.reshape(B, L, dil, H, Dh).transpose(0, 2, 3, 1, 4)

    qs = jnp.pad(to_sub(q), ((0, 0),) * 3 + ((0, Lp - L), (0, 0))).reshape(B, dil, H, nb, band, Dh)

    def windows(t):
        tp = jnp.pad(to_sub(t), ((0, 0),) * 3 + ((band, Lp - L + band), (0, 0)))
        tp = tp.reshape(B, dil, H, nb + 2, band, Dh)
        return jnp.concatenate([tp[:, :, :, 0:nb], tp[:, :, :, 1:nb + 1], tp[:, :, :, 2:nb + 2]], axis=4)

    kw, vw = windows(k), windows(v)
    s = jnp.einsum('brhnqc,brhnkc->brhnqk', qs, kw).astype(jnp.float32) * (Dh ** -0.5)
    a = jnp.arange(band)[:, None]
    j = jnp.arange(3 * band)[None, :]
    off = j - band - a
    kpos = (jnp.arange(nb)[:, None, None] - 1) * band + j[None]
    valid = (jnp.abs(off) <= band)[None] & (kpos >= 0) & (kpos < L)
    bias = -(slopes[:, None, None, None] * (dil * jnp.abs(off)).astype(jnp.float32))
    s = jnp.where(valid, s + bias, NEG_INF)
    m = s.max(-1, keepdims=True)
    p = jnp.exp(s - m)
    den = p.sum(-1, keepdims=True)
    o = jnp.einsum('brhnqk,brhnkc->brhnqc', (p / den).astype(v.dtype), vw)
    lse = (m + jnp.log(den))[..., 0]
    o = o.reshape(B, dil, H, Lp, Dh)[:, :, :, :L].transpose(0, 3, 1, 2, 4).reshape(B, S, H, Dh)
    lse = lse.reshape(B, dil, H, Lp)[..., :L].transpose(0, 3, 1, 2).reshape(B, S, H)
    return o, lse


def _dilated_attention(q, k, v, slopes):
    res = [_dilated_band_attention(q, k, v, w, d, slopes) for (w, d) in DILATED_PATTERNS]
    outs = jnp.stack([r[0] for r in res], 0)
    wts = jax.nn.softmax(jnp.stack([r[1] for r in res], 0), axis=0)
    return jnp.einsum('pbsh,pbshc->bshc', wts.astype(q.dtype), outs)


def _mla_attention(q, k, v):
    B, S, H, Dq = q.shape
    nq = S // QBLOCK
    qb = q.reshape(B, nq, QBLOCK, H, Dq).transpose(1, 0, 2, 3, 4)
    scale = Dq ** -0.5

    def block(qi):
        s = jnp.einsum('bqhc,bkhc->bhqk', qi, k).astype(jnp.float32) * scale
        p = jax.nn.softmax(s, axis=-1).astype(v.dtype)
        return jnp.einsum('bhqk,bkhc->bqhc', p, v)

    o = lax.map(block, qb)
    return o.transpose(1, 0, 2, 3, 4).reshape(B, S, H, v.shape[-1])


def _spatial_gating(u, v, g, b, w_s, b_s):
    v = _layer_norm(v, g, b)
    B, S, _ = v.shape
    vc = v.reshape(B, S // CHUNK, CHUNK, C_GROUPS, C_GROUP_DIM)
    mixed = jnp.einsum('gts,bnsgc->bntgc', w_s, vc) + b_s.T[None, None, :, :, None]
    return u * mixed.reshape(B, S, C_WIDTH)


def _layer(x, w_in, q_norm, w_q_up, kv_norm, w_kv_up, sgu_ln_g, sgu_ln_b, sgu_w, sgu_b, mix_norm,
           w_out, ln1_g, ln1_b, w_ff1, b_ff1, w_ff2, b_ff2, ln2_g, ln2_b, cos, sin, slopes):
    B, S, _ = x.shape
    h = x @ w_in
    qa = h[..., 0:A_WIDTH].reshape(B, S, A_HEADS, HEAD_DIM)
    ka = h[..., A_WIDTH:2 * A_WIDTH].reshape(B, S, A_HEADS, HEAD_DIM)
    va = h[..., 2 * A_WIDTH:3 * A_WIDTH].reshape(B, S, A_HEADS, HEAD_DIM)
    y_a = _dilated_attention(qa, ka, va, slopes).reshape(B, S, A_WIDTH)
    c_q = _rms(h[..., OFF_CQ:OFF_CKV]) * q_norm
    qb = (c_q @ w_q_up).reshape(B, S, B_HEADS, B_NOPE + B_ROPE)
    q_pe = _rope(qb[..., B_NOPE:], cos[:, None, :], sin[:, None, :])
    c_kv = _rms(h[..., OFF_CKV:OFF_KPE]) * kv_norm
    kv = (c_kv @ w_kv_up).reshape(B, S, B_HEADS, B_NOPE + B_VDIM)
    k_pe = _rope(h[..., OFF_KPE:OFF_U], cos, sin)
    q_full = jnp.concatenate([qb[..., :B_NOPE], q_pe], axis=-1)
    k_full = jnp.concatenate([kv[..., :B_NOPE],
                              jnp.broadcast_to(k_pe[:, :, None, :], (B, S, B_HEADS, B_ROPE))], axis=-1)
    y_b = _mla_attention(q_full, k_full, kv[..., B_NOPE:]).reshape(B, S, B_WIDTH)
    z = jax.nn.gelu(h[..., OFF_U:P_IN])
    y_c = _spatial_gating(z[..., :C_WIDTH], z[..., C_WIDTH:], sgu_ln_g, sgu_ln_b, sgu_w, sgu_b)
    y = jnp.concatenate([_rms(y_a), _rms(y_b), _rms(y_c)], axis=-1) * mix_norm
    x = _layer_norm(ALPHA * x + y @ w_out, ln1_g, ln1_b)
    f = jnp.square(jax.nn.relu(x @ w_ff1 + b_ff1)) @ w_ff2 + b_ff2
    return _layer_norm(ALPHA * x + f, ln2_g, ln2_b)


def setup_inputs(seed: int = 0) -> dict:
    key = jax.random.key(seed)
    ks = jax.random.split(key, 24)
    f32 = jnp.float32
    nrm = lambda k, shape, scale: jax.random.normal(k, shape, f32) * scale
    gain = lambda k, shape: 1.0 + 0.02 * jax.random.normal(k, shape, f32)
    return {
        'x': jax.random.normal(ks[0], (BATCH, SEQ, D_MODEL), f32),
        'w_in': nrm(ks[1], (DEPTH, D_MODEL, P_IN), D_MODEL ** -0.5),
        'q_norm': gain(ks[2], (DEPTH, Q_RANK)),
        'w_q_up': nrm(ks[3], (DEPTH, Q_RANK, B_HEADS * (B_NOPE + B_ROPE)), Q_RANK ** -0.5),
        'kv_norm': gain(ks[4], (DEPTH, KV_RANK)),
        'w_kv_up': nrm(ks[5], (DEPTH, KV_RANK, B_HEADS * (B_NOPE + B_VDIM)), KV_RANK ** -0.5),
        'sgu_ln_g': gain(ks[6], (DEPTH, C_WIDTH)),
        'sgu_ln_b': nrm(ks[7], (DEPTH, C_WIDTH), 0.02),
        'sgu_w': nrm(ks[8], (DEPTH, C_GROUPS, CHUNK, CHUNK), CHUNK ** -0.5),
        'sgu_b': gain(ks[9], (DEPTH, C_GROUPS, CHUNK)),
        'mix_norm': gain(ks[10], (DEPTH, MIX_WIDTH)),
        'w_out': nrm(ks[11], (DEPTH, MIX_WIDTH, D_MODEL), BETA * MIX_WIDTH ** -0.5),
        'ln1_g': gain(ks[12], (DEPTH, D_MODEL)),
        'ln1_b': nrm(ks[13], (DEPTH, D_MODEL), 0.02),
        'w_ff1': nrm(ks[14], (DEPTH, D_MODEL, D_FF), BETA * D_MODEL ** -0.5),
        'b_ff1': nrm(ks[15], (DEPTH, D_FF), 0.02),
        'w_ff2': nrm(ks[16], (DEPTH, D_FF, D_MODEL), BETA * D_FF ** -0.5),
        'b_ff2': nrm(ks[17], (DEPTH, D_MODEL), 0.02),
        'ln2_g': gain(ks[18], (DEPTH, D_MODEL)),
        'ln2_b': nrm(ks[19], (DEPTH, D_MODEL), 0.02),
    }


def reference(x, w_in, q_norm, w_q_up, kv_norm, w_kv_up, sgu_ln_g, sgu_ln_b, sgu_w, sgu_b, mix_norm,
              w_out, ln1_g, ln1_b, w_ff1, b_ff1, w_ff2, b_ff2, ln2_g, ln2_b):
    S = x.shape[1]
    inv_freq = ROPE_THETA ** (-jnp.arange(0, B_ROPE, 2, dtype=jnp.float32) / B_ROPE)
    ang = jnp.arange(S, dtype=jnp.float32)[:, None] * inv_freq[None, :]
    cos = jnp.cos(ang).astype(x.dtype)
    sin = jnp.sin(ang).astype(x.dtype)
    slopes = jnp.asarray(2.0 ** (-8.0 * np.arange(1, A_HEADS + 1) / A_HEADS), dtype=jnp.float32)
    for l in range(DEPTH):
        x = _layer(x, w_in[l], q_norm[l], w_q_up[l], kv_norm[l], w_kv_up[l], sgu_ln_g[l], sgu_ln_b[l],
                   sgu_w[l], sgu_b[l], mix_norm[l], w_out[l], ln1_g[l], ln1_b[l], w_ff1[l], b_ff1[l],
                   w_ff2[l], b_ff2[l], ln2_g[l], ln2_b[l], cos, sin, slopes)
    return x
```

```python
import types
import numpy as np
import ml_dtypes
import concourse.bass as bass
import concourse.mybir as mybir
from concourse.bass_utils import run_bass_kernel_spmd

F32 = mybir.dt.float32
BF16 = mybir.dt.bfloat16
I32 = mybir.dt.int32
AF = mybir.ActivationFunctionType
ALU = mybir.AluOpType
AX = mybir.AxisListType

ENGS = ("pe", "act", "dve", "pool", "sp")
SEM_LIMIT = 20000

D = 1024
PIN = 2016
HALO = 1024
DFF = 4096
EPS = 1e-5
ALPHA = float((2 * 2) ** 0.25)
TABW = 2944
A_SCALE = 0.125
B_SCALE = float(96 ** -0.5)
C_GELU = 0.044715
K_GELU = float(np.sqrt(2.0 / np.pi))


def _freeze(fn):
    if fn is None or fn.__closure__ is None:
        return fn
    cells = []
    for c in fn.__closure__:
        try:
            cells.append(types.CellType(c.cell_contents))
        except ValueError:
            cells.append(c)
    return types.FunctionType(fn.__code__, fn.__globals__, fn.__name__, fn.__defaults__, tuple(cells))


class Prog:
    def __init__(self, nc):
        self.nc = nc
        self.ops = {e: [] for e in ENGS}
        self.state = {}
        self.dma_sems = {}
        self.pending = {e: [] for e in ENGS}
        self._cms = []
        self._scopes = []

    def _reg(self, cm):
        t = cm.__enter__()
        (self._scopes[-1] if self._scopes else self._cms).append(cm)
        return t

    def sem(self, name):
        self._n = getattr(self, "_n", 0) + 1
        cm = self.nc.semaphore("m%d_%s" % (self._n, name))
        s = cm.__enter__()
        self._cms.append(cm)
        return s

    def sb(self, name, shape, dt):
        self._n = getattr(self, "_n", 0) + 1
        return self._reg(self.nc.sbuf_tensor("sb%d_%s" % (self._n, name), list(shape), dt))

    def ps(self, name, shape, dt=F32):
        self._n = getattr(self, "_n", 0) + 1
        return self._reg(self.nc.psum_tensor("ps%d_%s" % (self._n, name), list(shape), dt))

    def push(self):
        self._scopes.append([])

    def pop(self):
        self.barrier()
        for cm in reversed(self._scopes.pop()):
            cm.__exit__(None, None, None)

    def close(self):
        for cm in reversed(self._cms):
            cm.__exit__(None, None, None)
        self._cms = []

    def barrier(self):
        evs = []
        for e in ENGS:
            lst = self.ops[e]
            for i in range(len(lst) - 1, -1, -1):
                if lst[i]["dma"] is None and lst[i]["fn"] is not None:
                    evs.append(("eng", e, i))
                    break
        for tag, ds in self.dma_sems.items():
            evs.append(("dma", ds[0], ds[1]))
        for e in ENGS:
            self.pending[e].extend(evs)
        self.state = {}

    def op(self, eng, fn, reads=(), writes=(), dma=None, inc=16):
        fn = _freeze(fn)
        deps = []
        for k in reads:
            st = self.state.get(k)
            if st and st[0] is not None:
                deps.append((st[0], True))
        for k in writes:
            st = self.state.get(k)
            if st:
                if st[0] is not None:
                    deps.append((st[0], False))
                for r in st[1]:
                    deps.append((r, False))
        lst = self.ops[eng]
        idx = len(lst)
        if dma is not None:
            if dma not in self.dma_sems:
                self.dma_sems[dma] = [self.sem("d_" + dma), 0]
            ds = self.dma_sems[dma]
            ds[1] += inc
            ev = ("dma", ds[0], ds[1])
        else:
            ev = ("eng", eng, idx)
        fdeps = []
        for d, raw in deps:
            if d[0] == "eng" and d[1] == eng and dma is None:
                if eng == "pe" or not raw:
                    continue
            if d[0] == "dma":
                for ds_ in self.dma_sems.values():
                    if ds_[0] is d[1]:
                        cur = ds_[1] - (inc if (dma is not None and self.dma_sems[dma][0] is d[1]) else 0)
                        d = ("dma", d[1], max(d[2], cur))
            fdeps.append(d)
        for d in self.pending[eng]:
            if d[0] == "eng" and d[1] == eng and eng == "pe":
                continue
            fdeps.append(d)
        self.pending[eng] = []
        lst.append(dict(fn=fn, deps=fdeps, dma=dma, ev=ev, ms=False, inc=inc))
        for k in reads:
            st = self.state.setdefault(k, [None, []])
            st[1].append(ev)
        for k in writes:
            self.state[k] = [ev, []]
        return ev

    def wait_all_dma(self, eng, tags):
        deps = []
        for t in tags:
            ds = self.dma_sems[t]
            deps.append(("dma", ds[0], ds[1]))
        self.ops[eng].append(dict(fn=None, deps=deps, dma=None, ev=None, ms=False))

    def emit(self):
        nc = self.nc
        for e in ENGS:
            for o in self.ops[e]:
                for d in o["deps"]:
                    if d[0] == "eng":
                        self.ops[d[1]][d[2]]["ms"] = True
        msmap = {}
        for e in ENGS:
            cur = None
            cnt = 0
            for i, o in enumerate(self.ops[e]):
                if o["ms"]:
                    if cur is None or cnt >= SEM_LIMIT:
                        cur = self.sem("s_%s_%d" % (e, i))
                        cnt = 0
                    cnt += 1
                    msmap[(e, i)] = (cur, cnt)
                    o["inc"] = cur
        stats = {}

        def run(e, eng):
            waited = {}
            nw = 0
            for i, o in enumerate(self.ops[e]):
                for d in o["deps"]:
                    if d[0] == "eng":
                        sem, val = msmap[(d[1], d[2])]
                    else:
                        sem, val = d[1], d[2]
                    key = id(sem)
                    if waited.get(key, 0) >= val:
                        continue
                    waited[key] = val
                    eng.wait_ge(sem, val)
                    nw += 1
                if o["fn"] is None:
                    continue
                ins = o["fn"](eng)
                if o["dma"] is not None:
                    ins.then_inc(o["ev"][1], o.get("inc", 16))
                elif o["ms"]:
                    ins.then_inc(o["inc"], 1)
            stats[e] = (len(self.ops[e]), nw)

        with nc.Block() as block:
            @block.tensor
            def _(eng):
                run("pe", eng)

            @block.scalar
            def _(eng):
                run("act", eng)

            @block.vector
            def _(eng):
                run("dve", eng)

            @block.gpsimd
            def _(eng):
                run("pool", eng)

            @block.sync
            def _(eng):
                run("sp", eng)
        self.stats = stats
        return stats


def mask_table():
    slopes = (2.0 ** (-8.0 * np.arange(1, 7) / 6)).astype(np.float32)
    pp = np.arange(128)[:, None]
    col = np.arange(TABW)[None, :]
    delta = pp - col + 1408
    ad = np.abs(delta)
    c = (ad <= 64).astype(np.float32) + ((delta % 4 == 0) & (ad <= 256)).astype(np.float32) \
        + ((delta % 16 == 0) & (ad <= 1024)).astype(np.float32)
    tab = np.zeros((128, 6, TABW), np.float32)
    for h in range(6):
        tab[:, h, :] = c * np.exp(-(slopes[h] * ad.astype(np.float32)).astype(np.float32))
    return tab.astype(ml_dtypes.bfloat16)


def rope_tables(S):
    inv_freq = (10000.0 ** (-np.arange(0, 32, 2, dtype=np.float32) / 32)).astype(np.float32)
    ang = (np.arange(S, dtype=np.float32)[:, None] * inv_freq[None, :]).astype(np.float32)
    cos = np.cos(ang).astype(np.float32).T
    sin = np.sin(ang).astype(np.float32).T
    ct = np.concatenate([cos, cos], 0)
    st = np.concatenate([-sin, sin], 0)
    return np.ascontiguousarray(ct), np.ascontiguousarray(st)


class Cfg:
    def __init__(self, own, sf, depth=1, dbg=False, ncores=8):
        self.ncores = ncores
        self.OWN = own
        self.SF = sf
        self.OH = own + 2 * HALO
        self.NT = own // 128
        self.NQB = own // 512
        self.NG_OH = self.OH // 512
        self.NG_F = sf // 512
        self.NKT_OH = self.OH // 128
        self.NKT_F = sf // 128
        self.depth = depth
        self.dbg = dbg


W_NAMES = [
    ("w_in", [D, PIN]), ("w_q_up", [192, 576]), ("w_kv_up", [128, 768]), ("qn", [128, 2]), ("kvn", [128, 1]),
    ("sg_g", [256]), ("sg_b", [256]), ("sg_wT", [4, 128, 128]), ("sg_bT", [128, 4]), ("mixn", [128, 8]),
    ("w_out", [D, D]), ("ln1_g", [D]), ("ln1_b", [D]), ("w_ff1", [D, DFF]), ("b1T", [128, 32]),
    ("w_ff2", [DFF, D]), ("b_ff2", [D]), ("ln2_g", [D]), ("ln2_b", [D]),
]


def build_program(cfg):
    nc = bass.Bass("TRN2", target_bir_lowering=False)
    OWN, SF, OH, NT = cfg.OWN, cfg.SF, cfg.OH, cfg.NT
    io = {}
    io["xo"] = nc.dram_tensor("xo", [OH, D], F32, kind="ExternalInput").ap()
    io["xf"] = nc.dram_tensor("xf", [SF, D], F32, kind="ExternalInput").ap()
    io["vmask"] = nc.dram_tensor("vmask", [128, OH // 128], F32, kind="ExternalInput").ap()
    io["mtab"] = nc.dram_tensor("mtab", [128, 6, TABW], BF16, kind="ExternalInput").ap()
    io["ckt"] = nc.dram_tensor("ckt", [32, SF], F32, kind="ExternalInput").ap()
    io["skt"] = nc.dram_tensor("skt", [32, SF], F32, kind="ExternalInput").ap()
    io["cqt"] = nc.dram_tensor("cqt", [32, OWN], F32, kind="ExternalInput").ap()
    io["sqt"] = nc.dram_tensor("sqt", [32, OWN], F32, kind="ExternalInput").ap()
    for nm, shp in W_NAMES:
        io[nm] = nc.dram_tensor(nm, [cfg.depth] + shp, F32, kind="ExternalInput").ap()
    io["out"] = nc.dram_tensor("out", [OWN, D], F32, kind="ExternalOutput").ap()
    io["YT"] = nc.dram_tensor("YT", [D, OWN], BF16, kind="Internal").ap()
    io["X1"] = nc.dram_tensor("X1", [OWN, D], F32, kind="Internal").ap()
    if cfg.depth > 1:
        io["hsel"] = nc.dram_tensor("hsel", [128, 8], F32, kind="ExternalInput").ap()
        io["XL"] = nc.dram_tensor("XL", [OWN, D], F32, kind="Internal").ap()
        io["XBc"] = nc.dram_tensor("XBc", [OWN, D], BF16, kind="Internal").ap()
        io["XG"] = nc.dram_tensor("XG", [OWN // 512, (SF // OWN) * 512, D], BF16, kind="Internal").ap()
        io["XH"] = nc.dram_tensor("XH", [2 * HALO, D], BF16, kind="Internal").ap()
    if cfg.dbg:
        io["dbg_yt"] = nc.dram_tensor("dbg_yt", [D, OWN], BF16, kind="ExternalOutput").ap()
        io["dbg_ss"] = nc.dram_tensor("dbg_ss", [128, 13 * NT], F32, kind="ExternalOutput").ap()
        io["dbg_x1"] = nc.dram_tensor("dbg_x1", [OWN, D], F32, kind="ExternalOutput").ap()

    p = Prog(nc)
    DBG["on"] = cfg.dbg
    G = {}
    G["ident"] = p.sb("ident", [128, 128], BF16)
    identf = p.sb("identf", [128, 128], F32)
    G["e65"] = p.sb("e65", [128, 64], F32)
    G["ones"] = p.sb("onesf", [128, 128], F32)
    G["ss"] = p.sb("ss", [128, 13, NT], F32)
    p.op("pool", lambda e: e.memset(identf[:], 0.0), writes=["identf"])
    p.op("pool", lambda e: e.affine_select(out=identf[:], in_=identf[:], compare_op=ALU.not_equal, fill=1.0,
                                           base=0, pattern=[[-1, 128]], channel_multiplier=1),
         reads=["identf"], writes=["identf"])
    p.op("pool", lambda e: e.tensor_copy(G["ident"][:], identf[:]), reads=["identf"], writes=["ident"])
    p.op("pool", lambda e: e.memset(G["e65"][:], 0.0), writes=["e65"])
    p.op("pool", lambda e: e.memset(G["e65"][64:65, :], 1.0), reads=["e65"], writes=["e65"])
    p.op("pool", lambda e: e.memset(G["ones"][:], 1.0), writes=["ones"])
    p.barrier()

    ngo = cfg.NG_OH
    for l in range(cfg.depth):
        W = {nm: io[nm][l] for nm, _ in W_NAMES}
        last = (l == cfg.depth - 1)
        if l == 0:
            src = dict(oh=lambda g: io["xo"][g * 512:(g + 1) * 512, :],
                       full=lambda g: io["xf"][g * 512:(g + 1) * 512, :],
                       res=lambda ti: io["xo"][HALO + ti * 128:HALO + (ti + 1) * 128, :])
        else:
            def oh2(g):
                if g < 2:
                    return io["XH"][g * 512:(g + 1) * 512, :]
                if g >= ngo - 2:
                    return io["XH"][1024 + (g - (ngo - 2)) * 512:1024 + (g - (ngo - 2) + 1) * 512, :]
                return io["XL"][(g - 2) * 512:(g - 1) * 512, :]
            src = dict(oh=oh2, full=lambda g: xg_rows(io, cfg, g * 512, 512),
                       res=lambda ti: io["XL"][ti * 128:(ti + 1) * 128, :])
        if last:
            dst = dict(f32=io["out"], bf16=None, tag="out")
        else:
            dst = dict(f32=io["XL"], bf16=io["XBc"], tag="xl")
        layer(p, cfg, G, io, W, src, dst)
        if not last:
            exchange(p, cfg, G, io)

    if cfg.dbg:
        p.op("sp", lambda e: e.dma_start(out=io["dbg_yt"], in_=io["YT"]), reads=[], dma="out")
        p.op("sp", lambda e: e.dma_start(out=io["dbg_ss"], in_=G["ss"][:].rearrange("p a b -> p (a b)")), dma="out")
        p.op("sp", lambda e: e.dma_start(out=io["dbg_x1"], in_=io["X1"]), dma="out")
    p.wait_all_dma("sp", ["out"])
    stats = p.emit()
    p.close()
    return nc, stats


DBG = {}


def dbg_dump(p, name, ap, shape, dt):
    if not DBG.get("on"):
        return
    t = p.nc.dram_tensor("dd_" + name, list(shape), dt, kind="ExternalOutput").ap()
    p.barrier()
    p.op("sp", lambda e: e.dma_start(out=t, in_=ap), dma="out")
    p.barrier()


def load_x_group(p, src_ap, xb, key, tag):
    p.op("pool", lambda e: e.dma_start(out=xb[:], in_=src_ap.rearrange("(j p) d -> p j d", p=128)),
         writes=[key], dma=tag)


def transpose_group(p, G, xb, xbkey, xT, xTkey, trps, n_tok_tiles=4):
    for kc in range(8):
        tp = trps[kc % 2]
        tkey = "tr%d" % (kc % 2)
        for j in range(n_tok_tiles):
            p.op("pe", lambda e, tp=tp, j=j, kc=kc: e.transpose(tp[:, j * 128:(j + 1) * 128],
                                                                 xb[:, j, kc * 128:(kc + 1) * 128], G["ident"][:]),
                 reads=[xbkey, "ident"], writes=[tkey])
        w = n_tok_tiles * 128
        p.op("act", lambda e, tp=tp, kc=kc, w=w: e.copy(xT[:, kc, 0:w], tp[:, 0:w]), reads=[tkey], writes=[xTkey])


def attn_post(p, G, io, ops, okey, h_row, qb, ss_idx, bufs, it):
    r = it % 2
    osb, rden, ysq, ybf, denps, ssps = bufs["osb"][r], bufs["rden"], bufs["ysq"], bufs["ybf"][r], bufs["den"], bufs["ssps"]
    ko, kr, ks, kb = "osb%d" % r, "rden", "ysq", "ybf%d" % r
    kd, kss = bufs.get("denkey", "denps"), bufs.get("sskey", "ssps")
    p.op("dve", lambda e: e.tensor_copy(osb[0:65, :], ops[0:65, :]), reads=[okey], writes=[ko])

    def deferred():
        p.op("pe", lambda e: e.matmul(denps[0:64, :], lhsT=G["e65"][0:65, :], rhs=osb[0:65, :], start=True, stop=True),
             reads=[ko, "e65"], writes=[kd])
        p.op("dve", lambda e: e.reciprocal(rden[0:64, :], denps[0:64, :]), reads=[kd], writes=[kr])
        p.op("pool", lambda e: e.tensor_tensor(out=osb[0:64, :], in0=osb[0:64, :], in1=rden[0:64, :], op=ALU.mult),
             reads=[ko, kr], writes=[ko])
        p.op("pool", lambda e: e.tensor_copy(ybf[0:64, :], osb[0:64, :]), reads=[ko], writes=[kb])
        p.op("sp", lambda e: e.dma_start(out=io["YT"][h_row:h_row + 64, qb * 512:(qb + 1) * 512], in_=ybf[0:64, :]),
             reads=[kb], dma="yt%d" % r)
        p.op("pool", lambda e: e.tensor_tensor(out=ysq[0:64, :], in0=osb[0:64, :], in1=osb[0:64, :], op=ALU.mult),
             reads=[ko], writes=[ks])
        for j in range(4):
            p.op("pe", lambda e, j=j: e.matmul(ssps[:, j:j + 1], lhsT=ysq[0:64, j * 128:(j + 1) * 128],
                                               rhs=G["ones"][0:64, 0:1], start=True, stop=True),
                 reads=[ks, "ones"], writes=[kss])
        p.op("dve", lambda e: e.tensor_copy(G["ss"][:, ss_idx, qb * 4:(qb + 1) * 4], ssps[:, 0:4]),
             reads=[kss], writes=["ss"])
    return deferred


def xg_rows(io, cfg, R0, n):
    j, q = R0 // cfg.OWN, R0 % cfg.OWN
    i, t = q // 512, q % 512
    assert t + n <= 512
    return io["XG"][i, j * 512 + t:j * 512 + t + n, :]


def exchange(p, cfg, G, io):
    OWN = cfg.OWN
    ngrp = cfg.SF // OWN
    groups = [list(range(b * ngrp, (b + 1) * ngrp)) for b in range(cfg.ncores // ngrp)]
    p.barrier()
    for i in range(OWN // 512):
        p.op("pool", lambda e, i=i: e.collective_compute("AllGather", ALU.bypass, replica_groups=groups,
                                                         ins=[io["XBc"][i * 512:(i + 1) * 512, :].opt()],
                                                         outs=[io["XG"][i].opt()]),
             dma="cc", inc=1)
    p.barrier()
    p.push()
    hsel = p.sb("hsel", [128, 8], F32)
    cands = [p.sb("cand%d" % i, [128, ngrp, D], BF16) for i in range(2)]
    hacc = p.sb("hacc", [128, D], F32)
    houts = [p.sb("hout%d" % i, [128, D], BF16) for i in range(2)]
    p.op("sp", lambda e: e.dma_start(out=hsel[:], in_=io["hsel"]), writes=["hsel"], dma="hsel")
    it = 0
    for side in range(2):
        for t in range(HALO // 128):
            cand, ck = cands[it % 2], "cand%d" % (it % 2)
            hout, hk = houts[it % 2], "hout%d" % (it % 2)
            for j in range(ngrp):
                r0 = j * OWN + (OWN - HALO if side == 0 else 0) + t * 128
                p.op("sp", lambda e, j=j, r0=r0: e.dma_start(out=cand[:, j, :], in_=xg_rows(io, cfg, r0, 128)),
                     writes=[ck], dma=ck)
            p.op("dve", lambda e: e.tensor_scalar(hacc[:, :], cand[:, 0, :], hsel[:, side * 4:side * 4 + 1], None, op0=ALU.mult),
                 reads=[ck, "hsel"], writes=["hacc"])
            for j in range(1, ngrp):
                p.op("dve", lambda e, j=j: e.scalar_tensor_tensor(out=hacc[:, :], in0=cand[:, j, :],
                                                                  scalar=hsel[:, side * 4 + j:side * 4 + j + 1], in1=hacc[:, :],
                                                                  op0=ALU.mult, op1=ALU.add), reads=[ck, "hsel", "hacc"], writes=["hacc"])
            p.op("pool", lambda e: e.tensor_copy(hout[:, :], hacc[:, :]), reads=["hacc"], writes=[hk])
            r1 = side * HALO + t * 128
            p.op("sp", lambda e, r1=r1: e.dma_start(out=io["XH"][r1:r1 + 128, :], in_=hout[:, :]), reads=[hk], dma="xh%d" % (it % 2))
            it += 1
    p.pop()


def layer(p, cfg, G, io, W, src, dst):
    OWN, SF, OH, NT, NQB = cfg.OWN, cfg.SF, cfg.OH, cfg.NT, cfg.NQB
    ident = G["ident"]

    p.push()
    CQ = p.sb("CQ", [128, 2, OWN], BF16)
    p.push()
    KaT = p.sb("KaT", [128, 3, OH], BF16)
    QaT = p.sb("QaT", [128, 3, OWN], BF16)
    Va = p.sb("Va", [128, OH // 128, 6, 65], BF16)

    p.push()
    w_in = p.sb("w_in", [128, 8, PIN], BF16)
    wsT = p.sb("wsT", [128, 4, 128], BF16)
    sg_g = p.sb("sg_g", [128, 256], F32)
    sg_b = p.sb("sg_b", [128, 256], F32)
    bsT = p.sb("bsT", [128, 4], F32)
    vm = p.sb("vm", [128, OH // 128], F32)
    xbs = [p.sb("xb%d" % i, [128, 4, D], BF16) for i in range(2)]
    xTs = [p.sb("xT%d" % i, [128, 8, 512], BF16) for i in range(2)]
    zh = p.sb("zh", [128, 512], F32)
    w1 = p.sb("w1", [128, 512], F32)
    w2 = p.sb("w2", [128, 512], F32)
    zz = p.sb("zz", [128, 512], F32)
    vn = p.sb("vn", [128, 256], F32)
    vnb = p.sb("vnb", [128, 256], BF16)
    yc = p.sb("yc", [128, 256], F32)
    ycb = p.sb("ycb", [128, 256], BF16)
    ycT = p.sb("ycT", [128, 2, 512], BF16)
    junk = p.sb("junk", [128, 256], F32)
    st6 = p.sb("st6", [128, 6], F32)
    mv = p.sb("mv", [128, 2], F32)
    rs = p.sb("rs", [128, 1], F32)
    cqs = [p.sb("cqs%d" % i, [128, 512], F32) for i in range(2)]
    cqq = [p.sb("cqq%d" % i, [128, 512], F32) for i in range(2)]
    cqr = p.sb("cqr", [128, 512], F32)
    trps = [p.ps("trps%d" % i, [128, 1024], BF16) for i in range(2)]
    pps = [p.ps("pps%d" % i, [128, 512]) for i in range(3)]
    mixps = p.ps("mixps", [128, 512])
    ssq = p.ps("ssq", [128, 512])

    for kc in range(8):
        p.op("pool", lambda e, kc=kc: e.dma_start(out=w_in[:, kc, :], in_=W["w_in"][kc * 128:(kc + 1) * 128, :]),
             writes=["w_in"], dma="w_in")
    p.op("pool", lambda e: e.dma_start(out=wsT[:], in_=W["sg_wT"].rearrange("g s t -> s g t")), writes=["wsT"], dma="wsm")
    p.op("sp", lambda e: e.dma_start(out=sg_g[:], in_=W["sg_g"].partition_broadcast(128)), writes=["sg_g"], dma="wsm2")
    p.op("sp", lambda e: e.dma_start(out=sg_b[:], in_=W["sg_b"].partition_broadcast(128)), writes=["sg_b"], dma="wsm2")
    p.op("sp", lambda e: e.dma_start(out=bsT[:], in_=W["sg_bT"]), writes=["bsT"], dma="wsm2")
    p.op("sp", lambda e: e.dma_start(out=vm[:], in_=io["vmask"]), writes=["vm"], dma="wsm2")
    for h in range(6):
        p.op("dve", lambda e, h=h: e.tensor_copy(Va[:, :, h, 64:65], vm[:].rearrange("p (t o) -> p t o", o=1)),
             reads=["vm"], writes=["Va"])

    pp_i = [0]

    def next_pps():
        i = pp_i[0] % 3
        pp_i[0] += 1
        return pps[i], "pps%d" % i

    ngo = cfg.NG_OH
    load_x_group(p, src["oh"](0), xbs[0], "xb0", "xb0")
    for g in range(ngo):
        xb, xbk = xbs[g % 2], "xb%d" % (g % 2)
        xT, xTk = xTs[g % 2], "xT%d" % (g % 2)
        if g + 1 < ngo:
            load_x_group(p, src["oh"](g + 1), xbs[(g + 1) % 2], "xb%d" % ((g + 1) % 2), "xb%d" % ((g + 1) % 2))
        transpose_group(p, G, xb, xbk, xT, xTk, trps)
        own = (g * 512 >= HALO) and (g * 512 < HALO + OWN)
        go = g - HALO // 512
        for c in range(3):
            ps_, pk = next_pps()
            for kc in range(8):
                p.op("pe", lambda e, ps_=ps_, kc=kc, c=c: e.matmul(ps_[:, :], lhsT=w_in[:, kc, 384 + c * 128:384 + (c + 1) * 128],
                                                                   rhs=xT[:, kc, :], start=(kc == 0), stop=(kc == 7)),
                     reads=["w_in", xTk], writes=[pk])
            p.op("dve", lambda e, ps_=ps_, c=c, g=g: e.tensor_copy(KaT[:, c, g * 512:(g + 1) * 512], ps_[:, :]),
                 reads=[pk], writes=["KaT"])
        if own:
            for c in range(3):
                ps_, pk = next_pps()
                for kc in range(8):
                    p.op("pe", lambda e, ps_=ps_, kc=kc, c=c: e.matmul(ps_[:, :], lhsT=w_in[:, kc, c * 128:(c + 1) * 128],
                                                                       rhs=xT[:, kc, :], start=(kc == 0), stop=(kc == 7)),
                         reads=["w_in", xTk], writes=[pk])
                p.op("dve", lambda e, ps_=ps_, c=c, go=go: e.tensor_copy(QaT[:, c, go * 512:(go + 1) * 512], ps_[:, :]),
                     reads=[pk], writes=["QaT"])
        for j in range(4):
            ps_, pk = next_pps()
            for kc in range(8):
                p.op("pe", lambda e, ps_=ps_, kc=kc, j=j: e.matmul(ps_[:, 0:384], lhsT=xT[:, kc, j * 128:(j + 1) * 128],
                                                                   rhs=w_in[:, kc, 768:1152], start=(kc == 0), stop=(kc == 7)),
                     reads=["w_in", xTk], writes=[pk])
            p.op("dve", lambda e, ps_=ps_, j=j, g=g: e.tensor_copy(Va[:, g * 4 + j, :, 0:64],
                                                                   ps_[:, 0:384].rearrange("p (h c) -> p h c", h=6)),
                 reads=[pk], writes=["Va"])
        if not own:
            continue
        psa, pka = next_pps()
        psb, pkb = next_pps()
        for kc in range(8):
            p.op("pe", lambda e, kc=kc: e.matmul(psa[:, :], lhsT=w_in[:, kc, 1152:1280], rhs=xT[:, kc, :],
                                                 start=(kc == 0), stop=(kc == 7)), reads=["w_in", xTk], writes=[pka])
        for kc in range(8):
            p.op("pe", lambda e, kc=kc: e.matmul(psb[0:64, :], lhsT=w_in[:, kc, 1280:1344], rhs=xT[:, kc, :],
                                                 start=(kc == 0), stop=(kc == 7)), reads=["w_in", xTk], writes=[pkb])
        p.op("dve", lambda e: e.tensor_copy(cqs[0][:, :], psa[:, :]), reads=[pka], writes=["cqs0"])
        p.op("dve", lambda e: e.tensor_copy(cqs[1][0:64, :], psb[0:64, :]), reads=[pkb], writes=["cqs1"])
        p.op("pool", lambda e: e.tensor_tensor(out=cqq[0][:, :], in0=cqs[0][:, :], in1=cqs[0][:, :], op=ALU.mult),
             reads=["cqs0"], writes=["cqq0"])
        p.op("pool", lambda e: e.tensor_tensor(out=cqq[1][0:64, :], in0=cqs[1][0:64, :], in1=cqs[1][0:64, :], op=ALU.mult),
             reads=["cqs1"], writes=["cqq1"])
        p.op("pe", lambda e: e.matmul(ssq[:, :], lhsT=G["ones"][:, :], rhs=cqq[0][:, :], start=True, stop=False),
             reads=["cqq0", "ones"], writes=["ssq"])
        p.op("pe", lambda e: e.matmul(ssq[:, :], lhsT=G["ones"][0:64, :], rhs=cqq[1][0:64, :], start=False, stop=True),
             reads=["cqq1", "ones"], writes=["ssq"])
        p.op("act", lambda e: e.activation(out=cqr[:, :], in_=ssq[:, :], func=AF.Sqrt, scale=1.0 / 192, bias=EPS),
             reads=["ssq"], writes=["cqr"])
        p.op("dve", lambda e: e.reciprocal(cqr[:, :], cqr[:, :]), reads=["cqr"], writes=["cqr"])
        p.op("dve", lambda e, go=go: e.tensor_tensor(out=CQ[:, 0, go * 512:(go + 1) * 512], in0=cqs[0][:, :], in1=cqr[:, :],
                                                     op=ALU.mult), reads=["cqs0", "cqr"], writes=["CQ"])
        p.op("dve", lambda e, go=go: e.tensor_tensor(out=CQ[0:64, 1, go * 512:(go + 1) * 512], in0=cqs[1][0:64, :],
                                                     in1=cqr[0:64, :], op=ALU.mult), reads=["cqs1", "cqr"], writes=["CQ"])
        for j in range(4):
            ti = go * 4 + j
            ps_, pk = next_pps()
            for kc in range(8):
                p.op("pe", lambda e, ps_=ps_, kc=kc, j=j: e.matmul(ps_[:, :], lhsT=xT[:, kc, j * 128:(j + 1) * 128],
                                                                   rhs=w_in[:, kc, 1504:2016], start=(kc == 0), stop=(kc == 7)),
                     reads=["w_in", xTk], writes=[pk])
            p.op("act", lambda e, ps_=ps_: e.activation(out=zh[:, :], in_=ps_[:, :], func=AF.Copy, scale=0.5),
                 reads=[pk], writes=["zh"])
            p.op("pool", lambda e: e.tensor_tensor(out=w1[:, :], in0=zh[:, :], in1=zh[:, :], op=ALU.mult),
                 reads=["zh"], writes=["w1"])
            p.op("dve", lambda e: e.tensor_scalar(w1[:, :], w1[:, :], 4.0 * C_GELU, 1.0, op0=ALU.mult, op1=ALU.add),
                 reads=["w1"], writes=["w1"])
            p.op("pool", lambda e: e.tensor_tensor(out=w2[:, :], in0=w1[:, :], in1=zh[:, :], op=ALU.mult),
                 reads=["w1", "zh"], writes=["w2"])
            p.op("act", lambda e: e.activation(out=w2[:, :], in_=w2[:, :], func=AF.Tanh, scale=2.0 * K_GELU),
                 reads=["w2"], writes=["w2"])
            p.op("dve", lambda e: e.scalar_tensor_tensor(out=zz[:, :], in0=w2[:, :], scalar=1.0, in1=zh[:, :],
                                                         op0=ALU.add, op1=ALU.mult), reads=["w2", "zh"], writes=["zz"])
            p.op("dve", lambda e: e.bn_stats(st6[:, :], zz[:, 256:512]), reads=["zz"], writes=["st6"])
            p.op("dve", lambda e: e.bn_aggr(mv[:, :], st6[:, :]), reads=["st6"], writes=["mv"])
            p.op("act", lambda e: e.activation(out=rs[:, :], in_=mv[:, 1:2], func=AF.Sqrt, scale=1.0, bias=EPS),
                 reads=["mv"], writes=["rs"])
            p.op("dve", lambda e: e.reciprocal(rs[:, :], rs[:, :]), reads=["rs"], writes=["rs"])
            p.op("dve", lambda e: e.tensor_scalar(vn[:, :], zz[:, 256:512], mv[:, 0:1], rs[:, 0:1], op0=ALU.subtract,
                                                  op1=ALU.mult), reads=["zz", "mv", "rs"], writes=["vn"])
            p.op("pool", lambda e: e.tensor_tensor(out=vn[:, :], in0=vn[:, :], in1=sg_g[:, :], op=ALU.mult),
                 reads=["vn", "sg_g"], writes=["vn"])
            p.op("pool", lambda e: e.tensor_tensor(out=vnb[:, :], in0=vn[:, :], in1=sg_b[:, :], op=ALU.add),
                 reads=["vn", "sg_b"], writes=["vnb"])
            for gg in range(4):
                p.op("pe", lambda e, gg=gg: e.matmul(mixps[:, gg * 64:(gg + 1) * 64], lhsT=wsT[:, gg, :],
                                                     rhs=vnb[:, gg * 64:(gg + 1) * 64], start=True, stop=True),
                     reads=["wsT", "vnb"], writes=["mixps"])
            for gg in range(4):
                p.op("dve", lambda e, gg=gg: e.scalar_tensor_tensor(out=yc[:, gg * 64:(gg + 1) * 64],
                                                                    in0=mixps[:, gg * 64:(gg + 1) * 64], scalar=bsT[:, gg:gg + 1],
                                                                    in1=zz[:, gg * 64:(gg + 1) * 64], op0=ALU.add, op1=ALU.mult),
                     reads=["mixps", "bsT", "zz"], writes=["yc"])
            p.op("pool", lambda e: e.tensor_tensor(out=junk[:, :], in0=yc[:, :], in1=yc[:, :], op=ALU.mult),
                 reads=["yc"], writes=["junk"])
            p.op("dve", lambda e, ti=ti: e.reduce_sum(out=G["ss"][:, 12, ti:ti + 1], in_=junk[:, :], axis=AX.X),
                 reads=["junk"], writes=["ss"])
            p.op("pool", lambda e: e.tensor_copy(ycb[:, :], yc[:, :]), reads=["yc"], writes=["ycb"])
            for c2 in range(2):
                tp = trps[c2]
                p.op("pe", lambda e, tp=tp, c2=c2: e.transpose(tp[:, 512:640], ycb[:, c2 * 128:(c2 + 1) * 128], ident[:]),
                     reads=["ycb", "ident"], writes=["tr%d" % c2])
                p.op("act", lambda e, tp=tp, c2=c2, j=j: e.copy(ycT[:, c2, j * 128:(j + 1) * 128], tp[:, 512:640]),
                     reads=["tr%d" % c2], writes=["ycT"])
        p.op("sp", lambda e, go=go: e.dma_start(out=io["YT"][768:1024, go * 512:(go + 1) * 512].rearrange("(c p) t -> p c t", p=128),
                                                in_=ycT[:, :, :]), reads=["ycT"], dma="ytc")
    dbg_dump(p, "xT", xTs[(ngo - 1) % 2][:, :, :], [128, 8, 512], BF16)
    dbg_dump(p, "KaT", KaT[:, :, :], [128, 3, OH], BF16)
    dbg_dump(p, "QaT", QaT[:, :, :], [128, 3, OWN], BF16)
    dbg_dump(p, "Va", Va[:, :, :, :], [128, OH // 128, 6, 65], BF16)
    dbg_dump(p, "CQ", CQ[:, :, :], [128, 2, OWN], BF16)
    dbg_dump(p, "w_in", w_in[:, :, :], [128, 8, PIN], BF16)
    p.pop()

    p.push()
    tab = p.sb("tab", [128, 6, TABW], BF16)
    Es = [p.sb("E%d" % i, [128, 1024], BF16) for i in range(2)]
    PTs = [p.sb("PT%d" % i, [128, 1024], BF16) for i in range(2)]
    bufs = dict(osb=[p.sb("osb%d" % i, [128, 512], F32) for i in range(2)], rden=p.sb("rden", [128, 512], F32),
                ysq=p.sb("ysq", [128, 512], F32), ybf=[p.sb("ybf%d" % i, [128, 512], BF16) for i in range(2)],
                den=p.ps("denps", [128, 512]), ssps=p.ps("ssps", [128, 512]))
    Sps = [p.ps("S%d" % i, [128, 1024]) for i in range(2)]
    Ops = [p.ps("O%d" % i, [128, 512]) for i in range(2)]
    for h in range(6):
        p.op("sp", lambda e, h=h: e.dma_start(out=tab[:, h, :], in_=io["mtab"][:, h, :]), writes=["tab"], dma="tab")
    it = 0
    gi = 0
    pend = []
    for h in range(6):
        hp0 = (h % 2) * 64
        c = h // 2
        for qb in range(NQB):
            ops, okey = Ops[it % 2], "O%d" % (it % 2)
            ngrp = 10
            kt0 = 4 * qb

            def qk(gidx, i):
                S, sk = Sps[gidx % 2], "S%d" % (gidx % 2)
                for u in range(2):
                    kt = kt0 + 2 * i + u
                    p.op("pe", lambda e, S=S, u=u, kt=kt: e.matmul(S[:, u * 512:(u + 1) * 512],
                                                                   lhsT=KaT[hp0:hp0 + 64, c, kt * 128:(kt + 1) * 128],
                                                                   rhs=QaT[hp0:hp0 + 64, c, qb * 512:(qb + 1) * 512],
                                                                   start=True, stop=True),
                         reads=["KaT", "QaT"], writes=[sk])

            qk(gi, 0)
            for i in range(ngrp):
                if i + 1 < ngrp:
                    qk(gi + 1, i + 1)
                S, sk = Sps[gi % 2], "S%d" % (gi % 2)
                E, ek = Es[gi % 2], "E%d" % (gi % 2)
                PT, pk = PTs[gi % 2], "PT%d" % (gi % 2)
                p.op("act", lambda e, S=S, E=E: e.activation(out=E[:, :], in_=S[:, :], func=AF.Exp, scale=A_SCALE),
                     reads=[sk], writes=[ek])
                for u in range(2):
                    ii = 2 * i + u
                    start = 2432 - 128 * ii
                    eng = "dve"
                    p.op(eng, lambda e, E=E, PT=PT, u=u, start=start, h=h: e.tensor_tensor(
                        out=PT[:, u * 512:(u + 1) * 512], in0=E[:, u * 512:(u + 1) * 512],
                        in1=tab[:, h, start:start + 512], op=ALU.mult), reads=[ek, "tab"], writes=[pk + "_%d" % u])
                for u in range(2):
                    kt = kt0 + 2 * i + u
                    p.op("pe", lambda e, PT=PT, u=u, kt=kt, ops=ops, i=i, h=h: e.matmul(
                        ops[0:65, :], lhsT=Va[:, kt, h, 0:65], rhs=PT[:, u * 512:(u + 1) * 512],
                        start=(i == 0 and u == 0), stop=(i == ngrp - 1 and u == 1)),
                        reads=["Va", pk + "_%d" % u], writes=[okey])
                gi += 1
                if i == 1 and pend:
                    pend.pop()()
            pend.append(attn_post(p, G, io, ops, okey, h * 64, qb, h, bufs, it))
            it += 1
    while pend:
        pend.pop()()
    p.pop()
    p.pop()

    CKV = p.sb("CKV", [128, SF], BF16)
    KT = p.sb("KT", [128, SF], BF16)
    p.push()
    wB = p.sb("wB", [128, 8, 160], BF16)
    wk96 = p.sb("wk96", [128, 8, 96], BF16)
    wk96r = p.sb("wk96r", [128, 8, 96], BF16)
    xbs = [p.sb("bxb%d" % i, [128, 4, D], BF16) for i in range(2)]
    xTs = [p.sb("bxT%d" % i, [128, 8, 512], BF16) for i in range(2)]
    sq = p.sb("bsq", [128, 512], F32)
    rr = p.sb("brr", [128, 512], F32)
    cks = [p.sb("cks%d" % i, [128, 512], F32) for i in range(2)]
    sks = [p.sb("sks%d" % i, [128, 512], F32) for i in range(2)]
    t1 = p.sb("bt1", [128, 512], F32)
    t2 = p.sb("bt2", [128, 512], F32)
    trps = [p.ps("btrps%d" % i, [128, 1024], BF16) for i in range(2)]
    pps = [p.ps("bpps%d" % i, [128, 512]) for i in range(4)]
    ssq = p.ps("bssq", [128, 512])
    for kc in range(8):
        p.op("pool", lambda e, kc=kc: e.dma_start(out=wB[:, kc, :], in_=W["w_in"][kc * 128:(kc + 1) * 128, 1344:1504]),
             writes=["wB"], dma="wB")
    p.op("dve", lambda e: e.memset(wk96[:], 0.0), writes=["wk96"])
    p.op("dve", lambda e: e.memset(wk96r[:], 0.0), writes=["wk96r"])
    p.op("dve", lambda e: e.tensor_copy(wk96[:, :, 64:96], wB[:, :, 128:160]), reads=["wB", "wk96"], writes=["wk96"])
    p.op("dve", lambda e: e.tensor_copy(wk96r[:, :, 64:80], wB[:, :, 144:160]), reads=["wB", "wk96r"], writes=["wk96r"])
    p.op("dve", lambda e: e.tensor_copy(wk96r[:, :, 80:96], wB[:, :, 128:144]), reads=["wB", "wk96r"], writes=["wk96r"])
    ngf = cfg.NG_F
    load_x_group(p, src["full"](0), xbs[0], "bxb0", "bxb0")
    for g in range(ngf):
        xb, xbk = xbs[g % 2], "bxb%d" % (g % 2)
        xT, xTk = xTs[g % 2], "bxT%d" % (g % 2)
        ck, ckk = cks[g % 2], "cks%d" % (g % 2)
        sk_, skk = sks[g % 2], "sks%d" % (g % 2)
        if g + 1 < ngf:
            load_x_group(p, src["full"](g + 1), xbs[(g + 1) % 2], "bxb%d" % ((g + 1) % 2), "bxb%d" % ((g + 1) % 2))
        p.op("sp", lambda e, ck=ck, g=g: e.dma_start(out=ck[64:96, :], in_=io["ckt"][:, g * 512:(g + 1) * 512]),
             writes=[ckk], dma=ckk)
        p.op("sp", lambda e, sk_=sk_, g=g: e.dma_start(out=sk_[64:96, :], in_=io["skt"][:, g * 512:(g + 1) * 512]),
             writes=[skk], dma=skk)
        transpose_group(p, G, xb, xbk, xT, xTk, trps)
        pa, pb, pc = pps[(3 * g) % 4], pps[(3 * g + 1) % 4], pps[(3 * g + 2) % 4]
        ka, kb, kc_ = "bpps%d" % ((3 * g) % 4), "bpps%d" % ((3 * g + 1) % 4), "bpps%d" % ((3 * g + 2) % 4)
        for kc in range(8):
            p.op("pe", lambda e, kc=kc, pa=pa: e.matmul(pa[:, :], lhsT=wB[:, kc, 0:128], rhs=xT[:, kc, :],
                                                        start=(kc == 0), stop=(kc == 7)), reads=["wB", xTk], writes=[ka])
        for kc in range(8):
            p.op("pe", lambda e, kc=kc, pb=pb: e.matmul(pb[0:96, :], lhsT=wk96[:, kc, :], rhs=xT[:, kc, :],
                                                        start=(kc == 0), stop=(kc == 7)), reads=["wk96", xTk], writes=[kb])
        for kc in range(8):
            p.op("pe", lambda e, kc=kc, pc=pc: e.matmul(pc[0:96, :], lhsT=wk96r[:, kc, :], rhs=xT[:, kc, :],
                                                        start=(kc == 0), stop=(kc == 7)), reads=["wk96r", xTk], writes=[kc_])
        p.op("act", lambda e, pa=pa: e.activation(out=sq[:, :], in_=pa[:, :], func=AF.Square), reads=[ka], writes=["bsq"])
        p.op("pe", lambda e: e.matmul(ssq[:, :], lhsT=G["ones"][:, :], rhs=sq[:, :], start=True, stop=True),
             reads=["bsq", "ones"], writes=["bssq"])
        p.op("act", lambda e: e.activation(out=rr[:, :], in_=ssq[:, :], func=AF.Sqrt, scale=1.0 / 128, bias=EPS),
             reads=["bssq"], writes=["brr"])
        p.op("dve", lambda e: e.reciprocal(rr[:, :], rr[:, :]), reads=["brr"], writes=["brr"])
        p.op("dve", lambda e, pa=pa, g=g: e.tensor_tensor(out=CKV[:, g * 512:(g + 1) * 512], in0=pa[:, :], in1=rr[:, :],
                                                          op=ALU.mult), reads=[ka, "brr"], writes=["CKV"])
        p.op("dve", lambda e, pb=pb, ck=ck: e.tensor_tensor(out=t1[64:96, :], in0=pb[64:96, :], in1=ck[64:96, :], op=ALU.mult),
             reads=[kb, ckk], writes=["bt1"])
        p.op("dve", lambda e, pc=pc, sk_=sk_: e.tensor_tensor(out=t2[64:96, :], in0=pc[64:96, :], in1=sk_[64:96, :], op=ALU.mult),
             reads=[kc_, skk], writes=["bt2"])
        p.op("pool", lambda e, g=g: e.tensor_tensor(out=KT[64:96, g * 512:(g + 1) * 512], in0=t1[64:96, :], in1=t2[64:96, :],
                                                    op=ALU.add), reads=["bt1", "bt2"], writes=["KT"])
    p.pop()

    p.push()
    Vh = p.sb("Vh", [128, SF // 128, 65], BF16)
    QT = p.sb("QT", [128, OWN], BF16)
    cqt = p.sb("cqt", [128, OWN], F32)
    sqt = p.sb("sqt", [128, OWN], F32)
    wkv = p.sb("wkv", [128, 768], BF16)
    wkvf = p.sb("wkvf", [128, 768], F32)
    wqf = p.sb("wqf", [128, 2, 576], F32)
    wq = p.sb("wq", [128, 2, 6, 96], BF16)
    wqr = p.sb("wqr", [128, 2, 6, 96], BF16)
    qn = p.sb("qn", [128, 2], F32)
    kvn = p.sb("kvn", [128, 1], F32)
    q1 = p.sb("q1", [128, 512], F32)
    q2 = p.sb("q2", [128, 512], F32)
    NSB = 3
    PTs = [p.sb("bPT%d" % i, [128, 1024], BF16) for i in range(NSB)]
    miscps = p.ps("bmisc", [128, 512])
    bufs = dict(osb=[p.sb("bosb%d" % i, [128, 512], F32) for i in range(2)], rden=p.sb("brden", [128, 512], F32),
                ysq=p.sb("bysq", [128, 512], F32), ybf=[p.sb("bybf%d" % i, [128, 512], BF16) for i in range(2)],
                den=miscps, ssps=miscps, denkey="bmisc", sskey="bmisc")
    Sps = [p.ps("bS%d" % i, [128, 1024]) for i in range(NSB)]
    Ops = [p.ps("bO%d" % i, [128, 512]) for i in range(1)]
    pend = []
    p.op("sp", lambda e: e.dma_start(out=wkvf[:], in_=W["w_kv_up"]), writes=["wkvf"], dma="bw")
    p.op("sp", lambda e: e.dma_start(out=wqf[:, 0, :], in_=W["w_q_up"][0:128, :]), writes=["wqf"], dma="bw")
    p.op("sp", lambda e: e.dma_start(out=wqf[0:64, 1, :], in_=W["w_q_up"][128:192, :]), writes=["wqf"], dma="bw")
    p.op("sp", lambda e: e.dma_start(out=qn[:], in_=W["qn"]), writes=["qn"], dma="bw")
    p.op("sp", lambda e: e.dma_start(out=kvn[:], in_=W["kvn"]), writes=["kvn"], dma="bw")
    p.op("sp", lambda e: e.dma_start(out=cqt[64:96, :], in_=io["cqt"]), writes=["cqt"], dma="bw")
    p.op("sp", lambda e: e.dma_start(out=sqt[64:96, :], in_=io["sqt"]), writes=["sqt"], dma="bw")
    p.op("dve", lambda e: e.tensor_scalar(wkv[:, :], wkvf[:, :], kvn[:, 0:1], None, op0=ALU.mult), reads=["wkvf", "kvn"],
         writes=["wkv"])
    p.op("dve", lambda e: e.memset(wq[:], 0.0), writes=["wq"])
    p.op("dve", lambda e: e.memset(wqr[:], 0.0), writes=["wqr"])
    for cc, np_ in ((0, 128), (1, 64)):
        wv = wqf[0:np_, cc, :].rearrange("p (h c) -> p h c", h=6)
        p.op("dve", lambda e, cc=cc, np_=np_, wv=wv: e.tensor_scalar(wq[0:np_, cc, :, :], wv, qn[0:np_, cc:cc + 1], None,
                                                                      op0=ALU.mult), reads=["wqf", "qn", "wq"], writes=["wq"])
        p.op("dve", lambda e, cc=cc, np_=np_, wv=wv: e.tensor_scalar(wqr[0:np_, cc, :, 64:80], wv[:, :, 80:96],
                                                                      qn[0:np_, cc:cc + 1], None, op0=ALU.mult),
             reads=["wqf", "qn", "wqr"], writes=["wqr"])
        p.op("dve", lambda e, cc=cc, np_=np_, wv=wv: e.tensor_scalar(wqr[0:np_, cc, :, 80:96], wv[:, :, 64:80],
                                                                      qn[0:np_, cc:cc + 1], None, op0=ALU.mult),
             reads=["wqf", "qn", "wqr"], writes=["wqr"])
    p.op("pool", lambda e: e.memset(Vh[:, :, 64:65], 1.0), writes=["Vh"])
    it = 0
    gi = 0
    nkt = SF // 128
    for h in range(6):
        for g2 in range(SF // 1024):
            S, sk = Sps[gi % NSB], "bS%d" % (gi % NSB)
            for u in range(2):
                g = 2 * g2 + u
                p.op("pe", lambda e, S=S, u=u, g=g, h=h: e.matmul(S[0:64, u * 512:(u + 1) * 512], lhsT=wkv[:, h * 128:h * 128 + 64],
                                                                  rhs=CKV[:, g * 512:(g + 1) * 512], start=True, stop=True),
                     reads=["wkv", "CKV"], writes=[sk])
            p.op("dve", lambda e, S=S, g2=g2: e.tensor_copy(KT[0:64, g2 * 1024:(g2 + 1) * 1024], S[0:64, :]),
                 reads=[sk], writes=["KT"])
            gi += 1
        for t16 in range(nkt // 16):
            S, sk = Sps[gi % NSB], "bS%d" % (gi % NSB)
            for u in range(16):
                t = t16 * 16 + u
                p.op("pe", lambda e, S=S, u=u, t=t, h=h: e.matmul(S[:, u * 64:(u + 1) * 64], lhsT=CKV[:, t * 128:(t + 1) * 128],
                                                                  rhs=wkv[:, h * 128 + 64:h * 128 + 128], start=True, stop=True),
                     reads=["wkv", "CKV"], writes=[sk])
            p.op("dve", lambda e, S=S, t16=t16: e.tensor_copy(Vh[:, t16 * 16:(t16 + 1) * 16, 0:64],
                                                               S[:, :].rearrange("p (t c) -> p t c", c=64)),
                 reads=[sk], writes=["Vh"])
            gi += 1
        for qb in range(NQB):
            S, sk = Sps[gi % NSB], "bS%d" % (gi % NSB)
            for u, wsrc in ((0, wq), (1, wqr)):
                p.op("pe", lambda e, S=S, u=u, wsrc=wsrc, h=h, qb=qb: e.matmul(S[0:96, u * 512:(u + 1) * 512], lhsT=wsrc[:, 0, h, :],
                                                                               rhs=CQ[:, 0, qb * 512:(qb + 1) * 512],
                                                                               start=True, stop=False),
                     reads=["wq", "wqr", "CQ"], writes=[sk])
                p.op("pe", lambda e, S=S, u=u, wsrc=wsrc, h=h, qb=qb: e.matmul(S[0:96, u * 512:(u + 1) * 512], lhsT=wsrc[0:64, 1, h, :],
                                                                               rhs=CQ[0:64, 1, qb * 512:(qb + 1) * 512],
                                                                               start=False, stop=True),
                     reads=["wq", "wqr", "CQ"], writes=[sk])
            p.op("dve", lambda e, S=S, qb=qb: e.tensor_copy(QT[0:64, qb * 512:(qb + 1) * 512], S[0:64, 0:512]),
                 reads=[sk], writes=["QT"])
            p.op("dve", lambda e, S=S, qb=qb: e.tensor_tensor(out=q1[64:96, :], in0=S[64:96, 0:512],
                                                              in1=cqt[64:96, qb * 512:(qb + 1) * 512], op=ALU.mult),
                 reads=[sk, "cqt"], writes=["q1"])
            p.op("dve", lambda e, S=S, qb=qb: e.tensor_tensor(out=q2[64:96, :], in0=S[64:96, 512:1024],
                                                              in1=sqt[64:96, qb * 512:(qb + 1) * 512], op=ALU.mult),
                 reads=[sk, "sqt"], writes=["q2"])
            p.op("pool", lambda e, qb=qb: e.tensor_tensor(out=QT[64:96, qb * 512:(qb + 1) * 512], in0=q1[64:96, :],
                                                          in1=q2[64:96, :], op=ALU.add), reads=["q1", "q2"], writes=["QT"])
            gi += 1
        for qb in range(NQB):
            ops, okey = Ops[0], "bO0"
            ngrp = nkt // 2

            def qk(gidx, i):
                S, sk = Sps[gidx % NSB], "bS%d" % (gidx % NSB)
                for u in range(2):
                    kt = 2 * i + u
                    p.op("pe", lambda e, S=S, u=u, kt=kt: e.matmul(S[:, u * 512:(u + 1) * 512],
                                                                   lhsT=KT[0:96, kt * 128:(kt + 1) * 128],
                                                                   rhs=QT[0:96, qb * 512:(qb + 1) * 512], start=True, stop=True),
                         reads=["KT", "QT"], writes=[sk])

            qk(gi, 0)
            qk(gi + 1, 1)
            for i in range(ngrp):
                if i + 2 < ngrp:
                    qk(gi + 2, i + 2)
                S, sk = Sps[gi % NSB], "bS%d" % (gi % NSB)
                PT, pk = PTs[gi % NSB], "bPT%d" % (gi % NSB)
                p.op("act", lambda e, S=S, PT=PT: e.activation(out=PT[:, :], in_=S[:, :], func=AF.Exp, scale=B_SCALE),
                     reads=[sk], writes=[pk])
                for u in range(2):
                    kt = 2 * i + u
                    p.op("pe", lambda e, PT=PT, u=u, kt=kt, ops=ops, i=i: e.matmul(
                        ops[0:65, :], lhsT=Vh[:, kt, 0:65], rhs=PT[:, u * 512:(u + 1) * 512],
                        start=(i == 0 and u == 0), stop=(i == ngrp - 1 and u == 1)),
                        reads=["Vh", pk], writes=[okey])
                gi += 1
                if i == 2 and pend:
                    pend.pop()()
            pend.append(attn_post(p, G, io, ops, okey, 384 + h * 64, qb, 6 + h, bufs, it))
            it += 1
        while pend:
            pend.pop()()
    p.pop()
    p.pop()

    p.push()
    wf1 = p.sb("wf1", [128, 8, DFF], BF16)
    wf2 = p.sb("wf2", [128, 32, D], BF16)
    p.push()
    wo = p.sb("wo", [128, 8, D], BF16)
    wof = p.sb("wof", [128, D], F32)
    mixn = p.sb("mixn", [128, 8], F32)
    g1 = p.sb("g1", [128, D], F32)
    b1 = p.sb("b1", [128, D], F32)
    yTs = [p.sb("yT%d" % i, [128, 8, 512], BF16) for i in range(2)]
    xrs = [p.sb("xr%d" % i, [128, D], F32) for i in range(2)]
    accs = [p.sb("dacc%d" % i, [128, D], F32) for i in range(2)]
    x1s = [p.sb("x1s%d" % i, [128, D], F32) for i in range(2)]
    rst = p.sb("rst", [128, 3, NT], F32)
    sst = p.sb("sst", [128, 3, NT], F32)
    st12s = [p.sb("st12_%d" % i, [128, 12], F32) for i in range(2)]
    mvs = [p.sb("dmv%d" % i, [128, 2], F32) for i in range(2)]
    rss = [p.sb("drs%d" % i, [128, 1], F32) for i in range(2)]
    accps = [p.ps("accps%d" % i, [128, 1024]) for i in range(2)]
    p.op("sp", lambda e: e.dma_start(out=mixn[:], in_=W["mixn"]), writes=["mixn"], dma="dw")
    p.op("sp", lambda e: e.dma_start(out=g1[:], in_=W["ln1_g"].partition_broadcast(128)), writes=["g1"], dma="dw")
    p.op("sp", lambda e: e.dma_start(out=b1[:], in_=W["ln1_b"].partition_broadcast(128)), writes=["b1"], dma="dw")
    for kc in range(8):
        p.op("sp", lambda e, kc=kc: e.dma_start(out=wof[:], in_=W["w_out"][kc * 128:(kc + 1) * 128, :]), writes=["wof"], dma="dwo")
        p.op("dve", lambda e, kc=kc: e.tensor_scalar(wo[:, kc, :], wof[:, :], mixn[:, kc:kc + 1], None, op0=ALU.mult),
             reads=["wof", "mixn"], writes=["wo"])
    for kc in range(8):
        p.op("pool", lambda e, kc=kc: e.dma_start(out=wf1[:, kc, :], in_=W["w_ff1"][kc * 128:(kc + 1) * 128, :]),
             writes=["wf1"], dma="wf1")
    for c4 in range(8):
        p.op("pool", lambda e, c4=c4: e.dma_start(out=wf2[:, c4 * 4:(c4 + 1) * 4, :],
                                                  in_=W["w_ff2"][c4 * 512:(c4 + 1) * 512, :].rearrange("(c p) d -> p c d", p=128)),
             writes=["wf2"], dma="wf2")
    ssv = G["ss"]
    p.op("dve", lambda e: e.tensor_copy(sst[:, 0, :], ssv[:, 0, :]), reads=["ss"], writes=["sst"])
    p.op("dve", lambda e: e.tensor_copy(sst[:, 1, :], ssv[:, 6, :]), reads=["ss"], writes=["sst"])
    p.op("dve", lambda e: e.tensor_copy(sst[:, 2, :], ssv[:, 12, :]), reads=["ss"], writes=["sst"])
    for k in range(1, 6):
        p.op("dve", lambda e, k=k: e.tensor_tensor(out=sst[:, 0, :], in0=sst[:, 0, :], in1=ssv[:, k, :], op=ALU.add),
             reads=["ss", "sst"], writes=["sst"])
        p.op("dve", lambda e, k=k: e.tensor_tensor(out=sst[:, 1, :], in0=sst[:, 1, :], in1=ssv[:, 6 + k, :], op=ALU.add),
             reads=["ss", "sst"], writes=["sst"])
    for gidx, wdt in ((0, 384.0), (1, 384.0), (2, 256.0)):
        p.op("act", lambda e, gidx=gidx, wdt=wdt: e.activation(out=rst[:, gidx, :], in_=sst[:, gidx, :], func=AF.Sqrt,
                                                               scale=1.0 / wdt, bias=EPS), reads=["sst"], writes=["rst"])
    p.op("dve", lambda e: e.reciprocal(rst[:, :, :], rst[:, :, :]), reads=["rst"], writes=["rst"])
    KCG = ((0, 3), (3, 6), (6, 8))
    ai = 0
    for g in range(NQB):
        yT, yk = yTs[g % 2], "yT%d" % (g % 2)
        p.op("sp", lambda e, yT=yT, g=g: e.dma_start(out=yT[:, :, :], in_=io["YT"][:, g * 512:(g + 1) * 512].rearrange("(c p) t -> p c t", p=128)),
             writes=[yk], dma=yk)
        for j in range(4):
            ti = g * 4 + j
            xr, xk = xrs[ti % 2], "xr%d" % (ti % 2)
            x1, x1k = x1s[ti % 2], "x1s%d" % (ti % 2)
            acc, acck = accs[ti % 2], "dacc%d" % (ti % 2)
            p.op("sp", lambda e, xr=xr, ti=ti: e.dma_start(out=xr[:, :], in_=src["res"](ti)),
                 writes=[xk], dma=xk)
            for gidx, (k0, k1) in enumerate(KCG):
                ap_, ak = accps[ai % 2], "accps%d" % (ai % 2)
                ai += 1
                for half in range(2):
                    for kc in range(k0, k1):
                        p.op("pe", lambda e, ap_=ap_, half=half, kc=kc, k0=k0, k1=k1, j=j: e.matmul(
                            ap_[:, half * 512:(half + 1) * 512], lhsT=yT[:, kc, j * 128:(j + 1) * 128],
                            rhs=wo[:, kc, half * 512:(half + 1) * 512], start=(kc == k0), stop=(kc == k1 - 1)),
                            reads=[yk, "wo"], writes=[ak])
                if gidx == 0:
                    p.op("dve", lambda e, ap_=ap_, ti=ti: e.tensor_scalar(acc[:, :], ap_[:, :], rst[:, 0, ti:ti + 1], None, op0=ALU.mult),
                         reads=[ak, "rst"], writes=[acck])
                else:
                    p.op("dve", lambda e, ap_=ap_, ti=ti, gidx=gidx: e.scalar_tensor_tensor(
                        out=acc[:, :], in0=ap_[:, :], scalar=rst[:, gidx, ti:ti + 1], in1=acc[:, :], op0=ALU.mult, op1=ALU.add),
                        reads=[ak, "rst", acck], writes=[acck])
            p.op("dve", lambda e, xr=xr: e.scalar_tensor_tensor(out=acc[:, :], in0=xr[:, :], scalar=ALPHA, in1=acc[:, :],
                                                                op0=ALU.mult, op1=ALU.add), reads=[xk, acck], writes=[acck])
            layer_norm(p, acc, acck, x1, x1k, g1, "g1", b1, "b1", st12s[ti % 2], mvs[ti % 2], rss[ti % 2], "d1_%d" % (ti % 2))
            p.op("sp", lambda e, x1=x1, ti=ti: e.dma_start(out=io["X1"][ti * 128:(ti + 1) * 128, :], in_=x1[:, :]),
                 reads=[x1k], dma="x1o%d" % (ti % 2))
    p.pop()

    p.push()
    b1T = p.sb("b1T", [128, 32], F32)
    g2 = p.sb("g2", [128, D], F32)
    b2 = p.sb("b2", [128, D], F32)
    bf2 = p.sb("bf2", [128, D], F32)
    x1f = [p.sb("x1f%d" % i, [128, 2, D], F32) for i in range(2)]
    x1b = p.sb("x1b", [128, 2, D], BF16)
    x1Ts = [p.sb("x1T%d" % i, [128, 8, 256], BF16) for i in range(2)]
    hidT = p.sb("hidT", [128, 32, 256], BF16)
    rl = [p.sb("rl%d" % i, [128, 256], F32) for i in range(2)]
    t2ss = [p.sb("t2s%d" % i, [128, D], F32) for i in range(1)]
    outs = [p.sb("outs%d" % i, [128, D], F32) for i in range(2)]
    outb = [p.sb("outb%d" % i, [128, D], BF16) for i in range(2)]
    st12s = [p.sb("st12b%d" % i, [128, 12], F32) for i in range(2)]
    mvs = [p.sb("dmv2_%d" % i, [128, 2], F32) for i in range(2)]
    rss = [p.sb("drs2_%d" % i, [128, 1], F32) for i in range(2)]
    trps = [p.ps("dtrps%d" % i, [128, 1024], BF16) for i in range(2)]
    hps = [p.ps("hps%d" % i, [128, 512]) for i in range(2)]
    fps = [p.ps("fps%d" % i, [128, 1024]) for i in range(2)]
    p.op("sp", lambda e: e.dma_start(out=b1T[:], in_=W["b1T"]), writes=["b1T"], dma="dw2")
    p.op("sp", lambda e: e.dma_start(out=g2[:], in_=W["ln2_g"].partition_broadcast(128)), writes=["g2"], dma="dw2")
    p.op("sp", lambda e: e.dma_start(out=b2[:], in_=W["ln2_b"].partition_broadcast(128)), writes=["b2"], dma="dw2")
    p.op("sp", lambda e: e.dma_start(out=bf2[:], in_=W["b_ff2"].partition_broadcast(128)), writes=["bf2"], dma="dw2")
    ng2 = OWN // 256

    def prep(g):
        xf_, xfk = x1f[g % 2], "x1f%d" % (g % 2)
        p.op("sp", lambda e: e.dma_start(out=xf_[:, :, :], in_=io["X1"][g * 256:(g + 1) * 256, :].rearrange("(j p) d -> p j d", p=128)),
             writes=[xfk], dma=xfk)
        p.op("pool", lambda e: e.tensor_copy(x1b[:, :, :], xf_[:, :, :]), reads=[xfk], writes=["x1b"])
        transpose_group(p, G, x1b, "x1b", x1Ts[g % 2], "x1T%d" % (g % 2), trps, n_tok_tiles=2)

    prep(0)
    hi = 0
    fi = 0
    for g in range(ng2):
        xf_, xfk = x1f[g % 2], "x1f%d" % (g % 2)
        x1T, x1Tk = x1Ts[g % 2], "x1T%d" % (g % 2)
        for fc in range(32):
            hp_, hk = hps[hi % 2], "hps%d" % (hi % 2)
            r_, rk = rl[hi % 2], "rl%d" % (hi % 2)
            hi += 1
            for kc in range(8):
                p.op("pe", lambda e, hp_=hp_, kc=kc, fc=fc: e.matmul(hp_[:, 0:256], lhsT=wf1[:, kc, fc * 128:(fc + 1) * 128],
                                                                     rhs=x1T[:, kc, :], start=(kc == 0), stop=(kc == 7)),
                     reads=["wf1", x1Tk], writes=[hk])
            p.op("act", lambda e, hp_=hp_, r_=r_, fc=fc: e.activation(out=r_[:, :], in_=hp_[:, 0:256], func=AF.Relu,
                                                                      bias=b1T[:, fc:fc + 1], scale=1.0),
                 reads=[hk, "b1T"], writes=[rk])
            p.op("pool" if fc % 2 else "dve", lambda e, r_=r_, fc=fc: e.tensor_tensor(out=hidT[:, fc, :], in0=r_[:, :], in1=r_[:, :], op=ALU.mult),
                 reads=[rk], writes=["hidT"])
        if g + 1 < ng2:
            prep(g + 1)
        for j in range(2):
            ti = g * 2 + j
            fp_, fk = fps[fi % 2], "fps%d" % (fi % 2)
            o_, ok_ = outs[fi % 2], "outs%d" % (fi % 2)
            t2s, t2k = t2ss[0], "t2s0"
            par = fi % 2
            fi += 1
            for half in range(2):
                for fc in range(32):
                    p.op("pe", lambda e, fp_=fp_, half=half, fc=fc, j=j: e.matmul(
                        fp_[:, half * 512:(half + 1) * 512], lhsT=hidT[:, fc, j * 128:(j + 1) * 128],
                        rhs=wf2[:, fc, half * 512:(half + 1) * 512], start=(fc == 0), stop=(fc == 31)),
                        reads=["hidT", "wf2"], writes=[fk])
            p.op("dve", lambda e, fp_=fp_: e.tensor_tensor(out=t2s[:, :], in0=fp_[:, :], in1=bf2[:, :], op=ALU.add),
                 reads=[fk, "bf2"], writes=[t2k])
            p.op("dve", lambda e, xf_=xf_, j=j: e.scalar_tensor_tensor(out=t2s[:, :], in0=xf_[:, j, :], scalar=ALPHA, in1=t2s[:, :],
                                                                       op0=ALU.mult, op1=ALU.add), reads=[xfk, t2k], writes=[t2k])
            layer_norm(p, t2s, t2k, o_, ok_, g2, "g2", b2, "b2", st12s[par], mvs[par], rss[par], "d2_%d" % par)
            p.op("sp", lambda e, o_=o_, ti=ti: e.dma_start(out=dst["f32"][ti * 128:(ti + 1) * 128, :], in_=o_[:, :]),
                 reads=[ok_], dma=dst["tag"])
            if dst["bf16"] is not None:
                ob_, obk = outb[par], "outb%d" % par
                p.op("pool", lambda e, o_=o_, ob_=ob_: e.tensor_copy(ob_[:, :], o_[:, :]), reads=[ok_], writes=[obk])
                p.op("sp", lambda e, ob_=ob_, ti=ti: e.dma_start(out=dst["bf16"][ti * 128:(ti + 1) * 128, :], in_=ob_[:, :]),
                     reads=[obk], dma="xbc")
    p.pop()
    p.pop()


def layer_norm(p, src, skey, dstt, dkey, g, gk, b, bk, st12, mv, rs, tg):
    k6, kmv, krs = "st12" + tg, "mv" + tg, "rs" + tg
    p.op("dve", lambda e: e.bn_stats(st12[:, 0:6], src[:, 0:512]), reads=[skey], writes=[k6])
    p.op("dve", lambda e: e.bn_stats(st12[:, 6:12], src[:, 512:1024]), reads=[skey], writes=[k6])
    p.op("dve", lambda e: e.bn_aggr(mv[:, :], st12[:, :]), reads=[k6], writes=[kmv])
    p.op("act", lambda e: e.activation(out=rs[:, :], in_=mv[:, 1:2], func=AF.Sqrt, scale=1.0, bias=EPS), reads=[kmv], writes=[krs])
    p.op("dve", lambda e: e.reciprocal(rs[:, :], rs[:, :]), reads=[krs], writes=[krs])
    p.op("dve", lambda e: e.tensor_scalar(src[:, :], src[:, :], mv[:, 0:1], rs[:, 0:1], op0=ALU.subtract, op1=ALU.mult),
         reads=[skey, kmv, krs], writes=[skey])
    p.op("pool", lambda e: e.tensor_tensor(out=src[:, :], in0=src[:, :], in1=g[:, :], op=ALU.mult), reads=[skey, gk], writes=[skey])
    p.op("pool", lambda e: e.tensor_tensor(out=dstt[:, :], in0=src[:, :], in1=b[:, :], op=ALU.add), reads=[skey, bk], writes=[dkey])


_CACHE = {}


def host_weights(inp, layers):
    f = lambda a: np.ascontiguousarray(a, dtype=np.float32)
    L = list(layers)
    w = {}
    w["w_in"] = f(inp["w_in"][L])
    w["w_q_up"] = f(inp["w_q_up"][L])
    w["w_kv_up"] = f(inp["w_kv_up"][L])
    qn = np.zeros((len(L), 256), np.float32)
    qn[:, :192] = inp["q_norm"][L]
    w["qn"] = f(qn.reshape(len(L), 2, 128).transpose(0, 2, 1))
    w["kvn"] = f(inp["kv_norm"][L].reshape(len(L), 128, 1))
    w["sg_g"] = f(inp["sgu_ln_g"][L])
    w["sg_b"] = f(inp["sgu_ln_b"][L])
    w["sg_wT"] = f(np.transpose(inp["sgu_w"][L], (0, 1, 3, 2)))
    w["sg_bT"] = f(np.transpose(inp["sgu_b"][L], (0, 2, 1)))
    w["mixn"] = f(inp["mix_norm"][L].reshape(len(L), 8, 128).transpose(0, 2, 1))
    w["w_out"] = f(inp["w_out"][L])
    w["ln1_g"] = f(inp["ln1_g"][L])
    w["ln1_b"] = f(inp["ln1_b"][L])
    w["w_ff1"] = f(inp["w_ff1"][L])
    w["b1T"] = f(inp["b_ff1"][L].reshape(len(L), 32, 128).transpose(0, 2, 1))
    w["w_ff2"] = f(inp["w_ff2"][L])
    w["b_ff2"] = f(inp["b_ff2"][L])
    w["ln2_g"] = f(inp["ln2_g"][L])
    w["ln2_b"] = f(inp["ln2_b"][L])
    return w


def core_inputs(x_b, r, own, consts):
    S = x_b.shape[0]
    lo = r * own - HALO
    xo = np.zeros((own + 2 * HALO, D), np.float32)
    vm = np.zeros((own + 2 * HALO,), np.float32)
    a, b = max(lo, 0), min(lo + own + 2 * HALO, S)
    xo[a - lo:b - lo] = x_b[a:b]
    vm[a - lo:b - lo] = 1.0
    ct, st = consts["rope"]
    hs = np.zeros((128, 8), np.float32)
    if r - 1 >= 0:
        hs[:, r - 1] = 1.0
    if r + 1 < S // own:
        hs[:, 4 + r + 1] = 1.0
    d = dict(hsel=hs, xo=xo, xf=np.ascontiguousarray(x_b, dtype=np.float32),
             vmask=np.ascontiguousarray(vm.reshape(-1, 128).T), mtab=consts["mtab"],
             ckt=ct, skt=st, cqt=np.ascontiguousarray(ct[:, r * own:(r + 1) * own]),
             sqt=np.ascontiguousarray(st[:, r * own:(r + 1) * own]))
    return d


def run_layers(x, inp, own, n_groups_per_batch, fused_depth=1, dbg=False):
    B, S, _ = x.shape
    key = (own, S, fused_depth, dbg, B)
    if key not in _CACHE:
        _CACHE[key] = build_program(Cfg(own, S, depth=fused_depth, dbg=dbg, ncores=B * n_groups_per_batch))
    nc, stats = _CACHE[key]
    consts = dict(mtab=mask_table(), rope=rope_tables(S))
    depth = inp["w_in"].shape[0]
    cur = np.asarray(x, dtype=np.float32)
    extra = None
    for l0 in range(0, depth, fused_depth):
        w = host_weights(inp, range(l0, l0 + fused_depth))
        in_maps = []
        for c in range(B * n_groups_per_batch):
            b, r = c // n_groups_per_batch, c % n_groups_per_batch
            d = core_inputs(cur[b], r, own, consts)
            if fused_depth == 1:
                d.pop("hsel")
            d.update(w)
            in_maps.append(d)
        res = run_bass_kernel_spmd(nc, in_maps, core_ids=list(range(len(in_maps))))
        outs = [r_["out"] for r_ in res.results]
        cur = np.stack([np.concatenate(outs[b * n_groups_per_batch:(b + 1) * n_groups_per_batch], 0) for b in range(B)], 0)
        extra = res.results
    return cur, extra


def kernel(**inputs):
    x = np.asarray(inputs["x"], dtype=np.float32)
    inp = {k: np.asarray(v, dtype=np.float32) for k, v in inputs.items() if k != "x"}
    out, _ = run_layers(x, inp, own=x.shape[1] // 4, n_groups_per_batch=4, fused_depth=inp["w_in"].shape[0])
    return out.astype(np.float32)
```

```python
import types
import numpy as np
import ml_dtypes
import concourse.bass as bass
import concourse.mybir as mybir
from concourse.bass_utils import run_bass_kernel_spmd

F32 = mybir.dt.float32
BF16 = mybir.dt.bfloat16
I32 = mybir.dt.int32
AF = mybir.ActivationFunctionType
ALU = mybir.AluOpType
AX = mybir.AxisListType

ENGS = ("pe", "act", "dve", "pool", "sp")
SEM_LIMIT = 20000

D = 1024
PIN = 2016
HALO = 1024
DFF = 4096
EPS = 1e-5
ALPHA = float((2 * 2) ** 0.25)
TABW = 2944
A_SCALE = 0.125
B_SCALE = float(96 ** -0.5)
C_GELU = 0.044715
K_GELU = float(np.sqrt(2.0 / np.pi))


def _freeze(fn):
    if fn is None or fn.__closure__ is None:
        return fn
    cells = []
    for c in fn.__closure__:
        try:
            cells.append(types.CellType(c.cell_contents))
        except ValueError:
            cells.append(c)
    return types.FunctionType(fn.__code__, fn.__globals__, fn.__name__, fn.__defaults__, tuple(cells))


class Prog:
    def __init__(self, nc):
        self.nc = nc
        self.ops = {e: [] for e in ENGS}
        self.state = {}
        self.dma_sems = {}
        self.pending = {e: [] for e in ENGS}
        self._cms = []
        self._scopes = []

    def _reg(self, cm):
        t = cm.__enter__()
        (self._scopes[-1] if self._scopes else self._cms).append(cm)
        return t

    def sem(self, name):
        self._n = getattr(self, "_n", 0) + 1
        cm = self.nc.semaphore("m%d_%s" % (self._n, name))
        s = cm.__enter__()
        self._cms.append(cm)
        return s

    def sb(self, name, shape, dt):
        self._n = getattr(self, "_n", 0) + 1
        return self._reg(self.nc.sbuf_tensor("sb%d_%s" % (self._n, name), list(shape), dt))

    def ps(self, name, shape, dt=F32):
        self._n = getattr(self, "_n", 0) + 1
        return self._reg(self.nc.psum_tensor("ps%d_%s" % (self._n, name), list(shape), dt))

    def push(self):
        self._scopes.append([])

    def pop(self):
        self.barrier()
        for cm in reversed(self._scopes.pop()):
            cm.__exit__(None, None, None)

    def close(self):
        for cm in reversed(self._cms):
            cm.__exit__(None, None, None)
        self._cms = []

    def barrier(self):
        evs = []
        for e in ENGS:
            lst = self.ops[e]
            for i in range(len(lst) - 1, -1, -1):
                if lst[i]["dma"] is None and lst[i]["fn"] is not None:
                    evs.append(("eng", e, i))
                    break
        for tag, ds in self.dma_sems.items():
            evs.append(("dma", ds[0], ds[1]))
        for e in ENGS:
            self.pending[e].extend(evs)
        self.state = {}

    def op(self, eng, fn, reads=(), writes=(), dma=None, inc=16):
        fn = _freeze(fn)
        deps = []
        for k in reads:
            st = self.state.get(k)
            if st and st[0] is not None:
                deps.append((st[0], True))
        for k in writes:
            st = self.state.get(k)
            if st:
                if st[0] is not None:
                    deps.append((st[0], False))
                for r in st[1]:
                    deps.append((r, False))
        lst = self.ops[eng]
        idx = len(lst)
        if dma is not None:
            if dma not in self.dma_sems:
                self.dma_sems[dma] = [self.sem("d_" + dma), 0]
            ds = self.dma_sems[dma]
            ds[1] += inc
            ev = ("dma", ds[0], ds[1])
        else:
            ev = ("eng", eng, idx)
        fdeps = []
        for d, raw in deps:
            if d[0] == "eng" and d[1] == eng and dma is None:
                if eng == "pe" or not raw:
                    continue
            if d[0] == "dma":
                for ds_ in self.dma_sems.values():
                    if ds_[0] is d[1]:
                        cur = ds_[1] - (inc if (dma is not None and self.dma_sems[dma][0] is d[1]) else 0)
                        d = ("dma", d[1], max(d[2], cur))
            fdeps.append(d)
        for d in self.pending[eng]:
            if d[0] == "eng" and d[1] == eng and eng == "pe":
                continue
            fdeps.append(d)
        self.pending[eng] = []
        lst.append(dict(fn=fn, deps=fdeps, dma=dma, ev=ev, ms=False, inc=inc))
        for k in reads:
            st = self.state.setdefault(k, [None, []])
            st[1].append(ev)
        for k in writes:
            self.state[k] = [ev, []]
        return ev

    def wait_all_dma(self, eng, tags):
        deps = []
        for t in tags:
            ds = self.dma_sems[t]
            deps.append(("dma", ds[0], ds[1]))
        self.ops[eng].append(dict(fn=None, deps=deps, dma=None, ev=None, ms=False))

    def emit(self):
        nc = self.nc
        for e in ENGS:
            for o in self.ops[e]:
                for d in o["deps"]:
                    if d[0] == "eng":
                        self.ops[d[1]][d[2]]["ms"] = True
        msmap = {}
        for e in ENGS:
            cur = None
            cnt = 0
            for i, o in enumerate(self.ops[e]):
                if o["ms"]:
                    if cur is None or cnt >= SEM_LIMIT:
                        cur = self.sem("s_%s_%d" % (e, i))
                        cnt = 0
                    cnt += 1
                    msmap[(e, i)] = (cur, cnt)
                    o["inc"] = cur
        stats = {}

        def run(e, eng):
            waited = {}
            nw = 0
            for i, o in enumerate(self.ops[e]):
                for d in o["deps"]:
                    if d[0] == "eng":
                        sem, val = msmap[(d[1], d[2])]
                    else:
                        sem, val = d[1], d[2]
                    key = id(sem)
                    if waited.get(key, 0) >= val:
                        continue
                    waited[key] = val
                    eng.wait_ge(sem, val)
                    nw += 1
                if o["fn"] is None:
                    continue
                ins = o["fn"](eng)
                if o["dma"] is not None:
                    ins.then_inc(o["ev"][1], o.get("inc", 16))
                elif o["ms"]:
                    ins.then_inc(o["inc"], 1)
            stats[e] = (len(self.ops[e]), nw)

        with nc.Block() as block:
            @block.tensor
            def _(eng):
                run("pe", eng)

            @block.scalar
            def _(eng):
                run("act", eng)

            @block.vector
            def _(eng):
                run("dve", eng)

            @block.gpsimd
            def _(eng):
                run("pool", eng)

            @block.sync
            def _(eng):
                run("sp", eng)
        self.stats = stats
        return stats


def mask_table():
    slopes = (2.0 ** (-8.0 * np.arange(1, 7) / 6)).astype(np.float32)
    pp = np.arange(128)[:, None]
    col = np.arange(TABW)[None, :]
    delta = pp - col + 1408
    ad = np.abs(delta)
    c = (ad <= 64).astype(np.float32) + ((delta % 4 == 0) & (ad <= 256)).astype(np.float32) \
        + ((delta % 16 == 0) & (ad <= 1024)).astype(np.float32)
    tab = np.zeros((128, 6, TABW), np.float32)
    for h in range(6):
        tab[:, h, :] = c * np.exp(-(slopes[h] * ad.astype(np.float32)).astype(np.float32))
    return tab.astype(ml_dtypes.bfloat16)


def rope_tables(S):
    inv_freq = (10000.0 ** (-np.arange(0, 32, 2, dtype=np.float32) / 32)).astype(np.float32)
    ang = (np.arange(S, dtype=np.float32)[:, None] * inv_freq[None, :]).astype(np.float32)
    cos = np.cos(ang).astype(np.float32).T
    sin = np.sin(ang).astype(np.float32).T
    ct = np.concatenate([cos, cos], 0)
    st = np.concatenate([-sin, sin], 0)
    return np.ascontiguousarray(ct), np.ascontiguousarray(st)


class Cfg:
    def __init__(self, own, sf, depth=1, dbg=False, ncores=8):
        self.ncores = ncores
        self.OWN = own
        self.SF = sf
        self.OH = own + 2 * HALO
        self.NT = own // 128
        self.NQB = own // 512
        self.NG_OH = self.OH // 512
        self.NG_F = sf // 512
        self.NKT_OH = self.OH // 128
        self.NKT_F = sf // 128
        self.depth = depth
        self.dbg = dbg


W_NAMES = [
    ("w_in", [D, PIN]), ("w_q_up", [192, 576]), ("w_kv_up", [128, 768]), ("qn", [128, 2]), ("kvn", [128, 1]),
    ("sg_g", [256]), ("sg_b", [256]), ("sg_wT", [4, 128, 128]), ("sg_bT", [128, 4]), ("mixn", [128, 8]),
    ("w_out", [D, D]), ("ln1_g", [D]), ("ln1_b", [D]), ("w_ff1", [D, DFF]), ("b1T", [128, 32]),
    ("w_ff2", [DFF, D]), ("b_ff2", [D]), ("ln2_g", [D]), ("ln2_b", [D]),
]


def build_program(cfg):
    nc = bass.Bass("TRN2", target_bir_lowering=False)
    OWN, SF, OH, NT = cfg.OWN, cfg.SF, cfg.OH, cfg.NT
    io = {}
    io["xo"] = nc.dram_tensor("xo", [OH, D], F32, kind="ExternalInput").ap()
    io["xf"] = nc.dram_tensor("xf", [SF, D], F32, kind="ExternalInput").ap()
    io["vmask"] = nc.dram_tensor("vmask", [128, OH // 128], F32, kind="ExternalInput").ap()
    io["mtab"] = nc.dram_tensor("mtab", [128, 6, TABW], BF16, kind="ExternalInput").ap()
    io["ckt"] = nc.dram_tensor("ckt", [32, SF], F32, kind="ExternalInput").ap()
    io["skt"] = nc.dram_tensor("skt", [32, SF], F32, kind="ExternalInput").ap()
    io["cqt"] = nc.dram_tensor("cqt", [32, OWN], F32, kind="ExternalInput").ap()
    io["sqt"] = nc.dram_tensor("sqt", [32, OWN], F32, kind="ExternalInput").ap()
    for nm, shp in W_NAMES:
        io[nm] = nc.dram_tensor(nm, [cfg.depth] + shp, F32, kind="ExternalInput").ap()
    io["out"] = nc.dram_tensor("out", [OWN, D], F32, kind="ExternalOutput").ap()
    io["YT"] = nc.dram_tensor("YT", [D, OWN], BF16, kind="Internal").ap()
    io["X1"] = nc.dram_tensor("X1", [OWN, D], F32, kind="Internal").ap()
    if cfg.depth > 1:
        io["hsel"] = nc.dram_tensor("hsel", [128, 8], F32, kind="ExternalInput").ap()
        io["XL"] = nc.dram_tensor("XL", [OWN, D], F32, kind="Internal").ap()
        io["XBc"] = nc.dram_tensor("XBc", [OWN, D], BF16, kind="Internal").ap()
        io["XG"] = nc.dram_tensor("XG", [OWN // 512, (SF // OWN) * 512, D], BF16, kind="Internal").ap()
        io["XH"] = nc.dram_tensor("XH", [2 * HALO, D], BF16, kind="Internal").ap()
    if cfg.dbg:
        io["dbg_yt"] = nc.dram_tensor("dbg_yt", [D, OWN], BF16, kind="ExternalOutput").ap()
        io["dbg_ss"] = nc.dram_tensor("dbg_ss", [128, 13 * NT], F32, kind="ExternalOutput").ap()
        io["dbg_x1"] = nc.dram_tensor("dbg_x1", [OWN, D], F32, kind="ExternalOutput").ap()

    p = Prog(nc)
    DBG["on"] = cfg.dbg
    G = {}
    G["ident"] = p.sb("ident", [128, 128], BF16)
    identf = p.sb("identf", [128, 128], F32)
    G["e65"] = p.sb("e65", [128, 64], F32)
    G["ones"] = p.sb("onesf", [128, 128], F32)
    G["ss"] = p.sb("ss", [128, 13, NT], F32)
    p.op("pool", lambda e: e.memset(identf[:], 0.0), writes=["identf"])
    p.op("pool", lambda e: e.affine_select(out=identf[:], in_=identf[:], compare_op=ALU.not_equal, fill=1.0,
                                           base=0, pattern=[[-1, 128]], channel_multiplier=1),
         reads=["identf"], writes=["identf"])
    p.op("pool", lambda e: e.tensor_copy(G["ident"][:], identf[:]), reads=["identf"], writes=["ident"])
    p.op("pool", lambda e: e.memset(G["e65"][:], 0.0), writes=["e65"])
    p.op("pool", lambda e: e.memset(G["e65"][64:65, :], 1.0), reads=["e65"], writes=["e65"])
    p.op("pool", lambda e: e.memset(G["ones"][:], 1.0), writes=["ones"])
    p.barrier()

    ngo = cfg.NG_OH
    for l in range(cfg.depth):
        W = {nm: io[nm][l] for nm, _ in W_NAMES}
        last = (l == cfg.depth - 1)
        if l == 0:
            src = dict(oh=lambda g: io["xo"][g * 512:(g + 1) * 512, :],
                       full=lambda g: io["xf"][g * 512:(g + 1) * 512, :],
                       res=lambda ti: io["xo"][HALO + ti * 128:HALO + (ti + 1) * 128, :])
        else:
            def oh2(g):
                if g < 2:
                    return io["XH"][g * 512:(g + 1) * 512, :]
                if g >= ngo - 2:
                    return io["XH"][1024 + (g - (ngo - 2)) * 512:1024 + (g - (ngo - 2) + 1) * 512, :]
                return io["XL"][(g - 2) * 512:(g - 1) * 512, :]
            src = dict(oh=oh2, full=lambda g: xg_rows(io, cfg, g * 512, 512),
                       res=lambda ti: io["XL"][ti * 128:(ti + 1) * 128, :])
        if last:
            dst = dict(f32=io["out"], bf16=None, tag="out")
        else:
            dst = dict(f32=io["XL"], bf16=io["XBc"], tag="xl")
        layer(p, cfg, G, io, W, src, dst)
        if not last:
            exchange(p, cfg, G, io)

    if cfg.dbg:
        p.op("sp", lambda e: e.dma_start(out=io["dbg_yt"], in_=io["YT"]), reads=[], dma="out")
        p.op("sp", lambda e: e.dma_start(out=io["dbg_ss"], in_=G["ss"][:].rearrange("p a b -> p (a b)")), dma="out")
        p.op("sp", lambda e: e.dma_start(out=io["dbg_x1"], in_=io["X1"]), dma="out")
    p.wait_all_dma("sp", ["out"])
    stats = p.emit()
    p.close()
    return nc, stats


DBG = {}


def dbg_dump(p, name, ap, shape, dt):
    if not DBG.get("on"):
        return
    t = p.nc.dram_tensor("dd_" + name, list(shape), dt, kind="ExternalOutput").ap()
    p.barrier()
    p.op("sp", lambda e: e.dma_start(out=t, in_=ap), dma="out")
    p.barrier()


def load_x_group(p, src_ap, xb, key, tag):
    p.op("pool", lambda e: e.dma_start(out=xb[:], in_=src_ap.rearrange("(j p) d -> p j d", p=128)),
         writes=[key], dma=tag)


def transpose_group(p, G, xb, xbkey, xT, xTkey, trps, n_tok_tiles=4):
    for kc in range(8):
        tp = trps[kc % 2]
        tkey = "tr%d" % (kc % 2)
        for j in range(n_tok_tiles):
            p.op("pe", lambda e, tp=tp, j=j, kc=kc: e.transpose(tp[:, j * 128:(j + 1) * 128],
                                                                 xb[:, j, kc * 128:(kc + 1) * 128], G["ident"][:]),
                 reads=[xbkey, "ident"], writes=[tkey])
        w = n_tok_tiles * 128
        p.op("act", lambda e, tp=tp, kc=kc, w=w: e.copy(xT[:, kc, 0:w], tp[:, 0:w]), reads=[tkey], writes=[xTkey])


def attn_post(p, G, io, ops, okey, h_row, qb, ss_idx, bufs, it):
    r = it % 2
    osb, rden, ysq, ybf, denps, ssps = bufs["osb"][r], bufs["rden"], bufs["ysq"], bufs["ybf"][r], bufs["den"], bufs["ssps"]
    ko, kr, ks, kb = "osb%d" % r, "rden", "ysq", "ybf%d" % r
    kd, kss = bufs.get("denkey", "denps"), bufs.get("sskey", "ssps")
    p.op("dve", lambda e: e.tensor_copy(osb[0:65, :], ops[0:65, :]), reads=[okey], writes=[ko])

    def deferred():
        p.op("pe", lambda e: e.matmul(denps[0:64, :], lhsT=G["e65"][0:65, :], rhs=osb[0:65, :], start=True, stop=True),
             reads=[ko, "e65"], writes=[kd])
        p.op("dve", lambda e: e.reciprocal(rden[0:64, :], denps[0:64, :]), reads=[kd], writes=[kr])
        p.op("pool", lambda e: e.tensor_tensor(out=osb[0:64, :], in0=osb[0:64, :], in1=rden[0:64, :], op=ALU.mult),
             reads=[ko, kr], writes=[ko])
        p.op("pool", lambda e: e.tensor_copy(ybf[0:64, :], osb[0:64, :]), reads=[ko], writes=[kb])
        p.op("sp", lambda e: e.dma_start(out=io["YT"][h_row:h_row + 64, qb * 512:(qb + 1) * 512], in_=ybf[0:64, :]),
             reads=[kb], dma="yt%d" % r)
        p.op("pool", lambda e: e.tensor_tensor(out=ysq[0:64, :], in0=osb[0:64, :], in1=osb[0:64, :], op=ALU.mult),
             reads=[ko], writes=[ks])
        for j in range(4):
            p.op("pe", lambda e, j=j: e.matmul(ssps[:, j:j + 1], lhsT=ysq[0:64, j * 128:(j + 1) * 128],
                                               rhs=G["ones"][0:64, 0:1], start=True, stop=True),
                 reads=[ks, "ones"], writes=[kss])
        p.op("dve", lambda e: e.tensor_copy(G["ss"][:, ss_idx, qb * 4:(qb + 1) * 4], ssps[:, 0:4]),
             reads=[kss], writes=["ss"])
    return deferred


def xg_rows(io, cfg, R0, n):
    j, q = R0 // cfg.OWN, R0 % cfg.OWN
    i, t = q // 512, q % 512
    assert t + n <= 512
    return io["XG"][i, j * 512 + t:j * 512 + t + n, :]


def exchange(p, cfg, G, io):
    OWN = cfg.OWN
    ngrp = cfg.SF // OWN
    groups = [list(range(b * ngrp, (b + 1) * ngrp)) for b in range(cfg.ncores // ngrp)]
    p.barrier()
    for i in range(OWN // 512):
        p.op("pool", lambda e, i=i: e.collective_compute("AllGather", ALU.bypass, replica_groups=groups,
                                                         ins=[io["XBc"][i * 512:(i + 1) * 512, :].opt()],
                                                         outs=[io["XG"][i].opt()]),
             dma="cc", inc=1)
    p.barrier()
    p.push()
    hsel = p.sb("hsel", [128, 8], F32)
    cands = [p.sb("cand%d" % i, [128, ngrp, D], BF16) for i in range(2)]
    hacc = p.sb("hacc", [128, D], F32)
    houts = [p.sb("hout%d" % i, [128, D], BF16) for i in range(2)]
    p.op("sp", lambda e: e.dma_start(out=hsel[:], in_=io["hsel"]), writes=["hsel"], dma="hsel")
    it = 0
    for side in range(2):
        for t in range(HALO // 128):
            cand, ck = cands[it % 2], "cand%d" % (it % 2)
            hout, hk = houts[it % 2], "hout%d" % (it % 2)
            for j in range(ngrp):
                r0 = j * OWN + (OWN - HALO if side == 0 else 0) + t * 128
                p.op("sp", lambda e, j=j, r0=r0: e.dma_start(out=cand[:, j, :], in_=xg_rows(io, cfg, r0, 128)),
                     writes=[ck], dma=ck)
            p.op("dve", lambda e: e.tensor_scalar(hacc[:, :], cand[:, 0, :], hsel[:, side * 4:side * 4 + 1], None, op0=ALU.mult),
                 reads=[ck, "hsel"], writes=["hacc"])
            for j in range(1, ngrp):
                p.op("dve", lambda e, j=j: e.scalar_tensor_tensor(out=hacc[:, :], in0=cand[:, j, :],
                                                                  scalar=hsel[:, side * 4 + j:side * 4 + j + 1], in1=hacc[:, :],
                                                                  op0=ALU.mult, op1=ALU.add), reads=[ck, "hsel", "hacc"], writes=["hacc"])
            p.op("pool", lambda e: e.tensor_copy(hout[:, :], hacc[:, :]), reads=["hacc"], writes=[hk])
            r1 = side * HALO + t * 128
            p.op("sp", lambda e, r1=r1: e.dma_start(out=io["XH"][r1:r1 + 128, :], in_=hout[:, :]), reads=[hk], dma="xh%d" % (it % 2))
            it += 1
    p.pop()


def layer(p, cfg, G, io, W, src, dst):
    OWN, SF, OH, NT, NQB = cfg.OWN, cfg.SF, cfg.OH, cfg.NT, cfg.NQB
    ident = G["ident"]

    p.push()
    CQ = p.sb("CQ", [128, 2, OWN], BF16)
    p.push()
    KaT = p.sb("KaT", [128, 3, OH], BF16)
    QaT = p.sb("QaT", [128, 3, OWN], BF16)
    Va = p.sb("Va", [128, OH // 128, 6, 65], BF16)

    p.push()
    w_in = p.sb("w_in", [128, 8, PIN], BF16)
    wsT = p.sb("wsT", [128, 4, 128], BF16)
    sg_g = p.sb("sg_g", [128, 256], F32)
    sg_b = p.sb("sg_b", [128, 256], F32)
    bsT = p.sb("bsT", [128, 4], F32)
    vm = p.sb("vm", [128, OH // 128], F32)
    xbs = [p.sb("xb%d" % i, [128, 4, D], BF16) for i in range(2)]
    xTs = [p.sb("xT%d" % i, [128, 8, 512], BF16) for i in range(2)]
    zh = p.sb("zh", [128, 512], F32)
    w1 = p.sb("w1", [128, 512], F32)
    w2 = p.sb("w2", [128, 512], F32)
    zz = p.sb("zz", [128, 512], F32)
    vn = p.sb("vn", [128, 256], F32)
    vnb = p.sb("vnb", [128, 256], BF16)
    yc = p.sb("yc", [128, 256], F32)
    ycb = p.sb("ycb", [128, 256], BF16)
    ycT = p.sb("ycT", [128, 2, 512], BF16)
    junk = p.sb("junk", [128, 256], F32)
    st6 = p.sb("st6", [128, 6], F32)
    mv = p.sb("mv", [128, 2], F32)
    rs = p.sb("rs", [128, 1], F32)
    cqs = [p.sb("cqs%d" % i, [128, 512], F32) for i in range(2)]
    cqq = [p.sb("cqq%d" % i, [128, 512], F32) for i in range(2)]
    cqr = p.sb("cqr", [128, 512], F32)
    trps = [p.ps("trps%d" % i, [128, 1024], BF16) for i in range(2)]
    pps = [p.ps("pps%d" % i, [128, 512]) for i in range(3)]
    mixps = p.ps("mixps", [128, 512])
    ssq = p.ps("ssq", [128, 512])

    for kc in range(8):
        p.op("pool", lambda e, kc=kc: e.dma_start(out=w_in[:, kc, :], in_=W["w_in"][kc * 128:(kc + 1) * 128, :]),
             writes=["w_in"], dma="w_in")
    p.op("pool", lambda e: e.dma_start(out=wsT[:], in_=W["sg_wT"].rearrange("g s t -> s g t")), writes=["wsT"], dma="wsm")
    p.op("sp", lambda e: e.dma_start(out=sg_g[:], in_=W["sg_g"].partition_broadcast(128)), writes=["sg_g"], dma="wsm2")
    p.op("sp", lambda e: e.dma_start(out=sg_b[:], in_=W["sg_b"].partition_broadcast(128)), writes=["sg_b"], dma="wsm2")
    p.op("sp", lambda e: e.dma_start(out=bsT[:], in_=W["sg_bT"]), writes=["bsT"], dma="wsm2")
    p.op("sp", lambda e: e.dma_start(out=vm[:], in_=io["vmask"]), writes=["vm"], dma="wsm2")
    for h in range(6):
        p.op("dve", lambda e, h=h: e.tensor_copy(Va[:, :, h, 64:65], vm[:].rearrange("p (t o) -> p t o", o=1)),
             reads=["vm"], writes=["Va"])

    pp_i = [0]

    def next_pps():
        i = pp_i[0] % 3
        pp_i[0] += 1
        return pps[i], "pps%d" % i

    ngo = cfg.NG_OH
    load_x_group(p, src["oh"](0), xbs[0], "xb0", "xb0")
    for g in range(ngo):
        xb, xbk = xbs[g % 2], "xb%d" % (g % 2)
        xT, xTk = xTs[g % 2], "xT%d" % (g % 2)
        if g + 1 < ngo:
            load_x_group(p, src["oh"](g + 1), xbs[(g + 1) % 2], "xb%d" % ((g + 1) % 2), "xb%d" % ((g + 1) % 2))
        transpose_group(p, G, xb, xbk, xT, xTk, trps)
        own = (g * 512 >= HALO) and (g * 512 < HALO + OWN)
        go = g - HALO // 512
        for c in range(3):
            ps_, pk = next_pps()
            for kc in range(8):
                p.op("pe", lambda e, ps_=ps_, kc=kc, c=c: e.matmul(ps_[:, :], lhsT=w_in[:, kc, 384 + c * 128:384 + (c + 1) * 128],
                                                                   rhs=xT[:, kc, :], start=(kc == 0), stop=(kc == 7)),
                     reads=["w_in", xTk], writes=[pk])
            p.op("dve", lambda e, ps_=ps_, c=c, g=g: e.tensor_copy(KaT[:, c, g * 512:(g + 1) * 512], ps_[:, :]),
                 reads=[pk], writes=["KaT"])
        if own:
            for c in range(3):
                ps_, pk = next_pps()
                for kc in range(8):
                    p.op("pe", lambda e, ps_=ps_, kc=kc, c=c: e.matmul(ps_[:, :], lhsT=w_in[:, kc, c * 128:(c + 1) * 128],
                                                                       rhs=xT[:, kc, :], start=(kc == 0), stop=(kc == 7)),
                         reads=["w_in", xTk], writes=[pk])
                p.op("dve", lambda e, ps_=ps_, c=c, go=go: e.tensor_copy(QaT[:, c, go * 512:(go + 1) * 512], ps_[:, :]),
                     reads=[pk], writes=["QaT"])
        for j in range(4):
            ps_, pk = next_pps()
            for kc in range(8):
                p.op("pe", lambda e, ps_=ps_, kc=kc, j=j: e.matmul(ps_[:, 0:384], lhsT=xT[:, kc, j * 128:(j + 1) * 128],
                                                                   rhs=w_in[:, kc, 768:1152], start=(kc == 0), stop=(kc == 7)),
                     reads=["w_in", xTk], writes=[pk])
            p.op("dve", lambda e, ps_=ps_, j=j, g=g: e.tensor_copy(Va[:, g * 4 + j, :, 0:64],
                                                                   ps_[:, 0:384].rearrange("p (h c) -> p h c", h=6)),
                 reads=[pk], writes=["Va"])
        if not own:
            continue
        psa, pka = next_pps()
        psb, pkb = next_pps()
        for kc in range(8):
            p.op("pe", lambda e, kc=kc: e.matmul(psa[:, :], lhsT=w_in[:, kc, 1152:1280], rhs=xT[:, kc, :],
                                                 start=(kc == 0), stop=(kc == 7)), reads=["w_in", xTk], writes=[pka])
        for kc in range(8):
            p.op("pe", lambda e, kc=kc: e.matmul(psb[0:64, :], lhsT=w_in[:, kc, 1280:1344], rhs=xT[:, kc, :],
                                                 start=(kc == 0), stop=(kc == 7)), reads=["w_in", xTk], writes=[pkb])
        p.op("dve", lambda e: e.tensor_copy(cqs[0][:, :], psa[:, :]), reads=[pka], writes=["cqs0"])
        p.op("dve", lambda e: e.tensor_copy(cqs[1][0:64, :], psb[0:64, :]), reads=[pkb], writes=["cqs1"])
        p.op("pool", lambda e: e.tensor_tensor(out=cqq[0][:, :], in0=cqs[0][:, :], in1=cqs[0][:, :], op=ALU.mult),
             reads=["cqs0"], writes=["cqq0"])
        p.op("pool", lambda e: e.tensor_tensor(out=cqq[1][0:64, :], in0=cqs[1][0:64, :], in1=cqs[1][0:64, :], op=ALU.mult),
             reads=["cqs1"], writes=["cqq1"])
        p.op("pe", lambda e: e.matmul(ssq[:, :], lhsT=G["ones"][:, :], rhs=cqq[0][:, :], start=True, stop=False),
             reads=["cqq0", "ones"], writes=["ssq"])
        p.op("pe", lambda e: e.matmul(ssq[:, :], lhsT=G["ones"][0:64, :], rhs=cqq[1][0:64, :], start=False, stop=True),
             reads=["cqq1", "ones"], writes=["ssq"])
        p.op("act", lambda e: e.activation(out=cqr[:, :], in_=ssq[:, :], func=AF.Sqrt, scale=1.0 / 192, bias=EPS),
             reads=["ssq"], writes=["cqr"])
        p.op("dve", lambda e: e.reciprocal(cqr[:, :], cqr[:, :]), reads=["cqr"], writes=["cqr"])
        p.op("dve", lambda e, go=go: e.tensor_tensor(out=CQ[:, 0, go * 512:(go + 1) * 512], in0=cqs[0][:, :], in1=cqr[:, :],
                                                     op=ALU.mult), reads=["cqs0", "cqr"], writes=["CQ"])
        p.op("dve", lambda e, go=go: e.tensor_tensor(out=CQ[0:64, 1, go * 512:(go + 1) * 512], in0=cqs[1][0:64, :],
                                                     in1=cqr[0:64, :], op=ALU.mult), reads=["cqs1", "cqr"], writes=["CQ"])
        for j in range(4):
            ti = go * 4 + j
            ps_, pk = next_pps()
            for kc in range(8):
                p.op("pe", lambda e, ps_=ps_, kc=kc, j=j: e.matmul(ps_[:, :], lhsT=xT[:, kc, j * 128:(j + 1) * 128],
                                                                   rhs=w_in[:, kc, 1504:2016], start=(kc == 0), stop=(kc == 7)),
                     reads=["w_in", xTk], writes=[pk])
            p.op("act", lambda e, ps_=ps_: e.activation(out=zh[:, :], in_=ps_[:, :], func=AF.Copy, scale=0.5),
                 reads=[pk], writes=["zh"])
            p.op("pool", lambda e: e.tensor_tensor(out=w1[:, :], in0=zh[:, :], in1=zh[:, :], op=ALU.mult),
                 reads=["zh"], writes=["w1"])
            p.op("dve", lambda e: e.tensor_scalar(w1[:, :], w1[:, :], 4.0 * C_GELU, 1.0, op0=ALU.mult, op1=ALU.add),
                 reads=["w1"], writes=["w1"])
            p.op("pool", lambda e: e.tensor_tensor(out=w2[:, :], in0=w1[:, :], in1=zh[:, :], op=ALU.mult),
                 reads=["w1", "zh"], writes=["w2"])
            p.op("act", lambda e: e.activation(out=w2[:, :], in_=w2[:, :], func=AF.Tanh, scale=2.0 * K_GELU),
                 reads=["w2"], writes=["w2"])
            p.op("dve", lambda e: e.scalar_tensor_tensor(out=zz[:, :], in0=w2[:, :], scalar=1.0, in1=zh[:, :],
                                                         op0=ALU.add, op1=ALU.mult), reads=["w2", "zh"], writes=["zz"])
            p.op("dve", lambda e: e.bn_stats(st6[:, :], zz[:, 256:512]), reads=["zz"], writes=["st6"])
            p.op("dve", lambda e: e.bn_aggr(mv[:, :], st6[:, :]), reads=["st6"], writes=["mv"])
            p.op("act", lambda e: e.activation(out=rs[:, :], in_=mv[:, 1:2], func=AF.Sqrt, scale=1.0, bias=EPS),
                 reads=["mv"], writes=["rs"])
            p.op("dve", lambda e: e.reciprocal(rs[:, :], rs[:, :]), reads=["rs"], writes=["rs"])
            p.op("dve", lambda e: e.tensor_scalar(vn[:, :], zz[:, 256:512], mv[:, 0:1], rs[:, 0:1], op0=ALU.subtract,
                                                  op1=ALU.mult), reads=["zz", "mv", "rs"], writes=["vn"])
            p.op("pool", lambda e: e.tensor_tensor(out=vn[:, :], in0=vn[:, :], in1=sg_g[:, :], op=ALU.mult),
                 reads=["vn", "sg_g"], writes=["vn"])
            p.op("pool", lambda e: e.tensor_tensor(out=vnb[:, :], in0=vn[:, :], in1=sg_b[:, :], op=ALU.add),
                 reads=["vn", "sg_b"], writes=["vnb"])
            for gg in range(4):
                p.op("pe", lambda e, gg=gg: e.matmul(mixps[:, gg * 64:(gg + 1) * 64], lhsT=wsT[:, gg, :],
                                                     rhs=vnb[:, gg * 64:(gg + 1) * 64], start=True, stop=True),
                     reads=["wsT", "vnb"], writes=["mixps"])
            for gg in range(4):
                p.op("dve", lambda e, gg=gg: e.scalar_tensor_tensor(out=yc[:, gg * 64:(gg + 1) * 64],
                                                                    in0=mixps[:, gg * 64:(gg + 1) * 64], scalar=bsT[:, gg:gg + 1],
                                                                    in1=zz[:, gg * 64:(gg + 1) * 64], op0=ALU.add, op1=ALU.mult),
                     reads=["mixps", "bsT", "zz"], writes=["yc"])
            p.op("pool", lambda e: e.tensor_tensor(out=junk[:, :], in0=yc[:, :], in1=yc[:, :], op=ALU.mult),
                 reads=["yc"], writes=["junk"])
            p.op("dve", lambda e, ti=ti: e.reduce_sum(out=G["ss"][:, 12, ti:ti + 1], in_=junk[:, :], axis=AX.X),
                 reads=["junk"], writes=["ss"])
            p.op("pool", lambda e: e.tensor_copy(ycb[:, :], yc[:, :]), reads=["yc"], writes=["ycb"])
            for c2 in range(2):
                tp = trps[c2]
                p.op("pe", lambda e, tp=tp, c2=c2: e.transpose(tp[:, 512:640], ycb[:, c2 * 128:(c2 + 1) * 128], ident[:]),
                     reads=["ycb", "ident"], writes=["tr%d" % c2])
                p.op("act", lambda e, tp=tp, c2=c2, j=j: e.copy(ycT[:, c2, j * 128:(j + 1) * 128], tp[:, 512:640]),
                     reads=["tr%d" % c2], writes=["ycT"])
        p.op("sp", lambda e, go=go: e.dma_start(out=io["YT"][768:1024, go * 512:(go + 1) * 512].rearrange("(c p) t -> p c t", p=128),
                                                in_=ycT[:, :, :]), reads=["ycT"], dma="ytc")
    dbg_dump(p, "xT", xTs[(ngo - 1) % 2][:, :, :], [128, 8, 512], BF16)
    dbg_dump(p, "KaT", KaT[:, :, :], [128, 3, OH], BF16)
    dbg_dump(p, "QaT", QaT[:, :, :], [128, 3, OWN], BF16)
    dbg_dump(p, "Va", Va[:, :, :, :], [128, OH // 128, 6, 65], BF16)
    dbg_dump(p, "CQ", CQ[:, :, :], [128, 2, OWN], BF16)
    dbg_dump(p, "w_in", w_in[:, :, :], [128, 8, PIN], BF16)
    p.pop()

    p.push()
    tab = p.sb("tab", [128, 6, TABW], BF16)
    Es = [p.sb("E%d" % i, [128, 1024], BF16) for i in range(2)]
    PTs = [p.sb("PT%d" % i, [128, 1024], BF16) for i in range(2)]
    bufs = dict(osb=[p.sb("osb%d" % i, [128, 512], F32) for i in range(2)], rden=p.sb("rden", [128, 512], F32),
                ysq=p.sb("ysq", [128, 512], F32), ybf=[p.sb("ybf%d" % i, [128, 512], BF16) for i in range(2)],
                den=p.ps("denps", [128, 512]), ssps=p.ps("ssps", [128, 512]))
    Sps = [p.ps("S%d" % i, [128, 1024]) for i in range(2)]
    Ops = [p.ps("O%d" % i, [128, 512]) for i in range(2)]
    for h in range(6):
        p.op("sp", lambda e, h=h: e.dma_start(out=tab[:, h, :], in_=io["mtab"][:, h, :]), writes=["tab"], dma="tab")
    it = 0
    gi = 0
    pend = []
    NKT2 = 20
    for c in range(3):
        hh = (2 * c, 2 * c + 1)
        for qb in range(NQB):
            kt0 = 4 * qb

            def qk(gidx, ii):
                S, sk = Sps[gidx % 2], "S%d" % (gidx % 2)
                kt = kt0 + ii
                for u in range(2):
                    p.op("pe", lambda e, S=S, u=u, kt=kt: e.matmul(S[:, u * 512:(u + 1) * 512],
                                                                   lhsT=KaT[u * 64:u * 64 + 64, c, kt * 128:(kt + 1) * 128],
                                                                   rhs=QaT[u * 64:u * 64 + 64, c, qb * 512:(qb + 1) * 512],
                                                                   start=True, stop=True),
                         reads=["KaT", "QaT"], writes=[sk])

            qk(gi, 0)
            for ii in range(NKT2):
                if ii + 1 < NKT2:
                    qk(gi + 1, ii + 1)
                S, sk = Sps[gi % 2], "S%d" % (gi % 2)
                E, ek = Es[gi % 2], "E%d" % (gi % 2)
                PT, pk = PTs[gi % 2], "PT%d" % (gi % 2)
                kt = kt0 + ii
                start = 2432 - 128 * ii
                p.op("act", lambda e, S=S, E=E: e.activation(out=E[:, :], in_=S[:, :], func=AF.Exp, scale=A_SCALE),
                     reads=[sk], writes=[ek])
                for u in range(2):
                    p.op("dve", lambda e, E=E, PT=PT, u=u, start=start: e.tensor_tensor(
                        out=PT[:, u * 512:(u + 1) * 512], in0=E[:, u * 512:(u + 1) * 512],
                        in1=tab[:, hh[u], start:start + 512], op=ALU.mult), reads=[ek, "tab"], writes=[pk + "_%d" % u])
                for u in range(2):
                    p.op("pe", lambda e, PT=PT, u=u, kt=kt, ii=ii: e.matmul(
                        Ops[u][0:65, :], lhsT=Va[:, kt, hh[u], 0:65], rhs=PT[:, u * 512:(u + 1) * 512],
                        start=(ii == 0), stop=(ii == NKT2 - 1)),
                        reads=["Va", pk + "_%d" % u], writes=["O%d" % u])
                gi += 1
                if ii == 1:
                    while pend:
                        pend.pop(0)()
            for u in range(2):
                pend.append(attn_post(p, G, io, Ops[u], "O%d" % u, hh[u] * 64, qb, hh[u], bufs, u))
            it += 1
    while pend:
        pend.pop(0)()
    p.pop()
    p.pop()

    CKV = p.sb("CKV", [128, SF], BF16)
    KT = p.sb("KT", [128, SF], BF16)
    p.push()
    wB = p.sb("wB", [128, 8, 160], BF16)
    wk96 = p.sb("wk96", [128, 8, 96], BF16)
    wk96r = p.sb("wk96r", [128, 8, 96], BF16)
    xbs = [p.sb("bxb%d" % i, [128, 4, D], BF16) for i in range(2)]
    xTs = [p.sb("bxT%d" % i, [128, 8, 512], BF16) for i in range(2)]
    sq = p.sb("bsq", [128, 512], F32)
    rr = p.sb("brr", [128, 512], F32)
    cks = [p.sb("cks%d" % i, [128, 512], F32) for i in range(2)]
    sks = [p.sb("sks%d" % i, [128, 512], F32) for i in range(2)]
    t1 = p.sb("bt1", [128, 512], F32)
    t2 = p.sb("bt2", [128, 512], F32)
    trps = [p.ps("btrps%d" % i, [128, 1024], BF16) for i in range(2)]
    pps = [p.ps("bpps%d" % i, [128, 512]) for i in range(4)]
    ssq = p.ps("bssq", [128, 512])
    for kc in range(8):
        p.op("pool", lambda e, kc=kc: e.dma_start(out=wB[:, kc, :], in_=W["w_in"][kc * 128:(kc + 1) * 128, 1344:1504]),
             writes=["wB"], dma="wB")
    p.op("dve", lambda e: e.memset(wk96[:], 0.0), writes=["wk96"])
    p.op("dve", lambda e: e.memset(wk96r[:], 0.0), writes=["wk96r"])
    p.op("dve", lambda e: e.tensor_copy(wk96[:, :, 64:96], wB[:, :, 128:160]), reads=["wB", "wk96"], writes=["wk96"])
    p.op("dve", lambda e: e.tensor_copy(wk96r[:, :, 64:80], wB[:, :, 144:160]), reads=["wB", "wk96r"], writes=["wk96r"])
    p.op("dve", lambda e: e.tensor_copy(wk96r[:, :, 80:96], wB[:, :, 128:144]), reads=["wB", "wk96r"], writes=["wk96r"])
    ngf = cfg.NG_F
    load_x_group(p, src["full"](0), xbs[0], "bxb0", "bxb0")
    for g in range(ngf):
        xb, xbk = xbs[g % 2], "bxb%d" % (g % 2)
        xT, xTk = xTs[g % 2], "bxT%d" % (g % 2)
        ck, ckk = cks[g % 2], "cks%d" % (g % 2)
        sk_, skk = sks[g % 2], "sks%d" % (g % 2)
        if g + 1 < ngf:
            load_x_group(p, src["full"](g + 1), xbs[(g + 1) % 2], "bxb%d" % ((g + 1) % 2), "bxb%d" % ((g + 1) % 2))
        p.op("sp", lambda e, ck=ck, g=g: e.dma_start(out=ck[64:96, :], in_=io["ckt"][:, g * 512:(g + 1) * 512]),
             writes=[ckk], dma=ckk)
        p.op("sp", lambda e, sk_=sk_, g=g: e.dma_start(out=sk_[64:96, :], in_=io["skt"][:, g * 512:(g + 1) * 512]),
             writes=[skk], dma=skk)
        transpose_group(p, G, xb, xbk, xT, xTk, trps)
        pa, pb, pc = pps[(3 * g) % 4], pps[(3 * g + 1) % 4], pps[(3 * g + 2) % 4]
        ka, kb, kc_ = "bpps%d" % ((3 * g) % 4), "bpps%d" % ((3 * g + 1) % 4), "bpps%d" % ((3 * g + 2) % 4)
        for kc in range(8):
            p.op("pe", lambda e, kc=kc, pa=pa: e.matmul(pa[:, :], lhsT=wB[:, kc, 0:128], rhs=xT[:, kc, :],
                                                        start=(kc == 0), stop=(kc == 7)), reads=["wB", xTk], writes=[ka])
        for kc in range(8):
            p.op("pe", lambda e, kc=kc, pb=pb: e.matmul(pb[0:96, :], lhsT=wk96[:, kc, :], rhs=xT[:, kc, :],
                                                        start=(kc == 0), stop=(kc == 7)), reads=["wk96", xTk], writes=[kb])
        for kc in range(8):
            p.op("pe", lambda e, kc=kc, pc=pc: e.matmul(pc[0:96, :], lhsT=wk96r[:, kc, :], rhs=xT[:, kc, :],
                                                        start=(kc == 0), stop=(kc == 7)), reads=["wk96r", xTk], writes=[kc_])
        p.op("act", lambda e, pa=pa: e.activation(out=sq[:, :], in_=pa[:, :], func=AF.Square), reads=[ka], writes=["bsq"])
        p.op("pe", lambda e: e.matmul(ssq[:, :], lhsT=G["ones"][:, :], rhs=sq[:, :], start=True, stop=True),
             reads=["bsq", "ones"], writes=["bssq"])
        p.op("act", lambda e: e.activation(out=rr[:, :], in_=ssq[:, :], func=AF.Sqrt, scale=1.0 / 128, bias=EPS),
             reads=["bssq"], writes=["brr"])
        p.op("dve", lambda e: e.reciprocal(rr[:, :], rr[:, :]), reads=["brr"], writes=["brr"])
        p.op("dve", lambda e, pa=pa, g=g: e.tensor_tensor(out=CKV[:, g * 512:(g + 1) * 512], in0=pa[:, :], in1=rr[:, :],
                                                          op=ALU.mult), reads=[ka, "brr"], writes=["CKV"])
        p.op("dve", lambda e, pb=pb, ck=ck: e.tensor_tensor(out=t1[64:96, :], in0=pb[64:96, :], in1=ck[64:96, :], op=ALU.mult),
             reads=[kb, ckk], writes=["bt1"])
        p.op("dve", lambda e, pc=pc, sk_=sk_: e.tensor_tensor(out=t2[64:96, :], in0=pc[64:96, :], in1=sk_[64:96, :], op=ALU.mult),
             reads=[kc_, skk], writes=["bt2"])
        p.op("pool", lambda e, g=g: e.tensor_tensor(out=KT[64:96, g * 512:(g + 1) * 512], in0=t1[64:96, :], in1=t2[64:96, :],
                                                    op=ALU.add), reads=["bt1", "bt2"], writes=["KT"])
    p.pop()

    p.push()
    Vh = p.sb("Vh", [128, SF // 128, 65], BF16)
    QT = p.sb("QT", [128, OWN], BF16)
    cqt = p.sb("cqt", [128, OWN], F32)
    sqt = p.sb("sqt", [128, OWN], F32)
    wkv = p.sb("wkv", [128, 768], BF16)
    wkvf = p.sb("wkvf", [128, 768], F32)
    wqf = p.sb("wqf", [128, 2, 576], F32)
    wq = p.sb("wq", [128, 2, 6, 96], BF16)
    wqr = p.sb("wqr", [128, 2, 6, 96], BF16)
    qn = p.sb("qn", [128, 2], F32)
    kvn = p.sb("kvn", [128, 1], F32)
    q1 = p.sb("q1", [128, 512], F32)
    q2 = p.sb("q2", [128, 512], F32)
    NSB = 3
    PTs = [p.sb("bPT%d" % i, [128, 1024], BF16) for i in range(NSB)]
    miscps = p.ps("bmisc", [128, 512])
    bufs = dict(osb=[p.sb("bosb%d" % i, [128, 512], F32) for i in range(2)], rden=p.sb("brden", [128, 512], F32),
                ysq=p.sb("bysq", [128, 512], F32), ybf=[p.sb("bybf%d" % i, [128, 512], BF16) for i in range(2)],
                den=miscps, ssps=miscps, denkey="bmisc", sskey="bmisc")
    Sps = [p.ps("bS%d" % i, [128, 1024]) for i in range(NSB)]
    Ops = [p.ps("bO%d" % i, [128, 512]) for i in range(1)]
    pend = []
    p.op("sp", lambda e: e.dma_start(out=wkvf[:], in_=W["w_kv_up"]), writes=["wkvf"], dma="bw")
    p.op("sp", lambda e: e.dma_start(out=wqf[:, 0, :], in_=W["w_q_up"][0:128, :]), writes=["wqf"], dma="bw")
    p.op("sp", lambda e: e.dma_start(out=wqf[0:64, 1, :], in_=W["w_q_up"][128:192, :]), writes=["wqf"], dma="bw")
    p.op("sp", lambda e: e.dma_start(out=qn[:], in_=W["qn"]), writes=["qn"], dma="bw")
    p.op("sp", lambda e: e.dma_start(out=kvn[:], in_=W["kvn"]), writes=["kvn"], dma="bw")
    p.op("sp", lambda e: e.dma_start(out=cqt[64:96, :], in_=io["cqt"]), writes=["cqt"], dma="bw")
    p.op("sp", lambda e: e.dma_start(out=sqt[64:96, :], in_=io["sqt"]), writes=["sqt"], dma="bw")
    p.op("dve", lambda e: e.tensor_scalar(wkv[:, :], wkvf[:, :], kvn[:, 0:1], None, op0=ALU.mult), reads=["wkvf", "kvn"],
         writes=["wkv"])
    p.op("dve", lambda e: e.memset(wq[:], 0.0), writes=["wq"])
    p.op("dve", lambda e: e.memset(wqr[:], 0.0), writes=["wqr"])
    for cc, np_ in ((0, 128), (1, 64)):
        wv = wqf[0:np_, cc, :].rearrange("p (h c) -> p h c", h=6)
        p.op("dve", lambda e, cc=cc, np_=np_, wv=wv: e.tensor_scalar(wq[0:np_, cc, :, :], wv, qn[0:np_, cc:cc + 1], None,
                                                                      op0=ALU.mult), reads=["wqf", "qn", "wq"], writes=["wq"])
        p.op("dve", lambda e, cc=cc, np_=np_, wv=wv: e.tensor_scalar(wqr[0:np_, cc, :, 64:80], wv[:, :, 80:96],
                                                                      qn[0:np_, cc:cc + 1], None, op0=ALU.mult),
             reads=["wqf", "qn", "wqr"], writes=["wqr"])
        p.op("dve", lambda e, cc=cc, np_=np_, wv=wv: e.tensor_scalar(wqr[0:np_, cc, :, 80:96], wv[:, :, 64:80],
                                                                      qn[0:np_, cc:cc + 1], None, op0=ALU.mult),
             reads=["wqf", "qn", "wqr"], writes=["wqr"])
    p.op("pool", lambda e: e.memset(Vh[:, :, 64:65], 1.0), writes=["Vh"])
    it = 0
    gi = 0
    nkt = SF // 128
    for h in range(6):
        for g2 in range(SF // 1024):
            S, sk = Sps[gi % NSB], "bS%d" % (gi % NSB)
            for u in range(2):
                g = 2 * g2 + u
                p.op("pe", lambda e, S=S, u=u, g=g, h=h: e.matmul(S[0:64, u * 512:(u + 1) * 512], lhsT=wkv[:, h * 128:h * 128 + 64],
                                                                  rhs=CKV[:, g * 512:(g + 1) * 512], start=True, stop=True),
                     reads=["wkv", "CKV"], writes=[sk])
            p.op("dve", lambda e, S=S, g2=g2: e.tensor_copy(KT[0:64, g2 * 1024:(g2 + 1) * 1024], S[0:64, :]),
                 reads=[sk], writes=["KT"])
            gi += 1
        for t16 in range(nkt // 16):
            S, sk = Sps[gi % NSB], "bS%d" % (gi % NSB)
            for u in range(16):
                t = t16 * 16 + u
                p.op("pe", lambda e, S=S, u=u, t=t, h=h: e.matmul(S[:, u * 64:(u + 1) * 64], lhsT=CKV[:, t * 128:(t + 1) * 128],
                                                                  rhs=wkv[:, h * 128 + 64:h * 128 + 128], start=True, stop=True),
                     reads=["wkv", "CKV"], writes=[sk])
            p.op("dve", lambda e, S=S, t16=t16: e.tensor_copy(Vh[:, t16 * 16:(t16 + 1) * 16, 0:64],
                                                               S[:, :].rearrange("p (t c) -> p t c", c=64)),
                 reads=[sk], writes=["Vh"])
            gi += 1
        for qb in range(NQB):
            S, sk = Sps[gi % NSB], "bS%d" % (gi % NSB)
            for u, wsrc in ((0, wq), (1, wqr)):
                p.op("pe", lambda e, S=S, u=u, wsrc=wsrc, h=h, qb=qb: e.matmul(S[0:96, u * 512:(u + 1) * 512], lhsT=wsrc[:, 0, h, :],
                                                                               rhs=CQ[:, 0, qb * 512:(qb + 1) * 512],
                                                                               start=True, stop=False),
                     reads=["wq", "wqr", "CQ"], writes=[sk])
                p.op("pe", lambda e, S=S, u=u, wsrc=wsrc, h=h, qb=qb: e.matmul(S[0:96, u * 512:(u + 1) * 512], lhsT=wsrc[0:64, 1, h, :],
                                                                               rhs=CQ[0:64, 1, qb * 512:(qb + 1) * 512],
                                                                               start=False, stop=True),
                     reads=["wq", "wqr", "CQ"], writes=[sk])
            p.op("dve", lambda e, S=S, qb=qb: e.tensor_copy(QT[0:64, qb * 512:(qb + 1) * 512], S[0:64, 0:512]),
                 reads=[sk], writes=["QT"])
            p.op("dve", lambda e, S=S, qb=qb: e.tensor_tensor(out=q1[64:96, :], in0=S[64:96, 0:512],
                                                              in1=cqt[64:96, qb * 512:(qb + 1) * 512], op=ALU.mult),
                 reads=[sk, "cqt"], writes=["q1"])
            p.op("dve", lambda e, S=S, qb=qb: e.tensor_tensor(out=q2[64:96, :], in0=S[64:96, 512:1024],
                                                              in1=sqt[64:96, qb * 512:(qb + 1) * 512], op=ALU.mult),
                 reads=[sk, "sqt"], writes=["q2"])
            p.op("pool", lambda e, qb=qb: e.tensor_tensor(out=QT[64:96, qb * 512:(qb + 1) * 512], in0=q1[64:96, :],
                                                          in1=q2[64:96, :], op=ALU.add), reads=["q1", "q2"], writes=["QT"])
            gi += 1
        for qb in range(NQB):
            ops, okey = Ops[0], "bO0"
            ngrp = nkt // 2

            def qk(gidx, i):
                S, sk = Sps[gidx % NSB], "bS%d" % (gidx % NSB)
                for u in range(2):
                    kt = 2 * i + u
                    p.op("pe", lambda e, S=S, u=u, kt=kt: e.matmul(S[:, u * 512:(u + 1) * 512],
                                                                   lhsT=KT[0:96, kt * 128:(kt + 1) * 128],
                                                                   rhs=QT[0:96, qb * 512:(qb + 1) * 512], start=True, stop=True),
                         reads=["KT", "QT"], writes=[sk])

            qk(gi, 0)
            qk(gi + 1, 1)
            for i in range(ngrp):
                if i + 2 < ngrp:
                    qk(gi + 2, i + 2)
                S, sk = Sps[gi % NSB], "bS%d" % (gi % NSB)
                PT, pk = PTs[gi % NSB], "bPT%d" % (gi % NSB)
                p.op("act", lambda e, S=S, PT=PT: e.activation(out=PT[:, :], in_=S[:, :], func=AF.Exp, scale=B_SCALE),
                     reads=[sk], writes=[pk])
                for u in range(2):
                    kt = 2 * i + u
                    p.op("pe", lambda e, PT=PT, u=u, kt=kt, ops=ops, i=i: e.matmul(
                        ops[0:65, :], lhsT=Vh[:, kt, 0:65], rhs=PT[:, u * 512:(u + 1) * 512],
                        start=(i == 0 and u == 0), stop=(i == ngrp - 1 and u == 1)),
                        reads=["Vh", pk], writes=[okey])
                gi += 1
                if i == 2 and pend:
                    pend.pop()()
            pend.append(attn_post(p, G, io, ops, okey, 384 + h * 64, qb, 6 + h, bufs, it))
            it += 1
        while pend:
            pend.pop()()
    p.pop()
    p.pop()

    p.push()
    wf1 = p.sb("wf1", [128, 8, DFF], BF16)
    wf2 = p.sb("wf2", [128, 32, D], BF16)
    p.push()
    wo = p.sb("wo", [128, 8, D], BF16)
    wof = p.sb("wof", [128, D], F32)
    mixn = p.sb("mixn", [128, 8], F32)
    g1 = p.sb("g1", [128, D], F32)
    b1 = p.sb("b1", [128, D], F32)
    yTs = [p.sb("yT%d" % i, [128, 8, 512], BF16) for i in range(2)]
    xrs = [p.sb("xr%d" % i, [128, D], F32) for i in range(2)]
    accs = [p.sb("dacc%d" % i, [128, D], F32) for i in range(2)]
    x1s = [p.sb("x1s%d" % i, [128, D], F32) for i in range(2)]
    rst = p.sb("rst", [128, 3, NT], F32)
    sst = p.sb("sst", [128, 3, NT], F32)
    st12s = [p.sb("st12_%d" % i, [128, 12], F32) for i in range(2)]
    mvs = [p.sb("dmv%d" % i, [128, 2], F32) for i in range(2)]
    rss = [p.sb("drs%d" % i, [128, 1], F32) for i in range(2)]
    accps = [p.ps("accps%d" % i, [128, 1024]) for i in range(2)]
    p.op("sp", lambda e: e.dma_start(out=mixn[:], in_=W["mixn"]), writes=["mixn"], dma="dw")
    p.op("sp", lambda e: e.dma_start(out=g1[:], in_=W["ln1_g"].partition_broadcast(128)), writes=["g1"], dma="dw")
    p.op("sp", lambda e: e.dma_start(out=b1[:], in_=W["ln1_b"].partition_broadcast(128)), writes=["b1"], dma="dw")
    for kc in range(8):
        p.op("sp", lambda e, kc=kc: e.dma_start(out=wof[:], in_=W["w_out"][kc * 128:(kc + 1) * 128, :]), writes=["wof"], dma="dwo")
        p.op("dve", lambda e, kc=kc: e.tensor_scalar(wo[:, kc, :], wof[:, :], mixn[:, kc:kc + 1], None, op0=ALU.mult),
             reads=["wof", "mixn"], writes=["wo"])
    for kc in range(8):
        p.op("pool", lambda e, kc=kc: e.dma_start(out=wf1[:, kc, :], in_=W["w_ff1"][kc * 128:(kc + 1) * 128, :]),
             writes=["wf1"], dma="wf1")
    for c4 in range(8):
        p.op("pool", lambda e, c4=c4: e.dma_start(out=wf2[:, c4 * 4:(c4 + 1) * 4, :],
                                                  in_=W["w_ff2"][c4 * 512:(c4 + 1) * 512, :].rearrange("(c p) d -> p c d", p=128)),
             writes=["wf2"], dma="wf2")
    ssv = G["ss"]
    p.op("dve", lambda e: e.tensor_copy(sst[:, 0, :], ssv[:, 0, :]), reads=["ss"], writes=["sst"])
    p.op("dve", lambda e: e.tensor_copy(sst[:, 1, :], ssv[:, 6, :]), reads=["ss"], writes=["sst"])
    p.op("dve", lambda e: e.tensor_copy(sst[:, 2, :], ssv[:, 12, :]), reads=["ss"], writes=["sst"])
    for k in range(1, 6):
        p.op("dve", lambda e, k=k: e.tensor_tensor(out=sst[:, 0, :], in0=sst[:, 0, :], in1=ssv[:, k, :], op=ALU.add),
             reads=["ss", "sst"], writes=["sst"])
        p.op("dve", lambda e, k=k: e.tensor_tensor(out=sst[:, 1, :], in0=sst[:, 1, :], in1=ssv[:, 6 + k, :], op=ALU.add),
             reads=["ss", "sst"], writes=["sst"])
    for gidx, wdt in ((0, 384.0), (1, 384.0), (2, 256.0)):
        p.op("act", lambda e, gidx=gidx, wdt=wdt: e.activation(out=rst[:, gidx, :], in_=sst[:, gidx, :], func=AF.Sqrt,
                                                               scale=1.0 / wdt, bias=EPS), reads=["sst"], writes=["rst"])
    p.op("dve", lambda e: e.reciprocal(rst[:, :, :], rst[:, :, :]), reads=["rst"], writes=["rst"])
    KCG = ((0, 3), (3, 6), (6, 8))
    ai = 0
    for g in range(NQB):
        yT, yk = yTs[g % 2], "yT%d" % (g % 2)
        p.op("sp", lambda e, yT=yT, g=g: e.dma_start(out=yT[:, :, :], in_=io["YT"][:, g * 512:(g + 1) * 512].rearrange("(c p) t -> p c t", p=128)),
             writes=[yk], dma=yk)
        for j in range(4):
            ti = g * 4 + j
            xr, xk = xrs[ti % 2], "xr%d" % (ti % 2)
            x1, x1k = x1s[ti % 2], "x1s%d" % (ti % 2)
            acc, acck = accs[ti % 2], "dacc%d" % (ti % 2)
            p.op("sp", lambda e, xr=xr, ti=ti: e.dma_start(out=xr[:, :], in_=src["res"](ti)),
                 writes=[xk], dma=xk)
            for gidx, (k0, k1) in enumerate(KCG):
                ap_, ak = accps[ai % 2], "accps%d" % (ai % 2)
                ai += 1
                for half in range(2):
                    for kc in range(k0, k1):
                        p.op("pe", lambda e, ap_=ap_, half=half, kc=kc, k0=k0, k1=k1, j=j: e.matmul(
                            ap_[:, half * 512:(half + 1) * 512], lhsT=yT[:, kc, j * 128:(j + 1) * 128],
                            rhs=wo[:, kc, half * 512:(half + 1) * 512], start=(kc == k0), stop=(kc == k1 - 1)),
                            reads=[yk, "wo"], writes=[ak])
                if gidx == 0:
                    p.op("dve", lambda e, ap_=ap_, ti=ti: e.tensor_scalar(acc[:, :], ap_[:, :], rst[:, 0, ti:ti + 1], None, op0=ALU.mult),
                         reads=[ak, "rst"], writes=[acck])
                else:
                    p.op("dve", lambda e, ap_=ap_, ti=ti, gidx=gidx: e.scalar_tensor_tensor(
                        out=acc[:, :], in0=ap_[:, :], scalar=rst[:, gidx, ti:ti + 1], in1=acc[:, :], op0=ALU.mult, op1=ALU.add),
                        reads=[ak, "rst", acck], writes=[acck])
            p.op("dve", lambda e, xr=xr: e.scalar_tensor_tensor(out=acc[:, :], in0=xr[:, :], scalar=ALPHA, in1=acc[:, :],
                                                                op0=ALU.mult, op1=ALU.add), reads=[xk, acck], writes=[acck])
            layer_norm(p, acc, acck, x1, x1k, g1, "g1", b1, "b1", st12s[ti % 2], mvs[ti % 2], rss[ti % 2], "d1_%d" % (ti % 2))
            p.op("sp", lambda e, x1=x1, ti=ti: e.dma_start(out=io["X1"][ti * 128:(ti + 1) * 128, :], in_=x1[:, :]),
                 reads=[x1k], dma="x1o%d" % (ti % 2))
    p.pop()

    p.push()
    b1T = p.sb("b1T", [128, 32], F32)
    g2 = p.sb("g2", [128, D], F32)
    b2 = p.sb("b2", [128, D], F32)
    bf2 = p.sb("bf2", [128, D], F32)
    x1f = [p.sb("x1f%d" % i, [128, 2, D], F32) for i in range(2)]
    x1b = p.sb("x1b", [128, 2, D], BF16)
    x1Ts = [p.sb("x1T%d" % i, [128, 8, 256], BF16) for i in range(2)]
    hidT = p.sb("hidT", [128, 32, 256], BF16)
    rl = [p.sb("rl%d" % i, [128, 256], F32) for i in range(2)]
    t2ss = [p.sb("t2s%d" % i, [128, D], F32) for i in range(1)]
    outs = [p.sb("outs%d" % i, [128, D], F32) for i in range(2)]
    outb = [p.sb("outb%d" % i, [128, D], BF16) for i in range(2)]
    st12s = [p.sb("st12b%d" % i, [128, 12], F32) for i in range(2)]
    mvs = [p.sb("dmv2_%d" % i, [128, 2], F32) for i in range(2)]
    rss = [p.sb("drs2_%d" % i, [128, 1], F32) for i in range(2)]
    trps = [p.ps("dtrps%d" % i, [128, 1024], BF16) for i in range(2)]
    hps = [p.ps("hps%d" % i, [128, 512]) for i in range(2)]
    fps = [p.ps("fps%d" % i, [128, 1024]) for i in range(2)]
    p.op("sp", lambda e: e.dma_start(out=b1T[:], in_=W["b1T"]), writes=["b1T"], dma="dw2")
    p.op("sp", lambda e: e.dma_start(out=g2[:], in_=W["ln2_g"].partition_broadcast(128)), writes=["g2"], dma="dw2")
    p.op("sp", lambda e: e.dma_start(out=b2[:], in_=W["ln2_b"].partition_broadcast(128)), writes=["b2"], dma="dw2")
    p.op("sp", lambda e: e.dma_start(out=bf2[:], in_=W["b_ff2"].partition_broadcast(128)), writes=["bf2"], dma="dw2")
    ng2 = OWN // 256

    def prep(g):
        xf_, xfk = x1f[g % 2], "x1f%d" % (g % 2)
        p.op("sp", lambda e: e.dma_start(out=xf_[:, :, :], in_=io["X1"][g * 256:(g + 1) * 256, :].rearrange("(j p) d -> p j d", p=128)),
             writes=[xfk], dma=xfk)
        p.op("pool", lambda e: e.tensor_copy(x1b[:, :, :], xf_[:, :, :]), reads=[xfk], writes=["x1b"])
        transpose_group(p, G, x1b, "x1b", x1Ts[g % 2], "x1T%d" % (g % 2), trps, n_tok_tiles=2)

    prep(0)
    hi = 0
    fi = 0
    for g in range(ng2):
        xf_, xfk = x1f[g % 2], "x1f%d" % (g % 2)
        x1T, x1Tk = x1Ts[g % 2], "x1T%d" % (g % 2)
        for fc in range(32):
            hp_, hk = hps[hi % 2], "hps%d" % (hi % 2)
            r_, rk = rl[hi % 2], "rl%d" % (hi % 2)
            hi += 1
            for kc in range(8):
                p.op("pe", lambda e, hp_=hp_, kc=kc, fc=fc: e.matmul(hp_[:, 0:256], lhsT=wf1[:, kc, fc * 128:(fc + 1) * 128],
                                                                     rhs=x1T[:, kc, :], start=(kc == 0), stop=(kc == 7)),
                     reads=["wf1", x1Tk], writes=[hk])
            p.op("act", lambda e, hp_=hp_, r_=r_, fc=fc: e.activation(out=r_[:, :], in_=hp_[:, 0:256], func=AF.Relu,
                                                                      bias=b1T[:, fc:fc + 1], scale=1.0),
                 reads=[hk, "b1T"], writes=[rk])
            p.op("pool" if fc % 2 else "dve", lambda e, r_=r_, fc=fc: e.tensor_tensor(out=hidT[:, fc, :], in0=r_[:, :], in1=r_[:, :], op=ALU.mult),
                 reads=[rk], writes=["hidT"])
        if g + 1 < ng2:
            prep(g + 1)
        for j in range(2):
            ti = g * 2 + j
            fp_, fk = fps[fi % 2], "fps%d" % (fi % 2)
            o_, ok_ = outs[fi % 2], "outs%d" % (fi % 2)
            t2s, t2k = t2ss[0], "t2s0"
            par = fi % 2
            fi += 1
            for half in range(2):
                for fc in range(32):
                    p.op("pe", lambda e, fp_=fp_, half=half, fc=fc, j=j: e.matmul(
                        fp_[:, half * 512:(half + 1) * 512], lhsT=hidT[:, fc, j * 128:(j + 1) * 128],
                        rhs=wf2[:, fc, half * 512:(half + 1) * 512], start=(fc == 0), stop=(fc == 31)),
                        reads=["hidT", "wf2"], writes=[fk])
            p.op("dve", lambda e, fp_=fp_: e.tensor_tensor(out=t2s[:, :], in0=fp_[:, :], in1=bf2[:, :], op=ALU.add),
                 reads=[fk, "bf2"], writes=[t2k])
            p.op("dve", lambda e, xf_=xf_, j=j: e.scalar_tensor_tensor(out=t2s[:, :], in0=xf_[:, j, :], scalar=ALPHA, in1=t2s[:, :],
                                                                       op0=ALU.mult, op1=ALU.add), reads=[xfk, t2k], writes=[t2k])
            layer_norm(p, t2s, t2k, o_, ok_, g2, "g2", b2, "b2", st12s[par], mvs[par], rss[par], "d2_%d" % par)
            p.op("sp", lambda e, o_=o_, ti=ti: e.dma_start(out=dst["f32"][ti * 128:(ti + 1) * 128, :], in_=o_[:, :]),
                 reads=[ok_], dma=dst["tag"])
            if dst["bf16"] is not None:
                ob_, obk = outb[par], "outb%d" % par
                p.op("pool", lambda e, o_=o_, ob_=ob_: e.tensor_copy(ob_[:, :], o_[:, :]), reads=[ok_], writes=[obk])
                p.op("sp", lambda e, ob_=ob_, ti=ti: e.dma_start(out=dst["bf16"][ti * 128:(ti + 1) * 128, :], in_=ob_[:, :]),
                     reads=[obk], dma="xbc")
    p.pop()
    p.pop()


def layer_norm(p, src, skey, dstt, dkey, g, gk, b, bk, st12, mv, rs, tg):
    k6, kmv, krs = "st12" + tg, "mv" + tg, "rs" + tg
    p.op("dve", lambda e: e.bn_stats(st12[:, 0:6], src[:, 0:512]), reads=[skey], writes=[k6])
    p.op("dve", lambda e: e.bn_stats(st12[:, 6:12], src[:, 512:1024]), reads=[skey], writes=[k6])
    p.op("dve", lambda e: e.bn_aggr(mv[:, :], st12[:, :]), reads=[k6], writes=[kmv])
    p.op("act", lambda e: e.activation(out=rs[:, :], in_=mv[:, 1:2], func=AF.Sqrt, scale=1.0, bias=EPS), reads=[kmv], writes=[krs])
    p.op("dve", lambda e: e.reciprocal(rs[:, :], rs[:, :]), reads=[krs], writes=[krs])
    p.op("dve", lambda e: e.tensor_scalar(src[:, :], src[:, :], mv[:, 0:1], rs[:, 0:1], op0=ALU.subtract, op1=ALU.mult),
         reads=[skey, kmv, krs], writes=[skey])
    p.op("pool", lambda e: e.tensor_tensor(out=src[:, :], in0=src[:, :], in1=g[:, :], op=ALU.mult), reads=[skey, gk], writes=[skey])
    p.op("pool", lambda e: e.tensor_tensor(out=dstt[:, :], in0=src[:, :], in1=b[:, :], op=ALU.add), reads=[skey, bk], writes=[dkey])


_CACHE = {}


def host_weights(inp, layers):
    f = lambda a: np.ascontiguousarray(a, dtype=np.float32)
    L = list(layers)
    w = {}
    w["w_in"] = f(inp["w_in"][L])
    w["w_q_up"] = f(inp["w_q_up"][L])
    w["w_kv_up"] = f(inp["w_kv_up"][L])
    qn = np.zeros((len(L), 256), np.float32)
    qn[:, :192] = inp["q_norm"][L]
    w["qn"] = f(qn.reshape(len(L), 2, 128).transpose(0, 2, 1))
    w["kvn"] = f(inp["kv_norm"][L].reshape(len(L), 128, 1))
    w["sg_g"] = f(inp["sgu_ln_g"][L])
    w["sg_b"] = f(inp["sgu_ln_b"][L])
    w["sg_wT"] = f(np.transpose(inp["sgu_w"][L], (0, 1, 3, 2)))
    w["sg_bT"] = f(np.transpose(inp["sgu_b"][L], (0, 2, 1)))
    w["mixn"] = f(inp["mix_norm"][L].reshape(len(L), 8, 128).transpose(0, 2, 1))
    w["w_out"] = f(inp["w_out"][L])
    w["ln1_g"] = f(inp["ln1_g"][L])
    w["ln1_b"] = f(inp["ln1_b"][L])
    w["w_ff1"] = f(inp["w_ff1"][L])
    w["b1T"] = f(inp["b_ff1"][L].reshape(len(L), 32, 128).transpose(0, 2, 1))
    w["w_ff2"] = f(inp["w_ff2"][L])
    w["b_ff2"] = f(inp["b_ff2"][L])
    w["ln2_g"] = f(inp["ln2_g"][L])
    w["ln2_b"] = f(inp["ln2_b"][L])
    return w


def core_inputs(x_b, r, own, consts):
    S = x_b.shape[0]
    lo = r * own - HALO
    xo = np.zeros((own + 2 * HALO, D), np.float32)
    vm = np.zeros((own + 2 * HALO,), np.float32)
    a, b = max(lo, 0), min(lo + own + 2 * HALO, S)
    xo[a - lo:b - lo] = x_b[a:b]
    vm[a - lo:b - lo] = 1.0
    ct, st = consts["rope"]
    hs = np.zeros((128, 8), np.float32)
    if r - 1 >= 0:
        hs[:, r - 1] = 1.0
    if r + 1 < S // own:
        hs[:, 4 + r + 1] = 1.0
    d = dict(hsel=hs, xo=xo, xf=np.ascontiguousarray(x_b, dtype=np.float32),
             vmask=np.ascontiguousarray(vm.reshape(-1, 128).T), mtab=consts["mtab"],
             ckt=ct, skt=st, cqt=np.ascontiguousarray(ct[:, r * own:(r + 1) * own]),
             sqt=np.ascontiguousarray(st[:, r * own:(r + 1) * own]))
    return d


def run_layers(x, inp, own, n_groups_per_batch, fused_depth=1, dbg=False):
    B, S, _ = x.shape
    key = (own, S, fused_depth, dbg, B)
    if key not in _CACHE:
        _CACHE[key] = build_program(Cfg(own, S, depth=fused_depth, dbg=dbg, ncores=B * n_groups_per_batch))
    nc, stats = _CACHE[key]
    consts = dict(mtab=mask_table(), rope=rope_tables(S))
    depth = inp["w_in"].shape[0]
    cur = np.asarray(x, dtype=np.float32)
    extra = None
    for l0 in range(0, depth, fused_depth):
        w = host_weights(inp, range(l0, l0 + fused_depth))
        in_maps = []
        for c in range(B * n_groups_per_batch):
            b, r = c // n_groups_per_batch, c % n_groups_per_batch
            d = core_inputs(cur[b], r, own, consts)
            if fused_depth == 1:
                d.pop("hsel")
            d.update(w)
            in_maps.append(d)
        res = run_bass_kernel_spmd(nc, in_maps, core_ids=list(range(len(in_maps))))
        outs = [r_["out"] for r_ in res.results]
        cur = np.stack([np.concatenate(outs[b * n_groups_per_batch:(b + 1) * n_groups_per_batch], 0) for b in range(B)], 0)
        extra = res.results
    return cur, extra


def kernel(**inputs):
    x = np.asarray(inputs["x"], dtype=np.float32)
    inp = {k: np.asarray(v, dtype=np.float32) for k, v in inputs.items() if k != "x"}
    out, _ = run_layers(x, inp, own=x.shape[1] // 4, n_groups_per_batch=4, fused_depth=inp["w_in"].shape[0])
    return out.astype(np.float32)
```

```python
import types
import numpy as np
import ml_dtypes
import concourse.bass as bass
import concourse.mybir as mybir
from concourse.bass_utils import run_bass_kernel_spmd

F32 = mybir.dt.float32
BF16 = mybir.dt.bfloat16
I32 = mybir.dt.int32
AF = mybir.ActivationFunctionType
ALU = mybir.AluOpType
AX = mybir.AxisListType

ENGS = ("pe", "act", "dve", "pool", "sp")
SEM_LIMIT = 20000

D = 1024
PIN = 2016
HALO = 1024
DFF = 4096
EPS = 1e-5
ALPHA = float((2 * 2) ** 0.25)
TABW = 2944
A_SCALE = 0.125
B_SCALE = float(96 ** -0.5)
C_GELU = 0.044715
K_GELU = float(np.sqrt(2.0 / np.pi))


def _freeze(fn):
    if fn is None or fn.__closure__ is None:
        return fn
    cells = []
    for c in fn.__closure__:
        try:
            cells.append(types.CellType(c.cell_contents))
        except ValueError:
            cells.append(c)
    return types.FunctionType(fn.__code__, fn.__globals__, fn.__name__, fn.__defaults__, tuple(cells))


class Prog:
    def __init__(self, nc):
        self.nc = nc
        self.ops = {e: [] for e in ENGS}
        self.state = {}
        self.dma_sems = {}
        self.pending = {e: [] for e in ENGS}
        self._cms = []
        self._scopes = []

    def _reg(self, cm):
        t = cm.__enter__()
        (self._scopes[-1] if self._scopes else self._cms).append(cm)
        return t

    def sem(self, name):
        self._n = getattr(self, "_n", 0) + 1
        cm = self.nc.semaphore("m%d_%s" % (self._n, name))
        s = cm.__enter__()
        self._cms.append(cm)
        return s

    def sb(self, name, shape, dt):
        self._n = getattr(self, "_n", 0) + 1
        return self._reg(self.nc.sbuf_tensor("sb%d_%s" % (self._n, name), list(shape), dt))

    def ps(self, name, shape, dt=F32):
        self._n = getattr(self, "_n", 0) + 1
        return self._reg(self.nc.psum_tensor("ps%d_%s" % (self._n, name), list(shape), dt))

    def push(self):
        self._scopes.append([])

    def pop(self):
        self.barrier()
        for cm in reversed(self._scopes.pop()):
            cm.__exit__(None, None, None)

    def close(self):
        for cm in reversed(self._cms):
            cm.__exit__(None, None, None)
        self._cms = []

    def barrier(self):
        evs = []
        for e in ENGS:
            lst = self.ops[e]
            for i in range(len(lst) - 1, -1, -1):
                if lst[i]["dma"] is None and lst[i]["fn"] is not None:
                    evs.append(("eng", e, i))
                    break
        for tag, ds in self.dma_sems.items():
            evs.append(("dma", ds[0], ds[1]))
        for e in ENGS:
            self.pending[e].extend(evs)
        self.state = {}

    def op(self, eng, fn, reads=(), writes=(), dma=None, inc=16):
        fn = _freeze(fn)
        deps = []
        for k in reads:
            st = self.state.get(k)
            if st and st[0] is not None:
                deps.append((st[0], True))
        for k in writes:
            st = self.state.get(k)
            if st:
                if st[0] is not None:
                    deps.append((st[0], False))
                for r in st[1]:
                    deps.append((r, False))
        lst = self.ops[eng]
        idx = len(lst)
        if dma is not None:
            if dma not in self.dma_sems:
                self.dma_sems[dma] = [self.sem("d_" + dma), 0]
            ds = self.dma_sems[dma]
            ds[1] += inc
            ev = ("dma", ds[0], ds[1])
        else:
            ev = ("eng", eng, idx)
        fdeps = []
        for d, raw in deps:
            if d[0] == "eng" and d[1] == eng and dma is None:
                if eng == "pe" or not raw:
                    continue
            if d[0] == "dma":
                for ds_ in self.dma_sems.values():
                    if ds_[0] is d[1]:
                        cur = ds_[1] - (inc if (dma is not None and self.dma_sems[dma][0] is d[1]) else 0)
                        d = ("dma", d[1], max(d[2], cur))
            fdeps.append(d)
        for d in self.pending[eng]:
            if d[0] == "eng" and d[1] == eng and eng == "pe":
                continue
            fdeps.append(d)
        self.pending[eng] = []
        lst.append(dict(fn=fn, deps=fdeps, dma=dma, ev=ev, ms=False, inc=inc))
        for k in reads:
            st = self.state.setdefault(k, [None, []])
            st[1].append(ev)
        for k in writes:
            self.state[k] = [ev, []]
        return ev

    def wait_all_dma(self, eng, tags):
        deps = []
        for t in tags:
            ds = self.dma_sems[t]
            deps.append(("dma", ds[0], ds[1]))
        self.ops[eng].append(dict(fn=None, deps=deps, dma=None, ev=None, ms=False))

    def emit(self):
        nc = self.nc
        for e in ENGS:
            for o in self.ops[e]:
                for d in o["deps"]:
                    if d[0] == "eng":
                        self.ops[d[1]][d[2]]["ms"] = True
        msmap = {}
        for e in ENGS:
            cur = None
            cnt = 0
            for i, o in enumerate(self.ops[e]):
                if o["ms"]:
                    if cur is None or cnt >= SEM_LIMIT:
                        cur = self.sem("s_%s_%d" % (e, i))
                        cnt = 0
                    cnt += 1
                    msmap[(e, i)] = (cur, cnt)
                    o["inc"] = cur
        stats = {}

        def run(e, eng):
            waited = {}
            nw = 0
            for i, o in enumerate(self.ops[e]):
                for d in o["deps"]:
                    if d[0] == "eng":
                        sem, val = msmap[(d[1], d[2])]
                    else:
                        sem, val = d[1], d[2]
                    key = id(sem)
                    if waited.get(key, 0) >= val:
                        continue
                    waited[key] = val
                    eng.wait_ge(sem, val)
                    nw += 1
                if o["fn"] is None:
                    continue
                ins = o["fn"](eng)
                if o["dma"] is not None:
                    ins.then_inc(o["ev"][1], o.get("inc", 16))
                elif o["ms"]:
                    ins.then_inc(o["inc"], 1)
            stats[e] = (len(self.ops[e]), nw)

        with nc.Block() as block:
            @block.tensor
            def _(eng):
                run("pe", eng)

            @block.scalar
            def _(eng):
                run("act", eng)

            @block.vector
            def _(eng):
                run("dve", eng)

            @block.gpsimd
            def _(eng):
                run("pool", eng)

            @block.sync
            def _(eng):
                run("sp", eng)
        self.stats = stats
        return stats


def mask_table():
    slopes = (2.0 ** (-8.0 * np.arange(1, 7) / 6)).astype(np.float32)
    pp = np.arange(128)[:, None]
    col = np.arange(TABW)[None, :]
    delta = pp - col + 1408
    ad = np.abs(delta)
    c = (ad <= 64).astype(np.float32) + ((delta % 4 == 0) & (ad <= 256)).astype(np.float32) \
        + ((delta % 16 == 0) & (ad <= 1024)).astype(np.float32)
    tab = np.zeros((128, 6, TABW), np.float32)
    for h in range(6):
        tab[:, h, :] = c * np.exp(-(slopes[h] * ad.astype(np.float32)).astype(np.float32))
    return tab.astype(ml_dtypes.bfloat16)


def rope_tables(S):
    inv_freq = (10000.0 ** (-np.arange(0, 32, 2, dtype=np.float32) / 32)).astype(np.float32)
    ang = (np.arange(S, dtype=np.float32)[:, None] * inv_freq[None, :]).astype(np.float32)
    cos = np.cos(ang).astype(np.float32).T
    sin = np.sin(ang).astype(np.float32).T
    ct = np.concatenate([cos, cos], 0)
    st = np.concatenate([-sin, sin], 0)
    return np.ascontiguousarray(ct), np.ascontiguousarray(st)


class Cfg:
    def __init__(self, own, sf, depth=1, dbg=False, ncores=8):
        self.ncores = ncores
        self.OWN = own
        self.SF = sf
        self.OH = own + 2 * HALO
        self.NT = own // 128
        self.NQB = own // 512
        self.NG_OH = self.OH // 512
        self.NG_F = sf // 512
        self.NKT_OH = self.OH // 128
        self.NKT_F = sf // 128
        self.depth = depth
        self.dbg = dbg


W_NAMES = [
    ("w_in", [D, PIN]), ("w_q_up", [192, 576]), ("w_kv_up", [128, 768]), ("qn", [128, 2]), ("kvn", [128, 1]),
    ("sg_g", [256]), ("sg_b", [256]), ("sg_wT", [4, 128, 128]), ("sg_bT", [128, 4]), ("mixn", [128, 8]),
    ("w_out", [D, D]), ("ln1_g", [D]), ("ln1_b", [D]), ("w_ff1", [D, DFF]), ("b1T", [128, 32]),
    ("w_ff2", [DFF, D]), ("b_ff2", [D]), ("ln2_g", [D]), ("ln2_b", [D]),
]


def build_program(cfg):
    nc = bass.Bass("TRN2", target_bir_lowering=False)
    OWN, SF, OH, NT = cfg.OWN, cfg.SF, cfg.OH, cfg.NT
    io = {}
    io["xo"] = nc.dram_tensor("xo", [OH, D], F32, kind="ExternalInput").ap()
    io["xf"] = nc.dram_tensor("xf", [SF, D], F32, kind="ExternalInput").ap()
    io["vmask"] = nc.dram_tensor("vmask", [128, OH // 128], F32, kind="ExternalInput").ap()
    io["mtab"] = nc.dram_tensor("mtab", [128, 6, TABW], BF16, kind="ExternalInput").ap()
    io["ckt"] = nc.dram_tensor("ckt", [32, SF], F32, kind="ExternalInput").ap()
    io["skt"] = nc.dram_tensor("skt", [32, SF], F32, kind="ExternalInput").ap()
    io["cqt"] = nc.dram_tensor("cqt", [32, OWN], F32, kind="ExternalInput").ap()
    io["sqt"] = nc.dram_tensor("sqt", [32, OWN], F32, kind="ExternalInput").ap()
    for nm, shp in W_NAMES:
        io[nm] = nc.dram_tensor(nm, [cfg.depth] + shp, F32, kind="ExternalInput").ap()
    io["out"] = nc.dram_tensor("out", [OWN, D], F32, kind="ExternalOutput").ap()
    io["YT"] = nc.dram_tensor("YT", [D, OWN], BF16, kind="Internal").ap()
    io["X1"] = nc.dram_tensor("X1", [OWN, D], F32, kind="Internal").ap()
    if cfg.depth > 1:
        io["hsel"] = nc.dram_tensor("hsel", [128, 8], F32, kind="ExternalInput").ap()
        io["XL"] = nc.dram_tensor("XL", [OWN, D], F32, kind="Internal").ap()
        io["XBc"] = nc.dram_tensor("XBc", [OWN, D], BF16, kind="Internal").ap()
        io["XG"] = nc.dram_tensor("XG", [OWN // 512, (SF // OWN) * 512, D], BF16, kind="Internal").ap()
        io["XH"] = nc.dram_tensor("XH", [2 * HALO, D], BF16, kind="Internal").ap()
    if cfg.dbg:
        io["dbg_yt"] = nc.dram_tensor("dbg_yt", [D, OWN], BF16, kind="ExternalOutput").ap()
        io["dbg_ss"] = nc.dram_tensor("dbg_ss", [128, 13 * NT], F32, kind="ExternalOutput").ap()
        io["dbg_x1"] = nc.dram_tensor("dbg_x1", [OWN, D], F32, kind="ExternalOutput").ap()

    p = Prog(nc)
    DBG["on"] = cfg.dbg
    G = {}
    G["ident"] = p.sb("ident", [128, 128], BF16)
    identf = p.sb("identf", [128, 128], F32)
    G["e65"] = p.sb("e65", [128, 64], F32)
    G["ones"] = p.sb("onesf", [128, 128], F32)
    G["ss"] = p.sb("ss", [128, 13, NT], F32)
    p.op("pool", lambda e: e.memset(identf[:], 0.0), writes=["identf"])
    p.op("pool", lambda e: e.affine_select(out=identf[:], in_=identf[:], compare_op=ALU.not_equal, fill=1.0,
                                           base=0, pattern=[[-1, 128]], channel_multiplier=1),
         reads=["identf"], writes=["identf"])
    p.op("pool", lambda e: e.tensor_copy(G["ident"][:], identf[:]), reads=["identf"], writes=["ident"])
    p.op("pool", lambda e: e.memset(G["e65"][:], 0.0), writes=["e65"])
    p.op("pool", lambda e: e.memset(G["e65"][64:65, :], 1.0), reads=["e65"], writes=["e65"])
    p.op("pool", lambda e: e.memset(G["ones"][:], 1.0), writes=["ones"])
    p.barrier()

    ngo = cfg.NG_OH
    for l in range(cfg.depth):
        W = {nm: io[nm][l] for nm, _ in W_NAMES}
        last = (l == cfg.depth - 1)
        if l == 0:
            src = dict(oh=lambda g: io["xo"][g * 512:(g + 1) * 512, :],
                       full=lambda g: io["xf"][g * 512:(g + 1) * 512, :],
                       res=lambda ti: io["xo"][HALO + ti * 128:HALO + (ti + 1) * 128, :])
        else:
            def oh2(g):
                if g < 2:
                    return io["XH"][g * 512:(g + 1) * 512, :]
                if g >= ngo - 2:
                    return io["XH"][1024 + (g - (ngo - 2)) * 512:1024 + (g - (ngo - 2) + 1) * 512, :]
                return io["XL"][(g - 2) * 512:(g - 1) * 512, :]
            src = dict(oh=oh2, full=lambda g: xg_rows(io, cfg, g * 512, 512),
                       res=lambda ti: io["XL"][ti * 128:(ti + 1) * 128, :])
        if last:
            dst = dict(f32=io["out"], bf16=None, tag="out")
        else:
            dst = dict(f32=io["XL"], bf16=io["XBc"], tag="xl")
        layer(p, cfg, G, io, W, src, dst)
        if not last:
            exchange(p, cfg, G, io)

    if cfg.dbg:
        p.op("sp", lambda e: e.dma_start(out=io["dbg_yt"], in_=io["YT"]), reads=[], dma="out")
        p.op("sp", lambda e: e.dma_start(out=io["dbg_ss"], in_=G["ss"][:].rearrange("p a b -> p (a b)")), dma="out")
        p.op("sp", lambda e: e.dma_start(out=io["dbg_x1"], in_=io["X1"]), dma="out")
    p.wait_all_dma("sp", ["out"])
    stats = p.emit()
    p.close()
    return nc, stats


DBG = {}


def dbg_dump(p, name, ap, shape, dt):
    if not DBG.get("on"):
        return
    t = p.nc.dram_tensor("dd_" + name, list(shape), dt, kind="ExternalOutput").ap()
    p.barrier()
    p.op("sp", lambda e: e.dma_start(out=t, in_=ap), dma="out")
    p.barrier()


def load_x_group(p, src_ap, xb, key, tag):
    p.op("pool", lambda e: e.dma_start(out=xb[:], in_=src_ap.rearrange("(j p) d -> p j d", p=128)),
         writes=[key], dma=tag)


def transpose_group(p, G, xb, xbkey, xT, xTkey, trps, n_tok_tiles=4):
    for kc in range(8):
        tp = trps[kc % 2]
        tkey = "tr%d" % (kc % 2)
        for j in range(n_tok_tiles):
            p.op("pe", lambda e, tp=tp, j=j, kc=kc: e.transpose(tp[:, j * 128:(j + 1) * 128],
                                                                 xb[:, j, kc * 128:(kc + 1) * 128], G["ident"][:]),
                 reads=[xbkey, "ident"], writes=[tkey])
        w = n_tok_tiles * 128
        p.op("act", lambda e, tp=tp, kc=kc, w=w: e.copy(xT[:, kc, 0:w], tp[:, 0:w]), reads=[tkey], writes=[xTkey])


def attn_post(p, G, io, ops, okey, h_row, qb, ss_idx, bufs, it):
    r = it % 2
    osb, rden, ysq, ybf = bufs["osb"][r], bufs["rden"], bufs["ysq"], bufs["ybf"][r]
    ko, kr, ks, kb = "osb%d" % r, "rden", "ysq", "ybf%d" % r
    p.op("dve", lambda e: e.tensor_copy(osb[0:65, :], ops[0:65, :]), reads=[okey], writes=[ko])

    def deferred():
        mp, mk = bufs["peek"]()
        denps, ssps, kd, kss = mp[:, 0:512], mp[:, 512:1024], mk, mk
        p.op("pe", lambda e: e.matmul(denps[0:64, :], lhsT=G["e65"][0:65, :], rhs=osb[0:65, :], start=True, stop=True),
             reads=[ko, "e65"], writes=[kd])
        p.op("dve", lambda e: e.reciprocal(rden[0:64, :], denps[0:64, :]), reads=[kd], writes=[kr])
        p.op("pool", lambda e: e.tensor_tensor(out=osb[0:64, :], in0=osb[0:64, :], in1=rden[0:64, :], op=ALU.mult),
             reads=[ko, kr], writes=[ko])
        p.op("pool", lambda e: e.tensor_copy(ybf[0:64, :], osb[0:64, :]), reads=[ko], writes=[kb])
        p.op("sp", lambda e: e.dma_start(out=io["YT"][h_row:h_row + 64, qb * 512:(qb + 1) * 512], in_=ybf[0:64, :]),
             reads=[kb], dma="yt%d" % r)
        p.op("pool", lambda e: e.tensor_tensor(out=ysq[0:64, :], in0=osb[0:64, :], in1=osb[0:64, :], op=ALU.mult),
             reads=[ko], writes=[ks])
        for j in range(4):
            p.op("pe", lambda e, j=j: e.matmul(ssps[:, j:j + 1], lhsT=ysq[0:64, j * 128:(j + 1) * 128],
                                               rhs=G["ones"][0:64, 0:1], start=True, stop=True),
                 reads=[ks, "ones"], writes=[kss])
        p.op("dve", lambda e: e.tensor_copy(G["ss"][:, ss_idx, qb * 4:(qb + 1) * 4], ssps[:, 0:4]),
             reads=[kss], writes=["ss"])
    return deferred


def xg_rows(io, cfg, R0, n):
    j, q = R0 // cfg.OWN, R0 % cfg.OWN
    i, t = q // 512, q % 512
    assert t + n <= 512
    return io["XG"][i, j * 512 + t:j * 512 + t + n, :]


def exchange(p, cfg, G, io):
    OWN = cfg.OWN
    ngrp = cfg.SF // OWN
    groups = [list(range(b * ngrp, (b + 1) * ngrp)) for b in range(cfg.ncores // ngrp)]
    p.barrier()
    for i in range(OWN // 512):
        p.op("pool", lambda e, i=i: e.collective_compute("AllGather", ALU.bypass, replica_groups=groups,
                                                         ins=[io["XBc"][i * 512:(i + 1) * 512, :].opt()],
                                                         outs=[io["XG"][i].opt()]),
             dma="cc", inc=1)
    p.barrier()
    p.push()
    hsel = p.sb("hsel", [128, 8], F32)
    cands = [p.sb("cand%d" % i, [128, ngrp, D], BF16) for i in range(2)]
    hacc = p.sb("hacc", [128, D], F32)
    houts = [p.sb("hout%d" % i, [128, D], BF16) for i in range(2)]
    p.op("sp", lambda e: e.dma_start(out=hsel[:], in_=io["hsel"]), writes=["hsel"], dma="hsel")
    it = 0
    for side in range(2):
        for t in range(HALO // 128):
            cand, ck = cands[it % 2], "cand%d" % (it % 2)
            hout, hk = houts[it % 2], "hout%d" % (it % 2)
            for j in range(ngrp):
                r0 = j * OWN + (OWN - HALO if side == 0 else 0) + t * 128
                p.op("sp", lambda e, j=j, r0=r0: e.dma_start(out=cand[:, j, :], in_=xg_rows(io, cfg, r0, 128)),
                     writes=[ck], dma=ck)
            p.op("dve", lambda e: e.tensor_scalar(hacc[:, :], cand[:, 0, :], hsel[:, side * 4:side * 4 + 1], None, op0=ALU.mult),
                 reads=[ck, "hsel"], writes=["hacc"])
            for j in range(1, ngrp):
                p.op("dve", lambda e, j=j: e.scalar_tensor_tensor(out=hacc[:, :], in0=cand[:, j, :],
                                                                  scalar=hsel[:, side * 4 + j:side * 4 + j + 1], in1=hacc[:, :],
                                                                  op0=ALU.mult, op1=ALU.add), reads=[ck, "hsel", "hacc"], writes=["hacc"])
            p.op("pool", lambda e: e.tensor_copy(hout[:, :], hacc[:, :]), reads=["hacc"], writes=[hk])
            r1 = side * HALO + t * 128
            p.op("sp", lambda e, r1=r1: e.dma_start(out=io["XH"][r1:r1 + 128, :], in_=hout[:, :]), reads=[hk], dma="xh%d" % (it % 2))
            it += 1
    p.pop()


def layer(p, cfg, G, io, W, src, dst):
    OWN, SF, OH, NT, NQB = cfg.OWN, cfg.SF, cfg.OH, cfg.NT, cfg.NQB
    ident = G["ident"]

    p.push()
    CQ = p.sb("CQ", [128, 2, OWN], BF16)
    p.push()
    KaT = p.sb("KaT", [128, 3, OH], BF16)
    QaT = p.sb("QaT", [128, 3, OWN], BF16)
    Va = p.sb("Va", [128, OH // 128, 6, 65], BF16)

    p.push()
    w_in = p.sb("w_in", [128, 8, PIN], BF16)
    wsT = p.sb("wsT", [128, 4, 128], BF16)
    sg_g = p.sb("sg_g", [128, 256], F32)
    sg_b = p.sb("sg_b", [128, 256], F32)
    bsT = p.sb("bsT", [128, 4], F32)
    vm = p.sb("vm", [128, OH // 128], F32)
    xbs = [p.sb("xb%d" % i, [128, 4, D], BF16) for i in range(2)]
    xTs = [p.sb("xT%d" % i, [128, 8, 512], BF16) for i in range(2)]
    zh = p.sb("zh", [128, 512], F32)
    w1 = p.sb("w1", [128, 512], F32)
    w2 = p.sb("w2", [128, 512], F32)
    zz = p.sb("zz", [128, 512], F32)
    vn = p.sb("vn", [128, 256], F32)
    vnb = p.sb("vnb", [128, 256], BF16)
    yc = p.sb("yc", [128, 256], F32)
    ycb = p.sb("ycb", [128, 256], BF16)
    ycT = p.sb("ycT", [128, 2, 512], BF16)
    junk = p.sb("junk", [128, 256], F32)
    st6 = p.sb("st6", [128, 6], F32)
    mv = p.sb("mv", [128, 2], F32)
    rs = p.sb("rs", [128, 1], F32)
    cqs = [p.sb("cqs%d" % i, [128, 512], F32) for i in range(2)]
    cqq = [p.sb("cqq%d" % i, [128, 512], F32) for i in range(2)]
    cqr = p.sb("cqr", [128, 512], F32)
    trps = [p.ps("trps%d" % i, [128, 1024], BF16) for i in range(2)]
    pps = [p.ps("pps%d" % i, [128, 512]) for i in range(3)]
    mixps = p.ps("mixps", [128, 512])
    ssq = p.ps("ssq", [128, 512])

    for kc in range(8):
        p.op("pool", lambda e, kc=kc: e.dma_start(out=w_in[:, kc, :], in_=W["w_in"][kc * 128:(kc + 1) * 128, :]),
             writes=["w_in"], dma="w_in")
    p.op("pool", lambda e: e.dma_start(out=wsT[:], in_=W["sg_wT"].rearrange("g s t -> s g t")), writes=["wsT"], dma="wsm")
    p.op("sp", lambda e: e.dma_start(out=sg_g[:], in_=W["sg_g"].partition_broadcast(128)), writes=["sg_g"], dma="wsm2")
    p.op("sp", lambda e: e.dma_start(out=sg_b[:], in_=W["sg_b"].partition_broadcast(128)), writes=["sg_b"], dma="wsm2")
    p.op("sp", lambda e: e.dma_start(out=bsT[:], in_=W["sg_bT"]), writes=["bsT"], dma="wsm2")
    p.op("sp", lambda e: e.dma_start(out=vm[:], in_=io["vmask"]), writes=["vm"], dma="wsm2")
    for h in range(6):
        p.op("dve", lambda e, h=h: e.tensor_copy(Va[:, :, h, 64:65], vm[:].rearrange("p (t o) -> p t o", o=1)),
             reads=["vm"], writes=["Va"])

    pp_i = [0]

    def next_pps():
        i = pp_i[0] % 3
        pp_i[0] += 1
        return pps[i], "pps%d" % i

    ngo = cfg.NG_OH
    load_x_group(p, src["oh"](0), xbs[0], "xb0", "xb0")
    for g in range(ngo):
        xb, xbk = xbs[g % 2], "xb%d" % (g % 2)
        xT, xTk = xTs[g % 2], "xT%d" % (g % 2)
        if g + 1 < ngo:
            load_x_group(p, src["oh"](g + 1), xbs[(g + 1) % 2], "xb%d" % ((g + 1) % 2), "xb%d" % ((g + 1) % 2))
        transpose_group(p, G, xb, xbk, xT, xTk, trps)
        own = (g * 512 >= HALO) and (g * 512 < HALO + OWN)
        go = g - HALO // 512
        for c in range(3):
            ps_, pk = next_pps()
            for kc in range(8):
                p.op("pe", lambda e, ps_=ps_, kc=kc, c=c: e.matmul(ps_[:, :], lhsT=w_in[:, kc, 384 + c * 128:384 + (c + 1) * 128],
                                                                   rhs=xT[:, kc, :], start=(kc == 0), stop=(kc == 7)),
                     reads=["w_in", xTk], writes=[pk])
            p.op("dve", lambda e, ps_=ps_, c=c, g=g: e.tensor_copy(KaT[:, c, g * 512:(g + 1) * 512], ps_[:, :]),
                 reads=[pk], writes=["KaT"])
        if own:
            for c in range(3):
                ps_, pk = next_pps()
                for kc in range(8):
                    p.op("pe", lambda e, ps_=ps_, kc=kc, c=c: e.matmul(ps_[:, :], lhsT=w_in[:, kc, c * 128:(c + 1) * 128],
                                                                       rhs=xT[:, kc, :], start=(kc == 0), stop=(kc == 7)),
                         reads=["w_in", xTk], writes=[pk])
                p.op("dve", lambda e, ps_=ps_, c=c, go=go: e.tensor_copy(QaT[:, c, go * 512:(go + 1) * 512], ps_[:, :]),
                     reads=[pk], writes=["QaT"])
        for j in range(4):
            ps_, pk = next_pps()
            for kc in range(8):
                p.op("pe", lambda e, ps_=ps_, kc=kc, j=j: e.matmul(ps_[:, 0:384], lhsT=xT[:, kc, j * 128:(j + 1) * 128],
                                                                   rhs=w_in[:, kc, 768:1152], start=(kc == 0), stop=(kc == 7)),
                     reads=["w_in", xTk], writes=[pk])
            p.op("dve", lambda e, ps_=ps_, j=j, g=g: e.tensor_copy(Va[:, g * 4 + j, :, 0:64],
                                                                   ps_[:, 0:384].rearrange("p (h c) -> p h c", h=6)),
                 reads=[pk], writes=["Va"])
        if not own:
            continue
        psa, pka = next_pps()
        psb, pkb = next_pps()
        for kc in range(8):
            p.op("pe", lambda e, kc=kc: e.matmul(psa[:, :], lhsT=w_in[:, kc, 1152:1280], rhs=xT[:, kc, :],
                                                 start=(kc == 0), stop=(kc == 7)), reads=["w_in", xTk], writes=[pka])
        for kc in range(8):
            p.op("pe", lambda e, kc=kc: e.matmul(psb[0:64, :], lhsT=w_in[:, kc, 1280:1344], rhs=xT[:, kc, :],
                                                 start=(kc == 0), stop=(kc == 7)), reads=["w_in", xTk], writes=[pkb])
        p.op("dve", lambda e: e.tensor_copy(cqs[0][:, :], psa[:, :]), reads=[pka], writes=["cqs0"])
        p.op("dve", lambda e: e.tensor_copy(cqs[1][0:64, :], psb[0:64, :]), reads=[pkb], writes=["cqs1"])
        p.op("pool", lambda e: e.tensor_tensor(out=cqq[0][:, :], in0=cqs[0][:, :], in1=cqs[0][:, :], op=ALU.mult),
             reads=["cqs0"], writes=["cqq0"])
        p.op("pool", lambda e: e.tensor_tensor(out=cqq[1][0:64, :], in0=cqs[1][0:64, :], in1=cqs[1][0:64, :], op=ALU.mult),
             reads=["cqs1"], writes=["cqq1"])
        p.op("pe", lambda e: e.matmul(ssq[:, :], lhsT=G["ones"][:, :], rhs=cqq[0][:, :], start=True, stop=False),
             reads=["cqq0", "ones"], writes=["ssq"])
        p.op("pe", lambda e: e.matmul(ssq[:, :], lhsT=G["ones"][0:64, :], rhs=cqq[1][0:64, :], start=False, stop=True),
             reads=["cqq1", "ones"], writes=["ssq"])
        p.op("act", lambda e: e.activation(out=cqr[:, :], in_=ssq[:, :], func=AF.Sqrt, scale=1.0 / 192, bias=EPS),
             reads=["ssq"], writes=["cqr"])
        p.op("dve", lambda e: e.reciprocal(cqr[:, :], cqr[:, :]), reads=["cqr"], writes=["cqr"])
        p.op("dve", lambda e, go=go: e.tensor_tensor(out=CQ[:, 0, go * 512:(go + 1) * 512], in0=cqs[0][:, :], in1=cqr[:, :],
                                                     op=ALU.mult), reads=["cqs0", "cqr"], writes=["CQ"])
        p.op("dve", lambda e, go=go: e.tensor_tensor(out=CQ[0:64, 1, go * 512:(go + 1) * 512], in0=cqs[1][0:64, :],
                                                     in1=cqr[0:64, :], op=ALU.mult), reads=["cqs1", "cqr"], writes=["CQ"])
        for j in range(4):
            ti = go * 4 + j
            ps_, pk = next_pps()
            for kc in range(8):
                p.op("pe", lambda e, ps_=ps_, kc=kc, j=j: e.matmul(ps_[:, :], lhsT=xT[:, kc, j * 128:(j + 1) * 128],
                                                                   rhs=w_in[:, kc, 1504:2016], start=(kc == 0), stop=(kc == 7)),
                     reads=["w_in", xTk], writes=[pk])
            p.op("act", lambda e, ps_=ps_: e.activation(out=zh[:, :], in_=ps_[:, :], func=AF.Copy, scale=0.5),
                 reads=[pk], writes=["zh"])
            p.op("pool", lambda e: e.tensor_tensor(out=w1[:, :], in0=zh[:, :], in1=zh[:, :], op=ALU.mult),
                 reads=["zh"], writes=["w1"])
            p.op("dve", lambda e: e.tensor_scalar(w1[:, :], w1[:, :], 4.0 * C_GELU, 1.0, op0=ALU.mult, op1=ALU.add),
                 reads=["w1"], writes=["w1"])
            p.op("pool", lambda e: e.tensor_tensor(out=w2[:, :], in0=w1[:, :], in1=zh[:, :], op=ALU.mult),
                 reads=["w1", "zh"], writes=["w2"])
            p.op("act", lambda e: e.activation(out=w2[:, :], in_=w2[:, :], func=AF.Tanh, scale=2.0 * K_GELU),
                 reads=["w2"], writes=["w2"])
            p.op("dve", lambda e: e.scalar_tensor_tensor(out=zz[:, :], in0=w2[:, :], scalar=1.0, in1=zh[:, :],
                                                         op0=ALU.add, op1=ALU.mult), reads=["w2", "zh"], writes=["zz"])
            p.op("dve", lambda e: e.bn_stats(st6[:, :], zz[:, 256:512]), reads=["zz"], writes=["st6"])
            p.op("dve", lambda e: e.bn_aggr(mv[:, :], st6[:, :]), reads=["st6"], writes=["mv"])
            p.op("act", lambda e: e.activation(out=rs[:, :], in_=mv[:, 1:2], func=AF.Sqrt, scale=1.0, bias=EPS),
                 reads=["mv"], writes=["rs"])
            p.op("dve", lambda e: e.reciprocal(rs[:, :], rs[:, :]), reads=["rs"], writes=["rs"])
            p.op("dve", lambda e: e.tensor_scalar(vn[:, :], zz[:, 256:512], mv[:, 0:1], rs[:, 0:1], op0=ALU.subtract,
                                                  op1=ALU.mult), reads=["zz", "mv", "rs"], writes=["vn"])
            p.op("pool", lambda e: e.tensor_tensor(out=vn[:, :], in0=vn[:, :], in1=sg_g[:, :], op=ALU.mult),
                 reads=["vn", "sg_g"], writes=["vn"])
            p.op("pool", lambda e: e.tensor_tensor(out=vnb[:, :], in0=vn[:, :], in1=sg_b[:, :], op=ALU.add),
                 reads=["vn", "sg_b"], writes=["vnb"])
            for gg in range(4):
                p.op("pe", lambda e, gg=gg: e.matmul(mixps[:, gg * 64:(gg + 1) * 64], lhsT=wsT[:, gg, :],
                                                     rhs=vnb[:, gg * 64:(gg + 1) * 64], start=True, stop=True),
                     reads=["wsT", "vnb"], writes=["mixps"])
            for gg in range(4):
                p.op("dve", lambda e, gg=gg: e.scalar_tensor_tensor(out=yc[:, gg * 64:(gg + 1) * 64],
                                                                    in0=mixps[:, gg * 64:(gg + 1) * 64], scalar=bsT[:, gg:gg + 1],
                                                                    in1=zz[:, gg * 64:(gg + 1) * 64], op0=ALU.add, op1=ALU.mult),
                     reads=["mixps", "bsT", "zz"], writes=["yc"])
            p.op("pool", lambda e: e.tensor_tensor(out=junk[:, :], in0=yc[:, :], in1=yc[:, :], op=ALU.mult),
                 reads=["yc"], writes=["junk"])
            p.op("dve", lambda e, ti=ti: e.reduce_sum(out=G["ss"][:, 12, ti:ti + 1], in_=junk[:, :], axis=AX.X),
                 reads=["junk"], writes=["ss"])
            p.op("pool", lambda e: e.tensor_copy(ycb[:, :], yc[:, :]), reads=["yc"], writes=["ycb"])
            for c2 in range(2):
                tp = trps[c2]
                p.op("pe", lambda e, tp=tp, c2=c2: e.transpose(tp[:, 512:640], ycb[:, c2 * 128:(c2 + 1) * 128], ident[:]),
                     reads=["ycb", "ident"], writes=["tr%d" % c2])
                p.op("act", lambda e, tp=tp, c2=c2, j=j: e.copy(ycT[:, c2, j * 128:(j + 1) * 128], tp[:, 512:640]),
                     reads=["tr%d" % c2], writes=["ycT"])
        p.op("sp", lambda e, go=go: e.dma_start(out=io["YT"][768:1024, go * 512:(go + 1) * 512].rearrange("(c p) t -> p c t", p=128),
                                                in_=ycT[:, :, :]), reads=["ycT"], dma="ytc")
    dbg_dump(p, "xT", xTs[(ngo - 1) % 2][:, :, :], [128, 8, 512], BF16)
    dbg_dump(p, "KaT", KaT[:, :, :], [128, 3, OH], BF16)
    dbg_dump(p, "QaT", QaT[:, :, :], [128, 3, OWN], BF16)
    dbg_dump(p, "Va", Va[:, :, :, :], [128, OH // 128, 6, 65], BF16)
    dbg_dump(p, "CQ", CQ[:, :, :], [128, 2, OWN], BF16)
    dbg_dump(p, "w_in", w_in[:, :, :], [128, 8, PIN], BF16)
    p.pop()

    p.push()
    tab = p.sb("tab", [128, 6, TABW], BF16)
    NSA = 3
    Es = [p.sb("E%d" % i, [128, 1024], BF16) for i in range(NSA)]
    PTs = [p.sb("PT%d" % i, [128, 1024], BF16) for i in range(NSA)]
    Sps = [p.ps("S%d" % i, [128, 1024]) for i in range(NSA)]
    Ops = [p.ps("O%d" % i, [128, 512]) for i in range(2)]
    gic = [0]

    def alloc_s():
        i = gic[0] % NSA
        gic[0] += 1
        return Sps[i], "S%d" % i

    def peek_s():
        i = gic[0] % NSA
        return Sps[i], "S%d" % i
    bufs = dict(peek=peek_s, osb=[p.sb("osb%d" % i, [128, 512], F32) for i in range(2)], rden=p.sb("rden", [128, 512], F32),
                ysq=p.sb("ysq", [128, 512], F32), ybf=[p.sb("ybf%d" % i, [128, 512], BF16) for i in range(2)],
                alloc=alloc_s)
    for h in range(6):
        p.op("sp", lambda e, h=h: e.dma_start(out=tab[:, h, :], in_=io["mtab"][:, h, :]), writes=["tab"], dma="tab")
    it = 0
    pend = []
    NKT2 = 20
    for c in range(3):
        hh = (2 * c, 2 * c + 1)
        for qb in range(NQB):
            kt0 = 4 * qb

            def qk(ii):
                S, sk = alloc_s()
                kt = kt0 + ii
                for u in range(2):
                    p.op("pe", lambda e, S=S, u=u, kt=kt: e.matmul(S[:, u * 512:(u + 1) * 512],
                                                                   lhsT=KaT[u * 64:u * 64 + 64, c, kt * 128:(kt + 1) * 128],
                                                                   rhs=QaT[u * 64:u * 64 + 64, c, qb * 512:(qb + 1) * 512],
                                                                   start=True, stop=True),
                         reads=["KaT", "QaT"], writes=[sk])
                return S, sk

            inflight = [qk(0), qk(1)]
            for ii in range(NKT2):
                if ii + 2 < NKT2:
                    inflight.append(qk(ii + 2))
                S, sk = inflight.pop(0)
                E, ek = Es[ii % NSA], "E%d" % (ii % NSA)
                PT, pk = PTs[ii % NSA], "PT%d" % (ii % NSA)
                kt = kt0 + ii
                start = 2432 - 128 * ii
                p.op("act", lambda e, S=S, E=E: e.activation(out=E[:, :], in_=S[:, :], func=AF.Exp, scale=A_SCALE),
                     reads=[sk], writes=[ek])
                for u in range(2):
                    p.op("dve", lambda e, E=E, PT=PT, u=u, start=start: e.tensor_tensor(
                        out=PT[:, u * 512:(u + 1) * 512], in0=E[:, u * 512:(u + 1) * 512],
                        in1=tab[:, hh[u], start:start + 512], op=ALU.mult), reads=[ek, "tab"], writes=[pk + "_%d" % u])
                for u in range(2):
                    p.op("pe", lambda e, PT=PT, u=u, kt=kt, ii=ii: e.matmul(
                        Ops[u][0:65, :], lhsT=Va[:, kt, hh[u], 0:65], rhs=PT[:, u * 512:(u + 1) * 512],
                        start=(ii == 0), stop=(ii == NKT2 - 1)),
                        reads=["Va", pk + "_%d" % u], writes=["O%d" % u])
                if ii == 3:
                    while pend:
                        pend.pop(0)()
            for u in range(2):
                pend.append(attn_post(p, G, io, Ops[u], "O%d" % u, hh[u] * 64, qb, hh[u], bufs, u))
            it += 1
    while pend:
        pend.pop(0)()
    p.pop()
    p.pop()

    CKV = p.sb("CKV", [128, SF], BF16)
    KT = p.sb("KT", [128, SF], BF16)
    p.push()
    wB = p.sb("wB", [128, 8, 160], BF16)
    wk96 = p.sb("wk96", [128, 8, 96], BF16)
    wk96r = p.sb("wk96r", [128, 8, 96], BF16)
    xbs = [p.sb("bxb%d" % i, [128, 4, D], BF16) for i in range(2)]
    xTs = [p.sb("bxT%d" % i, [128, 8, 512], BF16) for i in range(2)]
    sq = p.sb("bsq", [128, 512], F32)
    rr = p.sb("brr", [128, 512], F32)
    cks = [p.sb("cks%d" % i, [128, 512], F32) for i in range(2)]
    sks = [p.sb("sks%d" % i, [128, 512], F32) for i in range(2)]
    t1 = p.sb("bt1", [128, 512], F32)
    t2 = p.sb("bt2", [128, 512], F32)
    trps = [p.ps("btrps%d" % i, [128, 1024], BF16) for i in range(2)]
    pps = [p.ps("bpps%d" % i, [128, 512]) for i in range(4)]
    ssq = p.ps("bssq", [128, 512])
    for kc in range(8):
        p.op("pool", lambda e, kc=kc: e.dma_start(out=wB[:, kc, :], in_=W["w_in"][kc * 128:(kc + 1) * 128, 1344:1504]),
             writes=["wB"], dma="wB")
    p.op("dve", lambda e: e.memset(wk96[:], 0.0), writes=["wk96"])
    p.op("dve", lambda e: e.memset(wk96r[:], 0.0), writes=["wk96r"])
    p.op("dve", lambda e: e.tensor_copy(wk96[:, :, 64:96], wB[:, :, 128:160]), reads=["wB", "wk96"], writes=["wk96"])
    p.op("dve", lambda e: e.tensor_copy(wk96r[:, :, 64:80], wB[:, :, 144:160]), reads=["wB", "wk96r"], writes=["wk96r"])
    p.op("dve", lambda e: e.tensor_copy(wk96r[:, :, 80:96], wB[:, :, 128:144]), reads=["wB", "wk96r"], writes=["wk96r"])
    ngf = cfg.NG_F
    load_x_group(p, src["full"](0), xbs[0], "bxb0", "bxb0")
    for g in range(ngf):
        xb, xbk = xbs[g % 2], "bxb%d" % (g % 2)
        xT, xTk = xTs[g % 2], "bxT%d" % (g % 2)
        ck, ckk = cks[g % 2], "cks%d" % (g % 2)
        sk_, skk = sks[g % 2], "sks%d" % (g % 2)
        if g + 1 < ngf:
            load_x_group(p, src["full"](g + 1), xbs[(g + 1) % 2], "bxb%d" % ((g + 1) % 2), "bxb%d" % ((g + 1) % 2))
        p.op("sp", lambda e, ck=ck, g=g: e.dma_start(out=ck[64:96, :], in_=io["ckt"][:, g * 512:(g + 1) * 512]),
             writes=[ckk], dma=ckk)
        p.op("sp", lambda e, sk_=sk_, g=g: e.dma_start(out=sk_[64:96, :], in_=io["skt"][:, g * 512:(g + 1) * 512]),
             writes=[skk], dma=skk)
        transpose_group(p, G, xb, xbk, xT, xTk, trps)
        pa, pb, pc = pps[(3 * g) % 4], pps[(3 * g + 1) % 4], pps[(3 * g + 2) % 4]
        ka, kb, kc_ = "bpps%d" % ((3 * g) % 4), "bpps%d" % ((3 * g + 1) % 4), "bpps%d" % ((3 * g + 2) % 4)
        for kc in range(8):
            p.op("pe", lambda e, kc=kc, pa=pa: e.matmul(pa[:, :], lhsT=wB[:, kc, 0:128], rhs=xT[:, kc, :],
                                                        start=(kc == 0), stop=(kc == 7)), reads=["wB", xTk], writes=[ka])
        for kc in range(8):
            p.op("pe", lambda e, kc=kc, pb=pb: e.matmul(pb[0:96, :], lhsT=wk96[:, kc, :], rhs=xT[:, kc, :],
                                                        start=(kc == 0), stop=(kc == 7)), reads=["wk96", xTk], writes=[kb])
        for kc in range(8):
            p.op("pe", lambda e, kc=kc, pc=pc: e.matmul(pc[0:96, :], lhsT=wk96r[:, kc, :], rhs=xT[:, kc, :],
                                                        start=(kc == 0), stop=(kc == 7)), reads=["wk96r", xTk], writes=[kc_])
        p.op("act", lambda e, pa=pa: e.activation(out=sq[:, :], in_=pa[:, :], func=AF.Square), reads=[ka], writes=["bsq"])
        p.op("pe", lambda e: e.matmul(ssq[:, :], lhsT=G["ones"][:, :], rhs=sq[:, :], start=True, stop=True),
             reads=["bsq", "ones"], writes=["bssq"])
        p.op("act", lambda e: e.activation(out=rr[:, :], in_=ssq[:, :], func=AF.Sqrt, scale=1.0 / 128, bias=EPS),
             reads=["bssq"], writes=["brr"])
        p.op("dve", lambda e: e.reciprocal(rr[:, :], rr[:, :]), reads=["brr"], writes=["brr"])
        p.op("dve", lambda e, pa=pa, g=g: e.tensor_tensor(out=CKV[:, g * 512:(g + 1) * 512], in0=pa[:, :], in1=rr[:, :],
                                                          op=ALU.mult), reads=[ka, "brr"], writes=["CKV"])
        p.op("dve", lambda e, pb=pb, ck=ck: e.tensor_tensor(out=t1[64:96, :], in0=pb[64:96, :], in1=ck[64:96, :], op=ALU.mult),
             reads=[kb, ckk], writes=["bt1"])
        p.op("dve", lambda e, pc=pc, sk_=sk_: e.tensor_tensor(out=t2[64:96, :], in0=pc[64:96, :], in1=sk_[64:96, :], op=ALU.mult),
             reads=[kc_, skk], writes=["bt2"])
        p.op("pool", lambda e, g=g: e.tensor_tensor(out=KT[64:96, g * 512:(g + 1) * 512], in0=t1[64:96, :], in1=t2[64:96, :],
                                                    op=ALU.add), reads=["bt1", "bt2"], writes=["KT"])
    p.pop()

    p.push()
    Vh = p.sb("Vh", [128, SF // 128, 65], BF16)
    QT = p.sb("QT", [128, OWN], BF16)
    cqt = p.sb("cqt", [128, OWN], F32)
    sqt = p.sb("sqt", [128, OWN], F32)
    wkv = p.sb("wkv", [128, 768], BF16)
    wkvf = p.sb("wkvf", [128, 768], F32)
    wqf = p.sb("wqf", [128, 2, 576], F32)
    wq = p.sb("wq", [128, 2, 6, 96], BF16)
    wqr = p.sb("wqr", [128, 2, 6, 96], BF16)
    qn = p.sb("qn", [128, 2], F32)
    kvn = p.sb("kvn", [128, 1], F32)
    q1 = p.sb("q1", [128, 512], F32)
    q2 = p.sb("q2", [128, 512], F32)
    NSB = 3
    PTs = [p.sb("bPT%d" % i, [128, 1024], BF16) for i in range(NSB)]
    Sps = [p.ps("bS%d" % i, [128, 1024]) for i in range(NSB)]
    Ops = [p.ps("bO%d" % i, [128, 512]) for i in range(2)]
    pend = []
    gib = [0]

    def alloc_sb():
        i = gib[0] % NSB
        gib[0] += 1
        return Sps[i], "bS%d" % i

    def peek_sb():
        i = gib[0] % NSB
        return Sps[i], "bS%d" % i
    bufs = dict(peek=peek_sb, osb=[p.sb("bosb%d" % i, [128, 512], F32) for i in range(2)], rden=p.sb("brden", [128, 512], F32),
                ysq=p.sb("bysq", [128, 512], F32), ybf=[p.sb("bybf%d" % i, [128, 512], BF16) for i in range(2)],
                alloc=alloc_sb)
    p.op("sp", lambda e: e.dma_start(out=wkvf[:], in_=W["w_kv_up"]), writes=["wkvf"], dma="bw")
    p.op("sp", lambda e: e.dma_start(out=wqf[:, 0, :], in_=W["w_q_up"][0:128, :]), writes=["wqf"], dma="bw")
    p.op("sp", lambda e: e.dma_start(out=wqf[0:64, 1, :], in_=W["w_q_up"][128:192, :]), writes=["wqf"], dma="bw")
    p.op("sp", lambda e: e.dma_start(out=qn[:], in_=W["qn"]), writes=["qn"], dma="bw")
    p.op("sp", lambda e: e.dma_start(out=kvn[:], in_=W["kvn"]), writes=["kvn"], dma="bw")
    p.op("sp", lambda e: e.dma_start(out=cqt[64:96, :], in_=io["cqt"]), writes=["cqt"], dma="bw")
    p.op("sp", lambda e: e.dma_start(out=sqt[64:96, :], in_=io["sqt"]), writes=["sqt"], dma="bw")
    p.op("dve", lambda e: e.tensor_scalar(wkv[:, :], wkvf[:, :], kvn[:, 0:1], None, op0=ALU.mult), reads=["wkvf", "kvn"],
         writes=["wkv"])
    p.op("dve", lambda e: e.memset(wq[:], 0.0), writes=["wq"])
    p.op("dve", lambda e: e.memset(wqr[:], 0.0), writes=["wqr"])
    for cc, np_ in ((0, 128), (1, 64)):
        wv = wqf[0:np_, cc, :].rearrange("p (h c) -> p h c", h=6)
        p.op("dve", lambda e, cc=cc, np_=np_, wv=wv: e.tensor_scalar(wq[0:np_, cc, :, :], wv, qn[0:np_, cc:cc + 1], None,
                                                                      op0=ALU.mult), reads=["wqf", "qn", "wq"], writes=["wq"])
        p.op("dve", lambda e, cc=cc, np_=np_, wv=wv: e.tensor_scalar(wqr[0:np_, cc, :, 64:80], wv[:, :, 80:96],
                                                                      qn[0:np_, cc:cc + 1], None, op0=ALU.mult),
             reads=["wqf", "qn", "wqr"], writes=["wqr"])
        p.op("dve", lambda e, cc=cc, np_=np_, wv=wv: e.tensor_scalar(wqr[0:np_, cc, :, 80:96], wv[:, :, 64:80],
                                                                      qn[0:np_, cc:cc + 1], None, op0=ALU.mult),
             reads=["wqf", "qn", "wqr"], writes=["wqr"])
    p.op("pool", lambda e: e.memset(Vh[:, :, 64:65], 1.0), writes=["Vh"])
    it = 0
    gi = 0
    nkt = SF // 128
    for h in range(6):
        for g2 in range(SF // 1024):
            S, sk = alloc_sb()
            for u in range(2):
                g = 2 * g2 + u
                p.op("pe", lambda e, S=S, u=u, g=g, h=h: e.matmul(S[0:64, u * 512:(u + 1) * 512], lhsT=wkv[:, h * 128:h * 128 + 64],
                                                                  rhs=CKV[:, g * 512:(g + 1) * 512], start=True, stop=True),
                     reads=["wkv", "CKV"], writes=[sk])
            p.op("dve", lambda e, S=S, g2=g2: e.tensor_copy(KT[0:64, g2 * 1024:(g2 + 1) * 1024], S[0:64, :]),
                 reads=[sk], writes=["KT"])
        for t16 in range(nkt // 16):
            S, sk = alloc_sb()
            for u in range(16):
                t = t16 * 16 + u
                p.op("pe", lambda e, S=S, u=u, t=t, h=h: e.matmul(S[:, u * 64:(u + 1) * 64], lhsT=CKV[:, t * 128:(t + 1) * 128],
                                                                  rhs=wkv[:, h * 128 + 64:h * 128 + 128], start=True, stop=True),
                     reads=["wkv", "CKV"], writes=[sk])
            p.op("dve", lambda e, S=S, t16=t16: e.tensor_copy(Vh[:, t16 * 16:(t16 + 1) * 16, 0:64],
                                                               S[:, :].rearrange("p (t c) -> p t c", c=64)),
                 reads=[sk], writes=["Vh"])
        for qb in range(NQB):
            S, sk = alloc_sb()
            for u, wsrc in ((0, wq), (1, wqr)):
                p.op("pe", lambda e, S=S, u=u, wsrc=wsrc, h=h, qb=qb: e.matmul(S[0:96, u * 512:(u + 1) * 512], lhsT=wsrc[:, 0, h, :],
                                                                               rhs=CQ[:, 0, qb * 512:(qb + 1) * 512],
                                                                               start=True, stop=False),
                     reads=["wq", "wqr", "CQ"], writes=[sk])
                p.op("pe", lambda e, S=S, u=u, wsrc=wsrc, h=h, qb=qb: e.matmul(S[0:96, u * 512:(u + 1) * 512], lhsT=wsrc[0:64, 1, h, :],
                                                                               rhs=CQ[0:64, 1, qb * 512:(qb + 1) * 512],
                                                                               start=False, stop=True),
                     reads=["wq", "wqr", "CQ"], writes=[sk])
            p.op("dve", lambda e, S=S, qb=qb: e.tensor_copy(QT[0:64, qb * 512:(qb + 1) * 512], S[0:64, 0:512]),
                 reads=[sk], writes=["QT"])
            p.op("dve", lambda e, S=S, qb=qb: e.tensor_tensor(out=q1[64:96, :], in0=S[64:96, 0:512],
                                                              in1=cqt[64:96, qb * 512:(qb + 1) * 512], op=ALU.mult),
                 reads=[sk, "cqt"], writes=["q1"])
            p.op("dve", lambda e, S=S, qb=qb: e.tensor_tensor(out=q2[64:96, :], in0=S[64:96, 512:1024],
                                                              in1=sqt[64:96, qb * 512:(qb + 1) * 512], op=ALU.mult),
                 reads=[sk, "sqt"], writes=["q2"])
            p.op("pool", lambda e, qb=qb: e.tensor_tensor(out=QT[64:96, qb * 512:(qb + 1) * 512], in0=q1[64:96, :],
                                                          in1=q2[64:96, :], op=ALU.add), reads=["q1", "q2"], writes=["QT"])
        for qb in range(NQB):
            ops, okey = Ops[it % 2], "bO%d" % (it % 2)
            ngrp = nkt // 2

            def qk(i):
                S, sk = alloc_sb()
                for u in range(2):
                    kt = 2 * i + u
                    p.op("pe", lambda e, S=S, u=u, kt=kt: e.matmul(S[:, u * 512:(u + 1) * 512],
                                                                   lhsT=KT[0:96, kt * 128:(kt + 1) * 128],
                                                                   rhs=QT[0:96, qb * 512:(qb + 1) * 512], start=True, stop=True),
                         reads=["KT", "QT"], writes=[sk])
                return S, sk

            inflight = [qk(0), qk(1)]
            for i in range(ngrp):
                if i + 2 < ngrp:
                    inflight.append(qk(i + 2))
                S, sk = inflight.pop(0)
                PT, pk = PTs[i % NSB], "bPT%d" % (i % NSB)
                p.op("act", lambda e, S=S, PT=PT: e.activation(out=PT[:, :], in_=S[:, :], func=AF.Exp, scale=B_SCALE),
                     reads=[sk], writes=[pk])
                for u in range(2):
                    kt = 2 * i + u
                    p.op("pe", lambda e, PT=PT, u=u, kt=kt, ops=ops, i=i: e.matmul(
                        ops[0:65, :], lhsT=Vh[:, kt, 0:65], rhs=PT[:, u * 512:(u + 1) * 512],
                        start=(i == 0 and u == 0), stop=(i == ngrp - 1 and u == 1)),
                        reads=["Vh", pk], writes=[okey])
                if i == 3 and pend:
                    pend.pop()()
            pend.append(attn_post(p, G, io, ops, okey, 384 + h * 64, qb, 6 + h, bufs, it))
            it += 1
        while pend:
            pend.pop()()
    p.pop()
    p.pop()

    p.push()
    wf1 = p.sb("wf1", [128, 8, DFF], BF16)
    wf2 = p.sb("wf2", [128, 32, D], BF16)
    p.push()
    wo = p.sb("wo", [128, 8, D], BF16)
    wof = p.sb("wof", [128, D], F32)
    mixn = p.sb("mixn", [128, 8], F32)
    g1 = p.sb("g1", [128, D], F32)
    b1 = p.sb("b1", [128, D], F32)
    yTs = [p.sb("yT%d" % i, [128, 8, 512], BF16) for i in range(2)]
    xrs = [p.sb("xr%d" % i, [128, D], F32) for i in range(2)]
    accs = [p.sb("dacc%d" % i, [128, D], F32) for i in range(2)]
    x1s = [p.sb("x1s%d" % i, [128, D], F32) for i in range(2)]
    rst = p.sb("rst", [128, 3, NT], F32)
    sst = p.sb("sst", [128, 3, NT], F32)
    st12s = [p.sb("st12_%d" % i, [128, 12], F32) for i in range(2)]
    mvs = [p.sb("dmv%d" % i, [128, 2], F32) for i in range(2)]
    rss = [p.sb("drs%d" % i, [128, 1], F32) for i in range(2)]
    accps = [p.ps("accps%d" % i, [128, 1024]) for i in range(2)]
    p.op("sp", lambda e: e.dma_start(out=mixn[:], in_=W["mixn"]), writes=["mixn"], dma="dw")
    p.op("sp", lambda e: e.dma_start(out=g1[:], in_=W["ln1_g"].partition_broadcast(128)), writes=["g1"], dma="dw")
    p.op("sp", lambda e: e.dma_start(out=b1[:], in_=W["ln1_b"].partition_broadcast(128)), writes=["b1"], dma="dw")
    for kc in range(8):
        p.op("sp", lambda e, kc=kc: e.dma_start(out=wof[:], in_=W["w_out"][kc * 128:(kc + 1) * 128, :]), writes=["wof"], dma="dwo")
        p.op("dve", lambda e, kc=kc: e.tensor_scalar(wo[:, kc, :], wof[:, :], mixn[:, kc:kc + 1], None, op0=ALU.mult),
             reads=["wof", "mixn"], writes=["wo"])
    for kc in range(8):
        p.op("pool", lambda e, kc=kc: e.dma_start(out=wf1[:, kc, :], in_=W["w_ff1"][kc * 128:(kc + 1) * 128, :]),
             writes=["wf1"], dma="wf1")
    for c4 in range(8):
        p.op("pool", lambda e, c4=c4: e.dma_start(out=wf2[:, c4 * 4:(c4 + 1) * 4, :],
                                                  in_=W["w_ff2"][c4 * 512:(c4 + 1) * 512, :].rearrange("(c p) d -> p c d", p=128)),
             writes=["wf2"], dma="wf2")
    ssv = G["ss"]
    p.op("dve", lambda e: e.tensor_copy(sst[:, 0, :], ssv[:, 0, :]), reads=["ss"], writes=["sst"])
    p.op("dve", lambda e: e.tensor_copy(sst[:, 1, :], ssv[:, 6, :]), reads=["ss"], writes=["sst"])
    p.op("dve", lambda e: e.tensor_copy(sst[:, 2, :], ssv[:, 12, :]), reads=["ss"], writes=["sst"])
    for k in range(1, 6):
        p.op("dve", lambda e, k=k: e.tensor_tensor(out=sst[:, 0, :], in0=sst[:, 0, :], in1=ssv[:, k, :], op=ALU.add),
             reads=["ss", "sst"], writes=["sst"])
        p.op("dve", lambda e, k=k: e.tensor_tensor(out=sst[:, 1, :], in0=sst[:, 1, :], in1=ssv[:, 6 + k, :], op=ALU.add),
             reads=["ss", "sst"], writes=["sst"])
    for gidx, wdt in ((0, 384.0), (1, 384.0), (2, 256.0)):
        p.op("act", lambda e, gidx=gidx, wdt=wdt: e.activation(out=rst[:, gidx, :], in_=sst[:, gidx, :], func=AF.Sqrt,
                                                               scale=1.0 / wdt, bias=EPS), reads=["sst"], writes=["rst"])
    p.op("dve", lambda e: e.reciprocal(rst[:, :, :], rst[:, :, :]), reads=["rst"], writes=["rst"])
    KCG = ((0, 3), (3, 6), (6, 8))
    ai = 0
    for g in range(NQB):
        yT, yk = yTs[g % 2], "yT%d" % (g % 2)
        p.op("sp", lambda e, yT=yT, g=g: e.dma_start(out=yT[:, :, :], in_=io["YT"][:, g * 512:(g + 1) * 512].rearrange("(c p) t -> p c t", p=128)),
             writes=[yk], dma=yk)
        for j in range(4):
            ti = g * 4 + j
            xr, xk = xrs[ti % 2], "xr%d" % (ti % 2)
            x1, x1k = x1s[ti % 2], "x1s%d" % (ti % 2)
            acc, acck = accs[ti % 2], "dacc%d" % (ti % 2)
            p.op("sp", lambda e, xr=xr, ti=ti: e.dma_start(out=xr[:, :], in_=src["res"](ti)),
                 writes=[xk], dma=xk)
            for gidx, (k0, k1) in enumerate(KCG):
                ap_, ak = accps[ai % 2], "accps%d" % (ai % 2)
                ai += 1
                for half in range(2):
                    for kc in range(k0, k1):
                        p.op("pe", lambda e, ap_=ap_, half=half, kc=kc, k0=k0, k1=k1, j=j: e.matmul(
                            ap_[:, half * 512:(half + 1) * 512], lhsT=yT[:, kc, j * 128:(j + 1) * 128],
                            rhs=wo[:, kc, half * 512:(half + 1) * 512], start=(kc == k0), stop=(kc == k1 - 1)),
                            reads=[yk, "wo"], writes=[ak])
                if gidx == 0:
                    p.op("dve", lambda e, ap_=ap_, ti=ti: e.tensor_scalar(acc[:, :], ap_[:, :], rst[:, 0, ti:ti + 1], None, op0=ALU.mult),
                         reads=[ak, "rst"], writes=[acck])
                else:
                    p.op("dve", lambda e, ap_=ap_, ti=ti, gidx=gidx: e.scalar_tensor_tensor(
                        out=acc[:, :], in0=ap_[:, :], scalar=rst[:, gidx, ti:ti + 1], in1=acc[:, :], op0=ALU.mult, op1=ALU.add),
                        reads=[ak, "rst", acck], writes=[acck])
            p.op("dve", lambda e, xr=xr: e.scalar_tensor_tensor(out=acc[:, :], in0=xr[:, :], scalar=ALPHA, in1=acc[:, :],
                                                                op0=ALU.mult, op1=ALU.add), reads=[xk, acck], writes=[acck])
            layer_norm(p, acc, acck, x1, x1k, g1, "g1", b1, "b1", st12s[ti % 2], mvs[ti % 2], rss[ti % 2], "d1_%d" % (ti % 2))
            p.op("sp", lambda e, x1=x1, ti=ti: e.dma_start(out=io["X1"][ti * 128:(ti + 1) * 128, :], in_=x1[:, :]),
                 reads=[x1k], dma="x1o%d" % (ti % 2))
    p.pop()

    p.push()
    b1T = p.sb("b1T", [128, 32], F32)
    g2 = p.sb("g2", [128, D], F32)
    b2 = p.sb("b2", [128, D], F32)
    bf2 = p.sb("bf2", [128, D], F32)
    x1f = [p.sb("x1f%d" % i, [128, 2, D], F32) for i in range(2)]
    x1b = p.sb("x1b", [128, 2, D], BF16)
    x1Ts = [p.sb("x1T%d" % i, [128, 8, 256], BF16) for i in range(2)]
    hidT = p.sb("hidT", [128, 32, 256], BF16)
    rl = [p.sb("rl%d" % i, [128, 256], F32) for i in range(2)]
    t2ss = [p.sb("t2s%d" % i, [128, D], F32) for i in range(1)]
    outs = [p.sb("outs%d" % i, [128, D], F32) for i in range(2)]
    outb = [p.sb("outb%d" % i, [128, D], BF16) for i in range(2)]
    st12s = [p.sb("st12b%d" % i, [128, 12], F32) for i in range(2)]
    mvs = [p.sb("dmv2_%d" % i, [128, 2], F32) for i in range(2)]
    rss = [p.sb("drs2_%d" % i, [128, 1], F32) for i in range(2)]
    trps = [p.ps("dtrps%d" % i, [128, 1024], BF16) for i in range(2)]
    hps = [p.ps("hps%d" % i, [128, 512]) for i in range(2)]
    fps = [p.ps("fps%d" % i, [128, 1024]) for i in range(2)]
    p.op("sp", lambda e: e.dma_start(out=b1T[:], in_=W["b1T"]), writes=["b1T"], dma="dw2")
    p.op("sp", lambda e: e.dma_start(out=g2[:], in_=W["ln2_g"].partition_broadcast(128)), writes=["g2"], dma="dw2")
    p.op("sp", lambda e: e.dma_start(out=b2[:], in_=W["ln2_b"].partition_broadcast(128)), writes=["b2"], dma="dw2")
    p.op("sp", lambda e: e.dma_start(out=bf2[:], in_=W["b_ff2"].partition_broadcast(128)), writes=["bf2"], dma="dw2")
    ng2 = OWN // 256

    def prep(g):
        xf_, xfk = x1f[g % 2], "x1f%d" % (g % 2)
        p.op("sp", lambda e: e.dma_start(out=xf_[:, :, :], in_=io["X1"][g * 256:(g + 1) * 256, :].rearrange("(j p) d -> p j d", p=128)),
             writes=[xfk], dma=xfk)
        p.op("pool", lambda e: e.tensor_copy(x1b[:, :, :], xf_[:, :, :]), reads=[xfk], writes=["x1b"])
        transpose_group(p, G, x1b, "x1b", x1Ts[g % 2], "x1T%d" % (g % 2), trps, n_tok_tiles=2)

    prep(0)
    hi = 0
    fi = 0
    for g in range(ng2):
        xf_, xfk = x1f[g % 2], "x1f%d" % (g % 2)
        x1T, x1Tk = x1Ts[g % 2], "x1T%d" % (g % 2)
        for fc in range(32):
            hp_, hk = hps[hi % 2], "hps%d" % (hi % 2)
            r_, rk = rl[hi % 2], "rl%d" % (hi % 2)
            hi += 1
            for kc in range(8):
                p.op("pe", lambda e, hp_=hp_, kc=kc, fc=fc: e.matmul(hp_[:, 0:256], lhsT=wf1[:, kc, fc * 128:(fc + 1) * 128],
                                                                     rhs=x1T[:, kc, :], start=(kc == 0), stop=(kc == 7)),
                     reads=["wf1", x1Tk], writes=[hk])
            p.op("act", lambda e, hp_=hp_, r_=r_, fc=fc: e.activation(out=r_[:, :], in_=hp_[:, 0:256], func=AF.Relu,
                                                                      bias=b1T[:, fc:fc + 1], scale=1.0),
                 reads=[hk, "b1T"], writes=[rk])
            p.op("pool" if fc % 2 else "dve", lambda e, r_=r_, fc=fc: e.tensor_tensor(out=hidT[:, fc, :], in0=r_[:, :], in1=r_[:, :], op=ALU.mult),
                 reads=[rk], writes=["hidT"])
        if g + 1 < ng2:
            prep(g + 1)
        for j in range(2):
            ti = g * 2 + j
            fp_, fk = fps[fi % 2], "fps%d" % (fi % 2)
            o_, ok_ = outs[fi % 2], "outs%d" % (fi % 2)
            t2s, t2k = t2ss[0], "t2s0"
            par = fi % 2
            fi += 1
            for half in range(2):
                for fc in range(32):
                    p.op("pe", lambda e, fp_=fp_, half=half, fc=fc, j=j: e.matmul(
                        fp_[:, half * 512:(half + 1) * 512], lhsT=hidT[:, fc, j * 128:(j + 1) * 128],
                        rhs=wf2[:, fc, half * 512:(half + 1) * 512], start=(fc == 0), stop=(fc == 31)),
                        reads=["hidT", "wf2"], writes=[fk])
            p.op("dve", lambda e, fp_=fp_: e.tensor_tensor(out=t2s[:, :], in0=fp_[:, :], in1=bf2[:, :], op=ALU.add),
                 reads=[fk, "bf2"], writes=[t2k])
            p.op("dve", lambda e, xf_=xf_, j=j: e.scalar_tensor_tensor(out=t2s[:, :], in0=xf_[:, j, :], scalar=ALPHA, in1=t2s[:, :],
                                                                       op0=ALU.mult, op1=ALU.add), reads=[xfk, t2k], writes=[t2k])
            layer_norm(p, t2s, t2k, o_, ok_, g2, "g2", b2, "b2", st12s[par], mvs[par], rss[par], "d2_%d" % par)
            p.op("sp", lambda e, o_=o_, ti=ti: e.dma_start(out=dst["f32"][ti * 128:(ti + 1) * 128, :], in_=o_[:, :]),
                 reads=[ok_], dma=dst["tag"])
            if dst["bf16"] is not None:
                ob_, obk = outb[par], "outb%d" % par
                p.op("pool", lambda e, o_=o_, ob_=ob_: e.tensor_copy(ob_[:, :], o_[:, :]), reads=[ok_], writes=[obk])
                p.op("sp", lambda e, ob_=ob_, ti=ti: e.dma_start(out=dst["bf16"][ti * 128:(ti + 1) * 128, :], in_=ob_[:, :]),
                     reads=[obk], dma="xbc")
    p.pop()
    p.pop()


def layer_norm(p, src, skey, dstt, dkey, g, gk, b, bk, st12, mv, rs, tg):
    k6, kmv, krs = "st12" + tg, "mv" + tg, "rs" + tg
    p.op("dve", lambda e: e.bn_stats(st12[:, 0:6], src[:, 0:512]), reads=[skey], writes=[k6])
    p.op("dve", lambda e: e.bn_stats(st12[:, 6:12], src[:, 512:1024]), reads=[skey], writes=[k6])
    p.op("dve", lambda e: e.bn_aggr(mv[:, :], st12[:, :]), reads=[k6], writes=[kmv])
    p.op("act", lambda e: e.activation(out=rs[:, :], in_=mv[:, 1:2], func=AF.Sqrt, scale=1.0, bias=EPS), reads=[kmv], writes=[krs])
    p.op("dve", lambda e: e.reciprocal(rs[:, :], rs[:, :]), reads=[krs], writes=[krs])
    p.op("dve", lambda e: e.tensor_scalar(src[:, :], src[:, :], mv[:, 0:1], rs[:, 0:1], op0=ALU.subtract, op1=ALU.mult),
         reads=[skey, kmv, krs], writes=[skey])
    p.op("pool", lambda e: e.tensor_tensor(out=src[:, :], in0=src[:, :], in1=g[:, :], op=ALU.mult), reads=[skey, gk], writes=[skey])
    p.op("pool", lambda e: e.tensor_tensor(out=dstt[:, :], in0=src[:, :], in1=b[:, :], op=ALU.add), reads=[skey, bk], writes=[dkey])


_CACHE = {}


def host_weights(inp, layers):
    f = lambda a: np.ascontiguousarray(a, dtype=np.float32)
    L = list(layers)
    w = {}
    w["w_in"] = f(inp["w_in"][L])
    w["w_q_up"] = f(inp["w_q_up"][L])
    w["w_kv_up"] = f(inp["w_kv_up"][L])
    qn = np.zeros((len(L), 256), np.float32)
    qn[:, :192] = inp["q_norm"][L]
    w["qn"] = f(qn.reshape(len(L), 2, 128).transpose(0, 2, 1))
    w["kvn"] = f(inp["kv_norm"][L].reshape(len(L), 128, 1))
    w["sg_g"] = f(inp["sgu_ln_g"][L])
    w["sg_b"] = f(inp["sgu_ln_b"][L])
    w["sg_wT"] = f(np.transpose(inp["sgu_w"][L], (0, 1, 3, 2)))
    w["sg_bT"] = f(np.transpose(inp["sgu_b"][L], (0, 2, 1)))
    w["mixn"] = f(inp["mix_norm"][L].reshape(len(L), 8, 128).transpose(0, 2, 1))
    w["w_out"] = f(inp["w_out"][L])
    w["ln1_g"] = f(inp["ln1_g"][L])
    w["ln1_b"] = f(inp["ln1_b"][L])
    w["w_ff1"] = f(inp["w_ff1"][L])
    w["b1T"] = f(inp["b_ff1"][L].reshape(len(L), 32, 128).transpose(0, 2, 1))
    w["w_ff2"] = f(inp["w_ff2"][L])
    w["b_ff2"] = f(inp["b_ff2"][L])
    w["ln2_g"] = f(inp["ln2_g"][L])
    w["ln2_b"] = f(inp["ln2_b"][L])
    return w


def core_inputs(x_b, r, own, consts):
    S = x_b.shape[0]
    lo = r * own - HALO
    xo = np.zeros((own + 2 * HALO, D), np.float32)
    vm = np.zeros((own + 2 * HALO,), np.float32)
    a, b = max(lo, 0), min(lo + own + 2 * HALO, S)
    xo[a - lo:b - lo] = x_b[a:b]
    vm[a - lo:b - lo] = 1.0
    ct, st = consts["rope"]
    hs = np.zeros((128, 8), np.float32)
    if r - 1 >= 0:
        hs[:, r - 1] = 1.0
    if r + 1 < S // own:
        hs[:, 4 + r + 1] = 1.0
    d = dict(hsel=hs, xo=xo, xf=np.ascontiguousarray(x_b, dtype=np.float32),
             vmask=np.ascontiguousarray(vm.reshape(-1, 128).T), mtab=consts["mtab"],
             ckt=ct, skt=st, cqt=np.ascontiguousarray(ct[:, r * own:(r + 1) * own]),
             sqt=np.ascontiguousarray(st[:, r * own:(r + 1) * own]))
    return d


def run_layers(x, inp, own, n_groups_per_batch, fused_depth=1, dbg=False):
    B, S, _ = x.shape
    key = (own, S, fused_depth, dbg, B)
    if key not in _CACHE:
        _CACHE[key] = build_program(Cfg(own, S, depth=fused_depth, dbg=dbg, ncores=B * n_groups_per_batch))
    nc, stats = _CACHE[key]
    consts = dict(mtab=mask_table(), rope=rope_tables(S))
    depth = inp["w_in"].shape[0]
    cur = np.asarray(x, dtype=np.float32)
    extra = None
    for l0 in range(0, depth, fused_depth):
        w = host_weights(inp, range(l0, l0 + fused_depth))
        in_maps = []
        for c in range(B * n_groups_per_batch):
            b, r = c // n_groups_per_batch, c % n_groups_per_batch
            d = core_inputs(cur[b], r, own, consts)
            if fused_depth == 1:
                d.pop("hsel")
            d.update(w)
            in_maps.append(d)
        res = run_bass_kernel_spmd(nc, in_maps, core_ids=list(range(len(in_maps))))
        outs = [r_["out"] for r_ in res.results]
        cur = np.stack([np.concatenate(outs[b * n_groups_per_batch:(b + 1) * n_groups_per_batch], 0) for b in range(B)], 0)
        extra = res.results
    return cur, extra


def kernel(**inputs):
    x = np.asarray(inputs["x"], dtype=np.float32)
    inp = {k: np.asarray(v, dtype=np.float32) for k, v in inputs.items() if k != "x"}
    out, _ = run_layers(x, inp, own=x.shape[1] // 4, n_groups_per_batch=4, fused_depth=inp["w_in"].shape[0])
    return out.astype(np.float32)
```

```python
import types
import numpy as np
import ml_dtypes
import concourse.bass as bass
import concourse.mybir as mybir
from concourse.bass_utils import run_bass_kernel_spmd

F32 = mybir.dt.float32
BF16 = mybir.dt.bfloat16
I32 = mybir.dt.int32
AF = mybir.ActivationFunctionType
ALU = mybir.AluOpType
AX = mybir.AxisListType

ENGS = ("pe", "act", "dve", "pool", "sp")
SEM_LIMIT = 20000

D = 1024
PIN = 2016
HALO = 1024
DFF = 4096
EPS = 1e-5
ALPHA = float((2 * 2) ** 0.25)
TABW = 2944
A_SCALE = 0.125
B_SCALE = float(96 ** -0.5)
C_GELU = 0.044715
K_GELU = float(np.sqrt(2.0 / np.pi))


def _freeze(fn):
    if fn is None or fn.__closure__ is None:
        return fn
    cells = []
    for c in fn.__closure__:
        try:
            cells.append(types.CellType(c.cell_contents))
        except ValueError:
            cells.append(c)
    return types.FunctionType(fn.__code__, fn.__globals__, fn.__name__, fn.__defaults__, tuple(cells))


class Prog:
    def __init__(self, nc):
        self.nc = nc
        self.ops = {e: [] for e in ENGS}
        self.state = {}
        self.dma_sems = {}
        self.pending = {e: [] for e in ENGS}
        self._cms = []
        self._scopes = []

    def _reg(self, cm):
        t = cm.__enter__()
        (self._scopes[-1] if self._scopes else self._cms).append(cm)
        return t

    def sem(self, name):
        self._n = getattr(self, "_n", 0) + 1
        cm = self.nc.semaphore("m%d_%s" % (self._n, name))
        s = cm.__enter__()
        self._cms.append(cm)
        return s

    def sb(self, name, shape, dt):
        self._n = getattr(self, "_n", 0) + 1
        return self._reg(self.nc.sbuf_tensor("sb%d_%s" % (self._n, name), list(shape), dt))

    def ps(self, name, shape, dt=F32):
        self._n = getattr(self, "_n", 0) + 1
        return self._reg(self.nc.psum_tensor("ps%d_%s" % (self._n, name), list(shape), dt))

    def push(self):
        self._scopes.append([])

    def pop(self):
        self.barrier()
        for cm in reversed(self._scopes.pop()):
            cm.__exit__(None, None, None)

    def close(self):
        for cm in reversed(self._cms):
            cm.__exit__(None, None, None)
        self._cms = []

    def barrier(self):
        evs = []
        for e in ENGS:
            lst = self.ops[e]
            for i in range(len(lst) - 1, -1, -1):
                if lst[i]["dma"] is None and lst[i]["fn"] is not None:
                    evs.append(("eng", e, i))
                    break
        for tag, ds in self.dma_sems.items():
            evs.append(("dma", ds[0], ds[1]))
        for e in ENGS:
            self.pending[e].extend(evs)
        self.state = {}

    def op(self, eng, fn, reads=(), writes=(), dma=None, inc=16):
        fn = _freeze(fn)
        deps = []
        for k in reads:
            st = self.state.get(k)
            if st and st[0] is not None:
                deps.append((st[0], True))
        for k in writes:
            st = self.state.get(k)
            if st:
                if st[0] is not None:
                    deps.append((st[0], False))
                for r in st[1]:
                    deps.append((r, False))
        lst = self.ops[eng]
        idx = len(lst)
        if dma is not None:
            if dma not in self.dma_sems:
                self.dma_sems[dma] = [self.sem("d_" + dma), 0]
            ds = self.dma_sems[dma]
            ds[1] += inc
            ev = ("dma", ds[0], ds[1])
        else:
            ev = ("eng", eng, idx)
        fdeps = []
        for d, raw in deps:
            if d[0] == "eng" and d[1] == eng and dma is None:
                if eng == "pe" or not raw:
                    continue
            if d[0] == "dma":
                for ds_ in self.dma_sems.values():
                    if ds_[0] is d[1]:
                        cur = ds_[1] - (inc if (dma is not None and self.dma_sems[dma][0] is d[1]) else 0)
                        d = ("dma", d[1], max(d[2], cur))
            fdeps.append(d)
        for d in self.pending[eng]:
            if d[0] == "eng" and d[1] == eng and eng == "pe":
                continue
            fdeps.append(d)
        self.pending[eng] = []
        lst.append(dict(fn=fn, deps=fdeps, dma=dma, ev=ev, ms=False, inc=inc))
        for k in reads:
            st = self.state.setdefault(k, [None, []])
            st[1].append(ev)
        for k in writes:
            self.state[k] = [ev, []]
        return ev

    def wait_all_dma(self, eng, tags):
        deps = []
        for t in tags:
            ds = self.dma_sems[t]
            deps.append(("dma", ds[0], ds[1]))
        self.ops[eng].append(dict(fn=None, deps=deps, dma=None, ev=None, ms=False))

    def emit(self):
        nc = self.nc
        for e in ENGS:
            for o in self.ops[e]:
                for d in o["deps"]:
                    if d[0] == "eng":
                        self.ops[d[1]][d[2]]["ms"] = True
        msmap = {}
        for e in ENGS:
            cur = None
            cnt = 0
            for i, o in enumerate(self.ops[e]):
                if o["ms"]:
                    if cur is None or cnt >= SEM_LIMIT:
                        cur = self.sem("s_%s_%d" % (e, i))
                        cnt = 0
                    cnt += 1
                    msmap[(e, i)] = (cur, cnt)
                    o["inc"] = cur
        stats = {}

        def run(e, eng):
            waited = {}
            nw = 0
            for i, o in enumerate(self.ops[e]):
                for d in o["deps"]:
                    if d[0] == "eng":
                        sem, val = msmap[(d[1], d[2])]
                    else:
                        sem, val = d[1], d[2]
                    key = id(sem)
                    if waited.get(key, 0) >= val:
                        continue
                    waited[key] = val
                    eng.wait_ge(sem, val)
                    nw += 1
                if o["fn"] is None:
                    continue
                ins = o["fn"](eng)
                if o["dma"] is not None:
                    ins.then_inc(o["ev"][1], o.get("inc", 16))
                elif o["ms"]:
                    ins.then_inc(o["inc"], 1)
            stats[e] = (len(self.ops[e]), nw)

        with nc.Block() as block:
            @block.tensor
            def _(eng):
                run("pe", eng)

            @block.scalar
            def _(eng):
                run("act", eng)

            @block.vector
            def _(eng):
                run("dve", eng)

            @block.gpsimd
            def _(eng):
                run("pool", eng)

            @block.sync
            def _(eng):
                run("sp", eng)
        self.stats = stats
        return stats


def mask_table():
    slopes = (2.0 ** (-8.0 * np.arange(1, 7) / 6)).astype(np.float32)
    pp = np.arange(128)[:, None]
    col = np.arange(TABW)[None, :]
    delta = pp - col + 1408
    ad = np.abs(delta)
    c = (ad <= 64).astype(np.float32) + ((delta % 4 == 0) & (ad <= 256)).astype(np.float32) \
        + ((delta % 16 == 0) & (ad <= 1024)).astype(np.float32)
    tab = np.zeros((128, 6, TABW), np.float32)
    for h in range(6):
        tab[:, h, :] = c * np.exp(-(slopes[h] * ad.astype(np.float32)).astype(np.float32))
    return tab.astype(ml_dtypes.bfloat16)


def rope_tables(S):
    inv_freq = (10000.0 ** (-np.arange(0, 32, 2, dtype=np.float32) / 32)).astype(np.float32)
    ang = (np.arange(S, dtype=np.float32)[:, None] * inv_freq[None, :]).astype(np.float32)
    cos = np.cos(ang).astype(np.float32).T
    sin = np.sin(ang).astype(np.float32).T
    ct = np.concatenate([cos, cos], 0)
    st = np.concatenate([-sin, sin], 0)
    return np.ascontiguousarray(ct), np.ascontiguousarray(st)


class Cfg:
    def __init__(self, own, sf, depth=1, dbg=False, ncores=8):
        self.ncores = ncores
        self.OWN = own
        self.SF = sf
        self.OH = own + 2 * HALO
        self.NT = own // 128
        self.NQB = own // 512
        self.NG_OH = self.OH // 512
        self.NG_F = sf // 512
        self.NKT_OH = self.OH // 128
        self.NKT_F = sf // 128
        self.depth = depth
        self.dbg = dbg


W_NAMES = [
    ("w_in", [D, PIN]), ("w_q_up", [192, 576]), ("w_kv_up", [128, 768]), ("qn", [128, 2]), ("kvn", [128, 1]),
    ("sg_g", [256]), ("sg_b", [256]), ("sg_wT", [4, 128, 128]), ("sg_bT", [128, 4]), ("mixn", [128, 8]),
    ("w_out", [D, D]), ("ln1_g", [D]), ("ln1_b", [D]), ("w_ff1", [D, DFF]), ("b1T", [128, 32]),
    ("w_ff2", [DFF, D]), ("b_ff2", [D]), ("ln2_g", [D]), ("ln2_b", [D]),
]


def build_program(cfg):
    nc = bass.Bass("TRN2", target_bir_lowering=False)
    OWN, SF, OH, NT = cfg.OWN, cfg.SF, cfg.OH, cfg.NT
    io = {}
    io["xo"] = nc.dram_tensor("xo", [OH, D], F32, kind="ExternalInput").ap()
    io["xf"] = nc.dram_tensor("xf", [SF, D], F32, kind="ExternalInput").ap()
    io["vmask"] = nc.dram_tensor("vmask", [128, OH // 128], F32, kind="ExternalInput").ap()
    io["mtab"] = nc.dram_tensor("mtab", [128, 6, TABW], BF16, kind="ExternalInput").ap()
    io["ckt"] = nc.dram_tensor("ckt", [32, SF], F32, kind="ExternalInput").ap()
    io["skt"] = nc.dram_tensor("skt", [32, SF], F32, kind="ExternalInput").ap()
    io["cqt"] = nc.dram_tensor("cqt", [32, OWN], F32, kind="ExternalInput").ap()
    io["sqt"] = nc.dram_tensor("sqt", [32, OWN], F32, kind="ExternalInput").ap()
    for nm, shp in W_NAMES:
        io[nm] = nc.dram_tensor(nm, [cfg.depth] + shp, F32, kind="ExternalInput").ap()
    io["out"] = nc.dram_tensor("out", [OWN, D], F32, kind="ExternalOutput").ap()
    io["YT"] = nc.dram_tensor("YT", [D, OWN], BF16, kind="Internal").ap()
    io["X1"] = nc.dram_tensor("X1", [OWN, D], F32, kind="Internal").ap()
    if cfg.depth > 1:
        io["hsel"] = nc.dram_tensor("hsel", [128, 8], F32, kind="ExternalInput").ap()
        io["XL"] = nc.dram_tensor("XL", [OWN, D], F32, kind="Internal").ap()
        io["XBc"] = nc.dram_tensor("XBc", [OWN, D], BF16, kind="Internal").ap()
        io["XG"] = nc.dram_tensor("XG", [OWN // 512, (SF // OWN) * 512, D], BF16, kind="Internal").ap()
        io["XH"] = nc.dram_tensor("XH", [2 * HALO, D], BF16, kind="Internal").ap()
    if cfg.dbg:
        io["dbg_yt"] = nc.dram_tensor("dbg_yt", [D, OWN], BF16, kind="ExternalOutput").ap()
        io["dbg_ss"] = nc.dram_tensor("dbg_ss", [128, 13 * NT], F32, kind="ExternalOutput").ap()
        io["dbg_x1"] = nc.dram_tensor("dbg_x1", [OWN, D], F32, kind="ExternalOutput").ap()

    p = Prog(nc)
    DBG["on"] = cfg.dbg
    G = {}
    G["ident"] = p.sb("ident", [128, 128], BF16)
    identf = p.sb("identf", [128, 128], F32)
    G["e65"] = p.sb("e65", [128, 64], F32)
    G["ones"] = p.sb("onesf", [128, 128], F32)
    G["ss"] = p.sb("ss", [128, 13, NT], F32)
    p.op("pool", lambda e: e.memset(identf[:], 0.0), writes=["identf"])
    p.op("pool", lambda e: e.affine_select(out=identf[:], in_=identf[:], compare_op=ALU.not_equal, fill=1.0,
                                           base=0, pattern=[[-1, 128]], channel_multiplier=1),
         reads=["identf"], writes=["identf"])
    p.op("pool", lambda e: e.tensor_copy(G["ident"][:], identf[:]), reads=["identf"], writes=["ident"])
    p.op("pool", lambda e: e.memset(G["e65"][:], 0.0), writes=["e65"])
    p.op("pool", lambda e: e.memset(G["e65"][64:65, :], 1.0), reads=["e65"], writes=["e65"])
    p.op("pool", lambda e: e.memset(G["ones"][:], 1.0), writes=["ones"])
    p.barrier()

    ngo = cfg.NG_OH
    for l in range(cfg.depth):
        W = {nm: io[nm][l] for nm, _ in W_NAMES}
        last = (l == cfg.depth - 1)
        if l == 0:
            src = dict(oh=lambda g: io["xo"][g * 512:(g + 1) * 512, :],
                       full=lambda g: io["xf"][g * 512:(g + 1) * 512, :],
                       res=lambda ti: io["xo"][HALO + ti * 128:HALO + (ti + 1) * 128, :])
        else:
            def oh2(g):
                if g < 2:
                    return io["XH"][g * 512:(g + 1) * 512, :]
                if g >= ngo - 2:
                    return io["XH"][1024 + (g - (ngo - 2)) * 512:1024 + (g - (ngo - 2) + 1) * 512, :]
                return io["XL"][(g - 2) * 512:(g - 1) * 512, :]
            src = dict(oh=oh2, full=lambda g: xg_rows(io, cfg, g * 512, 512),
                       res=lambda ti: io["XL"][ti * 128:(ti + 1) * 128, :])
        if last:
            dst = dict(f32=io["out"], bf16=None, tag="out")
        else:
            dst = dict(f32=io["XL"], bf16=io["XBc"], tag="xl")
        layer(p, cfg, G, io, W, src, dst)
        if not last:
            exchange(p, cfg, G, io)

    if cfg.dbg:
        p.op("sp", lambda e: e.dma_start(out=io["dbg_yt"], in_=io["YT"]), reads=[], dma="out")
        p.op("sp", lambda e: e.dma_start(out=io["dbg_ss"], in_=G["ss"][:].rearrange("p a b -> p (a b)")), dma="out")
        p.op("sp", lambda e: e.dma_start(out=io["dbg_x1"], in_=io["X1"]), dma="out")
    p.wait_all_dma("sp", ["out"])
    stats = p.emit()
    p.close()
    return nc, stats


DBG = {}


def dbg_dump(p, name, ap, shape, dt):
    if not DBG.get("on"):
        return
    t = p.nc.dram_tensor("dd_" + name, list(shape), dt, kind="ExternalOutput").ap()
    p.barrier()
    p.op("sp", lambda e: e.dma_start(out=t, in_=ap), dma="out")
    p.barrier()


def load_x_group(p, src_ap, xb, key, tag):
    p.op("pool", lambda e: e.dma_start(out=xb[:], in_=src_ap.rearrange("(j p) d -> p j d", p=128)),
         writes=[key], dma=tag)


def transpose_group(p, G, xb, xbkey, xT, xTkey, trps, n_tok_tiles=4):
    for kc in range(8):
        tp = trps[kc % 2]
        tkey = "tr%d" % (kc % 2)
        for j in range(n_tok_tiles):
            p.op("pe", lambda e, tp=tp, j=j, kc=kc: e.transpose(tp[:, j * 128:(j + 1) * 128],
                                                                 xb[:, j, kc * 128:(kc + 1) * 128], G["ident"][:]),
                 reads=[xbkey, "ident"], writes=[tkey])
        w = n_tok_tiles * 128
        p.op("act", lambda e, tp=tp, kc=kc, w=w: e.copy(xT[:, kc, 0:w], tp[:, 0:w]), reads=[tkey], writes=[xTkey])


def attn_post(p, G, io, ops, okey, h_row, qb, ss_idx, bufs, it):
    r = it % 2
    osb, rden, ysq, ybf = bufs["osb"][r], bufs["rden"], bufs["ysq"][r], bufs["ybf"][r]
    ko, kr, ks, kb = "osb%d" % r, "rden", "ysq%d" % r, "ybf%d" % r
    p.op("dve", lambda e: e.tensor_copy(osb[0:65, :], ops[0:65, :]), reads=[okey], writes=[ko])

    def deferred():
        mp, mk = bufs["peek"]()
        denps, kd = mp[:, 0:512], mk
        p.op("pe", lambda e: e.matmul(denps[0:64, :], lhsT=G["e65"][0:65, :], rhs=osb[0:65, :], start=True, stop=True),
             reads=[ko, "e65"], writes=[kd])
        p.op("dve", lambda e: e.tensor_copy(rden[0:64, :], denps[0:64, :]), reads=[kd], writes=[kr])
        p.op("dve", lambda e: e.reciprocal(rden[0:64, :], rden[0:64, :]), reads=[kr], writes=[kr])
        p.op("pool", lambda e: e.tensor_tensor(out=osb[0:64, :], in0=osb[0:64, :], in1=rden[0:64, :], op=ALU.mult),
             reads=[ko, kr], writes=[ko])
        p.op("pool", lambda e: e.tensor_copy(ybf[0:64, :], osb[0:64, :]), reads=[ko], writes=[kb])
        p.op("sp", lambda e: e.dma_start(out=io["YT"][h_row:h_row + 64, qb * 512:(qb + 1) * 512], in_=ybf[0:64, :]),
             reads=[kb], dma="yt%d" % r)
        p.op("pool", lambda e: e.tensor_tensor(out=ysq[0:64, :], in0=osb[0:64, :], in1=osb[0:64, :], op=ALU.mult),
             reads=[ko], writes=[ks])

        def stage_b():
            mp2, mk2 = bufs["peek"]()
            ssps = mp2[:, 512:1024]
            for j in range(4):
                p.op("pe", lambda e, j=j: e.matmul(ssps[:, j:j + 1], lhsT=ysq[0:64, j * 128:(j + 1) * 128],
                                                   rhs=G["ones"][0:64, 0:1], start=True, stop=True),
                     reads=[ks, "ones"], writes=[mk2])
            p.op("dve", lambda e: e.tensor_copy(G["ss"][:, ss_idx, qb * 4:(qb + 1) * 4], ssps[:, 0:4]),
                 reads=[mk2], writes=["ss"])
        return stage_b
    return deferred


def xg_rows(io, cfg, R0, n):
    j, q = R0 // cfg.OWN, R0 % cfg.OWN
    i, t = q // 512, q % 512
    assert t + n <= 512
    return io["XG"][i, j * 512 + t:j * 512 + t + n, :]


def exchange(p, cfg, G, io):
    OWN = cfg.OWN
    ngrp = cfg.SF // OWN
    groups = [list(range(b * ngrp, (b + 1) * ngrp)) for b in range(cfg.ncores // ngrp)]
    p.barrier()
    for i in range(OWN // 512):
        p.op("pool", lambda e, i=i: e.collective_compute("AllGather", ALU.bypass, replica_groups=groups,
                                                         ins=[io["XBc"][i * 512:(i + 1) * 512, :].opt()],
                                                         outs=[io["XG"][i].opt()]),
             dma="cc", inc=1)
    p.barrier()
    p.push()
    hsel = p.sb("hsel", [128, 8], F32)
    cands = [p.sb("cand%d" % i, [128, ngrp, D], BF16) for i in range(2)]
    hacc = p.sb("hacc", [128, D], F32)
    houts = [p.sb("hout%d" % i, [128, D], BF16) for i in range(2)]
    p.op("sp", lambda e: e.dma_start(out=hsel[:], in_=io["hsel"]), writes=["hsel"], dma="hsel")
    it = 0
    for side in range(2):
        for t in range(HALO // 128):
            cand, ck = cands[it % 2], "cand%d" % (it % 2)
            hout, hk = houts[it % 2], "hout%d" % (it % 2)
            for j in range(ngrp):
                r0 = j * OWN + (OWN - HALO if side == 0 else 0) + t * 128
                p.op("sp", lambda e, j=j, r0=r0: e.dma_start(out=cand[:, j, :], in_=xg_rows(io, cfg, r0, 128)),
                     writes=[ck], dma=ck)
            p.op("dve", lambda e: e.tensor_scalar(hacc[:, :], cand[:, 0, :], hsel[:, side * 4:side * 4 + 1], None, op0=ALU.mult),
                 reads=[ck, "hsel"], writes=["hacc"])
            for j in range(1, ngrp):
                p.op("dve", lambda e, j=j: e.scalar_tensor_tensor(out=hacc[:, :], in0=cand[:, j, :],
                                                                  scalar=hsel[:, side * 4 + j:side * 4 + j + 1], in1=hacc[:, :],
                                                                  op0=ALU.mult, op1=ALU.add), reads=[ck, "hsel", "hacc"], writes=["hacc"])
            p.op("pool", lambda e: e.tensor_copy(hout[:, :], hacc[:, :]), reads=["hacc"], writes=[hk])
            r1 = side * HALO + t * 128
            p.op("sp", lambda e, r1=r1: e.dma_start(out=io["XH"][r1:r1 + 128, :], in_=hout[:, :]), reads=[hk], dma="xh%d" % (it % 2))
            it += 1
    p.pop()


def layer(p, cfg, G, io, W, src, dst):
    OWN, SF, OH, NT, NQB = cfg.OWN, cfg.SF, cfg.OH, cfg.NT, cfg.NQB
    ident = G["ident"]

    p.push()
    CQ = p.sb("CQ", [128, 2, OWN], BF16)
    p.push()
    KaT = p.sb("KaT", [128, 3, OH], BF16)
    QaT = p.sb("QaT", [128, 3, OWN], BF16)
    Va = p.sb("Va", [128, OH // 128, 6, 65], BF16)

    p.push()
    w_in = p.sb("w_in", [128, 8, PIN], BF16)
    wsT = p.sb("wsT", [128, 4, 128], BF16)
    sg_g = p.sb("sg_g", [128, 256], F32)
    sg_b = p.sb("sg_b", [128, 256], F32)
    bsT = p.sb("bsT", [128, 4], F32)
    vm = p.sb("vm", [128, OH // 128], F32)
    xbs = [p.sb("xb%d" % i, [128, 4, D], BF16) for i in range(2)]
    xTs = [p.sb("xT%d" % i, [128, 8, 512], BF16) for i in range(2)]
    zh = p.sb("zh", [128, 512], F32)
    w1 = p.sb("w1", [128, 512], F32)
    w2 = p.sb("w2", [128, 512], F32)
    zz = p.sb("zz", [128, 512], F32)
    vn = p.sb("vn", [128, 256], F32)
    vnb = p.sb("vnb", [128, 256], BF16)
    yc = p.sb("yc", [128, 256], F32)
    ycb = p.sb("ycb", [128, 256], BF16)
    ycT = p.sb("ycT", [128, 2, 512], BF16)
    junk = p.sb("junk", [128, 256], F32)
    st6 = p.sb("st6", [128, 6], F32)
    mv = p.sb("mv", [128, 2], F32)
    rs = p.sb("rs", [128, 1], F32)
    cqs = [p.sb("cqs%d" % i, [128, 512], F32) for i in range(2)]
    cqq = [p.sb("cqq%d" % i, [128, 512], F32) for i in range(2)]
    cqr = p.sb("cqr", [128, 512], F32)
    trps = [p.ps("trps%d" % i, [128, 1024], BF16) for i in range(2)]
    pps = [p.ps("pps%d" % i, [128, 512]) for i in range(3)]
    mixps = p.ps("mixps", [128, 512])
    ssq = p.ps("ssq", [128, 512])

    for kc in range(8):
        p.op("pool", lambda e, kc=kc: e.dma_start(out=w_in[:, kc, :], in_=W["w_in"][kc * 128:(kc + 1) * 128, :]),
             writes=["w_in"], dma="w_in")
    p.op("pool", lambda e: e.dma_start(out=wsT[:], in_=W["sg_wT"].rearrange("g s t -> s g t")), writes=["wsT"], dma="wsm")
    p.op("sp", lambda e: e.dma_start(out=sg_g[:], in_=W["sg_g"].partition_broadcast(128)), writes=["sg_g"], dma="wsm2")
    p.op("sp", lambda e: e.dma_start(out=sg_b[:], in_=W["sg_b"].partition_broadcast(128)), writes=["sg_b"], dma="wsm2")
    p.op("sp", lambda e: e.dma_start(out=bsT[:], in_=W["sg_bT"]), writes=["bsT"], dma="wsm2")
    p.op("sp", lambda e: e.dma_start(out=vm[:], in_=io["vmask"]), writes=["vm"], dma="wsm2")
    for h in range(6):
        p.op("dve", lambda e, h=h: e.tensor_copy(Va[:, :, h, 64:65], vm[:].rearrange("p (t o) -> p t o", o=1)),
             reads=["vm"], writes=["Va"])

    pp_i = [0]

    def next_pps():
        i = pp_i[0] % 3
        pp_i[0] += 1
        return pps[i], "pps%d" % i

    ngo = cfg.NG_OH
    load_x_group(p, src["oh"](0), xbs[0], "xb0", "xb0")
    for g in range(ngo):
        xb, xbk = xbs[g % 2], "xb%d" % (g % 2)
        xT, xTk = xTs[g % 2], "xT%d" % (g % 2)
        if g + 1 < ngo:
            load_x_group(p, src["oh"](g + 1), xbs[(g + 1) % 2], "xb%d" % ((g + 1) % 2), "xb%d" % ((g + 1) % 2))
        transpose_group(p, G, xb, xbk, xT, xTk, trps)
        own = (g * 512 >= HALO) and (g * 512 < HALO + OWN)
        go = g - HALO // 512
        for c in range(3):
            ps_, pk = next_pps()
            for kc in range(8):
                p.op("pe", lambda e, ps_=ps_, kc=kc, c=c: e.matmul(ps_[:, :], lhsT=w_in[:, kc, 384 + c * 128:384 + (c + 1) * 128],
                                                                   rhs=xT[:, kc, :], start=(kc == 0), stop=(kc == 7)),
                     reads=["w_in", xTk], writes=[pk])
            p.op("dve", lambda e, ps_=ps_, c=c, g=g: e.tensor_copy(KaT[:, c, g * 512:(g + 1) * 512], ps_[:, :]),
                 reads=[pk], writes=["KaT"])
        if own:
            for c in range(3):
                ps_, pk = next_pps()
                for kc in range(8):
                    p.op("pe", lambda e, ps_=ps_, kc=kc, c=c: e.matmul(ps_[:, :], lhsT=w_in[:, kc, c * 128:(c + 1) * 128],
                                                                       rhs=xT[:, kc, :], start=(kc == 0), stop=(kc == 7)),
                         reads=["w_in", xTk], writes=[pk])
                p.op("dve", lambda e, ps_=ps_, c=c, go=go: e.tensor_copy(QaT[:, c, go * 512:(go + 1) * 512], ps_[:, :]),
                     reads=[pk], writes=["QaT"])
        for j in range(4):
            ps_, pk = next_pps()
            for kc in range(8):
                p.op("pe", lambda e, ps_=ps_, kc=kc, j=j: e.matmul(ps_[:, 0:384], lhsT=xT[:, kc, j * 128:(j + 1) * 128],
                                                                   rhs=w_in[:, kc, 768:1152], start=(kc == 0), stop=(kc == 7)),
                     reads=["w_in", xTk], writes=[pk])
            p.op("dve", lambda e, ps_=ps_, j=j, g=g: e.tensor_copy(Va[:, g * 4 + j, :, 0:64],
                                                                   ps_[:, 0:384].rearrange("p (h c) -> p h c", h=6)),
                 reads=[pk], writes=["Va"])
        if not own:
            continue
        psa, pka = next_pps()
        psb, pkb = next_pps()
        for kc in range(8):
            p.op("pe", lambda e, kc=kc: e.matmul(psa[:, :], lhsT=w_in[:, kc, 1152:1280], rhs=xT[:, kc, :],
                                                 start=(kc == 0), stop=(kc == 7)), reads=["w_in", xTk], writes=[pka])
        for kc in range(8):
            p.op("pe", lambda e, kc=kc: e.matmul(psb[0:64, :], lhsT=w_in[:, kc, 1280:1344], rhs=xT[:, kc, :],
                                                 start=(kc == 0), stop=(kc == 7)), reads=["w_in", xTk], writes=[pkb])
        p.op("dve", lambda e: e.tensor_copy(cqs[0][:, :], psa[:, :]), reads=[pka], writes=["cqs0"])
        p.op("dve", lambda e: e.tensor_copy(cqs[1][0:64, :], psb[0:64, :]), reads=[pkb], writes=["cqs1"])
        p.op("pool", lambda e: e.tensor_tensor(out=cqq[0][:, :], in0=cqs[0][:, :], in1=cqs[0][:, :], op=ALU.mult),
             reads=["cqs0"], writes=["cqq0"])
        p.op("pool", lambda e: e.tensor_tensor(out=cqq[1][0:64, :], in0=cqs[1][0:64, :], in1=cqs[1][0:64, :], op=ALU.mult),
             reads=["cqs1"], writes=["cqq1"])
        p.op("pe", lambda e: e.matmul(ssq[:, :], lhsT=G["ones"][:, :], rhs=cqq[0][:, :], start=True, stop=False),
             reads=["cqq0", "ones"], writes=["ssq"])
        p.op("pe", lambda e: e.matmul(ssq[:, :], lhsT=G["ones"][0:64, :], rhs=cqq[1][0:64, :], start=False, stop=True),
             reads=["cqq1", "ones"], writes=["ssq"])
        p.op("act", lambda e: e.activation(out=cqr[:, :], in_=ssq[:, :], func=AF.Sqrt, scale=1.0 / 192, bias=EPS),
             reads=["ssq"], writes=["cqr"])
        p.op("dve", lambda e: e.reciprocal(cqr[:, :], cqr[:, :]), reads=["cqr"], writes=["cqr"])
        p.op("dve", lambda e, go=go: e.tensor_tensor(out=CQ[:, 0, go * 512:(go + 1) * 512], in0=cqs[0][:, :], in1=cqr[:, :],
                                                     op=ALU.mult), reads=["cqs0", "cqr"], writes=["CQ"])
        p.op("dve", lambda e, go=go: e.tensor_tensor(out=CQ[0:64, 1, go * 512:(go + 1) * 512], in0=cqs[1][0:64, :],
                                                     in1=cqr[0:64, :], op=ALU.mult), reads=["cqs1", "cqr"], writes=["CQ"])
        for j in range(4):
            ti = go * 4 + j
            ps_, pk = next_pps()
            for kc in range(8):
                p.op("pe", lambda e, ps_=ps_, kc=kc, j=j: e.matmul(ps_[:, :], lhsT=xT[:, kc, j * 128:(j + 1) * 128],
                                                                   rhs=w_in[:, kc, 1504:2016], start=(kc == 0), stop=(kc == 7)),
                     reads=["w_in", xTk], writes=[pk])
            p.op("act", lambda e, ps_=ps_: e.activation(out=zh[:, :], in_=ps_[:, :], func=AF.Copy, scale=0.5),
                 reads=[pk], writes=["zh"])
            p.op("pool", lambda e: e.tensor_tensor(out=w1[:, :], in0=zh[:, :], in1=zh[:, :], op=ALU.mult),
                 reads=["zh"], writes=["w1"])
            p.op("dve", lambda e: e.tensor_scalar(w1[:, :], w1[:, :], 4.0 * C_GELU, 1.0, op0=ALU.mult, op1=ALU.add),
                 reads=["w1"], writes=["w1"])
            p.op("pool", lambda e: e.tensor_tensor(out=w2[:, :], in0=w1[:, :], in1=zh[:, :], op=ALU.mult),
                 reads=["w1", "zh"], writes=["w2"])
            p.op("act", lambda e: e.activation(out=w2[:, :], in_=w2[:, :], func=AF.Tanh, scale=2.0 * K_GELU),
                 reads=["w2"], writes=["w2"])
            p.op("dve", lambda e: e.scalar_tensor_tensor(out=zz[:, :], in0=w2[:, :], scalar=1.0, in1=zh[:, :],
                                                         op0=ALU.add, op1=ALU.mult), reads=["w2", "zh"], writes=["zz"])
            p.op("dve", lambda e: e.bn_stats(st6[:, :], zz[:, 256:512]), reads=["zz"], writes=["st6"])
            p.op("dve", lambda e: e.bn_aggr(mv[:, :], st6[:, :]), reads=["st6"], writes=["mv"])
            p.op("act", lambda e: e.activation(out=rs[:, :], in_=mv[:, 1:2], func=AF.Sqrt, scale=1.0, bias=EPS),
                 reads=["mv"], writes=["rs"])
            p.op("dve", lambda e: e.reciprocal(rs[:, :], rs[:, :]), reads=["rs"], writes=["rs"])
            p.op("dve", lambda e: e.tensor_scalar(vn[:, :], zz[:, 256:512], mv[:, 0:1], rs[:, 0:1], op0=ALU.subtract,
                                                  op1=ALU.mult), reads=["zz", "mv", "rs"], writes=["vn"])
            p.op("pool", lambda e: e.tensor_tensor(out=vn[:, :], in0=vn[:, :], in1=sg_g[:, :], op=ALU.mult),
                 reads=["vn", "sg_g"], writes=["vn"])
            p.op("pool", lambda e: e.tensor_tensor(out=vnb[:, :], in0=vn[:, :], in1=sg_b[:, :], op=ALU.add),
                 reads=["vn", "sg_b"], writes=["vnb"])
            for gg in range(4):
                p.op("pe", lambda e, gg=gg: e.matmul(mixps[:, gg * 64:(gg + 1) * 64], lhsT=wsT[:, gg, :],
                                                     rhs=vnb[:, gg * 64:(gg + 1) * 64], start=True, stop=True),
                     reads=["wsT", "vnb"], writes=["mixps"])
            for gg in range(4):
                p.op("dve", lambda e, gg=gg: e.scalar_tensor_tensor(out=yc[:, gg * 64:(gg + 1) * 64],
                                                                    in0=mixps[:, gg * 64:(gg + 1) * 64], scalar=bsT[:, gg:gg + 1],
                                                                    in1=zz[:, gg * 64:(gg + 1) * 64], op0=ALU.add, op1=ALU.mult),
                     reads=["mixps", "bsT", "zz"], writes=["yc"])
            p.op("pool", lambda e: e.tensor_tensor(out=junk[:, :], in0=yc[:, :], in1=yc[:, :], op=ALU.mult),
                 reads=["yc"], writes=["junk"])
            p.op("dve", lambda e, ti=ti: e.reduce_sum(out=G["ss"][:, 12, ti:ti + 1], in_=junk[:, :], axis=AX.X),
                 reads=["junk"], writes=["ss"])
            p.op("pool", lambda e: e.tensor_copy(ycb[:, :], yc[:, :]), reads=["yc"], writes=["ycb"])
            for c2 in range(2):
                tp = trps[c2]
                p.op("pe", lambda e, tp=tp, c2=c2: e.transpose(tp[:, 512:640], ycb[:, c2 * 128:(c2 + 1) * 128], ident[:]),
                     reads=["ycb", "ident"], writes=["tr%d" % c2])
                p.op("act", lambda e, tp=tp, c2=c2, j=j: e.copy(ycT[:, c2, j * 128:(j + 1) * 128], tp[:, 512:640]),
                     reads=["tr%d" % c2], writes=["ycT"])
        p.op("sp", lambda e, go=go: e.dma_start(out=io["YT"][768:1024, go * 512:(go + 1) * 512].rearrange("(c p) t -> p c t", p=128),
                                                in_=ycT[:, :, :]), reads=["ycT"], dma="ytc")
    dbg_dump(p, "xT", xTs[(ngo - 1) % 2][:, :, :], [128, 8, 512], BF16)
    dbg_dump(p, "KaT", KaT[:, :, :], [128, 3, OH], BF16)
    dbg_dump(p, "QaT", QaT[:, :, :], [128, 3, OWN], BF16)
    dbg_dump(p, "Va", Va[:, :, :, :], [128, OH // 128, 6, 65], BF16)
    dbg_dump(p, "CQ", CQ[:, :, :], [128, 2, OWN], BF16)
    dbg_dump(p, "w_in", w_in[:, :, :], [128, 8, PIN], BF16)
    p.pop()

    p.push()
    tab = p.sb("tab", [128, 6, TABW], BF16)
    NSA = 3
    Es = [p.sb("E%d" % i, [128, 1024], BF16) for i in range(NSA)]
    PTs = [p.sb("PT%d" % i, [128, 1024], BF16) for i in range(NSA)]
    Sps = [p.ps("S%d" % i, [128, 1024]) for i in range(NSA)]
    Ops = [p.ps("O%d" % i, [128, 512]) for i in range(2)]
    gic = [0]

    def alloc_s():
        i = gic[0] % NSA
        gic[0] += 1
        return Sps[i], "S%d" % i

    def peek_s():
        i = gic[0] % NSA
        return Sps[i], "S%d" % i
    bufs = dict(peek=peek_s, osb=[p.sb("osb%d" % i, [128, 512], F32) for i in range(2)], rden=p.sb("rden", [128, 512], F32),
                ysq=[p.sb("ysq%d" % i, [128, 512], F32) for i in range(2)], ybf=[p.sb("ybf%d" % i, [128, 512], BF16) for i in range(2)],
                alloc=alloc_s)
    for h in range(6):
        p.op("sp", lambda e, h=h: e.dma_start(out=tab[:, h, :], in_=io["mtab"][:, h, :]), writes=["tab"], dma="tab")
    it = 0
    pend = []
    pendb = []
    NKT2 = 20
    for c in range(3):
        hh = (2 * c, 2 * c + 1)
        for qb in range(NQB):
            kt0 = 4 * qb

            def qk(ii):
                S, sk = alloc_s()
                kt = kt0 + ii
                for u in range(2):
                    p.op("pe", lambda e, S=S, u=u, kt=kt: e.matmul(S[:, u * 512:(u + 1) * 512],
                                                                   lhsT=KaT[u * 64:u * 64 + 64, c, kt * 128:(kt + 1) * 128],
                                                                   rhs=QaT[u * 64:u * 64 + 64, c, qb * 512:(qb + 1) * 512],
                                                                   start=True, stop=True),
                         reads=["KaT", "QaT"], writes=[sk])
                return S, sk

            inflight = [qk(0), qk(1)]
            for ii in range(NKT2):
                if ii + 2 < NKT2:
                    inflight.append(qk(ii + 2))
                S, sk = inflight.pop(0)
                E, ek = Es[ii % NSA], "E%d" % (ii % NSA)
                PT, pk = PTs[ii % NSA], "PT%d" % (ii % NSA)
                kt = kt0 + ii
                start = 2432 - 128 * ii
                p.op("act", lambda e, S=S, E=E: e.activation(out=E[:, :], in_=S[:, :], func=AF.Exp, scale=A_SCALE),
                     reads=[sk], writes=[ek])
                for u in range(2):
                    p.op("dve", lambda e, E=E, PT=PT, u=u, start=start: e.tensor_tensor(
                        out=PT[:, u * 512:(u + 1) * 512], in0=E[:, u * 512:(u + 1) * 512],
                        in1=tab[:, hh[u], start:start + 512], op=ALU.mult), reads=[ek, "tab"], writes=[pk + "_%d" % u])
                for u in range(2):
                    p.op("pe", lambda e, PT=PT, u=u, kt=kt, ii=ii: e.matmul(
                        Ops[u][0:65, :], lhsT=Va[:, kt, hh[u], 0:65], rhs=PT[:, u * 512:(u + 1) * 512],
                        start=(ii == 0), stop=(ii == NKT2 - 1)),
                        reads=["Va", pk + "_%d" % u], writes=["O%d" % u])
                if ii == 3:
                    while pend:
                        pendb.append(pend.pop(0)())
                if ii == 9:
                    while pendb:
                        pendb.pop(0)()
            for u in range(2):
                pend.append(attn_post(p, G, io, Ops[u], "O%d" % u, hh[u] * 64, qb, hh[u], bufs, u))
            it += 1
    while pend:
        pendb.append(pend.pop(0)())
    while pendb:
        pendb.pop(0)()
    p.pop()
    p.pop()

    CKV = p.sb("CKV", [128, SF], BF16)
    KT = p.sb("KT", [128, SF], BF16)
    p.push()
    wB = p.sb("wB", [128, 8, 160], BF16)
    wk96 = p.sb("wk96", [128, 8, 96], BF16)
    wk96r = p.sb("wk96r", [128, 8, 96], BF16)
    xbs = [p.sb("bxb%d" % i, [128, 4, D], BF16) for i in range(2)]
    xTs = [p.sb("bxT%d" % i, [128, 8, 512], BF16) for i in range(2)]
    sq = p.sb("bsq", [128, 512], F32)
    rr = p.sb("brr", [128, 512], F32)
    cks = [p.sb("cks%d" % i, [128, 512], F32) for i in range(2)]
    sks = [p.sb("sks%d" % i, [128, 512], F32) for i in range(2)]
    t1 = p.sb("bt1", [128, 512], F32)
    t2 = p.sb("bt2", [128, 512], F32)
    trps = [p.ps("btrps%d" % i, [128, 1024], BF16) for i in range(2)]
    pps = [p.ps("bpps%d" % i, [128, 512]) for i in range(4)]
    ssq = p.ps("bssq", [128, 512])
    for kc in range(8):
        p.op("pool", lambda e, kc=kc: e.dma_start(out=wB[:, kc, :], in_=W["w_in"][kc * 128:(kc + 1) * 128, 1344:1504]),
             writes=["wB"], dma="wB")
    p.op("dve", lambda e: e.memset(wk96[:], 0.0), writes=["wk96"])
    p.op("dve", lambda e: e.memset(wk96r[:], 0.0), writes=["wk96r"])
    p.op("dve", lambda e: e.tensor_copy(wk96[:, :, 64:96], wB[:, :, 128:160]), reads=["wB", "wk96"], writes=["wk96"])
    p.op("dve", lambda e: e.tensor_copy(wk96r[:, :, 64:80], wB[:, :, 144:160]), reads=["wB", "wk96r"], writes=["wk96r"])
    p.op("dve", lambda e: e.tensor_copy(wk96r[:, :, 80:96], wB[:, :, 128:144]), reads=["wB", "wk96r"], writes=["wk96r"])
    ngf = cfg.NG_F
    load_x_group(p, src["full"](0), xbs[0], "bxb0", "bxb0")
    for g in range(ngf):
        xb, xbk = xbs[g % 2], "bxb%d" % (g % 2)
        xT, xTk = xTs[g % 2], "bxT%d" % (g % 2)
        ck, ckk = cks[g % 2], "cks%d" % (g % 2)
        sk_, skk = sks[g % 2], "sks%d" % (g % 2)
        if g + 1 < ngf:
            load_x_group(p, src["full"](g + 1), xbs[(g + 1) % 2], "bxb%d" % ((g + 1) % 2), "bxb%d" % ((g + 1) % 2))
        p.op("sp", lambda e, ck=ck, g=g: e.dma_start(out=ck[64:96, :], in_=io["ckt"][:, g * 512:(g + 1) * 512]),
             writes=[ckk], dma=ckk)
        p.op("sp", lambda e, sk_=sk_, g=g: e.dma_start(out=sk_[64:96, :], in_=io["skt"][:, g * 512:(g + 1) * 512]),
             writes=[skk], dma=skk)
        transpose_group(p, G, xb, xbk, xT, xTk, trps)
        pa, pb, pc = pps[(3 * g) % 4], pps[(3 * g + 1) % 4], pps[(3 * g + 2) % 4]
        ka, kb, kc_ = "bpps%d" % ((3 * g) % 4), "bpps%d" % ((3 * g + 1) % 4), "bpps%d" % ((3 * g + 2) % 4)
        for kc in range(8):
            p.op("pe", lambda e, kc=kc, pa=pa: e.matmul(pa[:, :], lhsT=wB[:, kc, 0:128], rhs=xT[:, kc, :],
                                                        start=(kc == 0), stop=(kc == 7)), reads=["wB", xTk], writes=[ka])
        for kc in range(8):
            p.op("pe", lambda e, kc=kc, pb=pb: e.matmul(pb[0:96, :], lhsT=wk96[:, kc, :], rhs=xT[:, kc, :],
                                                        start=(kc == 0), stop=(kc == 7)), reads=["wk96", xTk], writes=[kb])
        for kc in range(8):
            p.op("pe", lambda e, kc=kc, pc=pc: e.matmul(pc[0:96, :], lhsT=wk96r[:, kc, :], rhs=xT[:, kc, :],
                                                        start=(kc == 0), stop=(kc == 7)), reads=["wk96r", xTk], writes=[kc_])
        p.op("act", lambda e, pa=pa: e.activation(out=sq[:, :], in_=pa[:, :], func=AF.Square), reads=[ka], writes=["bsq"])
        p.op("pe", lambda e: e.matmul(ssq[:, :], lhsT=G["ones"][:, :], rhs=sq[:, :], start=True, stop=True),
             reads=["bsq", "ones"], writes=["bssq"])
        p.op("act", lambda e: e.activation(out=rr[:, :], in_=ssq[:, :], func=AF.Sqrt, scale=1.0 / 128, bias=EPS),
             reads=["bssq"], writes=["brr"])
        p.op("dve", lambda e: e.reciprocal(rr[:, :], rr[:, :]), reads=["brr"], writes=["brr"])
        p.op("dve", lambda e, pa=pa, g=g: e.tensor_tensor(out=CKV[:, g * 512:(g + 1) * 512], in0=pa[:, :], in1=rr[:, :],
                                                          op=ALU.mult), reads=[ka, "brr"], writes=["CKV"])
        p.op("dve", lambda e, pb=pb, ck=ck: e.tensor_tensor(out=t1[64:96, :], in0=pb[64:96, :], in1=ck[64:96, :], op=ALU.mult),
             reads=[kb, ckk], writes=["bt1"])
        p.op("dve", lambda e, pc=pc, sk_=sk_: e.tensor_tensor(out=t2[64:96, :], in0=pc[64:96, :], in1=sk_[64:96, :], op=ALU.mult),
             reads=[kc_, skk], writes=["bt2"])
        p.op("pool", lambda e, g=g: e.tensor_tensor(out=KT[64:96, g * 512:(g + 1) * 512], in0=t1[64:96, :], in1=t2[64:96, :],
                                                    op=ALU.add), reads=["bt1", "bt2"], writes=["KT"])
    p.pop()

    p.push()
    Vh = p.sb("Vh", [128, SF // 128, 65], BF16)
    QT = p.sb("QT", [128, OWN], BF16)
    cqt = p.sb("cqt", [128, OWN], F32)
    sqt = p.sb("sqt", [128, OWN], F32)
    wkv = p.sb("wkv", [128, 768], BF16)
    wkvf = p.sb("wkvf", [128, 768], F32)
    wqf = p.sb("wqf", [128, 2, 576], F32)
    wq = p.sb("wq", [128, 2, 6, 96], BF16)
    wqr = p.sb("wqr", [128, 2, 6, 96], BF16)
    qn = p.sb("qn", [128, 2], F32)
    kvn = p.sb("kvn", [128, 1], F32)
    q1 = p.sb("q1", [128, 512], F32)
    q2 = p.sb("q2", [128, 512], F32)
    NSB = 3
    PTs = [p.sb("bPT%d" % i, [128, 1024], BF16) for i in range(NSB)]
    Sps = [p.ps("bS%d" % i, [128, 1024]) for i in range(NSB)]
    Ops = [p.ps("bO%d" % i, [128, 512]) for i in range(2)]
    pend = []
    pendb = []
    gib = [0]

    def alloc_sb():
        i = gib[0] % NSB
        gib[0] += 1
        return Sps[i], "bS%d" % i

    def peek_sb():
        i = gib[0] % NSB
        return Sps[i], "bS%d" % i
    bufs = dict(peek=peek_sb, osb=[p.sb("bosb%d" % i, [128, 512], F32) for i in range(2)], rden=p.sb("brden", [128, 512], F32),
                ysq=[p.sb("bysq%d" % i, [128, 512], F32) for i in range(2)], ybf=[p.sb("bybf%d" % i, [128, 512], BF16) for i in range(2)],
                alloc=alloc_sb)
    p.op("sp", lambda e: e.dma_start(out=wkvf[:], in_=W["w_kv_up"]), writes=["wkvf"], dma="bw")
    p.op("sp", lambda e: e.dma_start(out=wqf[:, 0, :], in_=W["w_q_up"][0:128, :]), writes=["wqf"], dma="bw")
    p.op("sp", lambda e: e.dma_start(out=wqf[0:64, 1, :], in_=W["w_q_up"][128:192, :]), writes=["wqf"], dma="bw")
    p.op("sp", lambda e: e.dma_start(out=qn[:], in_=W["qn"]), writes=["qn"], dma="bw")
    p.op("sp", lambda e: e.dma_start(out=kvn[:], in_=W["kvn"]), writes=["kvn"], dma="bw")
    p.op("sp", lambda e: e.dma_start(out=cqt[64:96, :], in_=io["cqt"]), writes=["cqt"], dma="bw")
    p.op("sp", lambda e: e.dma_start(out=sqt[64:96, :], in_=io["sqt"]), writes=["sqt"], dma="bw")
    p.op("dve", lambda e: e.tensor_scalar(wkv[:, :], wkvf[:, :], kvn[:, 0:1], None, op0=ALU.mult), reads=["wkvf", "kvn"],
         writes=["wkv"])
    p.op("dve", lambda e: e.memset(wq[:], 0.0), writes=["wq"])
    p.op("dve", lambda e: e.memset(wqr[:], 0.0), writes=["wqr"])
    for cc, np_ in ((0, 128), (1, 64)):
        wv = wqf[0:np_, cc, :].rearrange("p (h c) -> p h c", h=6)
        p.op("dve", lambda e, cc=cc, np_=np_, wv=wv: e.tensor_scalar(wq[0:np_, cc, :, :], wv, qn[0:np_, cc:cc + 1], None,
                                                                      op0=ALU.mult), reads=["wqf", "qn", "wq"], writes=["wq"])
        p.op("dve", lambda e, cc=cc, np_=np_, wv=wv: e.tensor_scalar(wqr[0:np_, cc, :, 64:80], wv[:, :, 80:96],
                                                                      qn[0:np_, cc:cc + 1], None, op0=ALU.mult),
             reads=["wqf", "qn", "wqr"], writes=["wqr"])
        p.op("dve", lambda e, cc=cc, np_=np_, wv=wv: e.tensor_scalar(wqr[0:np_, cc, :, 80:96], wv[:, :, 64:80],
                                                                      qn[0:np_, cc:cc + 1], None, op0=ALU.mult),
             reads=["wqf", "qn", "wqr"], writes=["wqr"])
    p.op("pool", lambda e: e.memset(Vh[:, :, 64:65], 1.0), writes=["Vh"])
    it = 0
    gi = 0
    nkt = SF // 128
    for h in range(6):
        for g2 in range(SF // 1024):
            S, sk = alloc_sb()
            for u in range(2):
                g = 2 * g2 + u
                p.op("pe", lambda e, S=S, u=u, g=g, h=h: e.matmul(S[0:64, u * 512:(u + 1) * 512], lhsT=wkv[:, h * 128:h * 128 + 64],
                                                                  rhs=CKV[:, g * 512:(g + 1) * 512], start=True, stop=True),
                     reads=["wkv", "CKV"], writes=[sk])
            p.op("dve", lambda e, S=S, g2=g2: e.tensor_copy(KT[0:64, g2 * 1024:(g2 + 1) * 1024], S[0:64, :]),
                 reads=[sk], writes=["KT"])
        for t16 in range(nkt // 16):
            S, sk = alloc_sb()
            for u in range(16):
                t = t16 * 16 + u
                p.op("pe", lambda e, S=S, u=u, t=t, h=h: e.matmul(S[:, u * 64:(u + 1) * 64], lhsT=CKV[:, t * 128:(t + 1) * 128],
                                                                  rhs=wkv[:, h * 128 + 64:h * 128 + 128], start=True, stop=True),
                     reads=["wkv", "CKV"], writes=[sk])
            p.op("dve", lambda e, S=S, t16=t16: e.tensor_copy(Vh[:, t16 * 16:(t16 + 1) * 16, 0:64],
                                                               S[:, :].rearrange("p (t c) -> p t c", c=64)),
                 reads=[sk], writes=["Vh"])
        for qb in range(NQB):
            S, sk = alloc_sb()
            for u, wsrc in ((0, wq), (1, wqr)):
                p.op("pe", lambda e, S=S, u=u, wsrc=wsrc, h=h, qb=qb: e.matmul(S[0:96, u * 512:(u + 1) * 512], lhsT=wsrc[:, 0, h, :],
                                                                               rhs=CQ[:, 0, qb * 512:(qb + 1) * 512],
                                                                               start=True, stop=False),
                     reads=["wq", "wqr", "CQ"], writes=[sk])
                p.op("pe", lambda e, S=S, u=u, wsrc=wsrc, h=h, qb=qb: e.matmul(S[0:96, u * 512:(u + 1) * 512], lhsT=wsrc[0:64, 1, h, :],
                                                                               rhs=CQ[0:64, 1, qb * 512:(qb + 1) * 512],
                                                                               start=False, stop=True),
                     reads=["wq", "wqr", "CQ"], writes=[sk])
            p.op("dve", lambda e, S=S, qb=qb: e.tensor_copy(QT[0:64, qb * 512:(qb + 1) * 512], S[0:64, 0:512]),
                 reads=[sk], writes=["QT"])
            p.op("dve", lambda e, S=S, qb=qb: e.tensor_tensor(out=q1[64:96, :], in0=S[64:96, 0:512],
                                                              in1=cqt[64:96, qb * 512:(qb + 1) * 512], op=ALU.mult),
                 reads=[sk, "cqt"], writes=["q1"])
            p.op("dve", lambda e, S=S, qb=qb: e.tensor_tensor(out=q2[64:96, :], in0=S[64:96, 512:1024],
                                                              in1=sqt[64:96, qb * 512:(qb + 1) * 512], op=ALU.mult),
                 reads=[sk, "sqt"], writes=["q2"])
            p.op("pool", lambda e, qb=qb: e.tensor_tensor(out=QT[64:96, qb * 512:(qb + 1) * 512], in0=q1[64:96, :],
                                                          in1=q2[64:96, :], op=ALU.add), reads=["q1", "q2"], writes=["QT"])
        for qb in range(NQB):
            ops, okey = Ops[it % 2], "bO%d" % (it % 2)
            ngrp = nkt // 2

            def qk(i):
                S, sk = alloc_sb()
                for u in range(2):
                    kt = 2 * i + u
                    p.op("pe", lambda e, S=S, u=u, kt=kt: e.matmul(S[:, u * 512:(u + 1) * 512],
                                                                   lhsT=KT[0:96, kt * 128:(kt + 1) * 128],
                                                                   rhs=QT[0:96, qb * 512:(qb + 1) * 512], start=True, stop=True),
                         reads=["KT", "QT"], writes=[sk])
                return S, sk

            inflight = [qk(0), qk(1)]
            for i in range(ngrp):
                if i + 2 < ngrp:
                    inflight.append(qk(i + 2))
                S, sk = inflight.pop(0)
                PT, pk = PTs[i % NSB], "bPT%d" % (i % NSB)
                p.op("act", lambda e, S=S, PT=PT: e.activation(out=PT[:, :], in_=S[:, :], func=AF.Exp, scale=B_SCALE),
                     reads=[sk], writes=[pk])
                for u in range(2):
                    kt = 2 * i + u
                    p.op("pe", lambda e, PT=PT, u=u, kt=kt, ops=ops, i=i: e.matmul(
                        ops[0:65, :], lhsT=Vh[:, kt, 0:65], rhs=PT[:, u * 512:(u + 1) * 512],
                        start=(i == 0 and u == 0), stop=(i == ngrp - 1 and u == 1)),
                        reads=["Vh", pk], writes=[okey])
                if i == 3 and pend:
                    pendb.append(pend.pop()())
                if i == 9 and pendb:
                    pendb.pop()()
            pend.append(attn_post(p, G, io, ops, okey, 384 + h * 64, qb, 6 + h, bufs, it))
            it += 1
        while pend:
            pendb.append(pend.pop()())
        while pendb:
            pendb.pop()()
    p.pop()
    p.pop()

    p.push()
    wf1 = p.sb("wf1", [128, 8, DFF], BF16)
    wf2 = p.sb("wf2", [128, 32, D], BF16)
    p.push()
    wo = p.sb("wo", [128, 8, D], BF16)
    wof = p.sb("wof", [128, D], F32)
    mixn = p.sb("mixn", [128, 8], F32)
    g1 = p.sb("g1", [128, D], F32)
    b1 = p.sb("b1", [128, D], F32)
    yTs = [p.sb("yT%d" % i, [128, 8, 512], BF16) for i in range(2)]
    xrs = [p.sb("xr%d" % i, [128, D], F32) for i in range(2)]
    accs = [p.sb("dacc%d" % i, [128, D], F32) for i in range(2)]
    x1s = [p.sb("x1s%d" % i, [128, D], F32) for i in range(2)]
    rst = p.sb("rst", [128, 3, NT], F32)
    sst = p.sb("sst", [128, 3, NT], F32)
    st12s = [p.sb("st12_%d" % i, [128, 12], F32) for i in range(2)]
    mvs = [p.sb("dmv%d" % i, [128, 2], F32) for i in range(2)]
    rss = [p.sb("drs%d" % i, [128, 1], F32) for i in range(2)]
    accps = [p.ps("accps%d" % i, [128, 1024]) for i in range(2)]
    p.op("sp", lambda e: e.dma_start(out=mixn[:], in_=W["mixn"]), writes=["mixn"], dma="dw")
    p.op("sp", lambda e: e.dma_start(out=g1[:], in_=W["ln1_g"].partition_broadcast(128)), writes=["g1"], dma="dw")
    p.op("sp", lambda e: e.dma_start(out=b1[:], in_=W["ln1_b"].partition_broadcast(128)), writes=["b1"], dma="dw")
    for kc in range(8):
        p.op("sp", lambda e, kc=kc: e.dma_start(out=wof[:], in_=W["w_out"][kc * 128:(kc + 1) * 128, :]), writes=["wof"], dma="dwo")
        p.op("dve", lambda e, kc=kc: e.tensor_scalar(wo[:, kc, :], wof[:, :], mixn[:, kc:kc + 1], None, op0=ALU.mult),
             reads=["wof", "mixn"], writes=["wo"])
    for kc in range(8):
        p.op("pool", lambda e, kc=kc: e.dma_start(out=wf1[:, kc, :], in_=W["w_ff1"][kc * 128:(kc + 1) * 128, :]),
             writes=["wf1"], dma="wf1")
    for c4 in range(8):
        p.op("pool", lambda e, c4=c4: e.dma_start(out=wf2[:, c4 * 4:(c4 + 1) * 4, :],
                                                  in_=W["w_ff2"][c4 * 512:(c4 + 1) * 512, :].rearrange("(c p) d -> p c d", p=128)),
             writes=["wf2"], dma="wf2")
    ssv = G["ss"]
    p.op("dve", lambda e: e.tensor_copy(sst[:, 0, :], ssv[:, 0, :]), reads=["ss"], writes=["sst"])
    p.op("dve", lambda e: e.tensor_copy(sst[:, 1, :], ssv[:, 6, :]), reads=["ss"], writes=["sst"])
    p.op("dve", lambda e: e.tensor_copy(sst[:, 2, :], ssv[:, 12, :]), reads=["ss"], writes=["sst"])
    for k in range(1, 6):
        p.op("dve", lambda e, k=k: e.tensor_tensor(out=sst[:, 0, :], in0=sst[:, 0, :], in1=ssv[:, k, :], op=ALU.add),
             reads=["ss", "sst"], writes=["sst"])
        p.op("dve", lambda e, k=k: e.tensor_tensor(out=sst[:, 1, :], in0=sst[:, 1, :], in1=ssv[:, 6 + k, :], op=ALU.add),
             reads=["ss", "sst"], writes=["sst"])
    for gidx, wdt in ((0, 384.0), (1, 384.0), (2, 256.0)):
        p.op("act", lambda e, gidx=gidx, wdt=wdt: e.activation(out=rst[:, gidx, :], in_=sst[:, gidx, :], func=AF.Sqrt,
                                                               scale=1.0 / wdt, bias=EPS), reads=["sst"], writes=["rst"])
    p.op("dve", lambda e: e.reciprocal(rst[:, :, :], rst[:, :, :]), reads=["rst"], writes=["rst"])
    KCG = ((0, 3), (3, 6), (6, 8))
    ai = 0
    for g in range(NQB):
        yT, yk = yTs[g % 2], "yT%d" % (g % 2)
        p.op("sp", lambda e, yT=yT, g=g: e.dma_start(out=yT[:, :, :], in_=io["YT"][:, g * 512:(g + 1) * 512].rearrange("(c p) t -> p c t", p=128)),
             writes=[yk], dma=yk)
        for j in range(4):
            ti = g * 4 + j
            xr, xk = xrs[ti % 2], "xr%d" % (ti % 2)
            x1, x1k = x1s[ti % 2], "x1s%d" % (ti % 2)
            acc, acck = accs[ti % 2], "dacc%d" % (ti % 2)
            p.op("sp", lambda e, xr=xr, ti=ti: e.dma_start(out=xr[:, :], in_=src["res"](ti)),
                 writes=[xk], dma=xk)
            for gidx, (k0, k1) in enumerate(KCG):
                ap_, ak = accps[ai % 2], "accps%d" % (ai % 2)
                ai += 1
                for half in range(2):
                    for kc in range(k0, k1):
                        p.op("pe", lambda e, ap_=ap_, half=half, kc=kc, k0=k0, k1=k1, j=j: e.matmul(
                            ap_[:, half * 512:(half + 1) * 512], lhsT=yT[:, kc, j * 128:(j + 1) * 128],
                            rhs=wo[:, kc, half * 512:(half + 1) * 512], start=(kc == k0), stop=(kc == k1 - 1)),
                            reads=[yk, "wo"], writes=[ak])
                if gidx == 0:
                    p.op("dve", lambda e, ap_=ap_, ti=ti: e.tensor_scalar(acc[:, :], ap_[:, :], rst[:, 0, ti:ti + 1], None, op0=ALU.mult),
                         reads=[ak, "rst"], writes=[acck])
                else:
                    p.op("dve", lambda e, ap_=ap_, ti=ti, gidx=gidx: e.scalar_tensor_tensor(
                        out=acc[:, :], in0=ap_[:, :], scalar=rst[:, gidx, ti:ti + 1], in1=acc[:, :], op0=ALU.mult, op1=ALU.add),
                        reads=[ak, "rst", acck], writes=[acck])
            p.op("dve", lambda e, xr=xr: e.scalar_tensor_tensor(out=acc[:, :], in0=xr[:, :], scalar=ALPHA, in1=acc[:, :],
                                                                op0=ALU.mult, op1=ALU.add), reads=[xk, acck], writes=[acck])
            layer_norm(p, acc, acck, x1, x1k, g1, "g1", b1, "b1", st12s[ti % 2], mvs[ti % 2], rss[ti % 2], "d1_%d" % (ti % 2))
            p.op("sp", lambda e, x1=x1, ti=ti: e.dma_start(out=io["X1"][ti * 128:(ti + 1) * 128, :], in_=x1[:, :]),
                 reads=[x1k], dma="x1o%d" % (ti % 2))
    p.pop()

    p.push()
    b1T = p.sb("b1T", [128, 32], F32)
    g2 = p.sb("g2", [128, D], F32)
    b2 = p.sb("b2", [128, D], F32)
    bf2 = p.sb("bf2", [128, D], F32)
    x1f = [p.sb("x1f%d" % i, [128, 2, D], F32) for i in range(2)]
    x1b = p.sb("x1b", [128, 2, D], BF16)
    x1Ts = [p.sb("x1T%d" % i, [128, 8, 256], BF16) for i in range(2)]
    hidT = p.sb("hidT", [128, 32, 256], BF16)
    rl = [p.sb("rl%d" % i, [128, 256], F32) for i in range(2)]
    t2ss = [p.sb("t2s%d" % i, [128, D], F32) for i in range(1)]
    outs = [p.sb("outs%d" % i, [128, D], F32) for i in range(2)]
    outb = [p.sb("outb%d" % i, [128, D], BF16) for i in range(2)]
    st12s = [p.sb("st12b%d" % i, [128, 12], F32) for i in range(2)]
    mvs = [p.sb("dmv2_%d" % i, [128, 2], F32) for i in range(2)]
    rss = [p.sb("drs2_%d" % i, [128, 1], F32) for i in range(2)]
    trps = [p.ps("dtrps%d" % i, [128, 1024], BF16) for i in range(2)]
    hps = [p.ps("hps%d" % i, [128, 512]) for i in range(2)]
    fps = [p.ps("fps%d" % i, [128, 1024]) for i in range(2)]
    p.op("sp", lambda e: e.dma_start(out=b1T[:], in_=W["b1T"]), writes=["b1T"], dma="dw2")
    p.op("sp", lambda e: e.dma_start(out=g2[:], in_=W["ln2_g"].partition_broadcast(128)), writes=["g2"], dma="dw2")
    p.op("sp", lambda e: e.dma_start(out=b2[:], in_=W["ln2_b"].partition_broadcast(128)), writes=["b2"], dma="dw2")
    p.op("sp", lambda e: e.dma_start(out=bf2[:], in_=W["b_ff2"].partition_broadcast(128)), writes=["bf2"], dma="dw2")
    ng2 = OWN // 256

    def prep(g):
        xf_, xfk = x1f[g % 2], "x1f%d" % (g % 2)
        p.op("sp", lambda e: e.dma_start(out=xf_[:, :, :], in_=io["X1"][g * 256:(g + 1) * 256, :].rearrange("(j p) d -> p j d", p=128)),
             writes=[xfk], dma=xfk)
        p.op("pool", lambda e: e.tensor_copy(x1b[:, :, :], xf_[:, :, :]), reads=[xfk], writes=["x1b"])
        transpose_group(p, G, x1b, "x1b", x1Ts[g % 2], "x1T%d" % (g % 2), trps, n_tok_tiles=2)

    prep(0)
    hi = 0
    fi = 0
    for g in range(ng2):
        xf_, xfk = x1f[g % 2], "x1f%d" % (g % 2)
        x1T, x1Tk = x1Ts[g % 2], "x1T%d" % (g % 2)
        for fc in range(32):
            hp_, hk = hps[hi % 2], "hps%d" % (hi % 2)
            r_, rk = rl[hi % 2], "rl%d" % (hi % 2)
            hi += 1
            for kc in range(8):
                p.op("pe", lambda e, hp_=hp_, kc=kc, fc=fc: e.matmul(hp_[:, 0:256], lhsT=wf1[:, kc, fc * 128:(fc + 1) * 128],
                                                                     rhs=x1T[:, kc, :], start=(kc == 0), stop=(kc == 7)),
                     reads=["wf1", x1Tk], writes=[hk])
            p.op("act", lambda e, hp_=hp_, r_=r_, fc=fc: e.activation(out=r_[:, :], in_=hp_[:, 0:256], func=AF.Relu,
                                                                      bias=b1T[:, fc:fc + 1], scale=1.0),
                 reads=[hk, "b1T"], writes=[rk])
            p.op("pool" if fc % 2 else "dve", lambda e, r_=r_, fc=fc: e.tensor_tensor(out=hidT[:, fc, :], in0=r_[:, :], in1=r_[:, :], op=ALU.mult),
                 reads=[rk], writes=["hidT"])
        if g + 1 < ng2:
            prep(g + 1)
        for j in range(2):
            ti = g * 2 + j
            fp_, fk = fps[fi % 2], "fps%d" % (fi % 2)
            o_, ok_ = outs[fi % 2], "outs%d" % (fi % 2)
            t2s, t2k = t2ss[0], "t2s0"
            par = fi % 2
            fi += 1
            for half in range(2):
                for fc in range(32):
                    p.op("pe", lambda e, fp_=fp_, half=half, fc=fc, j=j: e.matmul(
                        fp_[:, half * 512:(half + 1) * 512], lhsT=hidT[:, fc, j * 128:(j + 1) * 128],
                        rhs=wf2[:, fc, half * 512:(half + 1) * 512], start=(fc == 0), stop=(fc == 31)),
                        reads=["hidT", "wf2"], writes=[fk])
            p.op("dve", lambda e, fp_=fp_: e.tensor_tensor(out=t2s[:, :], in0=fp_[:, :], in1=bf2[:, :], op=ALU.add),
                 reads=[fk, "bf2"], writes=[t2k])
            p.op("dve", lambda e, xf_=xf_, j=j: e.scalar_tensor_tensor(out=t2s[:, :], in0=xf_[:, j, :], scalar=ALPHA, in1=t2s[:, :],
                                                                       op0=ALU.mult, op1=ALU.add), reads=[xfk, t2k], writes=[t2k])
            layer_norm(p, t2s, t2k, o_, ok_, g2, "g2", b2, "b2", st12s[par], mvs[par], rss[par], "d2_%d" % par)
            p.op("sp", lambda e, o_=o_, ti=ti: e.dma_start(out=dst["f32"][ti * 128:(ti + 1) * 128, :], in_=o_[:, :]),
                 reads=[ok_], dma=dst["tag"])
            if dst["bf16"] is not None:
                ob_, obk = outb[par], "outb%d" % par
                p.op("pool", lambda e, o_=o_, ob_=ob_: e.tensor_copy(ob_[:, :], o_[:, :]), reads=[ok_], writes=[obk])
                p.op("sp", lambda e, ob_=ob_, ti=ti: e.dma_start(out=dst["bf16"][ti * 128:(ti + 1) * 128, :], in_=ob_[:, :]),
                     reads=[obk], dma="xbc")
    p.pop()
    p.pop()


def layer_norm(p, src, skey, dstt, dkey, g, gk, b, bk, st12, mv, rs, tg):
    k6, kmv, krs = "st12" + tg, "mv" + tg, "rs" + tg
    p.op("dve", lambda e: e.bn_stats(st12[:, 0:6], src[:, 0:512]), reads=[skey], writes=[k6])
    p.op("dve", lambda e: e.bn_stats(st12[:, 6:12], src[:, 512:1024]), reads=[skey], writes=[k6])
    p.op("dve", lambda e: e.bn_aggr(mv[:, :], st12[:, :]), reads=[k6], writes=[kmv])
    p.op("act", lambda e: e.activation(out=rs[:, :], in_=mv[:, 1:2], func=AF.Sqrt, scale=1.0, bias=EPS), reads=[kmv], writes=[krs])
    p.op("dve", lambda e: e.reciprocal(rs[:, :], rs[:, :]), reads=[krs], writes=[krs])
    p.op("dve", lambda e: e.tensor_scalar(src[:, :], src[:, :], mv[:, 0:1], rs[:, 0:1], op0=ALU.subtract, op1=ALU.mult),
         reads=[skey, kmv, krs], writes=[skey])
    p.op("pool", lambda e: e.tensor_tensor(out=src[:, :], in0=src[:, :], in1=g[:, :], op=ALU.mult), reads=[skey, gk], writes=[skey])
    p.op("pool", lambda e: e.tensor_tensor(out=dstt[:, :], in0=src[:, :], in1=b[:, :], op=ALU.add), reads=[skey, bk], writes=[dkey])


_CACHE = {}


def host_weights(inp, layers):
    f = lambda a: np.ascontiguousarray(a, dtype=np.float32)
    L = list(layers)
    w = {}
    w["w_in"] = f(inp["w_in"][L])
    w["w_q_up"] = f(inp["w_q_up"][L])
    w["w_kv_up"] = f(inp["w_kv_up"][L])
    qn = np.zeros((len(L), 256), np.float32)
    qn[:, :192] = inp["q_norm"][L]
    w["qn"] = f(qn.reshape(len(L), 2, 128).transpose(0, 2, 1))
    w["kvn"] = f(inp["kv_norm"][L].reshape(len(L), 128, 1))
    w["sg_g"] = f(inp["sgu_ln_g"][L])
    w["sg_b"] = f(inp["sgu_ln_b"][L])
    w["sg_wT"] = f(np.transpose(inp["sgu_w"][L], (0, 1, 3, 2)))
    w["sg_bT"] = f(np.transpose(inp["sgu_b"][L], (0, 2, 1)))
    w["mixn"] = f(inp["mix_norm"][L].reshape(len(L), 8, 128).transpose(0, 2, 1))
    w["w_out"] = f(inp["w_out"][L])
    w["ln1_g"] = f(inp["ln1_g"][L])
    w["ln1_b"] = f(inp["ln1_b"][L])
    w["w_ff1"] = f(inp["w_ff1"][L])
    w["b1T"] = f(inp["b_ff1"][L].reshape(len(L), 32, 128).transpose(0, 2, 1))
    w["w_ff2"] = f(inp["w_ff2"][L])
    w["b_ff2"] = f(inp["b_ff2"][L])
    w["ln2_g"] = f(inp["ln2_g"][L])
    w["ln2_b"] = f(inp["ln2_b"][L])
    return w


def core_inputs(x_b, r, own, consts):
    S = x_b.shape[0]
    lo = r * own - HALO
    xo = np.zeros((own + 2 * HALO, D), np.float32)
    vm = np.zeros((own + 2 * HALO,), np.float32)
    a, b = max(lo, 0), min(lo + own + 2 * HALO, S)
    xo[a - lo:b - lo] = x_b[a:b]
    vm[a - lo:b - lo] = 1.0
    ct, st = consts["rope"]
    hs = np.zeros((128, 8), np.float32)
    if r - 1 >= 0:
        hs[:, r - 1] = 1.0
    if r + 1 < S // own:
        hs[:, 4 + r + 1] = 1.0
    d = dict(hsel=hs, xo=xo, xf=np.ascontiguousarray(x_b, dtype=np.float32),
             vmask=np.ascontiguousarray(vm.reshape(-1, 128).T), mtab=consts["mtab"],
             ckt=ct, skt=st, cqt=np.ascontiguousarray(ct[:, r * own:(r + 1) * own]),
             sqt=np.ascontiguousarray(st[:, r * own:(r + 1) * own]))
    return d


def run_layers(x, inp, own, n_groups_per_batch, fused_depth=1, dbg=False):
    B, S, _ = x.shape
    key = (own, S, fused_depth, dbg, B)
    if key not in _CACHE:
        _CACHE[key] = build_program(Cfg(own, S, depth=fused_depth, dbg=dbg, ncores=B * n_groups_per_batch))
    nc, stats = _CACHE[key]
    consts = dict(mtab=mask_table(), rope=rope_tables(S))
    depth = inp["w_in"].shape[0]
    cur = np.asarray(x, dtype=np.float32)
    extra = None
    for l0 in range(0, depth, fused_depth):
        w = host_weights(inp, range(l0, l0 + fused_depth))
        in_maps = []
        for c in range(B * n_groups_per_batch):
            b, r = c // n_groups_per_batch, c % n_groups_per_batch
            d = core_inputs(cur[b], r, own, consts)
            if fused_depth == 1:
                d.pop("hsel")
            d.update(w)
            in_maps.append(d)
        res = run_bass_kernel_spmd(nc, in_maps, core_ids=list(range(len(in_maps))))
        outs = [r_["out"] for r_ in res.results]
        cur = np.stack([np.concatenate(outs[b * n_groups_per_batch:(b + 1) * n_groups_per_batch], 0) for b in range(B)], 0)
        extra = res.results
    return cur, extra


def kernel(**inputs):
    x = np.asarray(inputs["x"], dtype=np.float32)
    inp = {k: np.asarray(v, dtype=np.float32) for k, v in inputs.items() if k != "x"}
    out, _ = run_layers(x, inp, own=x.shape[1] // 4, n_groups_per_batch=4, fused_depth=inp["w_in"].shape[0])
    return out.astype(np.float32)
```

```python
import types
import numpy as np
import ml_dtypes
import concourse.bass as bass
import concourse.mybir as mybir
from concourse.bass_utils import run_bass_kernel_spmd

F32 = mybir.dt.float32
BF16 = mybir.dt.bfloat16
I32 = mybir.dt.int32
AF = mybir.ActivationFunctionType
ALU = mybir.AluOpType
AX = mybir.AxisListType

ENGS = ("pe", "act", "dve", "pool", "sp")
SEM_LIMIT = 20000

D = 1024
PIN = 2016
HALO = 1024
DFF = 4096
EPS = 1e-5
ALPHA = float((2 * 2) ** 0.25)
TABW = 2944
A_SCALE = 0.125
B_SCALE = float(96 ** -0.5)
C_GELU = 0.044715
K_GELU = float(np.sqrt(2.0 / np.pi))


def _freeze(fn):
    if fn is None or fn.__closure__ is None:
        return fn
    cells = []
    for c in fn.__closure__:
        try:
            cells.append(types.CellType(c.cell_contents))
        except ValueError:
            cells.append(c)
    return types.FunctionType(fn.__code__, fn.__globals__, fn.__name__, fn.__defaults__, tuple(cells))


class Prog:
    def __init__(self, nc):
        self.nc = nc
        self.ops = {e: [] for e in ENGS}
        self.state = {}
        self.dma_sems = {}
        self.pending = {e: [] for e in ENGS}
        self._cms = []
        self._scopes = []

    def _reg(self, cm):
        t = cm.__enter__()
        (self._scopes[-1] if self._scopes else self._cms).append(cm)
        return t

    def sem(self, name):
        self._n = getattr(self, "_n", 0) + 1
        cm = self.nc.semaphore("m%d_%s" % (self._n, name))
        s = cm.__enter__()
        self._cms.append(cm)
        return s

    def sb(self, name, shape, dt):
        self._n = getattr(self, "_n", 0) + 1
        return self._reg(self.nc.sbuf_tensor("sb%d_%s" % (self._n, name), list(shape), dt))

    def ps(self, name, shape, dt=F32):
        self._n = getattr(self, "_n", 0) + 1
        return self._reg(self.nc.psum_tensor("ps%d_%s" % (self._n, name), list(shape), dt))

    def push(self):
        self._scopes.append([])

    def pop(self):
        self.barrier()
        for cm in reversed(self._scopes.pop()):
            cm.__exit__(None, None, None)

    def close(self):
        for cm in reversed(self._cms):
            cm.__exit__(None, None, None)
        self._cms = []

    def barrier(self):
        evs = []
        for e in ENGS:
            lst = self.ops[e]
            for i in range(len(lst) - 1, -1, -1):
                if lst[i]["dma"] is None and lst[i]["fn"] is not None:
                    evs.append(("eng", e, i))
                    break
        for tag, ds in self.dma_sems.items():
            evs.append(("dma", ds[0], ds[1]))
        for e in ENGS:
            self.pending[e].extend(evs)
        self.state = {}

    def op(self, eng, fn, reads=(), writes=(), dma=None, inc=16):
        fn = _freeze(fn)
        deps = []
        for k in reads:
            st = self.state.get(k)
            if st and st[0] is not None:
                deps.append((st[0], True))
        for k in writes:
            st = self.state.get(k)
            if st:
                if st[0] is not None:
                    deps.append((st[0], False))
                for r in st[1]:
                    deps.append((r, False))
        lst = self.ops[eng]
        idx = len(lst)
        if dma is not None:
            if dma not in self.dma_sems:
                self.dma_sems[dma] = [self.sem("d_" + dma), 0]
            ds = self.dma_sems[dma]
            ds[1] += inc
            ev = ("dma", ds[0], ds[1])
        else:
            ev = ("eng", eng, idx)
        fdeps = []
        for d, raw in deps:
            if d[0] == "eng" and d[1] == eng and dma is None:
                if eng == "pe" or not raw:
                    continue
            if d[0] == "dma":
                for ds_ in self.dma_sems.values():
                    if ds_[0] is d[1]:
                        cur = ds_[1] - (inc if (dma is not None and self.dma_sems[dma][0] is d[1]) else 0)
                        d = ("dma", d[1], max(d[2], cur))
            fdeps.append(d)
        for d in self.pending[eng]:
            if d[0] == "eng" and d[1] == eng and eng == "pe":
                continue
            fdeps.append(d)
        self.pending[eng] = []
        lst.append(dict(fn=fn, deps=fdeps, dma=dma, ev=ev, ms=False, inc=inc))
        for k in reads:
            st = self.state.setdefault(k, [None, []])
            st[1].append(ev)
        for k in writes:
            self.state[k] = [ev, []]
        return ev

    def wait_all_dma(self, eng, tags):
        deps = []
        for t in tags:
            ds = self.dma_sems[t]
            deps.append(("dma", ds[0], ds[1]))
        self.ops[eng].append(dict(fn=None, deps=deps, dma=None, ev=None, ms=False))

    def emit(self):
        nc = self.nc
        for e in ENGS:
            for o in self.ops[e]:
                for d in o["deps"]:
                    if d[0] == "eng":
                        self.ops[d[1]][d[2]]["ms"] = True
        msmap = {}
        for e in ENGS:
            cur = None
            cnt = 0
            for i, o in enumerate(self.ops[e]):
                if o["ms"]:
                    if cur is None or cnt >= SEM_LIMIT:
                        cur = self.sem("s_%s_%d" % (e, i))
                        cnt = 0
                    cnt += 1
                    msmap[(e, i)] = (cur, cnt)
                    o["inc"] = cur
        stats = {}

        def run(e, eng):
            waited = {}
            nw = 0
            for i, o in enumerate(self.ops[e]):
                for d in o["deps"]:
                    if d[0] == "eng":
                        sem, val = msmap[(d[1], d[2])]
                    else:
                        sem, val = d[1], d[2]
                    key = id(sem)
                    if waited.get(key, 0) >= val:
                        continue
                    waited[key] = val
                    eng.wait_ge(sem, val)
                    nw += 1
                if o["fn"] is None:
                    continue
                ins = o["fn"](eng)
                if o["dma"] is not None:
                    ins.then_inc(o["ev"][1], o.get("inc", 16))
                elif o["ms"]:
                    ins.then_inc(o["inc"], 1)
            stats[e] = (len(self.ops[e]), nw)

        with nc.Block() as block:
            @block.tensor
            def _(eng):
                run("pe", eng)

            @block.scalar
            def _(eng):
                run("act", eng)

            @block.vector
            def _(eng):
                run("dve", eng)

            @block.gpsimd
            def _(eng):
                run("pool", eng)

            @block.sync
            def _(eng):
                run("sp", eng)
        self.stats = stats
        return stats


def mask_table():
    slopes = (2.0 ** (-8.0 * np.arange(1, 7) / 6)).astype(np.float32)
    pp = np.arange(128)[:, None]
    col = np.arange(TABW)[None, :]
    delta = pp - col + 1408
    ad = np.abs(delta)
    c = (ad <= 64).astype(np.float32) + ((delta % 4 == 0) & (ad <= 256)).astype(np.float32) \
        + ((delta % 16 == 0) & (ad <= 1024)).astype(np.float32)
    tab = np.zeros((128, 6, TABW), np.float32)
    for h in range(6):
        tab[:, h, :] = c * np.exp(-(slopes[h] * ad.astype(np.float32)).astype(np.float32))
    return tab.astype(ml_dtypes.bfloat16)


def rope_tables(S):
    inv_freq = (10000.0 ** (-np.arange(0, 32, 2, dtype=np.float32) / 32)).astype(np.float32)
    ang = (np.arange(S, dtype=np.float32)[:, None] * inv_freq[None, :]).astype(np.float32)
    cos = np.cos(ang).astype(np.float32).T
    sin = np.sin(ang).astype(np.float32).T
    ct = np.concatenate([cos, cos], 0)
    st = np.concatenate([-sin, sin], 0)
    return np.ascontiguousarray(ct), np.ascontiguousarray(st)


class Cfg:
    def __init__(self, own, sf, depth=1, dbg=False, ncores=8):
        self.ncores = ncores
        self.OWN = own
        self.SF = sf
        self.OH = own + 2 * HALO
        self.NT = own // 128
        self.NQB = own // 512
        self.NG_OH = self.OH // 512
        self.NG_F = sf // 512
        self.NKT_OH = self.OH // 128
        self.NKT_F = sf // 128
        self.depth = depth
        self.dbg = dbg


W_NAMES = [
    ("w_in", [D, PIN]), ("w_q_up", [192, 576]), ("w_kv_up", [128, 768]), ("qn", [128, 2]), ("kvn", [128, 1]),
    ("sg_g", [256]), ("sg_b", [256]), ("sg_wT", [4, 128, 128]), ("sg_bT", [128, 4]), ("mixn", [128, 8]),
    ("w_out", [D, D]), ("ln1_g", [D]), ("ln1_b", [D]), ("w_ff1", [D, DFF]), ("b1T", [128, 32]),
    ("w_ff2", [DFF, D]), ("b_ff2", [D]), ("ln2_g", [D]), ("ln2_b", [D]),
]


def build_program(cfg):
    nc = bass.Bass("TRN2", target_bir_lowering=False)
    OWN, SF, OH, NT = cfg.OWN, cfg.SF, cfg.OH, cfg.NT
    io = {}
    io["xo"] = nc.dram_tensor("xo", [OH, D], F32, kind="ExternalInput").ap()
    io["xf"] = nc.dram_tensor("xf", [SF, D], F32, kind="ExternalInput").ap()
    io["vmask"] = nc.dram_tensor("vmask", [128, OH // 128], F32, kind="ExternalInput").ap()
    io["mtab"] = nc.dram_tensor("mtab", [128, 6, TABW], BF16, kind="ExternalInput").ap()
    io["ckt"] = nc.dram_tensor("ckt", [32, SF], F32, kind="ExternalInput").ap()
    io["skt"] = nc.dram_tensor("skt", [32, SF], F32, kind="ExternalInput").ap()
    io["cqt"] = nc.dram_tensor("cqt", [32, OWN], F32, kind="ExternalInput").ap()
    io["sqt"] = nc.dram_tensor("sqt", [32, OWN], F32, kind="ExternalInput").ap()
    for nm, shp in W_NAMES:
        io[nm] = nc.dram_tensor(nm, [cfg.depth] + shp, F32, kind="ExternalInput").ap()
    io["out"] = nc.dram_tensor("out", [OWN, D], F32, kind="ExternalOutput").ap()
    io["YT"] = nc.dram_tensor("YT", [D, OWN], BF16, kind="Internal").ap()
    io["X1"] = nc.dram_tensor("X1", [OWN, D], F32, kind="Internal").ap()
    if cfg.depth > 1:
        io["hsel"] = nc.dram_tensor("hsel", [128, 8], F32, kind="ExternalInput").ap()
        io["XL"] = nc.dram_tensor("XL", [OWN, D], F32, kind="Internal").ap()
        io["XBc"] = nc.dram_tensor("XBc", [OWN, D], BF16, kind="Internal").ap()
        io["XG"] = nc.dram_tensor("XG", [OWN // 512, (SF // OWN) * 512, D], BF16, kind="Internal").ap()
        io["XH"] = nc.dram_tensor("XH", [2 * HALO, D], BF16, kind="Internal").ap()
    if cfg.dbg:
        io["dbg_yt"] = nc.dram_tensor("dbg_yt", [D, OWN], BF16, kind="ExternalOutput").ap()
        io["dbg_ss"] = nc.dram_tensor("dbg_ss", [128, 13 * NT], F32, kind="ExternalOutput").ap()
        io["dbg_x1"] = nc.dram_tensor("dbg_x1", [OWN, D], F32, kind="ExternalOutput").ap()

    p = Prog(nc)
    DBG["on"] = cfg.dbg
    G = {}
    G["ident"] = p.sb("ident", [128, 128], BF16)
    identf = p.sb("identf", [128, 128], F32)
    G["e65"] = p.sb("e65", [128, 64], F32)
    G["ones"] = p.sb("onesf", [128, 128], F32)
    G["ss"] = p.sb("ss", [128, 13, NT], F32)
    p.op("pool", lambda e: e.memset(identf[:], 0.0), writes=["identf"])
    p.op("pool", lambda e: e.affine_select(out=identf[:], in_=identf[:], compare_op=ALU.not_equal, fill=1.0,
                                           base=0, pattern=[[-1, 128]], channel_multiplier=1),
         reads=["identf"], writes=["identf"])
    p.op("pool", lambda e: e.tensor_copy(G["ident"][:], identf[:]), reads=["identf"], writes=["ident"])
    p.op("pool", lambda e: e.memset(G["e65"][:], 0.0), writes=["e65"])
    p.op("pool", lambda e: e.memset(G["e65"][64:65, :], 1.0), reads=["e65"], writes=["e65"])
    p.op("pool", lambda e: e.memset(G["ones"][:], 1.0), writes=["ones"])
    p.barrier()

    ngo = cfg.NG_OH
    for l in range(cfg.depth):
        W = {nm: io[nm][l] for nm, _ in W_NAMES}
        last = (l == cfg.depth - 1)
        if l == 0:
            src = dict(oh=lambda g: io["xo"][g * 512:(g + 1) * 512, :],
                       full=lambda g: io["xf"][g * 512:(g + 1) * 512, :],
                       res=lambda ti: io["xo"][HALO + ti * 128:HALO + (ti + 1) * 128, :])
        else:
            def oh2(g):
                if g < 2:
                    return io["XH"][g * 512:(g + 1) * 512, :]
                if g >= ngo - 2:
                    return io["XH"][1024 + (g - (ngo - 2)) * 512:1024 + (g - (ngo - 2) + 1) * 512, :]
                return io["XL"][(g - 2) * 512:(g - 1) * 512, :]
            src = dict(oh=oh2, full=lambda g: xg_rows(io, cfg, g * 512, 512),
                       res=lambda ti: io["XL"][ti * 128:(ti + 1) * 128, :])
        if last:
            dst = dict(f32=io["out"], bf16=None, tag="out")
        else:
            dst = dict(f32=io["XL"], bf16=io["XBc"], tag="xl")
        layer(p, cfg, G, io, W, src, dst)
        if not last:
            exchange(p, cfg, G, io)

    if cfg.dbg:
        p.op("sp", lambda e: e.dma_start(out=io["dbg_yt"], in_=io["YT"]), reads=[], dma="out")
        p.op("sp", lambda e: e.dma_start(out=io["dbg_ss"], in_=G["ss"][:].rearrange("p a b -> p (a b)")), dma="out")
        p.op("sp", lambda e: e.dma_start(out=io["dbg_x1"], in_=io["X1"]), dma="out")
    p.wait_all_dma("sp", ["out"])
    stats = p.emit()
    p.close()
    return nc, stats


DBG = {}


def dbg_dump(p, name, ap, shape, dt):
    if not DBG.get("on"):
        return
    t = p.nc.dram_tensor("dd_" + name, list(shape), dt, kind="ExternalOutput").ap()
    p.barrier()
    p.op("sp", lambda e: e.dma_start(out=t, in_=ap), dma="out")
    p.barrier()


def load_x_group(p, src_ap, xb, key, tag):
    p.op("pool", lambda e: e.dma_start(out=xb[:], in_=src_ap.rearrange("(j p) d -> p j d", p=128)),
         writes=[key], dma=tag)


def transpose_group(p, G, xb, xbkey, xT, xTkey, trps, n_tok_tiles=4):
    for kc in range(8):
        tp = trps[kc % 2]
        tkey = "tr%d" % (kc % 2)
        for j in range(n_tok_tiles):
            p.op("pe", lambda e, tp=tp, j=j, kc=kc: e.transpose(tp[:, j * 128:(j + 1) * 128],
                                                                 xb[:, j, kc * 128:(kc + 1) * 128], G["ident"][:]),
                 reads=[xbkey, "ident"], writes=[tkey])
        w = n_tok_tiles * 128
        p.op("act", lambda e, tp=tp, kc=kc, w=w: e.copy(xT[:, kc, 0:w], tp[:, 0:w]), reads=[tkey], writes=[xTkey])


def attn_post(p, G, io, ops, okey, h_row, qb, ss_idx, bufs, it):
    r = it % 2
    osb, rden, ysq, ybf = bufs["osb"][r], bufs["rden"], bufs["ysq"][r], bufs["ybf"][r]
    ko, kr, ks, kb = "osb%d" % r, "rden", "ysq%d" % r, "ybf%d" % r
    p.op("dve", lambda e: e.tensor_copy(osb[0:65, :], ops[0:65, :]), reads=[okey], writes=[ko])

    def deferred():
        mp, mk = bufs["peek"]()
        denps, kd = mp[:, 0:512], mk
        p.op("pe", lambda e: e.matmul(denps[0:64, :], lhsT=G["e65"][0:65, :], rhs=osb[0:65, :], start=True, stop=True),
             reads=[ko, "e65"], writes=[kd])
        p.op("dve", lambda e: e.tensor_copy(rden[0:64, :], denps[0:64, :]), reads=[kd], writes=[kr])
        p.op("dve", lambda e: e.reciprocal(rden[0:64, :], rden[0:64, :]), reads=[kr], writes=[kr])
        p.op("pool", lambda e: e.tensor_tensor(out=osb[0:64, :], in0=osb[0:64, :], in1=rden[0:64, :], op=ALU.mult),
             reads=[ko, kr], writes=[ko])
        p.op("pool", lambda e: e.tensor_copy(ybf[0:64, :], osb[0:64, :]), reads=[ko], writes=[kb])
        p.op("sp", lambda e: e.dma_start(out=io["YT"][h_row:h_row + 64, qb * 512:(qb + 1) * 512], in_=ybf[0:64, :]),
             reads=[kb], dma="yt%d" % r)
        p.op("pool", lambda e: e.tensor_tensor(out=ysq[0:64, :], in0=osb[0:64, :], in1=osb[0:64, :], op=ALU.mult),
             reads=[ko], writes=[ks])

        def stage_b():
            mp2, mk2 = bufs["peek"]()
            ssps = mp2[:, 512:1024]
            for j in range(4):
                p.op("pe", lambda e, j=j: e.matmul(ssps[:, j:j + 1], lhsT=ysq[0:64, j * 128:(j + 1) * 128],
                                                   rhs=G["ones"][0:64, 0:1], start=True, stop=True),
                     reads=[ks, "ones"], writes=[mk2])
            p.op("dve", lambda e: e.tensor_copy(G["ss"][:, ss_idx, qb * 4:(qb + 1) * 4], ssps[:, 0:4]),
                 reads=[mk2], writes=["ss"])
        return stage_b
    return deferred


def xg_rows(io, cfg, R0, n):
    j, q = R0 // cfg.OWN, R0 % cfg.OWN
    i, t = q // 512, q % 512
    assert t + n <= 512
    return io["XG"][i, j * 512 + t:j * 512 + t + n, :]


def exchange(p, cfg, G, io):
    OWN = cfg.OWN
    ngrp = cfg.SF // OWN
    groups = [list(range(b * ngrp, (b + 1) * ngrp)) for b in range(cfg.ncores // ngrp)]
    p.barrier()
    p.push()
    hsel = p.sb("hsel", [128, 8], F32)
    cands = [p.sb("cand%d" % i, [128, ngrp, D], BF16) for i in range(2)]
    hacc = p.sb("hacc", [128, D], F32)
    houts = [p.sb("hout%d" % i, [128, D], BF16) for i in range(2)]
    p.op("sp", lambda e: e.dma_start(out=hsel[:], in_=io["hsel"]), writes=["hsel"], dma="hsel")
    it = 0
    for side in range(2):
        for t in range(HALO // 128):
            cand, ck = cands[it % 2], "cand%d" % (it % 2)
            hout, hk = houts[it % 2], "hout%d" % (it % 2)
            for j in range(ngrp):
                r0 = j * OWN + (OWN - HALO if side == 0 else 0) + t * 128
                p.op("sp", lambda e, j=j, r0=r0: e.dma_start(out=cand[:, j, :], in_=xg_rows(io, cfg, r0, 128)),
                     writes=[ck], dma=ck)
            p.op("dve", lambda e: e.tensor_scalar(hacc[:, :], cand[:, 0, :], hsel[:, side * 4:side * 4 + 1], None, op0=ALU.mult),
                 reads=[ck, "hsel"], writes=["hacc"])
            for j in range(1, ngrp):
                p.op("dve", lambda e, j=j: e.scalar_tensor_tensor(out=hacc[:, :], in0=cand[:, j, :],
                                                                  scalar=hsel[:, side * 4 + j:side * 4 + j + 1], in1=hacc[:, :],
                                                                  op0=ALU.mult, op1=ALU.add), reads=[ck, "hsel", "hacc"], writes=["hacc"])
            p.op("pool", lambda e: e.tensor_copy(hout[:, :], hacc[:, :]), reads=["hacc"], writes=[hk])
            r1 = side * HALO + t * 128
            p.op("sp", lambda e, r1=r1: e.dma_start(out=io["XH"][r1:r1 + 128, :], in_=hout[:, :]), reads=[hk], dma="xh%d" % (it % 2))
            it += 1
    p.pop()


def layer(p, cfg, G, io, W, src, dst):
    OWN, SF, OH, NT, NQB = cfg.OWN, cfg.SF, cfg.OH, cfg.NT, cfg.NQB
    ident = G["ident"]

    p.push()
    CQ = p.sb("CQ", [128, 2, OWN], BF16)
    p.push()
    KaT = p.sb("KaT", [128, 3, OH], BF16)
    QaT = p.sb("QaT", [128, 3, OWN], BF16)
    Va = p.sb("Va", [128, OH // 128, 6, 65], BF16)

    p.push()
    w_in = p.sb("w_in", [128, 8, PIN], BF16)
    wsT = p.sb("wsT", [128, 4, 128], BF16)
    sg_g = p.sb("sg_g", [128, 256], F32)
    sg_b = p.sb("sg_b", [128, 256], F32)
    bsT = p.sb("bsT", [128, 4], F32)
    vm = p.sb("vm", [128, OH // 128], F32)
    xbs = [p.sb("xb%d" % i, [128, 4, D], BF16) for i in range(2)]
    xTs = [p.sb("xT%d" % i, [128, 8, 512], BF16) for i in range(2)]
    zh = p.sb("zh", [128, 512], F32)
    w1 = p.sb("w1", [128, 512], F32)
    w2 = p.sb("w2", [128, 512], F32)
    zz = p.sb("zz", [128, 512], F32)
    vn = p.sb("vn", [128, 256], F32)
    vnb = p.sb("vnb", [128, 256], BF16)
    yc = p.sb("yc", [128, 256], F32)
    ycb = p.sb("ycb", [128, 256], BF16)
    ycT = p.sb("ycT", [128, 2, 512], BF16)
    junk = p.sb("junk", [128, 256], F32)
    st6 = p.sb("st6", [128, 6], F32)
    mv = p.sb("mv", [128, 2], F32)
    rs = p.sb("rs", [128, 1], F32)
    cqs = [p.sb("cqs%d" % i, [128, 512], F32) for i in range(2)]
    cqq = [p.sb("cqq%d" % i, [128, 512], F32) for i in range(2)]
    cqr = p.sb("cqr", [128, 512], F32)
    trps = [p.ps("trps%d" % i, [128, 1024], BF16) for i in range(2)]
    pps = [p.ps("pps%d" % i, [128, 512]) for i in range(3)]
    mixps = p.ps("mixps", [128, 512])
    ssq = p.ps("ssq", [128, 512])

    for kc in range(8):
        p.op("pool", lambda e, kc=kc: e.dma_start(out=w_in[:, kc, :], in_=W["w_in"][kc * 128:(kc + 1) * 128, :]),
             writes=["w_in"], dma="w_in")
    p.op("pool", lambda e: e.dma_start(out=wsT[:], in_=W["sg_wT"].rearrange("g s t -> s g t")), writes=["wsT"], dma="wsm")
    p.op("sp", lambda e: e.dma_start(out=sg_g[:], in_=W["sg_g"].partition_broadcast(128)), writes=["sg_g"], dma="wsm2")
    p.op("sp", lambda e: e.dma_start(out=sg_b[:], in_=W["sg_b"].partition_broadcast(128)), writes=["sg_b"], dma="wsm2")
    p.op("sp", lambda e: e.dma_start(out=bsT[:], in_=W["sg_bT"]), writes=["bsT"], dma="wsm2")
    p.op("sp", lambda e: e.dma_start(out=vm[:], in_=io["vmask"]), writes=["vm"], dma="wsm2")
    for h in range(6):
        p.op("dve", lambda e, h=h: e.tensor_copy(Va[:, :, h, 64:65], vm[:].rearrange("p (t o) -> p t o", o=1)),
             reads=["vm"], writes=["Va"])

    pp_i = [0]

    def next_pps():
        i = pp_i[0] % 3
        pp_i[0] += 1
        return pps[i], "pps%d" % i

    ngo = cfg.NG_OH
    load_x_group(p, src["oh"](0), xbs[0], "xb0", "xb0")
    for g in range(ngo):
        xb, xbk = xbs[g % 2], "xb%d" % (g % 2)
        xT, xTk = xTs[g % 2], "xT%d" % (g % 2)
        if g + 1 < ngo:
            load_x_group(p, src["oh"](g + 1), xbs[(g + 1) % 2], "xb%d" % ((g + 1) % 2), "xb%d" % ((g + 1) % 2))
        transpose_group(p, G, xb, xbk, xT, xTk, trps)
        own = (g * 512 >= HALO) and (g * 512 < HALO + OWN)
        go = g - HALO // 512
        for c in range(3):
            ps_, pk = next_pps()
            for kc in range(8):
                p.op("pe", lambda e, ps_=ps_, kc=kc, c=c: e.matmul(ps_[:, :], lhsT=w_in[:, kc, 384 + c * 128:384 + (c + 1) * 128],
                                                                   rhs=xT[:, kc, :], start=(kc == 0), stop=(kc == 7)),
                     reads=["w_in", xTk], writes=[pk])
            p.op("dve", lambda e, ps_=ps_, c=c, g=g: e.tensor_copy(KaT[:, c, g * 512:(g + 1) * 512], ps_[:, :]),
                 reads=[pk], writes=["KaT"])
        if own:
            for c in range(3):
                ps_, pk = next_pps()
                for kc in range(8):
                    p.op("pe", lambda e, ps_=ps_, kc=kc, c=c: e.matmul(ps_[:, :], lhsT=w_in[:, kc, c * 128:(c + 1) * 128],
                                                                       rhs=xT[:, kc, :], start=(kc == 0), stop=(kc == 7)),
                         reads=["w_in", xTk], writes=[pk])
                p.op("dve", lambda e, ps_=ps_, c=c, go=go: e.tensor_copy(QaT[:, c, go * 512:(go + 1) * 512], ps_[:, :]),
                     reads=[pk], writes=["QaT"])
        for j in range(4):
            ps_, pk = next_pps()
            for kc in range(8):
                p.op("pe", lambda e, ps_=ps_, kc=kc, j=j: e.matmul(ps_[:, 0:384], lhsT=xT[:, kc, j * 128:(j + 1) * 128],
                                                                   rhs=w_in[:, kc, 768:1152], start=(kc == 0), stop=(kc == 7)),
                     reads=["w_in", xTk], writes=[pk])
            p.op("dve", lambda e, ps_=ps_, j=j, g=g: e.tensor_copy(Va[:, g * 4 + j, :, 0:64],
                                                                   ps_[:, 0:384].rearrange("p (h c) -> p h c", h=6)),
                 reads=[pk], writes=["Va"])
        if not own:
            continue
        psa, pka = next_pps()
        psb, pkb = next_pps()
        for kc in range(8):
            p.op("pe", lambda e, kc=kc: e.matmul(psa[:, :], lhsT=w_in[:, kc, 1152:1280], rhs=xT[:, kc, :],
                                                 start=(kc == 0), stop=(kc == 7)), reads=["w_in", xTk], writes=[pka])
        for kc in range(8):
            p.op("pe", lambda e, kc=kc: e.matmul(psb[0:64, :], lhsT=w_in[:, kc, 1280:1344], rhs=xT[:, kc, :],
                                                 start=(kc == 0), stop=(kc == 7)), reads=["w_in", xTk], writes=[pkb])
        p.op("dve", lambda e: e.tensor_copy(cqs[0][:, :], psa[:, :]), reads=[pka], writes=["cqs0"])
        p.op("dve", lambda e: e.tensor_copy(cqs[1][0:64, :], psb[0:64, :]), reads=[pkb], writes=["cqs1"])
        p.op("pool", lambda e: e.tensor_tensor(out=cqq[0][:, :], in0=cqs[0][:, :], in1=cqs[0][:, :], op=ALU.mult),
             reads=["cqs0"], writes=["cqq0"])
        p.op("pool", lambda e: e.tensor_tensor(out=cqq[1][0:64, :], in0=cqs[1][0:64, :], in1=cqs[1][0:64, :], op=ALU.mult),
             reads=["cqs1"], writes=["cqq1"])
        p.op("pe", lambda e: e.matmul(ssq[:, :], lhsT=G["ones"][:, :], rhs=cqq[0][:, :], start=True, stop=False),
             reads=["cqq0", "ones"], writes=["ssq"])
        p.op("pe", lambda e: e.matmul(ssq[:, :], lhsT=G["ones"][0:64, :], rhs=cqq[1][0:64, :], start=False, stop=True),
             reads=["cqq1", "ones"], writes=["ssq"])
        p.op("act", lambda e: e.activation(out=cqr[:, :], in_=ssq[:, :], func=AF.Sqrt, scale=1.0 / 192, bias=EPS),
             reads=["ssq"], writes=["cqr"])
        p.op("dve", lambda e: e.reciprocal(cqr[:, :], cqr[:, :]), reads=["cqr"], writes=["cqr"])
        p.op("dve", lambda e, go=go: e.tensor_tensor(out=CQ[:, 0, go * 512:(go + 1) * 512], in0=cqs[0][:, :], in1=cqr[:, :],
                                                     op=ALU.mult), reads=["cqs0", "cqr"], writes=["CQ"])
        p.op("dve", lambda e, go=go: e.tensor_tensor(out=CQ[0:64, 1, go * 512:(go + 1) * 512], in0=cqs[1][0:64, :],
                                                     in1=cqr[0:64, :], op=ALU.mult), reads=["cqs1", "cqr"], writes=["CQ"])
        for j in range(4):
            ti = go * 4 + j
            ps_, pk = next_pps()
            for kc in range(8):
                p.op("pe", lambda e, ps_=ps_, kc=kc, j=j: e.matmul(ps_[:, :], lhsT=xT[:, kc, j * 128:(j + 1) * 128],
                                                                   rhs=w_in[:, kc, 1504:2016], start=(kc == 0), stop=(kc == 7)),
                     reads=["w_in", xTk], writes=[pk])
            p.op("act", lambda e, ps_=ps_: e.activation(out=zh[:, :], in_=ps_[:, :], func=AF.Copy, scale=0.5),
                 reads=[pk], writes=["zh"])
            p.op("pool", lambda e: e.tensor_tensor(out=w1[:, :], in0=zh[:, :], in1=zh[:, :], op=ALU.mult),
                 reads=["zh"], writes=["w1"])
            p.op("dve", lambda e: e.tensor_scalar(w1[:, :], w1[:, :], 4.0 * C_GELU, 1.0, op0=ALU.mult, op1=ALU.add),
                 reads=["w1"], writes=["w1"])
            p.op("pool", lambda e: e.tensor_tensor(out=w2[:, :], in0=w1[:, :], in1=zh[:, :], op=ALU.mult),
                 reads=["w1", "zh"], writes=["w2"])
            p.op("act", lambda e: e.activation(out=w2[:, :], in_=w2[:, :], func=AF.Tanh, scale=2.0 * K_GELU),
                 reads=["w2"], writes=["w2"])
            p.op("dve", lambda e: e.scalar_tensor_tensor(out=zz[:, :], in0=w2[:, :], scalar=1.0, in1=zh[:, :],
                                                         op0=ALU.add, op1=ALU.mult), reads=["w2", "zh"], writes=["zz"])
            p.op("dve", lambda e: e.bn_stats(st6[:, :], zz[:, 256:512]), reads=["zz"], writes=["st6"])
            p.op("dve", lambda e: e.bn_aggr(mv[:, :], st6[:, :]), reads=["st6"], writes=["mv"])
            p.op("act", lambda e: e.activation(out=rs[:, :], in_=mv[:, 1:2], func=AF.Sqrt, scale=1.0, bias=EPS),
                 reads=["mv"], writes=["rs"])
            p.op("dve", lambda e: e.reciprocal(rs[:, :], rs[:, :]), reads=["rs"], writes=["rs"])
            p.op("dve", lambda e: e.tensor_scalar(vn[:, :], zz[:, 256:512], mv[:, 0:1], rs[:, 0:1], op0=ALU.subtract,
                                                  op1=ALU.mult), reads=["zz", "mv", "rs"], writes=["vn"])
            p.op("pool", lambda e: e.tensor_tensor(out=vn[:, :], in0=vn[:, :], in1=sg_g[:, :], op=ALU.mult),
                 reads=["vn", "sg_g"], writes=["vn"])
            p.op("pool", lambda e: e.tensor_tensor(out=vnb[:, :], in0=vn[:, :], in1=sg_b[:, :], op=ALU.add),
                 reads=["vn", "sg_b"], writes=["vnb"])
            for gg in range(4):
                p.op("pe", lambda e, gg=gg: e.matmul(mixps[:, gg * 64:(gg + 1) * 64], lhsT=wsT[:, gg, :],
                                                     rhs=vnb[:, gg * 64:(gg + 1) * 64], start=True, stop=True),
                     reads=["wsT", "vnb"], writes=["mixps"])
            for gg in range(4):
                p.op("dve", lambda e, gg=gg: e.scalar_tensor_tensor(out=yc[:, gg * 64:(gg + 1) * 64],
                                                                    in0=mixps[:, gg * 64:(gg + 1) * 64], scalar=bsT[:, gg:gg + 1],
                                                                    in1=zz[:, gg * 64:(gg + 1) * 64], op0=ALU.add, op1=ALU.mult),
                     reads=["mixps", "bsT", "zz"], writes=["yc"])
            p.op("pool", lambda e: e.tensor_tensor(out=junk[:, :], in0=yc[:, :], in1=yc[:, :], op=ALU.mult),
                 reads=["yc"], writes=["junk"])
            p.op("dve", lambda e, ti=ti: e.reduce_sum(out=G["ss"][:, 12, ti:ti + 1], in_=junk[:, :], axis=AX.X),
                 reads=["junk"], writes=["ss"])
            p.op("pool", lambda e: e.tensor_copy(ycb[:, :], yc[:, :]), reads=["yc"], writes=["ycb"])
            for c2 in range(2):
                tp = trps[c2]
                p.op("pe", lambda e, tp=tp, c2=c2: e.transpose(tp[:, 512:640], ycb[:, c2 * 128:(c2 + 1) * 128], ident[:]),
                     reads=["ycb", "ident"], writes=["tr%d" % c2])
                p.op("act", lambda e, tp=tp, c2=c2, j=j: e.copy(ycT[:, c2, j * 128:(j + 1) * 128], tp[:, 512:640]),
                     reads=["tr%d" % c2], writes=["ycT"])
        p.op("sp", lambda e, go=go: e.dma_start(out=io["YT"][768:1024, go * 512:(go + 1) * 512].rearrange("(c p) t -> p c t", p=128),
                                                in_=ycT[:, :, :]), reads=["ycT"], dma="ytc")
    dbg_dump(p, "xT", xTs[(ngo - 1) % 2][:, :, :], [128, 8, 512], BF16)
    dbg_dump(p, "KaT", KaT[:, :, :], [128, 3, OH], BF16)
    dbg_dump(p, "QaT", QaT[:, :, :], [128, 3, OWN], BF16)
    dbg_dump(p, "Va", Va[:, :, :, :], [128, OH // 128, 6, 65], BF16)
    dbg_dump(p, "CQ", CQ[:, :, :], [128, 2, OWN], BF16)
    dbg_dump(p, "w_in", w_in[:, :, :], [128, 8, PIN], BF16)
    p.pop()

    p.push()
    tab = p.sb("tab", [128, 6, TABW], BF16)
    NSA = 3
    Es = [p.sb("E%d" % i, [128, 1024], BF16) for i in range(NSA)]
    PTs = [p.sb("PT%d" % i, [128, 1024], BF16) for i in range(NSA)]
    Sps = [p.ps("S%d" % i, [128, 1024]) for i in range(NSA)]
    Ops = [p.ps("O%d" % i, [128, 512]) for i in range(2)]
    gic = [0]

    def alloc_s():
        i = gic[0] % NSA
        gic[0] += 1
        return Sps[i], "S%d" % i

    def peek_s():
        i = gic[0] % NSA
        return Sps[i], "S%d" % i
    bufs = dict(peek=peek_s, osb=[p.sb("osb%d" % i, [128, 512], F32) for i in range(2)], rden=p.sb("rden", [128, 512], F32),
                ysq=[p.sb("ysq%d" % i, [128, 512], F32) for i in range(2)], ybf=[p.sb("ybf%d" % i, [128, 512], BF16) for i in range(2)],
                alloc=alloc_s)
    for h in range(6):
        p.op("sp", lambda e, h=h: e.dma_start(out=tab[:, h, :], in_=io["mtab"][:, h, :]), writes=["tab"], dma="tab")
    it = 0
    pend = []
    pendb = []
    NKT2 = 20
    for c in range(3):
        hh = (2 * c, 2 * c + 1)
        for qb in range(NQB):
            kt0 = 4 * qb

            def qk(ii):
                S, sk = alloc_s()
                kt = kt0 + ii
                for u in range(2):
                    p.op("pe", lambda e, S=S, u=u, kt=kt: e.matmul(S[:, u * 512:(u + 1) * 512],
                                                                   lhsT=KaT[u * 64:u * 64 + 64, c, kt * 128:(kt + 1) * 128],
                                                                   rhs=QaT[u * 64:u * 64 + 64, c, qb * 512:(qb + 1) * 512],
                                                                   start=True, stop=True),
                         reads=["KaT", "QaT"], writes=[sk])
                return S, sk

            inflight = [qk(0), qk(1)]
            for ii in range(NKT2):
                if ii + 2 < NKT2:
                    inflight.append(qk(ii + 2))
                S, sk = inflight.pop(0)
                E, ek = Es[ii % NSA], "E%d" % (ii % NSA)
                PT, pk = PTs[ii % NSA], "PT%d" % (ii % NSA)
                kt = kt0 + ii
                start = 2432 - 128 * ii
                p.op("act", lambda e, S=S, E=E: e.activation(out=E[:, :], in_=S[:, :], func=AF.Exp, scale=A_SCALE),
                     reads=[sk], writes=[ek])
                for u in range(2):
                    p.op("dve", lambda e, E=E, PT=PT, u=u, start=start: e.tensor_tensor(
                        out=PT[:, u * 512:(u + 1) * 512], in0=E[:, u * 512:(u + 1) * 512],
                        in1=tab[:, hh[u], start:start + 512], op=ALU.mult), reads=[ek, "tab"], writes=[pk + "_%d" % u])
                for u in range(2):
                    p.op("pe", lambda e, PT=PT, u=u, kt=kt, ii=ii: e.matmul(
                        Ops[u][0:65, :], lhsT=Va[:, kt, hh[u], 0:65], rhs=PT[:, u * 512:(u + 1) * 512],
                        start=(ii == 0), stop=(ii == NKT2 - 1)),
                        reads=["Va", pk + "_%d" % u], writes=["O%d" % u])
                if ii == 3:
                    while pend:
                        pendb.append(pend.pop(0)())
                if ii == 9:
                    while pendb:
                        pendb.pop(0)()
            for u in range(2):
                pend.append(attn_post(p, G, io, Ops[u], "O%d" % u, hh[u] * 64, qb, hh[u], bufs, u))
            it += 1
    while pend:
        pendb.append(pend.pop(0)())
    while pendb:
        pendb.pop(0)()
    p.pop()
    p.pop()

    CKV = p.sb("CKV", [128, SF], BF16)
    KT = p.sb("KT", [128, SF], BF16)
    p.push()
    wB = p.sb("wB", [128, 8, 160], BF16)
    wk96 = p.sb("wk96", [128, 8, 96], BF16)
    wk96r = p.sb("wk96r", [128, 8, 96], BF16)
    xbs = [p.sb("bxb%d" % i, [128, 4, D], BF16) for i in range(2)]
    xTs = [p.sb("bxT%d" % i, [128, 8, 512], BF16) for i in range(2)]
    sqs = [p.sb("bsq%d" % i, [128, 512], F32) for i in range(2)]
    rrs = [p.sb("brr%d" % i, [128, 512], F32) for i in range(2)]
    cks = [p.sb("cks%d" % i, [128, 512], F32) for i in range(2)]
    sks = [p.sb("sks%d" % i, [128, 512], F32) for i in range(2)]
    t1s = [p.sb("bt1_%d" % i, [128, 512], F32) for i in range(2)]
    t2s_ = [p.sb("bt2_%d" % i, [128, 512], F32) for i in range(2)]
    trps = [p.ps("btrps%d" % i, [128, 1024], BF16) for i in range(2)]
    pps = [p.ps("bpps%d" % i, [128, 512]) for i in range(4)]
    ssqs = [p.ps("bssq%d" % i, [128, 512]) for i in range(2)]
    for kc in range(8):
        p.op("pool", lambda e, kc=kc: e.dma_start(out=wB[:, kc, :], in_=W["w_in"][kc * 128:(kc + 1) * 128, 1344:1504]),
             writes=["wB"], dma="wB")
    p.op("dve", lambda e: e.memset(wk96[:], 0.0), writes=["wk96"])
    p.op("dve", lambda e: e.memset(wk96r[:], 0.0), writes=["wk96r"])
    p.op("dve", lambda e: e.tensor_copy(wk96[:, :, 64:96], wB[:, :, 128:160]), reads=["wB", "wk96"], writes=["wk96"])
    p.op("dve", lambda e: e.tensor_copy(wk96r[:, :, 64:80], wB[:, :, 144:160]), reads=["wB", "wk96r"], writes=["wk96r"])
    p.op("dve", lambda e: e.tensor_copy(wk96r[:, :, 80:96], wB[:, :, 128:144]), reads=["wB", "wk96r"], writes=["wk96r"])
    ngf = cfg.NG_F
    load_x_group(p, src["full"](0), xbs[0], "bxb0", "bxb0")
    for g in range(ngf):
        xb, xbk = xbs[g % 2], "bxb%d" % (g % 2)
        xT, xTk = xTs[g % 2], "bxT%d" % (g % 2)
        ck, ckk = cks[g % 2], "cks%d" % (g % 2)
        sk_, skk = sks[g % 2], "sks%d" % (g % 2)
        if g + 1 < ngf:
            load_x_group(p, src["full"](g + 1), xbs[(g + 1) % 2], "bxb%d" % ((g + 1) % 2), "bxb%d" % ((g + 1) % 2))
        p.op("sp", lambda e, ck=ck, g=g: e.dma_start(out=ck[64:96, :], in_=io["ckt"][:, g * 512:(g + 1) * 512]),
             writes=[ckk], dma=ckk)
        p.op("sp", lambda e, sk_=sk_, g=g: e.dma_start(out=sk_[64:96, :], in_=io["skt"][:, g * 512:(g + 1) * 512]),
             writes=[skk], dma=skk)
        transpose_group(p, G, xb, xbk, xT, xTk, trps)
        sq, sqk = sqs[g % 2], "bsq%d" % (g % 2)
        rr, rrk = rrs[g % 2], "brr%d" % (g % 2)
        t1, t1k = t1s[g % 2], "bt1_%d" % (g % 2)
        t2, t2k = t2s_[g % 2], "bt2_%d" % (g % 2)
        ssq, ssqk = ssqs[g % 2], "bssq%d" % (g % 2)
        pa, pb, pc = pps[(3 * g) % 4], pps[(3 * g + 1) % 4], pps[(3 * g + 2) % 4]
        ka, kb, kc_ = "bpps%d" % ((3 * g) % 4), "bpps%d" % ((3 * g + 1) % 4), "bpps%d" % ((3 * g + 2) % 4)
        for kc in range(8):
            p.op("pe", lambda e, kc=kc, pa=pa: e.matmul(pa[:, :], lhsT=wB[:, kc, 0:128], rhs=xT[:, kc, :],
                                                        start=(kc == 0), stop=(kc == 7)), reads=["wB", xTk], writes=[ka])
        for kc in range(8):
            p.op("pe", lambda e, kc=kc, pb=pb: e.matmul(pb[0:96, :], lhsT=wk96[:, kc, :], rhs=xT[:, kc, :],
                                                        start=(kc == 0), stop=(kc == 7)), reads=["wk96", xTk], writes=[kb])
        for kc in range(8):
            p.op("pe", lambda e, kc=kc, pc=pc: e.matmul(pc[0:96, :], lhsT=wk96r[:, kc, :], rhs=xT[:, kc, :],
                                                        start=(kc == 0), stop=(kc == 7)), reads=["wk96r", xTk], writes=[kc_])
        p.op("act", lambda e, pa=pa: e.activation(out=sq[:, :], in_=pa[:, :], func=AF.Square), reads=[ka], writes=[sqk])
        p.op("pe", lambda e: e.matmul(ssq[:, :], lhsT=G["ones"][:, :], rhs=sq[:, :], start=True, stop=True),
             reads=[sqk, "ones"], writes=[ssqk])
        p.op("act", lambda e: e.activation(out=rr[:, :], in_=ssq[:, :], func=AF.Sqrt, scale=1.0 / 128, bias=EPS),
             reads=[ssqk], writes=[rrk])
        p.op("dve", lambda e: e.reciprocal(rr[:, :], rr[:, :]), reads=[rrk], writes=[rrk])
        p.op("dve", lambda e, pa=pa, g=g: e.tensor_tensor(out=CKV[:, g * 512:(g + 1) * 512], in0=pa[:, :], in1=rr[:, :],
                                                          op=ALU.mult), reads=[ka, rrk], writes=["CKV"])
        p.op("dve", lambda e, pb=pb, ck=ck: e.tensor_tensor(out=t1[64:96, :], in0=pb[64:96, :], in1=ck[64:96, :], op=ALU.mult),
             reads=[kb, ckk], writes=[t1k])
        p.op("dve", lambda e, pc=pc, sk_=sk_: e.tensor_tensor(out=t2[64:96, :], in0=pc[64:96, :], in1=sk_[64:96, :], op=ALU.mult),
             reads=[kc_, skk], writes=[t2k])
        p.op("pool", lambda e, g=g: e.tensor_tensor(out=KT[64:96, g * 512:(g + 1) * 512], in0=t1[64:96, :], in1=t2[64:96, :],
                                                    op=ALU.add), reads=[t1k, t2k], writes=["KT"])
    p.pop()

    p.push()
    Vh = p.sb("Vh", [128, SF // 128, 65], BF16)
    QT = p.sb("QT", [128, OWN], BF16)
    cqt = p.sb("cqt", [128, OWN], F32)
    sqt = p.sb("sqt", [128, OWN], F32)
    wkv = p.sb("wkv", [128, 768], BF16)
    wkvf = p.sb("wkvf", [128, 768], F32)
    wqf = p.sb("wqf", [128, 2, 576], F32)
    wq = p.sb("wq", [128, 2, 6, 96], BF16)
    wqr = p.sb("wqr", [128, 2, 6, 96], BF16)
    qn = p.sb("qn", [128, 2], F32)
    kvn = p.sb("kvn", [128, 1], F32)
    q1 = p.sb("q1", [128, 512], F32)
    q2 = p.sb("q2", [128, 512], F32)
    NSB = 3
    PTs = [p.sb("bPT%d" % i, [128, 1024], BF16) for i in range(NSB)]
    Sps = [p.ps("bS%d" % i, [128, 1024]) for i in range(NSB)]
    Ops = [p.ps("bO%d" % i, [128, 512]) for i in range(2)]
    pend = []
    pendb = []
    gib = [0]

    def alloc_sb():
        i = gib[0] % NSB
        gib[0] += 1
        return Sps[i], "bS%d" % i

    def peek_sb():
        i = gib[0] % NSB
        return Sps[i], "bS%d" % i
    bufs = dict(peek=peek_sb, osb=[p.sb("bosb%d" % i, [128, 512], F32) for i in range(2)], rden=p.sb("brden", [128, 512], F32),
                ysq=[p.sb("bysq%d" % i, [128, 512], F32) for i in range(2)], ybf=[p.sb("bybf%d" % i, [128, 512], BF16) for i in range(2)],
                alloc=alloc_sb)
    p.op("sp", lambda e: e.dma_start(out=wkvf[:], in_=W["w_kv_up"]), writes=["wkvf"], dma="bw")
    p.op("sp", lambda e: e.dma_start(out=wqf[:, 0, :], in_=W["w_q_up"][0:128, :]), writes=["wqf"], dma="bw")
    p.op("sp", lambda e: e.dma_start(out=wqf[0:64, 1, :], in_=W["w_q_up"][128:192, :]), writes=["wqf"], dma="bw")
    p.op("sp", lambda e: e.dma_start(out=qn[:], in_=W["qn"]), writes=["qn"], dma="bw")
    p.op("sp", lambda e: e.dma_start(out=kvn[:], in_=W["kvn"]), writes=["kvn"], dma="bw")
    p.op("sp", lambda e: e.dma_start(out=cqt[64:96, :], in_=io["cqt"]), writes=["cqt"], dma="bw")
    p.op("sp", lambda e: e.dma_start(out=sqt[64:96, :], in_=io["sqt"]), writes=["sqt"], dma="bw")
    p.op("dve", lambda e: e.tensor_scalar(wkv[:, :], wkvf[:, :], kvn[:, 0:1], None, op0=ALU.mult), reads=["wkvf", "kvn"],
         writes=["wkv"])
    p.op("dve", lambda e: e.memset(wq[:], 0.0), writes=["wq"])
    p.op("dve", lambda e: e.memset(wqr[:], 0.0), writes=["wqr"])
    for cc, np_ in ((0, 128), (1, 64)):
        wv = wqf[0:np_, cc, :].rearrange("p (h c) -> p h c", h=6)
        p.op("dve", lambda e, cc=cc, np_=np_, wv=wv: e.tensor_scalar(wq[0:np_, cc, :, :], wv, qn[0:np_, cc:cc + 1], None,
                                                                      op0=ALU.mult), reads=["wqf", "qn", "wq"], writes=["wq"])
        p.op("dve", lambda e, cc=cc, np_=np_, wv=wv: e.tensor_scalar(wqr[0:np_, cc, :, 64:80], wv[:, :, 80:96],
                                                                      qn[0:np_, cc:cc + 1], None, op0=ALU.mult),
             reads=["wqf", "qn", "wqr"], writes=["wqr"])
        p.op("dve", lambda e, cc=cc, np_=np_, wv=wv: e.tensor_scalar(wqr[0:np_, cc, :, 80:96], wv[:, :, 64:80],
                                                                      qn[0:np_, cc:cc + 1], None, op0=ALU.mult),
             reads=["wqf", "qn", "wqr"], writes=["wqr"])
    p.op("pool", lambda e: e.memset(Vh[:, :, 64:65], 1.0), writes=["Vh"])
    it = 0
    gi = 0
    nkt = SF // 128
    for h in range(6):
        for g2 in range(SF // 1024):
            S, sk = alloc_sb()
            for u in range(2):
                g = 2 * g2 + u
                p.op("pe", lambda e, S=S, u=u, g=g, h=h: e.matmul(S[0:64, u * 512:(u + 1) * 512], lhsT=wkv[:, h * 128:h * 128 + 64],
                                                                  rhs=CKV[:, g * 512:(g + 1) * 512], start=True, stop=True),
                     reads=["wkv", "CKV"], writes=[sk])
            p.op("dve", lambda e, S=S, g2=g2: e.tensor_copy(KT[0:64, g2 * 1024:(g2 + 1) * 1024], S[0:64, :]),
                 reads=[sk], writes=["KT"])
        for t16 in range(nkt // 16):
            S, sk = alloc_sb()
            for u in range(16):
                t = t16 * 16 + u
                p.op("pe", lambda e, S=S, u=u, t=t, h=h: e.matmul(S[:, u * 64:(u + 1) * 64], lhsT=CKV[:, t * 128:(t + 1) * 128],
                                                                  rhs=wkv[:, h * 128 + 64:h * 128 + 128], start=True, stop=True),
                     reads=["wkv", "CKV"], writes=[sk])
            p.op("dve", lambda e, S=S, t16=t16: e.tensor_copy(Vh[:, t16 * 16:(t16 + 1) * 16, 0:64],
                                                               S[:, :].rearrange("p (t c) -> p t c", c=64)),
                 reads=[sk], writes=["Vh"])
        for qb in range(NQB):
            S, sk = alloc_sb()
            for u, wsrc in ((0, wq), (1, wqr)):
                p.op("pe", lambda e, S=S, u=u, wsrc=wsrc, h=h, qb=qb: e.matmul(S[0:96, u * 512:(u + 1) * 512], lhsT=wsrc[:, 0, h, :],
                                                                               rhs=CQ[:, 0, qb * 512:(qb + 1) * 512],
                                                                               start=True, stop=False),
                     reads=["wq", "wqr", "CQ"], writes=[sk])
                p.op("pe", lambda e, S=S, u=u, wsrc=wsrc, h=h, qb=qb: e.matmul(S[0:96, u * 512:(u + 1) * 512], lhsT=wsrc[0:64, 1, h, :],
                                                                               rhs=CQ[0:64, 1, qb * 512:(qb + 1) * 512],
                                                                               start=False, stop=True),
                     reads=["wq", "wqr", "CQ"], writes=[sk])
            p.op("dve", lambda e, S=S, qb=qb: e.tensor_copy(QT[0:64, qb * 512:(qb + 1) * 512], S[0:64, 0:512]),
                 reads=[sk], writes=["QT"])
            p.op("dve", lambda e, S=S, qb=qb: e.tensor_tensor(out=q1[64:96, :], in0=S[64:96, 0:512],
                                                              in1=cqt[64:96, qb * 512:(qb + 1) * 512], op=ALU.mult),
                 reads=[sk, "cqt"], writes=["q1"])
            p.op("dve", lambda e, S=S, qb=qb: e.tensor_tensor(out=q2[64:96, :], in0=S[64:96, 512:1024],
                                                              in1=sqt[64:96, qb * 512:(qb + 1) * 512], op=ALU.mult),
                 reads=[sk, "sqt"], writes=["q2"])
            p.op("pool", lambda e, qb=qb: e.tensor_tensor(out=QT[64:96, qb * 512:(qb + 1) * 512], in0=q1[64:96, :],
                                                          in1=q2[64:96, :], op=ALU.add), reads=["q1", "q2"], writes=["QT"])
        for qb in range(NQB):
            ops, okey = Ops[it % 2], "bO%d" % (it % 2)
            ngrp = nkt // 2

            def qk(i):
                S, sk = alloc_sb()
                for u in range(2):
                    kt = 2 * i + u
                    p.op("pe", lambda e, S=S, u=u, kt=kt: e.matmul(S[:, u * 512:(u + 1) * 512],
                                                                   lhsT=KT[0:96, kt * 128:(kt + 1) * 128],
                                                                   rhs=QT[0:96, qb * 512:(qb + 1) * 512], start=True, stop=True),
                         reads=["KT", "QT"], writes=[sk])
                return S, sk

            inflight = [qk(0), qk(1)]
            for i in range(ngrp):
                if i + 2 < ngrp:
                    inflight.append(qk(i + 2))
                S, sk = inflight.pop(0)
                PT, pk = PTs[i % NSB], "bPT%d" % (i % NSB)
                p.op("act", lambda e, S=S, PT=PT: e.activation(out=PT[:, :], in_=S[:, :], func=AF.Exp, scale=B_SCALE),
                     reads=[sk], writes=[pk])
                for u in range(2):
                    kt = 2 * i + u
                    p.op("pe", lambda e, PT=PT, u=u, kt=kt, ops=ops, i=i: e.matmul(
                        ops[0:65, :], lhsT=Vh[:, kt, 0:65], rhs=PT[:, u * 512:(u + 1) * 512],
                        start=(i == 0 and u == 0), stop=(i == ngrp - 1 and u == 1)),
                        reads=["Vh", pk], writes=[okey])
                if i == 3 and pend:
                    pendb.append(pend.pop()())
                if i == 9 and pendb:
                    pendb.pop()()
            pend.append(attn_post(p, G, io, ops, okey, 384 + h * 64, qb, 6 + h, bufs, it))
            it += 1
        while pend:
            pendb.append(pend.pop()())
        while pendb:
            pendb.pop()()
    p.pop()
    p.pop()

    p.push()
    wf1 = p.sb("wf1", [128, 8, DFF], BF16)
    wf2 = p.sb("wf2", [128, 32, D], BF16)
    p.push()
    wo = p.sb("wo", [128, 8, D], BF16)
    wof = p.sb("wof", [128, D], F32)
    mixn = p.sb("mixn", [128, 8], F32)
    g1 = p.sb("g1", [128, D], F32)
    b1 = p.sb("b1", [128, D], F32)
    yTs = [p.sb("yT%d" % i, [128, 8, 512], BF16) for i in range(2)]
    xrs = [p.sb("xr%d" % i, [128, D], F32) for i in range(2)]
    accs = [p.sb("dacc%d" % i, [128, D], F32) for i in range(2)]
    x1s = [p.sb("x1s%d" % i, [128, D], F32) for i in range(2)]
    rst = p.sb("rst", [128, 3, NT], F32)
    sst = p.sb("sst", [128, 3, NT], F32)
    st12s = [p.sb("st12_%d" % i, [128, 12], F32) for i in range(2)]
    mvs = [p.sb("dmv%d" % i, [128, 2], F32) for i in range(2)]
    rss = [p.sb("drs%d" % i, [128, 1], F32) for i in range(2)]
    accps = [p.ps("accps%d" % i, [128, 1024]) for i in range(2)]
    p.op("sp", lambda e: e.dma_start(out=mixn[:], in_=W["mixn"]), writes=["mixn"], dma="dw")
    p.op("sp", lambda e: e.dma_start(out=g1[:], in_=W["ln1_g"].partition_broadcast(128)), writes=["g1"], dma="dw")
    p.op("sp", lambda e: e.dma_start(out=b1[:], in_=W["ln1_b"].partition_broadcast(128)), writes=["b1"], dma="dw")
    for kc in range(8):
        p.op("sp", lambda e, kc=kc: e.dma_start(out=wof[:], in_=W["w_out"][kc * 128:(kc + 1) * 128, :]), writes=["wof"], dma="dwo")
        p.op("dve", lambda e, kc=kc: e.tensor_scalar(wo[:, kc, :], wof[:, :], mixn[:, kc:kc + 1], None, op0=ALU.mult),
             reads=["wof", "mixn"], writes=["wo"])
    for kc in range(8):
        p.op("pool", lambda e, kc=kc: e.dma_start(out=wf1[:, kc, :], in_=W["w_ff1"][kc * 128:(kc + 1) * 128, :]),
             writes=["wf1"], dma="wf1")
    for c4 in range(8):
        p.op("pool", lambda e, c4=c4: e.dma_start(out=wf2[:, c4 * 4:(c4 + 1) * 4, :],
                                                  in_=W["w_ff2"][c4 * 512:(c4 + 1) * 512, :].rearrange("(c p) d -> p c d", p=128)),
             writes=["wf2"], dma="wf2")
    ssv = G["ss"]
    p.op("dve", lambda e: e.tensor_copy(sst[:, 0, :], ssv[:, 0, :]), reads=["ss"], writes=["sst"])
    p.op("dve", lambda e: e.tensor_copy(sst[:, 1, :], ssv[:, 6, :]), reads=["ss"], writes=["sst"])
    p.op("dve", lambda e: e.tensor_copy(sst[:, 2, :], ssv[:, 12, :]), reads=["ss"], writes=["sst"])
    for k in range(1, 6):
        p.op("dve", lambda e, k=k: e.tensor_tensor(out=sst[:, 0, :], in0=sst[:, 0, :], in1=ssv[:, k, :], op=ALU.add),
             reads=["ss", "sst"], writes=["sst"])
        p.op("dve", lambda e, k=k: e.tensor_tensor(out=sst[:, 1, :], in0=sst[:, 1, :], in1=ssv[:, 6 + k, :], op=ALU.add),
             reads=["ss", "sst"], writes=["sst"])
    for gidx, wdt in ((0, 384.0), (1, 384.0), (2, 256.0)):
        p.op("act", lambda e, gidx=gidx, wdt=wdt: e.activation(out=rst[:, gidx, :], in_=sst[:, gidx, :], func=AF.Sqrt,
                                                               scale=1.0 / wdt, bias=EPS), reads=["sst"], writes=["rst"])
    p.op("dve", lambda e: e.reciprocal(rst[:, :, :], rst[:, :, :]), reads=["rst"], writes=["rst"])
    KCG = ((0, 3), (3, 6), (6, 8))
    ai = 0
    for g in range(NQB):
        yT, yk = yTs[g % 2], "yT%d" % (g % 2)
        p.op("sp", lambda e, yT=yT, g=g: e.dma_start(out=yT[:, :, :], in_=io["YT"][:, g * 512:(g + 1) * 512].rearrange("(c p) t -> p c t", p=128)),
             writes=[yk], dma=yk)
        for j in range(4):
            ti = g * 4 + j
            xr, xk = xrs[ti % 2], "xr%d" % (ti % 2)
            x1, x1k = x1s[ti % 2], "x1s%d" % (ti % 2)
            acc, acck = accs[ti % 2], "dacc%d" % (ti % 2)
            p.op("sp", lambda e, xr=xr, ti=ti: e.dma_start(out=xr[:, :], in_=src["res"](ti)),
                 writes=[xk], dma=xk)
            for gidx, (k0, k1) in enumerate(KCG):
                ap_, ak = accps[ai % 2], "accps%d" % (ai % 2)
                ai += 1
                for half in range(2):
                    for kc in range(k0, k1):
                        p.op("pe", lambda e, ap_=ap_, half=half, kc=kc, k0=k0, k1=k1, j=j: e.matmul(
                            ap_[:, half * 512:(half + 1) * 512], lhsT=yT[:, kc, j * 128:(j + 1) * 128],
                            rhs=wo[:, kc, half * 512:(half + 1) * 512], start=(kc == k0), stop=(kc == k1 - 1)),
                            reads=[yk, "wo"], writes=[ak])
                if gidx == 0:
                    p.op("dve", lambda e, ap_=ap_, ti=ti: e.tensor_scalar(acc[:, :], ap_[:, :], rst[:, 0, ti:ti + 1], None, op0=ALU.mult),
                         reads=[ak, "rst"], writes=[acck])
                else:
                    p.op("dve", lambda e, ap_=ap_, ti=ti, gidx=gidx: e.scalar_tensor_tensor(
                        out=acc[:, :], in0=ap_[:, :], scalar=rst[:, gidx, ti:ti + 1], in1=acc[:, :], op0=ALU.mult, op1=ALU.add),
                        reads=[ak, "rst", acck], writes=[acck])
            p.op("dve", lambda e, xr=xr: e.scalar_tensor_tensor(out=acc[:, :], in0=xr[:, :], scalar=ALPHA, in1=acc[:, :],
                                                                op0=ALU.mult, op1=ALU.add), reads=[xk, acck], writes=[acck])
            layer_norm(p, acc, acck, x1, x1k, g1, "g1", b1, "b1", st12s[ti % 2], mvs[ti % 2], rss[ti % 2], "d1_%d" % (ti % 2))
            p.op("sp", lambda e, x1=x1, ti=ti: e.dma_start(out=io["X1"][ti * 128:(ti + 1) * 128, :], in_=x1[:, :]),
                 reads=[x1k], dma="x1o%d" % (ti % 2))
    p.pop()

    p.push()
    b1T = p.sb("b1T", [128, 32], F32)
    g2 = p.sb("g2", [128, D], F32)
    b2 = p.sb("b2", [128, D], F32)
    bf2 = p.sb("bf2", [128, D], F32)
    x1f = [p.sb("x1f%d" % i, [128, 2, D], F32) for i in range(2)]
    x1b = p.sb("x1b", [128, 2, D], BF16)
    x1Ts = [p.sb("x1T%d" % i, [128, 8, 256], BF16) for i in range(2)]
    hidT = p.sb("hidT", [128, 32, 256], BF16)
    rl = [p.sb("rl%d" % i, [128, 256], F32) for i in range(2)]
    t2ss = [p.sb("t2s%d" % i, [128, D], F32) for i in range(1)]
    outs = [p.sb("outs%d" % i, [128, D], F32) for i in range(2)]
    outb = [p.sb("outb%d" % i, [128, D], BF16) for i in range(2)]
    st12s = [p.sb("st12b%d" % i, [128, 12], F32) for i in range(2)]
    mvs = [p.sb("dmv2_%d" % i, [128, 2], F32) for i in range(2)]
    rss = [p.sb("drs2_%d" % i, [128, 1], F32) for i in range(2)]
    trps = [p.ps("dtrps%d" % i, [128, 1024], BF16) for i in range(2)]
    hps = [p.ps("hps%d" % i, [128, 512]) for i in range(2)]
    fps = [p.ps("fps%d" % i, [128, 1024]) for i in range(2)]
    p.op("sp", lambda e: e.dma_start(out=b1T[:], in_=W["b1T"]), writes=["b1T"], dma="dw2")
    p.op("sp", lambda e: e.dma_start(out=g2[:], in_=W["ln2_g"].partition_broadcast(128)), writes=["g2"], dma="dw2")
    p.op("sp", lambda e: e.dma_start(out=b2[:], in_=W["ln2_b"].partition_broadcast(128)), writes=["b2"], dma="dw2")
    p.op("sp", lambda e: e.dma_start(out=bf2[:], in_=W["b_ff2"].partition_broadcast(128)), writes=["bf2"], dma="dw2")
    ng2 = OWN // 256

    def prep(g):
        xf_, xfk = x1f[g % 2], "x1f%d" % (g % 2)
        p.op("sp", lambda e: e.dma_start(out=xf_[:, :, :], in_=io["X1"][g * 256:(g + 1) * 256, :].rearrange("(j p) d -> p j d", p=128)),
             writes=[xfk], dma=xfk)
        p.op("pool", lambda e: e.tensor_copy(x1b[:, :, :], xf_[:, :, :]), reads=[xfk], writes=["x1b"])
        transpose_group(p, G, x1b, "x1b", x1Ts[g % 2], "x1T%d" % (g % 2), trps, n_tok_tiles=2)

    prep(0)
    hi = 0
    fi = 0
    for g in range(ng2):
        xf_, xfk = x1f[g % 2], "x1f%d" % (g % 2)
        x1T, x1Tk = x1Ts[g % 2], "x1T%d" % (g % 2)
        for fc in range(32):
            hp_, hk = hps[hi % 2], "hps%d" % (hi % 2)
            r_, rk = rl[hi % 2], "rl%d" % (hi % 2)
            hi += 1
            for kc in range(8):
                p.op("pe", lambda e, hp_=hp_, kc=kc, fc=fc: e.matmul(hp_[:, 0:256], lhsT=wf1[:, kc, fc * 128:(fc + 1) * 128],
                                                                     rhs=x1T[:, kc, :], start=(kc == 0), stop=(kc == 7)),
                     reads=["wf1", x1Tk], writes=[hk])
            p.op("act", lambda e, hp_=hp_, r_=r_, fc=fc: e.activation(out=r_[:, :], in_=hp_[:, 0:256], func=AF.Relu,
                                                                      bias=b1T[:, fc:fc + 1], scale=1.0),
                 reads=[hk, "b1T"], writes=[rk])
            p.op("pool" if fc % 2 else "dve", lambda e, r_=r_, fc=fc: e.tensor_tensor(out=hidT[:, fc, :], in0=r_[:, :], in1=r_[:, :], op=ALU.mult),
                 reads=[rk], writes=["hidT"])
        if g + 1 < ng2:
            prep(g + 1)
        for j in range(2):
            ti = g * 2 + j
            fp_, fk = fps[fi % 2], "fps%d" % (fi % 2)
            o_, ok_ = outs[fi % 2], "outs%d" % (fi % 2)
            t2s, t2k = t2ss[0], "t2s0"
            par = fi % 2
            fi += 1
            for half in range(2):
                for fc in range(32):
                    p.op("pe", lambda e, fp_=fp_, half=half, fc=fc, j=j: e.matmul(
                        fp_[:, half * 512:(half + 1) * 512], lhsT=hidT[:, fc, j * 128:(j + 1) * 128],
                        rhs=wf2[:, fc, half * 512:(half + 1) * 512], start=(fc == 0), stop=(fc == 31)),
                        reads=["hidT", "wf2"], writes=[fk])
            p.op("dve", lambda e, fp_=fp_: e.tensor_tensor(out=t2s[:, :], in0=fp_[:, :], in1=bf2[:, :], op=ALU.add),
                 reads=[fk, "bf2"], writes=[t2k])
            p.op("dve", lambda e, xf_=xf_, j=j: e.scalar_tensor_tensor(out=t2s[:, :], in0=xf_[:, j, :], scalar=ALPHA, in1=t2s[:, :],
                                                                       op0=ALU.mult, op1=ALU.add), reads=[xfk, t2k], writes=[t2k])
            layer_norm(p, t2s, t2k, o_, ok_, g2, "g2", b2, "b2", st12s[par], mvs[par], rss[par], "d2_%d" % par)
            p.op("sp", lambda e, o_=o_, ti=ti: e.dma_start(out=dst["f32"][ti * 128:(ti + 1) * 128, :], in_=o_[:, :]),
                 reads=[ok_], dma=dst["tag"])
            if dst["bf16"] is not None:
                ob_, obk = outb[par], "outb%d" % par
                p.op("pool", lambda e, o_=o_, ob_=ob_: e.tensor_copy(ob_[:, :], o_[:, :]), reads=[ok_], writes=[obk])
                ch = ti // 4
                p.op("sp", lambda e, ob_=ob_, ti=ti: e.dma_start(out=dst["bf16"][ti * 128:(ti + 1) * 128, :], in_=ob_[:, :]),
                     reads=[obk], writes=["XBc%d" % ch], dma="xbc%d" % (ch % 2))
                if ti % 4 == 3:
                    ngrp_ = cfg.SF // OWN
                    groups_ = [list(range(b_ * ngrp_, (b_ + 1) * ngrp_)) for b_ in range(cfg.ncores // ngrp_)]
                    p.op("pool", lambda e, ch=ch: e.collective_compute("AllGather", ALU.bypass, replica_groups=groups_,
                                                                       ins=[io["XBc"][ch * 512:(ch + 1) * 512, :].opt()],
                                                                       outs=[io["XG"][ch].opt()]),
                         reads=["XBc%d" % ch], dma="cc", inc=1)
    p.pop()
    p.pop()


def layer_norm(p, src, skey, dstt, dkey, g, gk, b, bk, st12, mv, rs, tg):
    k6, kmv, krs = "st12" + tg, "mv" + tg, "rs" + tg
    p.op("dve", lambda e: e.bn_stats(st12[:, 0:6], src[:, 0:512]), reads=[skey], writes=[k6])
    p.op("dve", lambda e: e.bn_stats(st12[:, 6:12], src[:, 512:1024]), reads=[skey], writes=[k6])
    p.op("dve", lambda e: e.bn_aggr(mv[:, :], st12[:, :]), reads=[k6], writes=[kmv])
    p.op("act", lambda e: e.activation(out=rs[:, :], in_=mv[:, 1:2], func=AF.Sqrt, scale=1.0, bias=EPS), reads=[kmv], writes=[krs])
    p.op("dve", lambda e: e.reciprocal(rs[:, :], rs[:, :]), reads=[krs], writes=[krs])
    p.op("dve", lambda e: e.tensor_scalar(src[:, :], src[:, :], mv[:, 0:1], rs[:, 0:1], op0=ALU.subtract, op1=ALU.mult),
         reads=[skey, kmv, krs], writes=[skey])
    p.op("pool", lambda e: e.tensor_tensor(out=src[:, :], in0=src[:, :], in1=g[:, :], op=ALU.mult), reads=[skey, gk], writes=[skey])
    p.op("pool", lambda e: e.tensor_tensor(out=dstt[:, :], in0=src[:, :], in1=b[:, :], op=ALU.add), reads=[skey, bk], writes=[dkey])


_CACHE = {}


def host_weights(inp, layers):
    f = lambda a: np.ascontiguousarray(a, dtype=np.float32)
    L = list(layers)
    w = {}
    w["w_in"] = f(inp["w_in"][L])
    w["w_q_up"] = f(inp["w_q_up"][L])
    w["w_kv_up"] = f(inp["w_kv_up"][L])
    qn = np.zeros((len(L), 256), np.float32)
    qn[:, :192] = inp["q_norm"][L]
    w["qn"] = f(qn.reshape(len(L), 2, 128).transpose(0, 2, 1))
    w["kvn"] = f(inp["kv_norm"][L].reshape(len(L), 128, 1))
    w["sg_g"] = f(inp["sgu_ln_g"][L])
    w["sg_b"] = f(inp["sgu_ln_b"][L])
    w["sg_wT"] = f(np.transpose(inp["sgu_w"][L], (0, 1, 3, 2)))
    w["sg_bT"] = f(np.transpose(inp["sgu_b"][L], (0, 2, 1)))
    w["mixn"] = f(inp["mix_norm"][L].reshape(len(L), 8, 128).transpose(0, 2, 1))
    w["w_out"] = f(inp["w_out"][L])
    w["ln1_g"] = f(inp["ln1_g"][L])
    w["ln1_b"] = f(inp["ln1_b"][L])
    w["w_ff1"] = f(inp["w_ff1"][L])
    w["b1T"] = f(inp["b_ff1"][L].reshape(len(L), 32, 128).transpose(0, 2, 1))
    w["w_ff2"] = f(inp["w_ff2"][L])
    w["b_ff2"] = f(inp["b_ff2"][L])
    w["ln2_g"] = f(inp["ln2_g"][L])
    w["ln2_b"] = f(inp["ln2_b"][L])
    return w


def core_inputs(x_b, r, own, consts):
    S = x_b.shape[0]
    lo = r * own - HALO
    xo = np.zeros((own + 2 * HALO, D), np.float32)
    vm = np.zeros((own + 2 * HALO,), np.float32)
    a, b = max(lo, 0), min(lo + own + 2 * HALO, S)
    xo[a - lo:b - lo] = x_b[a:b]
    vm[a - lo:b - lo] = 1.0
    ct, st = consts["rope"]
    hs = np.zeros((128, 8), np.float32)
    if r - 1 >= 0:
        hs[:, r - 1] = 1.0
    if r + 1 < S // own:
        hs[:, 4 + r + 1] = 1.0
    d = dict(hsel=hs, xo=xo, xf=np.ascontiguousarray(x_b, dtype=np.float32),
             vmask=np.ascontiguousarray(vm.reshape(-1, 128).T), mtab=consts["mtab"],
             ckt=ct, skt=st, cqt=np.ascontiguousarray(ct[:, r * own:(r + 1) * own]),
             sqt=np.ascontiguousarray(st[:, r * own:(r + 1) * own]))
    return d


def run_layers(x, inp, own, n_groups_per_batch, fused_depth=1, dbg=False):
    B, S, _ = x.shape
    key = (own, S, fused_depth, dbg, B)
    if key not in _CACHE:
        _CACHE[key] = build_program(Cfg(own, S, depth=fused_depth, dbg=dbg, ncores=B * n_groups_per_batch))
    nc, stats = _CACHE[key]
    consts = dict(mtab=mask_table(), rope=rope_tables(S))
    depth = inp["w_in"].shape[0]
    cur = np.asarray(x, dtype=np.float32)
    extra = None
    for l0 in range(0, depth, fused_depth):
        w = host_weights(inp, range(l0, l0 + fused_depth))
        in_maps = []
        for c in range(B * n_groups_per_batch):
            b, r = c // n_groups_per_batch, c % n_groups_per_batch
            d = core_inputs(cur[b], r, own, consts)
            if fused_depth == 1:
                d.pop("hsel")
            d.update(w)
            in_maps.append(d)
        res = run_bass_kernel_spmd(nc, in_maps, core_ids=list(range(len(in_maps))))
        outs = [r_["out"] for r_ in res.results]
        cur = np.stack([np.concatenate(outs[b * n_groups_per_batch:(b + 1) * n_groups_per_batch], 0) for b in range(B)], 0)
        extra = res.results
    return cur, extra


def kernel(**inputs):
    x = np.asarray(inputs["x"], dtype=np.float32)
    inp = {k: np.asarray(v, dtype=np.float32) for k, v in inputs.items() if k != "x"}
    out, _ = run_layers(x, inp, own=x.shape[1] // 4, n_groups_per_batch=4, fused_depth=inp["w_in"].shape[0])
    return out.astype(np.float32)
```

```python
import types
import numpy as np
import ml_dtypes
import concourse.bass as bass
import concourse.mybir as mybir
from concourse.bass_utils import run_bass_kernel_spmd

F32 = mybir.dt.float32
BF16 = mybir.dt.bfloat16
I32 = mybir.dt.int32
AF = mybir.ActivationFunctionType
ALU = mybir.AluOpType
AX = mybir.AxisListType

ENGS = ("pe", "act", "dve", "pool", "sp")
SEM_LIMIT = 20000

D = 1024
PIN = 2016
HALO = 1024
DFF = 4096
EPS = 1e-5
ALPHA = float((2 * 2) ** 0.25)
TABW = 2944
A_SCALE = 0.125
B_SCALE = float(96 ** -0.5)
C_GELU = 0.044715
K_GELU = float(np.sqrt(2.0 / np.pi))


def _freeze(fn):
    if fn is None or fn.__closure__ is None:
        return fn
    cells = []
    for c in fn.__closure__:
        try:
            cells.append(types.CellType(c.cell_contents))
        except ValueError:
            cells.append(c)
    return types.FunctionType(fn.__code__, fn.__globals__, fn.__name__, fn.__defaults__, tuple(cells))


class Prog:
    def __init__(self, nc):
        self.nc = nc
        self.ops = {e: [] for e in ENGS}
        self.state = {}
        self.dma_sems = {}
        self.pending = {e: [] for e in ENGS}
        self._cms = []
        self._scopes = []

    def _reg(self, cm):
        t = cm.__enter__()
        (self._scopes[-1] if self._scopes else self._cms).append(cm)
        return t

    def sem(self, name):
        self._n = getattr(self, "_n", 0) + 1
        cm = self.nc.semaphore("m%d_%s" % (self._n, name))
        s = cm.__enter__()
        self._cms.append(cm)
        return s

    def sb(self, name, shape, dt):
        self._n = getattr(self, "_n", 0) + 1
        return self._reg(self.nc.sbuf_tensor("sb%d_%s" % (self._n, name), list(shape), dt))

    def ps(self, name, shape, dt=F32):
        self._n = getattr(self, "_n", 0) + 1
        return self._reg(self.nc.psum_tensor("ps%d_%s" % (self._n, name), list(shape), dt))

    def push(self):
        self._scopes.append([])

    def pop(self):
        self.barrier()
        for cm in reversed(self._scopes.pop()):
            cm.__exit__(None, None, None)

    def close(self):
        for cm in reversed(self._cms):
            cm.__exit__(None, None, None)
        self._cms = []

    def barrier(self):
        evs = []
        for e in ENGS:
            lst = self.ops[e]
            for i in range(len(lst) - 1, -1, -1):
                if lst[i]["dma"] is None and lst[i]["fn"] is not None:
                    evs.append(("eng", e, i))
                    break
        for tag, ds in self.dma_sems.items():
            evs.append(("dma", ds[0], ds[1]))
        for e in ENGS:
            self.pending[e].extend(evs)
        self.state = {}

    def op(self, eng, fn, reads=(), writes=(), dma=None, inc=16):
        fn = _freeze(fn)
        deps = []
        for k in reads:
            st = self.state.get(k)
            if st and st[0] is not None:
                deps.append((st[0], True))
        for k in writes:
            st = self.state.get(k)
            if st:
                if st[0] is not None:
                    deps.append((st[0], False))
                for r in st[1]:
                    deps.append((r, False))
        lst = self.ops[eng]
        idx = len(lst)
        if dma is not None:
            if dma not in self.dma_sems:
                self.dma_sems[dma] = [self.sem("d_" + dma), 0]
            ds = self.dma_sems[dma]
            ds[1] += inc
            ev = ("dma", ds[0], ds[1])
        else:
            ev = ("eng", eng, idx)
        fdeps = []
        for d, raw in deps:
            if d[0] == "eng" and d[1] == eng and dma is None:
                if eng == "pe" or not raw:
                    continue
            if d[0] == "dma":
                for ds_ in self.dma_sems.values():
                    if ds_[0] is d[1]:
                        cur = ds_[1] - (inc if (dma is not None and self.dma_sems[dma][0] is d[1]) else 0)
                        d = ("dma", d[1], max(d[2], cur))
            fdeps.append(d)
        for d in self.pending[eng]:
            if d[0] == "eng" and d[1] == eng and eng == "pe":
                continue
            fdeps.append(d)
        self.pending[eng] = []
        lst.append(dict(fn=fn, deps=fdeps, dma=dma, ev=ev, ms=False, inc=inc))
        for k in reads:
            st = self.state.setdefault(k, [None, []])
            st[1].append(ev)
        for k in writes:
            self.state[k] = [ev, []]
        return ev

    def wait_all_dma(self, eng, tags):
        deps = []
        for t in tags:
            ds = self.dma_sems[t]
            deps.append(("dma", ds[0], ds[1]))
        self.ops[eng].append(dict(fn=None, deps=deps, dma=None, ev=None, ms=False))

    def emit(self):
        nc = self.nc
        for e in ENGS:
            for o in self.ops[e]:
                for d in o["deps"]:
                    if d[0] == "eng":
                        self.ops[d[1]][d[2]]["ms"] = True
        msmap = {}
        for e in ENGS:
            cur = None
            cnt = 0
            for i, o in enumerate(self.ops[e]):
                if o["ms"]:
                    if cur is None or cnt >= SEM_LIMIT:
                        cur = self.sem("s_%s_%d" % (e, i))
                        cnt = 0
                    cnt += 1
                    msmap[(e, i)] = (cur, cnt)
                    o["inc"] = cur
        stats = {}

        def run(e, eng):
            waited = {}
            nw = 0
            for i, o in enumerate(self.ops[e]):
                for d in o["deps"]:
                    if d[0] == "eng":
                        sem, val = msmap[(d[1], d[2])]
                    else:
                        sem, val = d[1], d[2]
                    key = id(sem)
                    if waited.get(key, 0) >= val:
                        continue
                    waited[key] = val
                    eng.wait_ge(sem, val)
                    nw += 1
                if o["fn"] is None:
                    continue
                ins = o["fn"](eng)
                if o["dma"] is not None:
                    ins.then_inc(o["ev"][1], o.get("inc", 16))
                elif o["ms"]:
                    ins.then_inc(o["inc"], 1)
            stats[e] = (len(self.ops[e]), nw)

        with nc.Block() as block:
            @block.tensor
            def _(eng):
                run("pe", eng)

            @block.scalar
            def _(eng):
                run("act", eng)

            @block.vector
            def _(eng):
                run("dve", eng)

            @block.gpsimd
            def _(eng):
                run("pool", eng)

            @block.sync
            def _(eng):
                run("sp", eng)
        self.stats = stats
        return stats


def mask_table():
    slopes = (2.0 ** (-8.0 * np.arange(1, 7) / 6)).astype(np.float32)
    pp = np.arange(128)[:, None]
    col = np.arange(TABW)[None, :]
    delta = pp - col + 1408
    ad = np.abs(delta)
    c = (ad <= 64).astype(np.float32) + ((delta % 4 == 0) & (ad <= 256)).astype(np.float32) \
        + ((delta % 16 == 0) & (ad <= 1024)).astype(np.float32)
    tab = np.zeros((128, 6, TABW), np.float32)
    for h in range(6):
        tab[:, h, :] = c * np.exp(-(slopes[h] * ad.astype(np.float32)).astype(np.float32))
    return tab.astype(ml_dtypes.bfloat16)


def rope_tables(S):
    inv_freq = (10000.0 ** (-np.arange(0, 32, 2, dtype=np.float32) / 32)).astype(np.float32)
    ang = (np.arange(S, dtype=np.float32)[:, None] * inv_freq[None, :]).astype(np.float32)
    cos = np.cos(ang).astype(np.float32).T
    sin = np.sin(ang).astype(np.float32).T
    ct = np.concatenate([cos, cos], 0)
    st = np.concatenate([-sin, sin], 0)
    return np.ascontiguousarray(ct), np.ascontiguousarray(st)


class Cfg:
    def __init__(self, own, sf, depth=1, dbg=False, ncores=8):
        self.ncores = ncores
        self.OWN = own
        self.SF = sf
        self.OH = own + 2 * HALO
        self.NT = own // 128
        self.NQB = own // 512
        self.NG_OH = self.OH // 512
        self.NG_F = sf // 512
        self.NKT_OH = self.OH // 128
        self.NKT_F = sf // 128
        self.depth = depth
        self.dbg = dbg


W_NAMES = [
    ("w_in", [D, PIN]), ("w_q_up", [192, 576]), ("w_kv_up", [128, 768]), ("qn", [128, 2]), ("kvn", [128, 1]),
    ("sg_g", [256]), ("sg_b", [256]), ("sg_wT", [4, 128, 128]), ("sg_bT", [128, 4]), ("mixn", [128, 8]),
    ("w_out", [D, D]), ("ln1_g", [D]), ("ln1_b", [D]), ("w_ff1", [D, DFF]), ("b1T", [128, 32]),
    ("w_ff2", [DFF, D]), ("b_ff2", [D]), ("ln2_g", [D]), ("ln2_b", [D]),
]


def build_program(cfg):
    nc = bass.Bass("TRN2", target_bir_lowering=False)
    OWN, SF, OH, NT = cfg.OWN, cfg.SF, cfg.OH, cfg.NT
    io = {}
    io["xo"] = nc.dram_tensor("xo", [OH, D], F32, kind="ExternalInput").ap()
    io["xf"] = nc.dram_tensor("xf", [SF, D], F32, kind="ExternalInput").ap()
    io["vmask"] = nc.dram_tensor("vmask", [128, OH // 128], F32, kind="ExternalInput").ap()
    io["mtab"] = nc.dram_tensor("mtab", [128, 6, TABW], BF16, kind="ExternalInput").ap()
    io["ckt"] = nc.dram_tensor("ckt", [32, SF], F32, kind="ExternalInput").ap()
    io["skt"] = nc.dram_tensor("skt", [32, SF], F32, kind="ExternalInput").ap()
    io["cqt"] = nc.dram_tensor("cqt", [32, OWN], F32, kind="ExternalInput").ap()
    io["sqt"] = nc.dram_tensor("sqt", [32, OWN], F32, kind="ExternalInput").ap()
    for nm, shp in W_NAMES:
        io[nm] = nc.dram_tensor(nm, [cfg.depth] + shp, F32, kind="ExternalInput").ap()
    io["out"] = nc.dram_tensor("out", [OWN, D], F32, kind="ExternalOutput").ap()
    io["YT"] = nc.dram_tensor("YT", [D, OWN], BF16, kind="Internal").ap()
    io["X1"] = nc.dram_tensor("X1", [OWN, D], F32, kind="Internal").ap()
    if cfg.depth > 1:
        io["hsel"] = nc.dram_tensor("hsel", [128, 8], F32, kind="ExternalInput").ap()
        io["XL"] = nc.dram_tensor("XL", [OWN, D], F32, kind="Internal").ap()
        io["XBc"] = nc.dram_tensor("XBc", [OWN, D], BF16, kind="Internal").ap()
        io["XG"] = nc.dram_tensor("XG", [OWN // 512, (SF // OWN) * 512, D], BF16, kind="Internal").ap()
        io["XH"] = nc.dram_tensor("XH", [2 * HALO, D], BF16, kind="Internal").ap()
    if cfg.dbg:
        io["dbg_yt"] = nc.dram_tensor("dbg_yt", [D, OWN], BF16, kind="ExternalOutput").ap()
        io["dbg_ss"] = nc.dram_tensor("dbg_ss", [128, 13 * NT], F32, kind="ExternalOutput").ap()
        io["dbg_x1"] = nc.dram_tensor("dbg_x1", [OWN, D], F32, kind="ExternalOutput").ap()

    p = Prog(nc)
    DBG["on"] = cfg.dbg
    G = {}
    G["ident"] = p.sb("ident", [128, 128], BF16)
    identf = p.sb("identf", [128, 128], F32)
    G["e65"] = p.sb("e65", [128, 64], F32)
    G["ones"] = p.sb("onesf", [128, 128], F32)
    G["ss"] = p.sb("ss", [128, 13, NT], F32)
    p.op("pool", lambda e: e.memset(identf[:], 0.0), writes=["identf"])
    p.op("pool", lambda e: e.affine_select(out=identf[:], in_=identf[:], compare_op=ALU.not_equal, fill=1.0,
                                           base=0, pattern=[[-1, 128]], channel_multiplier=1),
         reads=["identf"], writes=["identf"])
    p.op("pool", lambda e: e.tensor_copy(G["ident"][:], identf[:]), reads=["identf"], writes=["ident"])
    p.op("pool", lambda e: e.memset(G["e65"][:], 0.0), writes=["e65"])
    p.op("pool", lambda e: e.memset(G["e65"][64:65, :], 1.0), reads=["e65"], writes=["e65"])
    p.op("pool", lambda e: e.memset(G["ones"][:], 1.0), writes=["ones"])
    p.barrier()

    ngo = cfg.NG_OH
    for l in range(cfg.depth):
        W = {nm: io[nm][l] for nm, _ in W_NAMES}
        last = (l == cfg.depth - 1)
        if l == 0:
            src = dict(oh=lambda g: io["xo"][g * 512:(g + 1) * 512, :],
                       full=lambda g: io["xf"][g * 512:(g + 1) * 512, :],
                       res=lambda ti: io["xo"][HALO + ti * 128:HALO + (ti + 1) * 128, :])
        else:
            def oh2(g):
                if g < 2:
                    return io["XH"][g * 512:(g + 1) * 512, :]
                if g >= ngo - 2:
                    return io["XH"][1024 + (g - (ngo - 2)) * 512:1024 + (g - (ngo - 2) + 1) * 512, :]
                return io["XL"][(g - 2) * 512:(g - 1) * 512, :]
            src = dict(oh=oh2, full=lambda g: xg_rows(io, cfg, g * 512, 512),
                       res=lambda ti: io["XL"][ti * 128:(ti + 1) * 128, :])
        if last:
            dst = dict(f32=io["out"], bf16=None, tag="out")
        else:
            dst = dict(f32=io["XL"], bf16=io["XBc"], tag="xl")
        layer(p, cfg, G, io, W, src, dst)
        if not last:
            exchange(p, cfg, G, io)

    if cfg.dbg:
        p.op("sp", lambda e: e.dma_start(out=io["dbg_yt"], in_=io["YT"]), reads=[], dma="out")
        p.op("sp", lambda e: e.dma_start(out=io["dbg_ss"], in_=G["ss"][:].rearrange("p a b -> p (a b)")), dma="out")
        p.op("sp", lambda e: e.dma_start(out=io["dbg_x1"], in_=io["X1"]), dma="out")
    p.wait_all_dma("sp", ["out"])
    stats = p.emit()
    p.close()
    return nc, stats


DBG = {}


def dbg_dump(p, name, ap, shape, dt):
    if not DBG.get("on"):
        return
    t = p.nc.dram_tensor("dd_" + name, list(shape), dt, kind="ExternalOutput").ap()
    p.barrier()
    p.op("sp", lambda e: e.dma_start(out=t, in_=ap), dma="out")
    p.barrier()


def load_x_group(p, src_ap, xb, key, tag):
    p.op("pool", lambda e: e.dma_start(out=xb[:], in_=src_ap.rearrange("(j p) d -> p j d", p=128)),
         writes=[key], dma=tag)


def transpose_group(p, G, xb, xbkey, xT, xTkey, trps, n_tok_tiles=4):
    for kc in range(8):
        tp = trps[kc % 2]
        tkey = "tr%d" % (kc % 2)
        for j in range(n_tok_tiles):
            p.op("pe", lambda e, tp=tp, j=j, kc=kc: e.transpose(tp[:, j * 128:(j + 1) * 128],
                                                                 xb[:, j, kc * 128:(kc + 1) * 128], G["ident"][:]),
                 reads=[xbkey, "ident"], writes=[tkey])
        w = n_tok_tiles * 128
        p.op("act", lambda e, tp=tp, kc=kc, w=w: e.copy(xT[:, kc, 0:w], tp[:, 0:w]), reads=[tkey], writes=[xTkey])


def attn_post(p, G, io, ops, okey, h_row, qb, ss_idx, bufs, it):
    r = it % 2
    osb, rden, ysq, ybf = bufs["osb"][r], bufs["rden"], bufs["ysq"][r], bufs["ybf"][r]
    ko, kr, ks, kb = "osb%d" % r, "rden", "ysq%d" % r, "ybf%d" % r
    p.op("dve", lambda e: e.tensor_copy(osb[0:65, :], ops[0:65, :]), reads=[okey], writes=[ko])

    def deferred():
        mp, mk = bufs["peek"]()
        denps, kd = mp[:, 0:512], mk
        p.op("pe", lambda e: e.matmul(denps[0:64, :], lhsT=G["e65"][0:65, :], rhs=osb[0:65, :], start=True, stop=True),
             reads=[ko, "e65"], writes=[kd])
        p.op("dve", lambda e: e.tensor_copy(rden[0:64, :], denps[0:64, :]), reads=[kd], writes=[kr])
        p.op("dve", lambda e: e.reciprocal(rden[0:64, :], rden[0:64, :]), reads=[kr], writes=[kr])
        p.op("pool", lambda e: e.tensor_tensor(out=osb[0:64, :], in0=osb[0:64, :], in1=rden[0:64, :], op=ALU.mult),
             reads=[ko, kr], writes=[ko])
        p.op("pool", lambda e: e.tensor_copy(ybf[0:64, :], osb[0:64, :]), reads=[ko], writes=[kb])
        p.op("sp", lambda e: e.dma_start(out=io["YT"][h_row:h_row + 64, qb * 512:(qb + 1) * 512], in_=ybf[0:64, :]),
             reads=[kb], dma="yt%d" % r)
        p.op("pool", lambda e: e.tensor_tensor(out=ysq[0:64, :], in0=osb[0:64, :], in1=osb[0:64, :], op=ALU.mult),
             reads=[ko], writes=[ks])

        def stage_b():
            mp2, mk2 = bufs["peek"]()
            ssps = mp2[:, 512:1024]
            for j in range(4):
                p.op("pe", lambda e, j=j: e.matmul(ssps[:, j:j + 1], lhsT=ysq[0:64, j * 128:(j + 1) * 128],
                                                   rhs=G["ones"][0:64, 0:1], start=True, stop=True),
                     reads=[ks, "ones"], writes=[mk2])
            p.op("dve", lambda e: e.tensor_copy(G["ss"][:, ss_idx, qb * 4:(qb + 1) * 4], ssps[:, 0:4]),
                 reads=[mk2], writes=["ss"])
        return stage_b
    return deferred


def xg_rows(io, cfg, R0, n):
    j, q = R0 // cfg.OWN, R0 % cfg.OWN
    i, t = q // 512, q % 512
    assert t + n <= 512
    return io["XG"][i, j * 512 + t:j * 512 + t + n, :]


def exchange(p, cfg, G, io):
    OWN = cfg.OWN
    ngrp = cfg.SF // OWN
    groups = [list(range(b * ngrp, (b + 1) * ngrp)) for b in range(cfg.ncores // ngrp)]
    p.barrier()
    p.push()
    hsel = p.sb("hsel", [128, 8], F32)
    cands = [p.sb("cand%d" % i, [128, ngrp, D], BF16) for i in range(2)]
    hacc = p.sb("hacc", [128, D], F32)
    houts = [p.sb("hout%d" % i, [128, D], BF16) for i in range(2)]
    p.op("sp", lambda e: e.dma_start(out=hsel[:], in_=io["hsel"]), writes=["hsel"], dma="hsel")
    it = 0
    for side in range(2):
        for t in range(HALO // 128):
            cand, ck = cands[it % 2], "cand%d" % (it % 2)
            hout, hk = houts[it % 2], "hout%d" % (it % 2)
            for j in range(ngrp):
                r0 = j * OWN + (OWN - HALO if side == 0 else 0) + t * 128
                p.op("sp", lambda e, j=j, r0=r0: e.dma_start(out=cand[:, j, :], in_=xg_rows(io, cfg, r0, 128)),
                     writes=[ck], dma=ck)
            p.op("dve", lambda e: e.tensor_scalar(hacc[:, :], cand[:, 0, :], hsel[:, side * 4:side * 4 + 1], None, op0=ALU.mult),
                 reads=[ck, "hsel"], writes=["hacc"])
            for j in range(1, ngrp):
                p.op("dve", lambda e, j=j: e.scalar_tensor_tensor(out=hacc[:, :], in0=cand[:, j, :],
                                                                  scalar=hsel[:, side * 4 + j:side * 4 + j + 1], in1=hacc[:, :],
                                                                  op0=ALU.mult, op1=ALU.add), reads=[ck, "hsel", "hacc"], writes=["hacc"])
            p.op("pool", lambda e: e.tensor_copy(hout[:, :], hacc[:, :]), reads=["hacc"], writes=[hk])
            r1 = side * HALO + t * 128
            p.op("sp", lambda e, r1=r1: e.dma_start(out=io["XH"][r1:r1 + 128, :], in_=hout[:, :]), reads=[hk], dma="xh%d" % (it % 2))
            it += 1
    p.pop()


def layer(p, cfg, G, io, W, src, dst):
    OWN, SF, OH, NT, NQB = cfg.OWN, cfg.SF, cfg.OH, cfg.NT, cfg.NQB
    ident = G["ident"]

    p.push()
    CQ = p.sb("CQ", [128, 2, OWN], BF16)
    p.push()
    KaT = p.sb("KaT", [128, 3, OH], BF16)
    QaT = p.sb("QaT", [128, 3, OWN], BF16)
    Va = p.sb("Va", [128, OH // 128, 6, 65], BF16)

    p.push()
    w_in = p.sb("w_in", [128, 8, PIN], BF16)
    wsT = p.sb("wsT", [128, 4, 128], BF16)
    sg_g = p.sb("sg_g", [128, 256], F32)
    sg_b = p.sb("sg_b", [128, 256], F32)
    bsT = p.sb("bsT", [128, 4], F32)
    vm = p.sb("vm", [128, OH // 128], F32)
    xbs = [p.sb("xb%d" % i, [128, 4, D], BF16) for i in range(2)]
    xTs = [p.sb("xT%d" % i, [128, 8, 512], BF16) for i in range(2)]
    zh = p.sb("zh", [128, 512], F32)
    w1 = p.sb("w1", [128, 512], F32)
    w2 = p.sb("w2", [128, 512], F32)
    zz = p.sb("zz", [128, 512], F32)
    vn = p.sb("vn", [128, 256], F32)
    vnb = p.sb("vnb", [128, 256], BF16)
    yc = p.sb("yc", [128, 256], F32)
    ycb = p.sb("ycb", [128, 256], BF16)
    ycT = p.sb("ycT", [128, 2, 512], BF16)
    junk = p.sb("junk", [128, 256], F32)
    st6 = p.sb("st6", [128, 6], F32)
    mv = p.sb("mv", [128, 2], F32)
    rs = p.sb("rs", [128, 1], F32)
    cqs = [p.sb("cqs%d" % i, [128, 512], F32) for i in range(2)]
    cqq = [p.sb("cqq%d" % i, [128, 512], F32) for i in range(2)]
    cqr = p.sb("cqr", [128, 512], F32)
    trps = [p.ps("trps%d" % i, [128, 1024], BF16) for i in range(2)]
    pps = [p.ps("pps%d" % i, [128, 512]) for i in range(3)]
    mixps = p.ps("mixps", [128, 512])
    ssq = p.ps("ssq", [128, 512])

    for kc in range(8):
        p.op("pool", lambda e, kc=kc: e.dma_start(out=w_in[:, kc, :], in_=W["w_in"][kc * 128:(kc + 1) * 128, :]),
             writes=["w_in"], dma="w_in")
    p.op("pool", lambda e: e.dma_start(out=wsT[:], in_=W["sg_wT"].rearrange("g s t -> s g t")), writes=["wsT"], dma="wsm")
    p.op("sp", lambda e: e.dma_start(out=sg_g[:], in_=W["sg_g"].partition_broadcast(128)), writes=["sg_g"], dma="wsm2")
    p.op("sp", lambda e: e.dma_start(out=sg_b[:], in_=W["sg_b"].partition_broadcast(128)), writes=["sg_b"], dma="wsm2")
    p.op("sp", lambda e: e.dma_start(out=bsT[:], in_=W["sg_bT"]), writes=["bsT"], dma="wsm2")
    p.op("sp", lambda e: e.dma_start(out=vm[:], in_=io["vmask"]), writes=["vm"], dma="wsm2")
    for h in range(6):
        p.op("dve", lambda e, h=h: e.tensor_copy(Va[:, :, h, 64:65], vm[:].rearrange("p (t o) -> p t o", o=1)),
             reads=["vm"], writes=["Va"])

    pp_i = [0]

    def next_pps():
        i = pp_i[0] % 3
        pp_i[0] += 1
        return pps[i], "pps%d" % i

    ngo = cfg.NG_OH
    load_x_group(p, src["oh"](0), xbs[0], "xb0", "xb0")
    for g in range(ngo):
        xb, xbk = xbs[g % 2], "xb%d" % (g % 2)
        xT, xTk = xTs[g % 2], "xT%d" % (g % 2)
        if g + 1 < ngo:
            load_x_group(p, src["oh"](g + 1), xbs[(g + 1) % 2], "xb%d" % ((g + 1) % 2), "xb%d" % ((g + 1) % 2))
        transpose_group(p, G, xb, xbk, xT, xTk, trps)
        own = (g * 512 >= HALO) and (g * 512 < HALO + OWN)
        go = g - HALO // 512
        for c in range(3):
            ps_, pk = next_pps()
            for kc in range(8):
                p.op("pe", lambda e, ps_=ps_, kc=kc, c=c: e.matmul(ps_[:, :], lhsT=w_in[:, kc, 384 + c * 128:384 + (c + 1) * 128],
                                                                   rhs=xT[:, kc, :], start=(kc == 0), stop=(kc == 7)),
                     reads=["w_in", xTk], writes=[pk])
            p.op("dve", lambda e, ps_=ps_, c=c, g=g: e.tensor_copy(KaT[:, c, g * 512:(g + 1) * 512], ps_[:, :]),
                 reads=[pk], writes=["KaT"])
        if own:
            for c in range(3):
                ps_, pk = next_pps()
                for kc in range(8):
                    p.op("pe", lambda e, ps_=ps_, kc=kc, c=c: e.matmul(ps_[:, :], lhsT=w_in[:, kc, c * 128:(c + 1) * 128],
                                                                       rhs=xT[:, kc, :], start=(kc == 0), stop=(kc == 7)),
                         reads=["w_in", xTk], writes=[pk])
                p.op("dve", lambda e, ps_=ps_, c=c, go=go: e.tensor_copy(QaT[:, c, go * 512:(go + 1) * 512], ps_[:, :]),
                     reads=[pk], writes=["QaT"])
        for j in range(4):
            ps_, pk = next_pps()
            for kc in range(8):
                p.op("pe", lambda e, ps_=ps_, kc=kc, j=j: e.matmul(ps_[:, 0:384], lhsT=xT[:, kc, j * 128:(j + 1) * 128],
                                                                   rhs=w_in[:, kc, 768:1152], start=(kc == 0), stop=(kc == 7)),
                     reads=["w_in", xTk], writes=[pk])
            p.op("dve", lambda e, ps_=ps_, j=j, g=g: e.tensor_copy(Va[:, g * 4 + j, :, 0:64],
                                                                   ps_[:, 0:384].rearrange("p (h c) -> p h c", h=6)),
                 reads=[pk], writes=["Va"])
        if not own:
            continue
        psa, pka = next_pps()
        psb, pkb = next_pps()
        for kc in range(8):
            p.op("pe", lambda e, kc=kc: e.matmul(psa[:, :], lhsT=w_in[:, kc, 1152:1280], rhs=xT[:, kc, :],
                                                 start=(kc == 0), stop=(kc == 7)), reads=["w_in", xTk], writes=[pka])
        for kc in range(8):
            p.op("pe", lambda e, kc=kc: e.matmul(psb[0:64, :], lhsT=w_in[:, kc, 1280:1344], rhs=xT[:, kc, :],
                                                 start=(kc == 0), stop=(kc == 7)), reads=["w_in", xTk], writes=[pkb])
        p.op("dve", lambda e: e.tensor_copy(cqs[0][:, :], psa[:, :]), reads=[pka], writes=["cqs0"])
        p.op("dve", lambda e: e.tensor_copy(cqs[1][0:64, :], psb[0:64, :]), reads=[pkb], writes=["cqs1"])
        p.op("pool", lambda e: e.tensor_tensor(out=cqq[0][:, :], in0=cqs[0][:, :], in1=cqs[0][:, :], op=ALU.mult),
             reads=["cqs0"], writes=["cqq0"])
        p.op("pool", lambda e: e.tensor_tensor(out=cqq[1][0:64, :], in0=cqs[1][0:64, :], in1=cqs[1][0:64, :], op=ALU.mult),
             reads=["cqs1"], writes=["cqq1"])
        p.op("pe", lambda e: e.matmul(ssq[:, :], lhsT=G["ones"][:, :], rhs=cqq[0][:, :], start=True, stop=False),
             reads=["cqq0", "ones"], writes=["ssq"])
        p.op("pe", lambda e: e.matmul(ssq[:, :], lhsT=G["ones"][0:64, :], rhs=cqq[1][0:64, :], start=False, stop=True),
             reads=["cqq1", "ones"], writes=["ssq"])
        p.op("act", lambda e: e.activation(out=cqr[:, :], in_=ssq[:, :], func=AF.Sqrt, scale=1.0 / 192, bias=EPS),
             reads=["ssq"], writes=["cqr"])
        p.op("dve", lambda e: e.reciprocal(cqr[:, :], cqr[:, :]), reads=["cqr"], writes=["cqr"])
        p.op("dve", lambda e, go=go: e.tensor_tensor(out=CQ[:, 0, go * 512:(go + 1) * 512], in0=cqs[0][:, :], in1=cqr[:, :],
                                                     op=ALU.mult), reads=["cqs0", "cqr"], writes=["CQ"])
        p.op("dve", lambda e, go=go: e.tensor_tensor(out=CQ[0:64, 1, go * 512:(go + 1) * 512], in0=cqs[1][0:64, :],
                                                     in1=cqr[0:64, :], op=ALU.mult), reads=["cqs1", "cqr"], writes=["CQ"])
        for j in range(4):
            ti = go * 4 + j
            ps_, pk = next_pps()
            for kc in range(8):
                p.op("pe", lambda e, ps_=ps_, kc=kc, j=j: e.matmul(ps_[:, :], lhsT=xT[:, kc, j * 128:(j + 1) * 128],
                                                                   rhs=w_in[:, kc, 1504:2016], start=(kc == 0), stop=(kc == 7)),
                     reads=["w_in", xTk], writes=[pk])
            p.op("act", lambda e, ps_=ps_: e.activation(out=zh[:, :], in_=ps_[:, :], func=AF.Copy, scale=0.5),
                 reads=[pk], writes=["zh"])
            p.op("pool", lambda e: e.tensor_tensor(out=w1[:, :], in0=zh[:, :], in1=zh[:, :], op=ALU.mult),
                 reads=["zh"], writes=["w1"])
            p.op("dve", lambda e: e.tensor_scalar(w1[:, :], w1[:, :], 4.0 * C_GELU, 1.0, op0=ALU.mult, op1=ALU.add),
                 reads=["w1"], writes=["w1"])
            p.op("pool", lambda e: e.tensor_tensor(out=w2[:, :], in0=w1[:, :], in1=zh[:, :], op=ALU.mult),
                 reads=["w1", "zh"], writes=["w2"])
            p.op("act", lambda e: e.activation(out=w2[:, :], in_=w2[:, :], func=AF.Tanh, scale=2.0 * K_GELU),
                 reads=["w2"], writes=["w2"])
            p.op("dve", lambda e: e.scalar_tensor_tensor(out=zz[:, :], in0=w2[:, :], scalar=1.0, in1=zh[:, :],
                                                         op0=ALU.add, op1=ALU.mult), reads=["w2", "zh"], writes=["zz"])
            p.op("dve", lambda e: e.bn_stats(st6[:, :], zz[:, 256:512]), reads=["zz"], writes=["st6"])
            p.op("dve", lambda e: e.bn_aggr(mv[:, :], st6[:, :]), reads=["st6"], writes=["mv"])
            p.op("act", lambda e: e.activation(out=rs[:, :], in_=mv[:, 1:2], func=AF.Sqrt, scale=1.0, bias=EPS),
                 reads=["mv"], writes=["rs"])
            p.op("dve", lambda e: e.reciprocal(rs[:, :], rs[:, :]), reads=["rs"], writes=["rs"])
            p.op("dve", lambda e: e.tensor_scalar(vn[:, :], zz[:, 256:512], mv[:, 0:1], rs[:, 0:1], op0=ALU.subtract,
                                                  op1=ALU.mult), reads=["zz", "mv", "rs"], writes=["vn"])
            p.op("pool", lambda e: e.tensor_tensor(out=vn[:, :], in0=vn[:, :], in1=sg_g[:, :], op=ALU.mult),
                 reads=["vn", "sg_g"], writes=["vn"])
            p.op("pool", lambda e: e.tensor_tensor(out=vnb[:, :], in0=vn[:, :], in1=sg_b[:, :], op=ALU.add),
                 reads=["vn", "sg_b"], writes=["vnb"])
            for gg in range(4):
                p.op("pe", lambda e, gg=gg: e.matmul(mixps[:, gg * 64:(gg + 1) * 64], lhsT=wsT[:, gg, :],
                                                     rhs=vnb[:, gg * 64:(gg + 1) * 64], start=True, stop=True),
                     reads=["wsT", "vnb"], writes=["mixps"])
            for gg in range(4):
                p.op("dve", lambda e, gg=gg: e.scalar_tensor_tensor(out=yc[:, gg * 64:(gg + 1) * 64],
                                                                    in0=mixps[:, gg * 64:(gg + 1) * 64], scalar=bsT[:, gg:gg + 1],
                                                                    in1=zz[:, gg * 64:(gg + 1) * 64], op0=ALU.add, op1=ALU.mult),
                     reads=["mixps", "bsT", "zz"], writes=["yc"])
            p.op("pool", lambda e: e.tensor_tensor(out=junk[:, :], in0=yc[:, :], in1=yc[:, :], op=ALU.mult),
                 reads=["yc"], writes=["junk"])
            p.op("dve", lambda e, ti=ti: e.reduce_sum(out=G["ss"][:, 12, ti:ti + 1], in_=junk[:, :], axis=AX.X),
                 reads=["junk"], writes=["ss"])
            p.op("pool", lambda e: e.tensor_copy(ycb[:, :], yc[:, :]), reads=["yc"], writes=["ycb"])
            for c2 in range(2):
                tp = trps[c2]
                p.op("pe", lambda e, tp=tp, c2=c2: e.transpose(tp[:, 512:640], ycb[:, c2 * 128:(c2 + 1) * 128], ident[:]),
                     reads=["ycb", "ident"], writes=["tr%d" % c2])
                p.op("act", lambda e, tp=tp, c2=c2, j=j: e.copy(ycT[:, c2, j * 128:(j + 1) * 128], tp[:, 512:640]),
                     reads=["tr%d" % c2], writes=["ycT"])
        p.op("sp", lambda e, go=go: e.dma_start(out=io["YT"][768:1024, go * 512:(go + 1) * 512].rearrange("(c p) t -> p c t", p=128),
                                                in_=ycT[:, :, :]), reads=["ycT"], dma="ytc")
    dbg_dump(p, "xT", xTs[(ngo - 1) % 2][:, :, :], [128, 8, 512], BF16)
    dbg_dump(p, "KaT", KaT[:, :, :], [128, 3, OH], BF16)
    dbg_dump(p, "QaT", QaT[:, :, :], [128, 3, OWN], BF16)
    dbg_dump(p, "Va", Va[:, :, :, :], [128, OH // 128, 6, 65], BF16)
    dbg_dump(p, "CQ", CQ[:, :, :], [128, 2, OWN], BF16)
    dbg_dump(p, "w_in", w_in[:, :, :], [128, 8, PIN], BF16)
    p.pop()

    p.push()
    tab = p.sb("tab", [128, 6, TABW], BF16)
    NSA = 3
    Es = [p.sb("E%d" % i, [128, 1024], BF16) for i in range(NSA)]
    PTs = [p.sb("PT%d" % i, [128, 1024], BF16) for i in range(NSA)]
    Sps = [p.ps("S%d" % i, [128, 1024]) for i in range(NSA)]
    Ops = [p.ps("O%d" % i, [128, 512]) for i in range(2)]
    gic = [0]

    def alloc_s():
        i = gic[0] % NSA
        gic[0] += 1
        return Sps[i], "S%d" % i

    def peek_s():
        i = gic[0] % NSA
        return Sps[i], "S%d" % i
    bufs = dict(peek=peek_s, osb=[p.sb("osb%d" % i, [128, 512], F32) for i in range(2)], rden=p.sb("rden", [128, 512], F32),
                ysq=[p.sb("ysq%d" % i, [128, 512], F32) for i in range(2)], ybf=[p.sb("ybf%d" % i, [128, 512], BF16) for i in range(2)],
                alloc=alloc_s)
    for h in range(6):
        p.op("sp", lambda e, h=h: e.dma_start(out=tab[:, h, :], in_=io["mtab"][:, h, :]), writes=["tab"], dma="tab")
    it = 0
    pend = []
    pendb = []
    NKT2 = 20
    for c in range(3):
        hh = (2 * c, 2 * c + 1)
        for qb in range(NQB):
            kt0 = 4 * qb

            def qk(ii):
                S, sk = alloc_s()
                kt = kt0 + ii
                for u in range(2):
                    p.op("pe", lambda e, S=S, u=u, kt=kt: e.matmul(S[:, u * 512:(u + 1) * 512],
                                                                   lhsT=KaT[u * 64:u * 64 + 64, c, kt * 128:(kt + 1) * 128],
                                                                   rhs=QaT[u * 64:u * 64 + 64, c, qb * 512:(qb + 1) * 512],
                                                                   start=True, stop=True),
                         reads=["KaT", "QaT"], writes=[sk])
                return S, sk

            inflight = [qk(0), qk(1)]
            for ii in range(NKT2):
                if ii + 2 < NKT2:
                    inflight.append(qk(ii + 2))
                S, sk = inflight.pop(0)
                E, ek = Es[ii % NSA], "E%d" % (ii % NSA)
                PT, pk = PTs[ii % NSA], "PT%d" % (ii % NSA)
                kt = kt0 + ii
                start = 2432 - 128 * ii
                p.op("act", lambda e, S=S, E=E: e.activation(out=E[:, :], in_=S[:, :], func=AF.Exp, scale=A_SCALE),
                     reads=[sk], writes=[ek])
                for u in range(2):
                    p.op("dve", lambda e, E=E, PT=PT, u=u, start=start: e.tensor_tensor(
                        out=PT[:, u * 512:(u + 1) * 512], in0=E[:, u * 512:(u + 1) * 512],
                        in1=tab[:, hh[u], start:start + 512], op=ALU.mult), reads=[ek, "tab"], writes=[pk + "_%d" % u])
                for u in range(2):
                    p.op("pe", lambda e, PT=PT, u=u, kt=kt, ii=ii: e.matmul(
                        Ops[u][0:65, :], lhsT=Va[:, kt, hh[u], 0:65], rhs=PT[:, u * 512:(u + 1) * 512],
                        start=(ii == 0), stop=(ii == NKT2 - 1)),
                        reads=["Va", pk + "_%d" % u], writes=["O%d" % u])
                if ii == 3:
                    while pend:
                        pendb.append(pend.pop(0)())
                if ii == 9:
                    while pendb:
                        pendb.pop(0)()
            for u in range(2):
                pend.append(attn_post(p, G, io, Ops[u], "O%d" % u, hh[u] * 64, qb, hh[u], bufs, u))
            it += 1
    while pend:
        pendb.append(pend.pop(0)())
    while pendb:
        pendb.pop(0)()
    p.pop()
    p.pop()

    CKV = p.sb("CKV", [128, SF], BF16)
    KT = p.sb("KT", [128, SF], BF16)
    p.push()
    wB = p.sb("wB", [128, 8, 160], BF16)
    wk96 = p.sb("wk96", [128, 8, 96], BF16)
    wk96r = p.sb("wk96r", [128, 8, 96], BF16)
    xbs = [p.sb("bxb%d" % i, [128, 4, D], BF16) for i in range(2)]
    xTs = [p.sb("bxT%d" % i, [128, 8, 512], BF16) for i in range(2)]
    sqs = [p.sb("bsq%d" % i, [128, 512], F32) for i in range(2)]
    rrs = [p.sb("brr%d" % i, [128, 512], F32) for i in range(2)]
    cks = [p.sb("cks%d" % i, [128, 512], F32) for i in range(2)]
    sks = [p.sb("sks%d" % i, [128, 512], F32) for i in range(2)]
    t1s = [p.sb("bt1_%d" % i, [128, 512], F32) for i in range(2)]
    t2s_ = [p.sb("bt2_%d" % i, [128, 512], F32) for i in range(2)]
    trps = [p.ps("btrps%d" % i, [128, 1024], BF16) for i in range(2)]
    pps = [p.ps("bpps%d" % i, [128, 512]) for i in range(4)]
    ssqs = [p.ps("bssq%d" % i, [128, 512]) for i in range(2)]
    for kc in range(8):
        p.op("pool", lambda e, kc=kc: e.dma_start(out=wB[:, kc, :], in_=W["w_in"][kc * 128:(kc + 1) * 128, 1344:1504]),
             writes=["wB"], dma="wB")
    p.op("dve", lambda e: e.memset(wk96[:], 0.0), writes=["wk96"])
    p.op("dve", lambda e: e.memset(wk96r[:], 0.0), writes=["wk96r"])
    p.op("dve", lambda e: e.tensor_copy(wk96[:, :, 64:96], wB[:, :, 128:160]), reads=["wB", "wk96"], writes=["wk96"])
    p.op("dve", lambda e: e.tensor_copy(wk96r[:, :, 64:80], wB[:, :, 144:160]), reads=["wB", "wk96r"], writes=["wk96r"])
    p.op("dve", lambda e: e.tensor_copy(wk96r[:, :, 80:96], wB[:, :, 128:144]), reads=["wB", "wk96r"], writes=["wk96r"])
    ngf = cfg.NG_F
    load_x_group(p, src["full"](0), xbs[0], "bxb0", "bxb0")
    for g in range(ngf):
        xb, xbk = xbs[g % 2], "bxb%d" % (g % 2)
        xT, xTk = xTs[g % 2], "bxT%d" % (g % 2)
        ck, ckk = cks[g % 2], "cks%d" % (g % 2)
        sk_, skk = sks[g % 2], "sks%d" % (g % 2)
        if g + 1 < ngf:
            load_x_group(p, src["full"](g + 1), xbs[(g + 1) % 2], "bxb%d" % ((g + 1) % 2), "bxb%d" % ((g + 1) % 2))
        p.op("sp", lambda e, ck=ck, g=g: e.dma_start(out=ck[64:96, :], in_=io["ckt"][:, g * 512:(g + 1) * 512]),
             writes=[ckk], dma=ckk)
        p.op("sp", lambda e, sk_=sk_, g=g: e.dma_start(out=sk_[64:96, :], in_=io["skt"][:, g * 512:(g + 1) * 512]),
             writes=[skk], dma=skk)
        transpose_group(p, G, xb, xbk, xT, xTk, trps)
        sq, sqk = sqs[g % 2], "bsq%d" % (g % 2)
        rr, rrk = rrs[g % 2], "brr%d" % (g % 2)
        t1, t1k = t1s[g % 2], "bt1_%d" % (g % 2)
        t2, t2k = t2s_[g % 2], "bt2_%d" % (g % 2)
        ssq, ssqk = ssqs[g % 2], "bssq%d" % (g % 2)
        pa, pb, pc = pps[(3 * g) % 4], pps[(3 * g + 1) % 4], pps[(3 * g + 2) % 4]
        ka, kb, kc_ = "bpps%d" % ((3 * g) % 4), "bpps%d" % ((3 * g + 1) % 4), "bpps%d" % ((3 * g + 2) % 4)
        for kc in range(8):
            p.op("pe", lambda e, kc=kc, pa=pa: e.matmul(pa[:, :], lhsT=wB[:, kc, 0:128], rhs=xT[:, kc, :],
                                                        start=(kc == 0), stop=(kc == 7)), reads=["wB", xTk], writes=[ka])
        for kc in range(8):
            p.op("pe", lambda e, kc=kc, pb=pb: e.matmul(pb[0:96, :], lhsT=wk96[:, kc, :], rhs=xT[:, kc, :],
                                                        start=(kc == 0), stop=(kc == 7)), reads=["wk96", xTk], writes=[kb])
        for kc in range(8):
            p.op("pe", lambda e, kc=kc, pc=pc: e.matmul(pc[0:96, :], lhsT=wk96r[:, kc, :], rhs=xT[:, kc, :],
                                                        start=(kc == 0), stop=(kc == 7)), reads=["wk96r", xTk], writes=[kc_])
        p.op("act", lambda e, pa=pa: e.activation(out=sq[:, :], in_=pa[:, :], func=AF.Square), reads=[ka], writes=[sqk])
        p.op("pe", lambda e: e.matmul(ssq[:, :], lhsT=G["ones"][:, :], rhs=sq[:, :], start=True, stop=True),
             reads=[sqk, "ones"], writes=[ssqk])
        p.op("act", lambda e: e.activation(out=rr[:, :], in_=ssq[:, :], func=AF.Sqrt, scale=1.0 / 128, bias=EPS),
             reads=[ssqk], writes=[rrk])
        p.op("dve", lambda e: e.reciprocal(rr[:, :], rr[:, :]), reads=[rrk], writes=[rrk])
        p.op("dve", lambda e, pa=pa, g=g: e.tensor_tensor(out=CKV[:, g * 512:(g + 1) * 512], in0=pa[:, :], in1=rr[:, :],
                                                          op=ALU.mult), reads=[ka, rrk], writes=["CKV"])
        p.op("dve", lambda e, pb=pb, ck=ck: e.tensor_tensor(out=t1[64:96, :], in0=pb[64:96, :], in1=ck[64:96, :], op=ALU.mult),
             reads=[kb, ckk], writes=[t1k])
        p.op("dve", lambda e, pc=pc, sk_=sk_: e.tensor_tensor(out=t2[64:96, :], in0=pc[64:96, :], in1=sk_[64:96, :], op=ALU.mult),
             reads=[kc_, skk], writes=[t2k])
        p.op("pool", lambda e, g=g: e.tensor_tensor(out=KT[64:96, g * 512:(g + 1) * 512], in0=t1[64:96, :], in1=t2[64:96, :],
                                                    op=ALU.add), reads=[t1k, t2k], writes=["KT"])
    p.pop()

    p.push()
    Vh = p.sb("Vh", [128, SF // 128, 65], BF16)
    QT = p.sb("QT", [128, OWN], BF16)
    cqt = p.sb("cqt", [128, OWN], F32)
    sqt = p.sb("sqt", [128, OWN], F32)
    wkv = p.sb("wkv", [128, 768], BF16)
    wkvf = p.sb("wkvf", [128, 768], F32)
    wqf = p.sb("wqf", [128, 2, 576], F32)
    wq = p.sb("wq", [128, 2, 6, 96], BF16)
    wqr = p.sb("wqr", [128, 2, 6, 96], BF16)
    qn = p.sb("qn", [128, 2], F32)
    kvn = p.sb("kvn", [128, 1], F32)
    q1 = p.sb("q1", [128, 512], F32)
    q2 = p.sb("q2", [128, 512], F32)
    NSB = 3
    PTs = [p.sb("bPT%d" % i, [128, 1024], BF16) for i in range(NSB)]
    Sps = [p.ps("bS%d" % i, [128, 1024]) for i in range(NSB)]
    Ops = [p.ps("bO%d" % i, [128, 512]) for i in range(2)]
    pend = []
    pendb = []
    gib = [0]

    def alloc_sb():
        i = gib[0] % NSB
        gib[0] += 1
        return Sps[i], "bS%d" % i

    def peek_sb():
        i = gib[0] % NSB
        return Sps[i], "bS%d" % i
    bufs = dict(peek=peek_sb, osb=[p.sb("bosb%d" % i, [128, 512], F32) for i in range(2)], rden=p.sb("brden", [128, 512], F32),
                ysq=[p.sb("bysq%d" % i, [128, 512], F32) for i in range(2)], ybf=[p.sb("bybf%d" % i, [128, 512], BF16) for i in range(2)],
                alloc=alloc_sb)
    p.op("sp", lambda e: e.dma_start(out=wkvf[:], in_=W["w_kv_up"]), writes=["wkvf"], dma="bw")
    p.op("sp", lambda e: e.dma_start(out=wqf[:, 0, :], in_=W["w_q_up"][0:128, :]), writes=["wqf"], dma="bw")
    p.op("sp", lambda e: e.dma_start(out=wqf[0:64, 1, :], in_=W["w_q_up"][128:192, :]), writes=["wqf"], dma="bw")
    p.op("sp", lambda e: e.dma_start(out=qn[:], in_=W["qn"]), writes=["qn"], dma="bw")
    p.op("sp", lambda e: e.dma_start(out=kvn[:], in_=W["kvn"]), writes=["kvn"], dma="bw")
    p.op("sp", lambda e: e.dma_start(out=cqt[64:96, :], in_=io["cqt"]), writes=["cqt"], dma="bw")
    p.op("sp", lambda e: e.dma_start(out=sqt[64:96, :], in_=io["sqt"]), writes=["sqt"], dma="bw")
    p.op("dve", lambda e: e.tensor_scalar(wkv[:, :], wkvf[:, :], kvn[:, 0:1], None, op0=ALU.mult), reads=["wkvf", "kvn"],
         writes=["wkv"])
    p.op("dve", lambda e: e.memset(wq[:], 0.0), writes=["wq"])
    p.op("dve", lambda e: e.memset(wqr[:], 0.0), writes=["wqr"])
    for cc, np_ in ((0, 128), (1, 64)):
        wv = wqf[0:np_, cc, :].rearrange("p (h c) -> p h c", h=6)
        p.op("dve", lambda e, cc=cc, np_=np_, wv=wv: e.tensor_scalar(wq[0:np_, cc, :, :], wv, qn[0:np_, cc:cc + 1], None,
                                                                      op0=ALU.mult), reads=["wqf", "qn", "wq"], writes=["wq"])
        p.op("dve", lambda e, cc=cc, np_=np_, wv=wv: e.tensor_scalar(wqr[0:np_, cc, :, 64:80], wv[:, :, 80:96],
                                                                      qn[0:np_, cc:cc + 1], None, op0=ALU.mult),
             reads=["wqf", "qn", "wqr"], writes=["wqr"])
        p.op("dve", lambda e, cc=cc, np_=np_, wv=wv: e.tensor_scalar(wqr[0:np_, cc, :, 80:96], wv[:, :, 64:80],
                                                                      qn[0:np_, cc:cc + 1], None, op0=ALU.mult),
             reads=["wqf", "qn", "wqr"], writes=["wqr"])
    p.op("pool", lambda e: e.memset(Vh[:, :, 64:65], 1.0), writes=["Vh"])
    it = 0
    gi = 0
    nkt = SF // 128
    for h in range(6):
        for g2 in range(SF // 1024):
            S, sk = alloc_sb()
            for u in range(2):
                g = 2 * g2 + u
                p.op("pe", lambda e, S=S, u=u, g=g, h=h: e.matmul(S[0:64, u * 512:(u + 1) * 512], lhsT=wkv[:, h * 128:h * 128 + 64],
                                                                  rhs=CKV[:, g * 512:(g + 1) * 512], start=True, stop=True),
                     reads=["wkv", "CKV"], writes=[sk])
            p.op("dve", lambda e, S=S, g2=g2: e.tensor_copy(KT[0:64, g2 * 1024:(g2 + 1) * 1024], S[0:64, :]),
                 reads=[sk], writes=["KT"])
        for t16 in range(nkt // 16):
            S, sk = alloc_sb()
            for u in range(16):
                t = t16 * 16 + u
                p.op("pe", lambda e, S=S, u=u, t=t, h=h: e.matmul(S[:, u * 64:(u + 1) * 64], lhsT=CKV[:, t * 128:(t + 1) * 128],
                                                                  rhs=wkv[:, h * 128 + 64:h * 128 + 128], start=True, stop=True),
                     reads=["wkv", "CKV"], writes=[sk])
            p.op("dve", lambda e, S=S, t16=t16: e.tensor_copy(Vh[:, t16 * 16:(t16 + 1) * 16, 0:64],
                                                               S[:, :].rearrange("p (t c) -> p t c", c=64)),
                 reads=[sk], writes=["Vh"])
        for qb in range(NQB):
            S, sk = alloc_sb()
            for u, wsrc in ((0, wq), (1, wqr)):
                p.op("pe", lambda e, S=S, u=u, wsrc=wsrc, h=h, qb=qb: e.matmul(S[0:96, u * 512:(u + 1) * 512], lhsT=wsrc[:, 0, h, :],
                                                                               rhs=CQ[:, 0, qb * 512:(qb + 1) * 512],
                                                                               start=True, stop=False),
                     reads=["wq", "wqr", "CQ"], writes=[sk])
                p.op("pe", lambda e, S=S, u=u, wsrc=wsrc, h=h, qb=qb: e.matmul(S[0:96, u * 512:(u + 1) * 512], lhsT=wsrc[0:64, 1, h, :],
                                                                               rhs=CQ[0:64, 1, qb * 512:(qb + 1) * 512],
                                                                               start=False, stop=True),
                     reads=["wq", "wqr", "CQ"], writes=[sk])
            p.op("dve", lambda e, S=S, qb=qb: e.tensor_copy(QT[0:64, qb * 512:(qb + 1) * 512], S[0:64, 0:512]),
                 reads=[sk], writes=["QT"])
            p.op("dve", lambda e, S=S, qb=qb: e.tensor_tensor(out=q1[64:96, :], in0=S[64:96, 0:512],
                                                              in1=cqt[64:96, qb * 512:(qb + 1) * 512], op=ALU.mult),
                 reads=[sk, "cqt"], writes=["q1"])
            p.op("dve", lambda e, S=S, qb=qb: e.tensor_tensor(out=q2[64:96, :], in0=S[64:96, 512:1024],
                                                              in1=sqt[64:96, qb * 512:(qb + 1) * 512], op=ALU.mult),
                 reads=[sk, "sqt"], writes=["q2"])
            p.op("pool", lambda e, qb=qb: e.tensor_tensor(out=QT[64:96, qb * 512:(qb + 1) * 512], in0=q1[64:96, :],
                                                          in1=q2[64:96, :], op=ALU.add), reads=["q1", "q2"], writes=["QT"])
        for qb in range(NQB):
            ops, okey = Ops[it % 2], "bO%d" % (it % 2)
            ngrp = nkt // 2

            def qk(i):
                S, sk = alloc_sb()
                for u in range(2):
                    kt = 2 * i + u
                    p.op("pe", lambda e, S=S, u=u, kt=kt: e.matmul(S[:, u * 512:(u + 1) * 512],
                                                                   lhsT=KT[0:96, kt * 128:(kt + 1) * 128],
                                                                   rhs=QT[0:96, qb * 512:(qb + 1) * 512], start=True, stop=True),
                         reads=["KT", "QT"], writes=[sk])
                return S, sk

            inflight = [qk(0), qk(1)]
            for i in range(ngrp):
                if i + 2 < ngrp:
                    inflight.append(qk(i + 2))
                S, sk = inflight.pop(0)
                PT, pk = PTs[i % NSB], "bPT%d" % (i % NSB)
                p.op("act", lambda e, S=S, PT=PT: e.activation(out=PT[:, :], in_=S[:, :], func=AF.Exp, scale=B_SCALE),
                     reads=[sk], writes=[pk])
                for u in range(2):
                    kt = 2 * i + u
                    p.op("pe", lambda e, PT=PT, u=u, kt=kt, ops=ops, i=i: e.matmul(
                        ops[0:65, :], lhsT=Vh[:, kt, 0:65], rhs=PT[:, u * 512:(u + 1) * 512],
                        start=(i == 0 and u == 0), stop=(i == ngrp - 1 and u == 1)),
                        reads=["Vh", pk], writes=[okey])
                if i == 3 and pend:
                    pendb.append(pend.pop()())
                if i == 9 and pendb:
                    pendb.pop()()
            pend.append(attn_post(p, G, io, ops, okey, 384 + h * 64, qb, 6 + h, bufs, it))
            it += 1
        while pend:
            pendb.append(pend.pop()())
        while pendb:
            pendb.pop()()
    p.pop()
    p.pop()

    p.push()
    wf1 = p.sb("wf1", [128, 8, DFF], BF16)
    wf2 = p.sb("wf2", [128, 32, D], BF16)
    p.push()
    wo = p.sb("wo", [128, 8, D], BF16)
    wofs = [p.sb("wof%d" % i, [128, D], F32) for i in range(2)]
    mixn = p.sb("mixn", [128, 8], F32)
    g1 = p.sb("g1", [128, D], F32)
    b1 = p.sb("b1", [128, D], F32)
    yTs = [p.sb("yT%d" % i, [128, 8, 512], BF16) for i in range(2)]
    xrs = [p.sb("xr%d" % i, [128, D], F32) for i in range(2)]
    accs = [p.sb("dacc%d" % i, [128, D], F32) for i in range(2)]
    x1s = [p.sb("x1s%d" % i, [128, D], F32) for i in range(2)]
    rst = p.sb("rst", [128, 3, NT], F32)
    sst = p.sb("sst", [128, 3, NT], F32)
    st12s = [p.sb("st12_%d" % i, [128, 12], F32) for i in range(2)]
    mvs = [p.sb("dmv%d" % i, [128, 2], F32) for i in range(2)]
    rss = [p.sb("drs%d" % i, [128, 1], F32) for i in range(2)]
    accps = [p.ps("accps%d" % i, [128, 1024]) for i in range(2)]
    p.op("sp", lambda e: e.dma_start(out=mixn[:], in_=W["mixn"]), writes=["mixn"], dma="dw")
    p.op("sp", lambda e: e.dma_start(out=g1[:], in_=W["ln1_g"].partition_broadcast(128)), writes=["g1"], dma="dw")
    p.op("sp", lambda e: e.dma_start(out=b1[:], in_=W["ln1_b"].partition_broadcast(128)), writes=["b1"], dma="dw")
    for kc in range(8):
        wf_, wfk = wofs[kc % 2], "wof%d" % (kc % 2)
        p.op("sp", lambda e, kc=kc: e.dma_start(out=wf_[:], in_=W["w_out"][kc * 128:(kc + 1) * 128, :]), writes=[wfk], dma=wfk)
        p.op("dve", lambda e, kc=kc: e.tensor_scalar(wo[:, kc, :], wf_[:, :], mixn[:, kc:kc + 1], None, op0=ALU.mult),
             reads=[wfk, "mixn"], writes=["wo"])

    def ffn_weight_prefetch():
        for kc in range(8):
            p.op("pool", lambda e, kc=kc: e.dma_start(out=wf1[:, kc, :], in_=W["w_ff1"][kc * 128:(kc + 1) * 128, :]),
                 writes=["wf1"], dma="wf1")
        for c4 in range(8):
            p.op("pool", lambda e, c4=c4: e.dma_start(out=wf2[:, c4 * 4:(c4 + 1) * 4, :],
                                                      in_=W["w_ff2"][c4 * 512:(c4 + 1) * 512, :].rearrange("(c p) d -> p c d", p=128)),
                 writes=["wf2"], dma="wf2")
    ssv = G["ss"]
    p.op("dve", lambda e: e.tensor_copy(sst[:, 0, :], ssv[:, 0, :]), reads=["ss"], writes=["sst"])
    p.op("dve", lambda e: e.tensor_copy(sst[:, 1, :], ssv[:, 6, :]), reads=["ss"], writes=["sst"])
    p.op("dve", lambda e: e.tensor_copy(sst[:, 2, :], ssv[:, 12, :]), reads=["ss"], writes=["sst"])
    for k in range(1, 6):
        p.op("dve", lambda e, k=k: e.tensor_tensor(out=sst[:, 0, :], in0=sst[:, 0, :], in1=ssv[:, k, :], op=ALU.add),
             reads=["ss", "sst"], writes=["sst"])
        p.op("dve", lambda e, k=k: e.tensor_tensor(out=sst[:, 1, :], in0=sst[:, 1, :], in1=ssv[:, 6 + k, :], op=ALU.add),
             reads=["ss", "sst"], writes=["sst"])
    for gidx, wdt in ((0, 384.0), (1, 384.0), (2, 256.0)):
        p.op("act", lambda e, gidx=gidx, wdt=wdt: e.activation(out=rst[:, gidx, :], in_=sst[:, gidx, :], func=AF.Sqrt,
                                                               scale=1.0 / wdt, bias=EPS), reads=["sst"], writes=["rst"])
    p.op("dve", lambda e: e.reciprocal(rst[:, :, :], rst[:, :, :]), reads=["rst"], writes=["rst"])
    KCG = ((0, 3), (3, 6), (6, 8))
    ai = 0
    for g in range(NQB):
        yT, yk = yTs[g % 2], "yT%d" % (g % 2)
        p.op("sp", lambda e, yT=yT, g=g: e.dma_start(out=yT[:, :, :], in_=io["YT"][:, g * 512:(g + 1) * 512].rearrange("(c p) t -> p c t", p=128)),
             writes=[yk], dma=yk)
        if g == 1:
            ffn_weight_prefetch()
        for j in range(4):
            ti = g * 4 + j
            xr, xk = xrs[ti % 2], "xr%d" % (ti % 2)
            x1, x1k = x1s[ti % 2], "x1s%d" % (ti % 2)
            acc, acck = accs[ti % 2], "dacc%d" % (ti % 2)
            p.op("sp", lambda e, xr=xr, ti=ti: e.dma_start(out=xr[:, :], in_=src["res"](ti)),
                 writes=[xk], dma=xk)
            for gidx, (k0, k1) in enumerate(KCG):
                ap_, ak = accps[ai % 2], "accps%d" % (ai % 2)
                ai += 1
                for half in range(2):
                    for kc in range(k0, k1):
                        p.op("pe", lambda e, ap_=ap_, half=half, kc=kc, k0=k0, k1=k1, j=j: e.matmul(
                            ap_[:, half * 512:(half + 1) * 512], lhsT=yT[:, kc, j * 128:(j + 1) * 128],
                            rhs=wo[:, kc, half * 512:(half + 1) * 512], start=(kc == k0), stop=(kc == k1 - 1)),
                            reads=[yk, "wo"], writes=[ak])
                if gidx == 0:
                    p.op("dve", lambda e, ap_=ap_, ti=ti: e.tensor_scalar(acc[:, :], ap_[:, :], rst[:, 0, ti:ti + 1], None, op0=ALU.mult),
                         reads=[ak, "rst"], writes=[acck])
                else:
                    p.op("dve", lambda e, ap_=ap_, ti=ti, gidx=gidx: e.scalar_tensor_tensor(
                        out=acc[:, :], in0=ap_[:, :], scalar=rst[:, gidx, ti:ti + 1], in1=acc[:, :], op0=ALU.mult, op1=ALU.add),
                        reads=[ak, "rst", acck], writes=[acck])
            p.op("dve", lambda e, xr=xr: e.scalar_tensor_tensor(out=acc[:, :], in0=xr[:, :], scalar=ALPHA, in1=acc[:, :],
                                                                op0=ALU.mult, op1=ALU.add), reads=[xk, acck], writes=[acck])
            layer_norm(p, acc, acck, x1, x1k, g1, "g1", b1, "b1", st12s[ti % 2], mvs[ti % 2], rss[ti % 2], "d1_%d" % (ti % 2))
            p.op("sp", lambda e, x1=x1, ti=ti: e.dma_start(out=io["X1"][ti * 128:(ti + 1) * 128, :], in_=x1[:, :]),
                 reads=[x1k], dma="x1o%d" % (ti % 2))
    p.pop()

    p.push()
    b1T = p.sb("b1T", [128, 32], F32)
    g2 = p.sb("g2", [128, D], F32)
    b2 = p.sb("b2", [128, D], F32)
    bf2 = p.sb("bf2", [128, D], F32)
    x1f = [p.sb("x1f%d" % i, [128, 2, D], F32) for i in range(2)]
    x1b = p.sb("x1b", [128, 2, D], BF16)
    x1Ts = [p.sb("x1T%d" % i, [128, 8, 256], BF16) for i in range(2)]
    hidT = p.sb("hidT", [128, 32, 256], BF16)
    rl = [p.sb("rl%d" % i, [128, 256], F32) for i in range(2)]
    t2ss = [p.sb("t2s%d" % i, [128, D], F32) for i in range(1)]
    outs = [p.sb("outs%d" % i, [128, D], F32) for i in range(2)]
    outb = [p.sb("outb%d" % i, [128, D], BF16) for i in range(2)]
    st12s = [p.sb("st12b%d" % i, [128, 12], F32) for i in range(2)]
    mvs = [p.sb("dmv2_%d" % i, [128, 2], F32) for i in range(2)]
    rss = [p.sb("drs2_%d" % i, [128, 1], F32) for i in range(2)]
    trps = [p.ps("dtrps%d" % i, [128, 1024], BF16) for i in range(2)]
    hps = [p.ps("hps%d" % i, [128, 512]) for i in range(2)]
    fps = [p.ps("fps%d" % i, [128, 1024]) for i in range(2)]
    p.op("sp", lambda e: e.dma_start(out=b1T[:], in_=W["b1T"]), writes=["b1T"], dma="dw2")
    p.op("sp", lambda e: e.dma_start(out=g2[:], in_=W["ln2_g"].partition_broadcast(128)), writes=["g2"], dma="dw2")
    p.op("sp", lambda e: e.dma_start(out=b2[:], in_=W["ln2_b"].partition_broadcast(128)), writes=["b2"], dma="dw2")
    p.op("sp", lambda e: e.dma_start(out=bf2[:], in_=W["b_ff2"].partition_broadcast(128)), writes=["bf2"], dma="dw2")
    ng2 = OWN // 256

    def prep(g):
        xf_, xfk = x1f[g % 2], "x1f%d" % (g % 2)
        p.op("sp", lambda e: e.dma_start(out=xf_[:, :, :], in_=io["X1"][g * 256:(g + 1) * 256, :].rearrange("(j p) d -> p j d", p=128)),
             writes=[xfk], dma=xfk)
        p.op("pool", lambda e: e.tensor_copy(x1b[:, :, :], xf_[:, :, :]), reads=[xfk], writes=["x1b"])
        transpose_group(p, G, x1b, "x1b", x1Ts[g % 2], "x1T%d" % (g % 2), trps, n_tok_tiles=2)

    prep(0)
    hi = 0
    fi = 0
    for g in range(ng2):
        xf_, xfk = x1f[g % 2], "x1f%d" % (g % 2)
        x1T, x1Tk = x1Ts[g % 2], "x1T%d" % (g % 2)
        for fc in range(32):
            hp_, hk = hps[hi % 2], "hps%d" % (hi % 2)
            r_, rk = rl[hi % 2], "rl%d" % (hi % 2)
            hi += 1
            for kc in range(8):
                p.op("pe", lambda e, hp_=hp_, kc=kc, fc=fc: e.matmul(hp_[:, 0:256], lhsT=wf1[:, kc, fc * 128:(fc + 1) * 128],
                                                                     rhs=x1T[:, kc, :], start=(kc == 0), stop=(kc == 7)),
                     reads=["wf1", x1Tk], writes=[hk])
            p.op("act", lambda e, hp_=hp_, r_=r_, fc=fc: e.activation(out=r_[:, :], in_=hp_[:, 0:256], func=AF.Relu,
                                                                      bias=b1T[:, fc:fc + 1], scale=1.0),
                 reads=[hk, "b1T"], writes=[rk])
            p.op("dve", lambda e, r_=r_, fc=fc: e.tensor_tensor(out=hidT[:, fc, :], in0=r_[:, :], in1=r_[:, :], op=ALU.mult),
                 reads=[rk], writes=["hidT"])
        if g + 1 < ng2:
            prep(g + 1)
        for j in range(2):
            ti = g * 2 + j
            fp_, fk = fps[fi % 2], "fps%d" % (fi % 2)
            o_, ok_ = outs[fi % 2], "outs%d" % (fi % 2)
            t2s, t2k = t2ss[0], "t2s0"
            par = fi % 2
            fi += 1
            for half in range(2):
                for fc in range(32):
                    p.op("pe", lambda e, fp_=fp_, half=half, fc=fc, j=j: e.matmul(
                        fp_[:, half * 512:(half + 1) * 512], lhsT=hidT[:, fc, j * 128:(j + 1) * 128],
                        rhs=wf2[:, fc, half * 512:(half + 1) * 512], start=(fc == 0), stop=(fc == 31)),
                        reads=["hidT", "wf2"], writes=[fk])
            p.op("dve", lambda e, fp_=fp_: e.tensor_tensor(out=t2s[:, :], in0=fp_[:, :], in1=bf2[:, :], op=ALU.add),
                 reads=[fk, "bf2"], writes=[t2k])
            p.op("dve", lambda e, xf_=xf_, j=j: e.scalar_tensor_tensor(out=t2s[:, :], in0=xf_[:, j, :], scalar=ALPHA, in1=t2s[:, :],
                                                                       op0=ALU.mult, op1=ALU.add), reads=[xfk, t2k], writes=[t2k])
            layer_norm(p, t2s, t2k, o_, ok_, g2, "g2", b2, "b2", st12s[par], mvs[par], rss[par], "d2_%d" % par)
            p.op("sp", lambda e, o_=o_, ti=ti: e.dma_start(out=dst["f32"][ti * 128:(ti + 1) * 128, :], in_=o_[:, :]),
                 reads=[ok_], dma=dst["tag"])
            if dst["bf16"] is not None:
                ob_, obk = outb[par], "outb%d" % par
                p.op("pool", lambda e, o_=o_, ob_=ob_: e.tensor_copy(ob_[:, :], o_[:, :]), reads=[ok_], writes=[obk])
                ch = ti // 4
                p.op("sp", lambda e, ob_=ob_, ti=ti: e.dma_start(out=dst["bf16"][ti * 128:(ti + 1) * 128, :], in_=ob_[:, :]),
                     reads=[obk], writes=["XBc%d" % ch], dma="xbc%d" % (ch % 2))
                if ti % 4 == 3:
                    ngrp_ = cfg.SF // OWN
                    groups_ = [list(range(b_ * ngrp_, (b_ + 1) * ngrp_)) for b_ in range(cfg.ncores // ngrp_)]
                    p.op("pool", lambda e, ch=ch: e.collective_compute("AllGather", ALU.bypass, replica_groups=groups_,
                                                                       ins=[io["XBc"][ch * 512:(ch + 1) * 512, :].opt()],
                                                                       outs=[io["XG"][ch].opt()]),
                         reads=["XBc%d" % ch], dma="cc", inc=1)
    p.pop()
    p.pop()


def layer_norm(p, src, skey, dstt, dkey, g, gk, b, bk, st12, mv, rs, tg):
    k6, kmv, krs = "st12" + tg, "mv" + tg, "rs" + tg
    p.op("dve", lambda e: e.bn_stats(st12[:, 0:6], src[:, 0:512]), reads=[skey], writes=[k6])
    p.op("dve", lambda e: e.bn_stats(st12[:, 6:12], src[:, 512:1024]), reads=[skey], writes=[k6])
    p.op("dve", lambda e: e.bn_aggr(mv[:, :], st12[:, :]), reads=[k6], writes=[kmv])
    p.op("act", lambda e: e.activation(out=rs[:, :], in_=mv[:, 1:2], func=AF.Sqrt, scale=1.0, bias=EPS), reads=[kmv], writes=[krs])
    p.op("dve", lambda e: e.reciprocal(rs[:, :], rs[:, :]), reads=[krs], writes=[krs])
    p.op("dve", lambda e: e.tensor_scalar(src[:, :], src[:, :], mv[:, 0:1], rs[:, 0:1], op0=ALU.subtract, op1=ALU.mult),
         reads=[skey, kmv, krs], writes=[skey])
    p.op("pool", lambda e: e.tensor_tensor(out=src[:, :], in0=src[:, :], in1=g[:, :], op=ALU.mult), reads=[skey, gk], writes=[skey])
    p.op("pool", lambda e: e.tensor_tensor(out=dstt[:, :], in0=src[:, :], in1=b[:, :], op=ALU.add), reads=[skey, bk], writes=[dkey])


_CACHE = {}


def host_weights(inp, layers):
    f = lambda a: np.ascontiguousarray(a, dtype=np.float32)
    L = list(layers)
    w = {}
    w["w_in"] = f(inp["w_in"][L])
    w["w_q_up"] = f(inp["w_q_up"][L])
    w["w_kv_up"] = f(inp["w_kv_up"][L])
    qn = np.zeros((len(L), 256), np.float32)
    qn[:, :192] = inp["q_norm"][L]
    w["qn"] = f(qn.reshape(len(L), 2, 128).transpose(0, 2, 1))
    w["kvn"] = f(inp["kv_norm"][L].reshape(len(L), 128, 1))
    w["sg_g"] = f(inp["sgu_ln_g"][L])
    w["sg_b"] = f(inp["sgu_ln_b"][L])
    w["sg_wT"] = f(np.transpose(inp["sgu_w"][L], (0, 1, 3, 2)))
    w["sg_bT"] = f(np.transpose(inp["sgu_b"][L], (0, 2, 1)))
    w["mixn"] = f(inp["mix_norm"][L].reshape(len(L), 8, 128).transpose(0, 2, 1))
    w["w_out"] = f(inp["w_out"][L])
    w["ln1_g"] = f(inp["ln1_g"][L])
    w["ln1_b"] = f(inp["ln1_b"][L])
    w["w_ff1"] = f(inp["w_ff1"][L])
    w["b1T"] = f(inp["b_ff1"][L].reshape(len(L), 32, 128).transpose(0, 2, 1))
    w["w_ff2"] = f(inp["w_ff2"][L])
    w["b_ff2"] = f(inp["b_ff2"][L])
    w["ln2_g"] = f(inp["ln2_g"][L])
    w["ln2_b"] = f(inp["ln2_b"][L])
    return w


def core_inputs(x_b, r, own, consts):
    S = x_b.shape[0]
    lo = r * own - HALO
    xo = np.zeros((own + 2 * HALO, D), np.float32)
    vm = np.zeros((own + 2 * HALO,), np.float32)
    a, b = max(lo, 0), min(lo + own + 2 * HALO, S)
    xo[a - lo:b - lo] = x_b[a:b]
    vm[a - lo:b - lo] = 1.0
    ct, st = consts["rope"]
    hs = np.zeros((128, 8), np.float32)
    if r - 1 >= 0:
        hs[:, r - 1] = 1.0
    if r + 1 < S // own:
        hs[:, 4 + r + 1] = 1.0
    d = dict(hsel=hs, xo=xo, xf=np.ascontiguousarray(x_b, dtype=np.float32),
             vmask=np.ascontiguousarray(vm.reshape(-1, 128).T), mtab=consts["mtab"],
             ckt=ct, skt=st, cqt=np.ascontiguousarray(ct[:, r * own:(r + 1) * own]),
             sqt=np.ascontiguousarray(st[:, r * own:(r + 1) * own]))
    return d


def run_layers(x, inp, own, n_groups_per_batch, fused_depth=1, dbg=False):
    B, S, _ = x.shape
    key = (own, S, fused_depth, dbg, B)
    if key not in _CACHE:
        _CACHE[key] = build_program(Cfg(own, S, depth=fused_depth, dbg=dbg, ncores=B * n_groups_per_batch))
    nc, stats = _CACHE[key]
    consts = dict(mtab=mask_table(), rope=rope_tables(S))
    depth = inp["w_in"].shape[0]
    cur = np.asarray(x, dtype=np.float32)
    extra = None
    for l0 in range(0, depth, fused_depth):
        w = host_weights(inp, range(l0, l0 + fused_depth))
        in_maps = []
        for c in range(B * n_groups_per_batch):
            b, r = c // n_groups_per_batch, c % n_groups_per_batch
            d = core_inputs(cur[b], r, own, consts)
            if fused_depth == 1:
                d.pop("hsel")
            d.update(w)
            in_maps.append(d)
        res = run_bass_kernel_spmd(nc, in_maps, core_ids=list(range(len(in_maps))))
        outs = [r_["out"] for r_ in res.results]
        cur = np.stack([np.concatenate(outs[b * n_groups_per_batch:(b + 1) * n_groups_per_batch], 0) for b in range(B)], 0)
        extra = res.results
    return cur, extra


def kernel(**inputs):
    x = np.asarray(inputs["x"], dtype=np.float32)
    inp = {k: np.asarray(v, dtype=np.float32) for k, v in inputs.items() if k != "x"}
    out, _ = run_layers(x, inp, own=x.shape[1] // 4, n_groups_per_batch=4, fused_depth=inp["w_in"].shape[0])
    return out.astype(np.float32)
```

```python
import types
import numpy as np
import ml_dtypes
import concourse.bass as bass
import concourse.mybir as mybir
from concourse.bass_utils import run_bass_kernel_spmd

F32 = mybir.dt.float32
BF16 = mybir.dt.bfloat16
I32 = mybir.dt.int32
AF = mybir.ActivationFunctionType
ALU = mybir.AluOpType
AX = mybir.AxisListType

ENGS = ("pe", "act", "dve", "pool", "sp")
SEM_LIMIT = 20000

D = 1024
PIN = 2016
HALO = 1024
DFF = 4096
EPS = 1e-5
ALPHA = float((2 * 2) ** 0.25)
TABW = 2944
A_SCALE = 0.125
B_SCALE = float(96 ** -0.5)
C_GELU = 0.044715
K_GELU = float(np.sqrt(2.0 / np.pi))


def _freeze(fn):
    if fn is None or fn.__closure__ is None:
        return fn
    cells = []
    for c in fn.__closure__:
        try:
            cells.append(types.CellType(c.cell_contents))
        except ValueError:
            cells.append(c)
    return types.FunctionType(fn.__code__, fn.__globals__, fn.__name__, fn.__defaults__, tuple(cells))


class Prog:
    def __init__(self, nc):
        self.nc = nc
        self.ops = {e: [] for e in ENGS}
        self.state = {}
        self.dma_sems = {}
        self.pending = {e: [] for e in ENGS}
        self._cms = []
        self._scopes = []

    def _reg(self, cm):
        t = cm.__enter__()
        (self._scopes[-1] if self._scopes else self._cms).append(cm)
        return t

    def sem(self, name):
        self._n = getattr(self, "_n", 0) + 1
        cm = self.nc.semaphore("m%d_%s" % (self._n, name))
        s = cm.__enter__()
        self._cms.append(cm)
        return s

    def sb(self, name, shape, dt):
        self._n = getattr(self, "_n", 0) + 1
        return self._reg(self.nc.sbuf_tensor("sb%d_%s" % (self._n, name), list(shape), dt))

    def ps(self, name, shape, dt=F32):
        self._n = getattr(self, "_n", 0) + 1
        return self._reg(self.nc.psum_tensor("ps%d_%s" % (self._n, name), list(shape), dt))

    def push(self):
        self._scopes.append([])

    def pop(self):
        self.barrier()
        for cm in reversed(self._scopes.pop()):
            cm.__exit__(None, None, None)

    def close(self):
        for cm in reversed(self._cms):
            cm.__exit__(None, None, None)
        self._cms = []

    def barrier(self):
        evs = []
        for e in ENGS:
            lst = self.ops[e]
            for i in range(len(lst) - 1, -1, -1):
                if lst[i]["dma"] is None and lst[i]["fn"] is not None:
                    evs.append(("eng", e, i))
                    break
        for tag, ds in self.dma_sems.items():
            evs.append(("dma", ds[0], ds[1]))
        for e in ENGS:
            self.pending[e].extend(evs)
        self.state = {}

    def op(self, eng, fn, reads=(), writes=(), dma=None, inc=16):
        fn = _freeze(fn)
        deps = []
        for k in reads:
            st = self.state.get(k)
            if st and st[0] is not None:
                deps.append((st[0], True))
        for k in writes:
            st = self.state.get(k)
            if st:
                if st[0] is not None:
                    deps.append((st[0], False))
                for r in st[1]:
                    deps.append((r, False))
        lst = self.ops[eng]
        idx = len(lst)
        if dma is not None:
            if dma not in self.dma_sems:
                self.dma_sems[dma] = [self.sem("d_" + dma), 0]
            ds = self.dma_sems[dma]
            ds[1] += inc
            ev = ("dma", ds[0], ds[1])
        else:
            ev = ("eng", eng, idx)
        fdeps = []
        for d, raw in deps:
            if d[0] == "eng" and d[1] == eng and dma is None:
                if eng == "pe" or not raw:
                    continue
            if d[0] == "dma":
                for ds_ in self.dma_sems.values():
                    if ds_[0] is d[1]:
                        cur = ds_[1] - (inc if (dma is not None and self.dma_sems[dma][0] is d[1]) else 0)
                        d = ("dma", d[1], max(d[2], cur))
            fdeps.append(d)
        for d in self.pending[eng]:
            if d[0] == "eng" and d[1] == eng and eng == "pe":
                continue
            fdeps.append(d)
        self.pending[eng] = []
        lst.append(dict(fn=fn, deps=fdeps, dma=dma, ev=ev, ms=False, inc=inc))
        for k in reads:
            st = self.state.setdefault(k, [None, []])
            st[1].append(ev)
        for k in writes:
            self.state[k] = [ev, []]
        return ev

    def wait_all_dma(self, eng, tags):
        deps = []
        for t in tags:
            ds = self.dma_sems[t]
            deps.append(("dma", ds[0], ds[1]))
        self.ops[eng].append(dict(fn=None, deps=deps, dma=None, ev=None, ms=False))

    def emit(self):
        nc = self.nc
        for e in ENGS:
            for o in self.ops[e]:
                for d in o["deps"]:
                    if d[0] == "eng":
                        self.ops[d[1]][d[2]]["ms"] = True
        msmap = {}
        for e in ENGS:
            cur = None
            cnt = 0
            for i, o in enumerate(self.ops[e]):
                if o["ms"]:
                    if cur is None or cnt >= SEM_LIMIT:
                        cur = self.sem("s_%s_%d" % (e, i))
                        cnt = 0
                    cnt += 1
                    msmap[(e, i)] = (cur, cnt)
                    o["inc"] = cur
        stats = {}

        def run(e, eng):
            waited = {}
            nw = 0
            for i, o in enumerate(self.ops[e]):
                for d in o["deps"]:
                    if d[0] == "eng":
                        sem, val = msmap[(d[1], d[2])]
                    else:
                        sem, val = d[1], d[2]
                    key = id(sem)
                    if waited.get(key, 0) >= val:
                        continue
                    waited[key] = val
                    eng.wait_ge(sem, val)
                    nw += 1
                if o["fn"] is None:
                    continue
                ins = o["fn"](eng)
                if o["dma"] is not None:
                    ins.then_inc(o["ev"][1], o.get("inc", 16))
                elif o["ms"]:
                    ins.then_inc(o["inc"], 1)
            stats[e] = (len(self.ops[e]), nw)

        with nc.Block() as block:
            @block.tensor
            def _(eng):
                run("pe", eng)

            @block.scalar
            def _(eng):
                run("act", eng)

            @block.vector
            def _(eng):
                run("dve", eng)

            @block.gpsimd
            def _(eng):
                run("pool", eng)

            @block.sync
            def _(eng):
                run("sp", eng)
        self.stats = stats
        return stats


def mask_table():
    slopes = (2.0 ** (-8.0 * np.arange(1, 7) / 6)).astype(np.float32)
    pp = np.arange(128)[:, None]
    col = np.arange(TABW)[None, :]
    delta = pp - col + 1408
    ad = np.abs(delta)
    c = (ad <= 64).astype(np.float32) + ((delta % 4 == 0) & (ad <= 256)).astype(np.float32) \
        + ((delta % 16 == 0) & (ad <= 1024)).astype(np.float32)
    tab = np.zeros((128, 6, TABW), np.float32)
    for h in range(6):
        tab[:, h, :] = c * np.exp(-(slopes[h] * ad.astype(np.float32)).astype(np.float32))
    return tab.astype(ml_dtypes.bfloat16)


def rope_tables(S):
    inv_freq = (10000.0 ** (-np.arange(0, 32, 2, dtype=np.float32) / 32)).astype(np.float32)
    ang = (np.arange(S, dtype=np.float32)[:, None] * inv_freq[None, :]).astype(np.float32)
    cos = np.cos(ang).astype(np.float32).T
    sin = np.sin(ang).astype(np.float32).T
    ct = np.concatenate([cos, cos], 0)
    st = np.concatenate([-sin, sin], 0)
    return np.ascontiguousarray(ct), np.ascontiguousarray(st)


class Cfg:
    def __init__(self, own, sf, depth=1, dbg=False, ncores=8):
        self.ncores = ncores
        self.OWN = own
        self.SF = sf
        self.OH = own + 2 * HALO
        self.NT = own // 128
        self.NQB = own // 512
        self.NG_OH = self.OH // 512
        self.NG_F = sf // 512
        self.NKT_OH = self.OH // 128
        self.NKT_F = sf // 128
        self.depth = depth
        self.dbg = dbg


W_NAMES = [
    ("w_in", [D, PIN]), ("w_q_up", [192, 576]), ("w_kv_up", [128, 768]), ("qn", [128, 2]), ("kvn", [128, 1]),
    ("sg_g", [256]), ("sg_b", [256]), ("sg_wT", [4, 128, 128]), ("sg_bT", [128, 4]), ("mixn", [128, 8]),
    ("w_out", [D, D]), ("ln1_g", [D]), ("ln1_b", [D]), ("w_ff1", [D, DFF]), ("b1T", [128, 32]),
    ("w_ff2", [DFF, D]), ("b_ff2", [D]), ("ln2_g", [D]), ("ln2_b", [D]),
]


def build_program(cfg):
    nc = bass.Bass("TRN2", target_bir_lowering=False)
    OWN, SF, OH, NT = cfg.OWN, cfg.SF, cfg.OH, cfg.NT
    io = {}
    io["xo"] = nc.dram_tensor("xo", [OH, D], F32, kind="ExternalInput").ap()
    io["xf"] = nc.dram_tensor("xf", [SF, D], F32, kind="ExternalInput").ap()
    io["vmask"] = nc.dram_tensor("vmask", [128, OH // 128], F32, kind="ExternalInput").ap()
    io["mtab"] = nc.dram_tensor("mtab", [128, 6, TABW], BF16, kind="ExternalInput").ap()
    io["ckt"] = nc.dram_tensor("ckt", [32, SF], F32, kind="ExternalInput").ap()
    io["skt"] = nc.dram_tensor("skt", [32, SF], F32, kind="ExternalInput").ap()
    io["cqt"] = nc.dram_tensor("cqt", [32, OWN], F32, kind="ExternalInput").ap()
    io["sqt"] = nc.dram_tensor("sqt", [32, OWN], F32, kind="ExternalInput").ap()
    for nm, shp in W_NAMES:
        io[nm] = nc.dram_tensor(nm, [cfg.depth] + shp, F32, kind="ExternalInput").ap()
    io["out"] = nc.dram_tensor("out", [OWN, D], F32, kind="ExternalOutput").ap()
    io["YT"] = nc.dram_tensor("YT", [D, OWN], BF16, kind="Internal").ap()
    io["X1"] = nc.dram_tensor("X1", [OWN, D], F32, kind="Internal").ap()
    if cfg.depth > 1:
        io["hsel"] = nc.dram_tensor("hsel", [128, 8], F32, kind="ExternalInput").ap()
        io["XL"] = nc.dram_tensor("XL", [OWN, D], F32, kind="Internal").ap()
        io["XBc"] = nc.dram_tensor("XBc", [OWN, D], BF16, kind="Internal").ap()
        io["XG"] = nc.dram_tensor("XG", [OWN // 512, (SF // OWN) * 512, D], BF16, kind="Internal").ap()
        io["XH"] = nc.dram_tensor("XH", [2 * HALO, D], BF16, kind="Internal").ap()
    if cfg.dbg:
        io["dbg_yt"] = nc.dram_tensor("dbg_yt", [D, OWN], BF16, kind="ExternalOutput").ap()
        io["dbg_ss"] = nc.dram_tensor("dbg_ss", [128, 13 * NT], F32, kind="ExternalOutput").ap()
        io["dbg_x1"] = nc.dram_tensor("dbg_x1", [OWN, D], F32, kind="ExternalOutput").ap()

    p = Prog(nc)
    DBG["on"] = cfg.dbg
    G = {}
    G["ident"] = p.sb("ident", [128, 128], BF16)
    identf = p.sb("identf", [128, 128], F32)
    G["e65"] = p.sb("e65", [128, 64], F32)
    G["ones"] = p.sb("onesf", [128, 128], F32)
    G["ss"] = p.sb("ss", [128, 13, NT], F32)
    p.op("pool", lambda e: e.memset(identf[:], 0.0), writes=["identf"])
    p.op("pool", lambda e: e.affine_select(out=identf[:], in_=identf[:], compare_op=ALU.not_equal, fill=1.0,
                                           base=0, pattern=[[-1, 128]], channel_multiplier=1),
         reads=["identf"], writes=["identf"])
    p.op("pool", lambda e: e.tensor_copy(G["ident"][:], identf[:]), reads=["identf"], writes=["ident"])
    p.op("pool", lambda e: e.memset(G["e65"][:], 0.0), writes=["e65"])
    p.op("pool", lambda e: e.memset(G["e65"][64:65, :], 1.0), reads=["e65"], writes=["e65"])
    p.op("pool", lambda e: e.memset(G["ones"][:], 1.0), writes=["ones"])
    p.barrier()

    ngo = cfg.NG_OH
    for l in range(cfg.depth):
        W = {nm: io[nm][l] for nm, _ in W_NAMES}
        last = (l == cfg.depth - 1)
        if l == 0:
            src = dict(oh=lambda g: io["xo"][g * 512:(g + 1) * 512, :],
                       full=lambda g: io["xf"][g * 512:(g + 1) * 512, :],
                       res=lambda ti: io["xo"][HALO + ti * 128:HALO + (ti + 1) * 128, :])
        else:
            def oh2(g):
                if g < 2:
                    return io["XH"][g * 512:(g + 1) * 512, :]
                if g >= ngo - 2:
                    return io["XH"][1024 + (g - (ngo - 2)) * 512:1024 + (g - (ngo - 2) + 1) * 512, :]
                return io["XL"][(g - 2) * 512:(g - 1) * 512, :]
            src = dict(oh=oh2, full=lambda g: xg_rows(io, cfg, g * 512, 512),
                       res=lambda ti: io["XL"][ti * 128:(ti + 1) * 128, :])
        if last:
            dst = dict(f32=io["out"], bf16=None, tag="out")
        else:
            dst = dict(f32=io["XL"], bf16=io["XBc"], tag="xl")
        layer(p, cfg, G, io, W, src, dst)
        if not last:
            exchange(p, cfg, G, io)

    if cfg.dbg:
        p.op("sp", lambda e: e.dma_start(out=io["dbg_yt"], in_=io["YT"]), reads=[], dma="out")
        p.op("sp", lambda e: e.dma_start(out=io["dbg_ss"], in_=G["ss"][:].rearrange("p a b -> p (a b)")), dma="out")
        p.op("sp", lambda e: e.dma_start(out=io["dbg_x1"], in_=io["X1"]), dma="out")
    p.wait_all_dma("sp", ["out"])
    stats = p.emit()
    p.close()
    return nc, stats


DBG = {}


def dbg_dump(p, name, ap, shape, dt):
    if not DBG.get("on"):
        return
    t = p.nc.dram_tensor("dd_" + name, list(shape), dt, kind="ExternalOutput").ap()
    p.barrier()
    p.op("sp", lambda e: e.dma_start(out=t, in_=ap), dma="out")
    p.barrier()


def load_x_group(p, src_ap, xb, key, tag):
    p.op("pool", lambda e: e.dma_start(out=xb[:], in_=src_ap.rearrange("(j p) d -> p j d", p=128)),
         writes=[key], dma=tag)


def transpose_group(p, G, xb, xbkey, xT, xTkey, trps, n_tok_tiles=4):
    for kc in range(8):
        tp = trps[kc % 2]
        tkey = "tr%d" % (kc % 2)
        for j in range(n_tok_tiles):
            p.op("pe", lambda e, tp=tp, j=j, kc=kc: e.transpose(tp[:, j * 128:(j + 1) * 128],
                                                                 xb[:, j, kc * 128:(kc + 1) * 128], G["ident"][:]),
                 reads=[xbkey, "ident"], writes=[tkey])
        w = n_tok_tiles * 128
        p.op("act", lambda e, tp=tp, kc=kc, w=w: e.copy(xT[:, kc, 0:w], tp[:, 0:w]), reads=[tkey], writes=[xTkey])


def attn_post(p, G, io, ops, okey, h_row, qb, ss_idx, bufs, it):
    r = it % 2
    osb, rden, ysq, ybf = bufs["osb"][r], bufs["rden"], bufs["ysq"][r], bufs["ybf"][r]
    ko, kr, ks, kb = "osb%d" % r, "rden", "ysq%d" % r, "ybf%d" % r
    p.op("dve", lambda e: e.tensor_copy(osb[0:65, :], ops[0:65, :]), reads=[okey], writes=[ko])

    def deferred():
        mp, mk = bufs["peek"]()
        denps, kd = mp[:, 0:512], mk
        p.op("pe", lambda e: e.matmul(denps[0:64, :], lhsT=G["e65"][0:65, :], rhs=osb[0:65, :], start=True, stop=True),
             reads=[ko, "e65"], writes=[kd])
        p.op("dve", lambda e: e.tensor_copy(rden[0:64, :], denps[0:64, :]), reads=[kd], writes=[kr])
        p.op("dve", lambda e: e.reciprocal(rden[0:64, :], rden[0:64, :]), reads=[kr], writes=[kr])
        p.op("pool", lambda e: e.tensor_tensor(out=osb[0:64, :], in0=osb[0:64, :], in1=rden[0:64, :], op=ALU.mult),
             reads=[ko, kr], writes=[ko])
        p.op("pool", lambda e: e.tensor_copy(ybf[0:64, :], osb[0:64, :]), reads=[ko], writes=[kb])
        p.op("sp", lambda e: e.dma_start(out=io["YT"][h_row:h_row + 64, qb * 512:(qb + 1) * 512], in_=ybf[0:64, :]),
             reads=[kb], dma="yt%d" % r)
        p.op("pool", lambda e: e.tensor_tensor(out=ysq[0:64, :], in0=osb[0:64, :], in1=osb[0:64, :], op=ALU.mult),
             reads=[ko], writes=[ks])

        def stage_b():
            mp2, mk2 = bufs["peek"]()
            ssps = mp2[:, 512:1024]
            for j in range(4):
                p.op("pe", lambda e, j=j: e.matmul(ssps[:, j:j + 1], lhsT=ysq[0:64, j * 128:(j + 1) * 128],
                                                   rhs=G["ones"][0:64, 0:1], start=True, stop=True),
                     reads=[ks, "ones"], writes=[mk2])
            p.op("dve", lambda e: e.tensor_copy(G["ss"][:, ss_idx, qb * 4:(qb + 1) * 4], ssps[:, 0:4]),
                 reads=[mk2], writes=["ss"])
        return stage_b
    return deferred


def xg_rows(io, cfg, R0, n):
    j, q = R0 // cfg.OWN, R0 % cfg.OWN
    i, t = q // 512, q % 512
    assert t + n <= 512
    return io["XG"][i, j * 512 + t:j * 512 + t + n, :]


def exchange(p, cfg, G, io):
    OWN = cfg.OWN
    ngrp = cfg.SF // OWN
    groups = [list(range(b * ngrp, (b + 1) * ngrp)) for b in range(cfg.ncores // ngrp)]
    p.barrier()
    p.push()
    hsel = p.sb("hsel", [128, 8], F32)
    cands = [p.sb("cand%d" % i, [128, ngrp, D], BF16) for i in range(2)]
    hacc = p.sb("hacc", [128, D], F32)
    houts = [p.sb("hout%d" % i, [128, D], BF16) for i in range(2)]
    p.op("sp", lambda e: e.dma_start(out=hsel[:], in_=io["hsel"]), writes=["hsel"], dma="hsel")
    it = 0
    for side in range(2):
        for t in range(HALO // 128):
            cand, ck = cands[it % 2], "cand%d" % (it % 2)
            hout, hk = houts[it % 2], "hout%d" % (it % 2)
            for j in range(ngrp):
                r0 = j * OWN + (OWN - HALO if side == 0 else 0) + t * 128
                p.op("sp", lambda e, j=j, r0=r0: e.dma_start(out=cand[:, j, :], in_=xg_rows(io, cfg, r0, 128)),
                     writes=[ck], dma=ck)
            p.op("dve", lambda e: e.tensor_scalar(hacc[:, :], cand[:, 0, :], hsel[:, side * 4:side * 4 + 1], None, op0=ALU.mult),
                 reads=[ck, "hsel"], writes=["hacc"])
            for j in range(1, ngrp):
                p.op("dve", lambda e, j=j: e.scalar_tensor_tensor(out=hacc[:, :], in0=cand[:, j, :],
                                                                  scalar=hsel[:, side * 4 + j:side * 4 + j + 1], in1=hacc[:, :],
                                                                  op0=ALU.mult, op1=ALU.add), reads=[ck, "hsel", "hacc"], writes=["hacc"])
            p.op("pool", lambda e: e.tensor_copy(hout[:, :], hacc[:, :]), reads=["hacc"], writes=[hk])
            r1 = side * HALO + t * 128
            p.op("sp", lambda e, r1=r1: e.dma_start(out=io["XH"][r1:r1 + 128, :], in_=hout[:, :]), reads=[hk], dma="xh%d" % (it % 2))
            it += 1
    p.pop()


def layer(p, cfg, G, io, W, src, dst):
    OWN, SF, OH, NT, NQB = cfg.OWN, cfg.SF, cfg.OH, cfg.NT, cfg.NQB
    ident = G["ident"]

    p.push()
    CQ = p.sb("CQ", [128, 2, OWN], BF16)
    p.push()
    KaT = p.sb("KaT", [128, 3, OH], BF16)
    QaT = p.sb("QaT", [128, 3, OWN], BF16)
    Va = p.sb("Va", [128, OH // 128, 6, 65], BF16)

    p.push()
    w_in = p.sb("w_in", [128, 8, PIN], BF16)
    wsT = p.sb("wsT", [128, 4, 128], BF16)
    sg_g = p.sb("sg_g", [128, 256], F32)
    sg_b = p.sb("sg_b", [128, 256], F32)
    bsT = p.sb("bsT", [128, 4], F32)
    vm = p.sb("vm", [128, OH // 128], F32)
    xbs = [p.sb("xb%d" % i, [128, 4, D], BF16) for i in range(2)]
    xTs = [p.sb("xT%d" % i, [128, 8, 512], BF16) for i in range(2)]
    zh = p.sb("zh", [128, 512], F32)
    w1 = p.sb("w1", [128, 512], F32)
    w2 = p.sb("w2", [128, 512], F32)
    zzs = [p.sb("zz%d" % i, [128, 512], F32) for i in range(4)]
    vn = p.sb("vn", [128, 256], F32)
    vnbs = [p.sb("vnb%d" % i, [128, 256], BF16) for i in range(4)]
    yc = p.sb("yc", [128, 256], F32)
    ycbs = [p.sb("ycb%d" % i, [128, 256], BF16) for i in range(2)] * 2
    sgu_pending = []
    ycT = p.sb("ycT", [128, 2, 512], BF16)
    junk = vn
    st6 = p.sb("st6", [128, 6], F32)
    mv = p.sb("mv", [128, 2], F32)
    rs = p.sb("rs", [128, 1], F32)
    cqs = [zzs[0], zzs[1]]
    cqq = [zh, w1]
    cqr = w2
    trps = [p.ps("trps%d" % i, [128, 1024], BF16) for i in range(2)]
    pps = [p.ps("pps%d" % i, [128, 512]) for i in range(3)]
    ssq = p.ps("ssq", [128, 512])

    for kc in range(8):
        p.op("pool", lambda e, kc=kc: e.dma_start(out=w_in[:, kc, :], in_=W["w_in"][kc * 128:(kc + 1) * 128, :]),
             writes=["w_in"], dma="w_in")
    p.op("pool", lambda e: e.dma_start(out=wsT[:], in_=W["sg_wT"].rearrange("g s t -> s g t")), writes=["wsT"], dma="wsm")
    p.op("sp", lambda e: e.dma_start(out=sg_g[:], in_=W["sg_g"].partition_broadcast(128)), writes=["sg_g"], dma="wsm2")
    p.op("sp", lambda e: e.dma_start(out=sg_b[:], in_=W["sg_b"].partition_broadcast(128)), writes=["sg_b"], dma="wsm2")
    p.op("sp", lambda e: e.dma_start(out=bsT[:], in_=W["sg_bT"]), writes=["bsT"], dma="wsm2")
    p.op("sp", lambda e: e.dma_start(out=vm[:], in_=io["vmask"]), writes=["vm"], dma="wsm2")
    for h in range(6):
        p.op("dve", lambda e, h=h: e.tensor_copy(Va[:, :, h, 64:65], vm[:].rearrange("p (t o) -> p t o", o=1)),
             reads=["vm"], writes=["Va"])

    pp_i = [0]

    def next_pps():
        i = pp_i[0] % 3
        pp_i[0] += 1
        return pps[i], "pps%d" % i

    ngo = cfg.NG_OH
    load_x_group(p, src["oh"](0), xbs[0], "xb0", "xb0")
    for g in range(ngo):
        xb, xbk = xbs[g % 2], "xb%d" % (g % 2)
        xT, xTk = xTs[g % 2], "xT%d" % (g % 2)
        if g + 1 < ngo:
            load_x_group(p, src["oh"](g + 1), xbs[(g + 1) % 2], "xb%d" % ((g + 1) % 2), "xb%d" % ((g + 1) % 2))
        transpose_group(p, G, xb, xbk, xT, xTk, trps)
        own = (g * 512 >= HALO) and (g * 512 < HALO + OWN)
        go = g - HALO // 512
        for c in range(3):
            ps_, pk = next_pps()
            for kc in range(8):
                p.op("pe", lambda e, ps_=ps_, kc=kc, c=c: e.matmul(ps_[:, :], lhsT=w_in[:, kc, 384 + c * 128:384 + (c + 1) * 128],
                                                                   rhs=xT[:, kc, :], start=(kc == 0), stop=(kc == 7)),
                     reads=["w_in", xTk], writes=[pk])
            p.op("dve", lambda e, ps_=ps_, c=c, g=g: e.tensor_copy(KaT[:, c, g * 512:(g + 1) * 512], ps_[:, :]),
                 reads=[pk], writes=["KaT"])
        if own:
            for c in range(3):
                ps_, pk = next_pps()
                for kc in range(8):
                    p.op("pe", lambda e, ps_=ps_, kc=kc, c=c: e.matmul(ps_[:, :], lhsT=w_in[:, kc, c * 128:(c + 1) * 128],
                                                                       rhs=xT[:, kc, :], start=(kc == 0), stop=(kc == 7)),
                         reads=["w_in", xTk], writes=[pk])
                p.op("dve", lambda e, ps_=ps_, c=c, go=go: e.tensor_copy(QaT[:, c, go * 512:(go + 1) * 512], ps_[:, :]),
                     reads=[pk], writes=["QaT"])
        for j in range(4):
            ps_, pk = next_pps()
            for kc in range(8):
                p.op("pe", lambda e, ps_=ps_, kc=kc, j=j: e.matmul(ps_[:, 0:384], lhsT=xT[:, kc, j * 128:(j + 1) * 128],
                                                                   rhs=w_in[:, kc, 768:1152], start=(kc == 0), stop=(kc == 7)),
                     reads=["w_in", xTk], writes=[pk])
            p.op("dve", lambda e, ps_=ps_, j=j, g=g: e.tensor_copy(Va[:, g * 4 + j, :, 0:64],
                                                                   ps_[:, 0:384].rearrange("p (h c) -> p h c", h=6)),
                 reads=[pk], writes=["Va"])
        if not own:
            continue
        for f_ in sgu_pending:
            f_()
        sgu_pending = []
        psa, pka = next_pps()
        psb, pkb = next_pps()
        for kc in range(8):
            p.op("pe", lambda e, kc=kc: e.matmul(psa[:, :], lhsT=w_in[:, kc, 1152:1280], rhs=xT[:, kc, :],
                                                 start=(kc == 0), stop=(kc == 7)), reads=["w_in", xTk], writes=[pka])
        for kc in range(8):
            p.op("pe", lambda e, kc=kc: e.matmul(psb[0:64, :], lhsT=w_in[:, kc, 1280:1344], rhs=xT[:, kc, :],
                                                 start=(kc == 0), stop=(kc == 7)), reads=["w_in", xTk], writes=[pkb])
        p.op("dve", lambda e: e.tensor_copy(cqs[0][:, :], psa[:, :]), reads=[pka], writes=["zz0"])
        p.op("dve", lambda e: e.tensor_copy(cqs[1][0:64, :], psb[0:64, :]), reads=[pkb], writes=["zz1"])
        p.op("pool", lambda e: e.tensor_tensor(out=cqq[0][:, :], in0=cqs[0][:, :], in1=cqs[0][:, :], op=ALU.mult),
             reads=["zz0"], writes=["zh"])
        p.op("pool", lambda e: e.tensor_tensor(out=cqq[1][0:64, :], in0=cqs[1][0:64, :], in1=cqs[1][0:64, :], op=ALU.mult),
             reads=["zz1"], writes=["w1"])
        p.op("pe", lambda e: e.matmul(ssq[:, :], lhsT=G["ones"][:, :], rhs=cqq[0][:, :], start=True, stop=False),
             reads=["zh", "ones"], writes=["ssq"])
        p.op("pe", lambda e: e.matmul(ssq[:, :], lhsT=G["ones"][0:64, :], rhs=cqq[1][0:64, :], start=False, stop=True),
             reads=["w1", "ones"], writes=["ssq"])
        p.op("act", lambda e: e.activation(out=cqr[:, :], in_=ssq[:, :], func=AF.Sqrt, scale=1.0 / 192, bias=EPS),
             reads=["ssq"], writes=["w2"])
        p.op("dve", lambda e: e.reciprocal(cqr[:, :], cqr[:, :]), reads=["w2"], writes=["w2"])
        p.op("dve", lambda e, go=go: e.tensor_tensor(out=CQ[:, 0, go * 512:(go + 1) * 512], in0=cqs[0][:, :], in1=cqr[:, :],
                                                     op=ALU.mult), reads=["zz0", "w2"], writes=["CQ"])
        p.op("dve", lambda e, go=go: e.tensor_tensor(out=CQ[0:64, 1, go * 512:(go + 1) * 512], in0=cqs[1][0:64, :],
                                                     in1=cqr[0:64, :], op=ALU.mult), reads=["zz1", "w2"], writes=["CQ"])
        for f_ in sgu_pending:
            f_()
        sgu_pending = []
        for j in range(4):
            ti = go * 4 + j
            zz = zzs[j]
            zk = "zz%d" % j
            ps_, pk = next_pps()
            for kc in range(8):
                p.op("pe", lambda e, ps_=ps_, kc=kc, j=j: e.matmul(ps_[:, :], lhsT=xT[:, kc, j * 128:(j + 1) * 128],
                                                                   rhs=w_in[:, kc, 1504:2016], start=(kc == 0), stop=(kc == 7)),
                     reads=["w_in", xTk], writes=[pk])
            p.op("act", lambda e, ps_=ps_: e.activation(out=zh[:, :], in_=ps_[:, :], func=AF.Copy, scale=0.5),
                 reads=[pk], writes=["zh"])
            p.op("pool", lambda e: e.tensor_tensor(out=w1[:, :], in0=zh[:, :], in1=zh[:, :], op=ALU.mult),
                 reads=["zh"], writes=["w1"])
            p.op("dve", lambda e: e.tensor_scalar(w1[:, :], w1[:, :], 4.0 * C_GELU, 1.0, op0=ALU.mult, op1=ALU.add),
                 reads=["w1"], writes=["w1"])
            p.op("pool", lambda e: e.tensor_tensor(out=w2[:, :], in0=w1[:, :], in1=zh[:, :], op=ALU.mult),
                 reads=["w1", "zh"], writes=["w2"])
            p.op("act", lambda e: e.activation(out=w2[:, :], in_=w2[:, :], func=AF.Tanh, scale=2.0 * K_GELU),
                 reads=["w2"], writes=["w2"])
            p.op("dve", lambda e, zz=zz: e.scalar_tensor_tensor(out=zz[:, :], in0=w2[:, :], scalar=1.0, in1=zh[:, :],
                                                                op0=ALU.add, op1=ALU.mult), reads=["w2", "zh"], writes=[zk])

        def sgu_ln(j, zz, zk):
            vnb_, vk = vnbs[j], "vnb%d" % j
            p.op("dve", lambda e: e.bn_stats(st6[:, :], zz[:, 256:512]), reads=[zk], writes=["st6"])
            p.op("dve", lambda e: e.bn_aggr(mv[:, :], st6[:, :]), reads=["st6"], writes=["mv"])
            p.op("act", lambda e: e.activation(out=rs[:, :], in_=mv[:, 1:2], func=AF.Sqrt, scale=1.0, bias=EPS),
                 reads=["mv"], writes=["rs"])
            p.op("dve", lambda e: e.reciprocal(rs[:, :], rs[:, :]), reads=["rs"], writes=["rs"])
            p.op("dve", lambda e: e.tensor_scalar(vn[:, :], zz[:, 256:512], mv[:, 0:1], rs[:, 0:1], op0=ALU.subtract,
                                                  op1=ALU.mult), reads=[zk, "mv", "rs"], writes=["vn"])
            p.op("pool", lambda e: e.tensor_tensor(out=vn[:, :], in0=vn[:, :], in1=sg_g[:, :], op=ALU.mult),
                 reads=["vn", "sg_g"], writes=["vn"])
            p.op("pool", lambda e: e.tensor_tensor(out=vnb_[:, :], in0=vn[:, :], in1=sg_b[:, :], op=ALU.add),
                 reads=["vn", "sg_b"], writes=[vk])

        def sgu_mix(j, zz, zk, ti):
            vnb_, vk = vnbs[j], "vnb%d" % j
            ycb_, yk_ = ycbs[j], "ycb%d" % (j % 2)
            mps, mk_ = next_pps()
            for gg in range(4):
                p.op("pe", lambda e, gg=gg: e.matmul(mps[:, gg * 64:(gg + 1) * 64], lhsT=wsT[:, gg, :],
                                                     rhs=vnb_[:, gg * 64:(gg + 1) * 64], start=True, stop=True),
                     reads=["wsT", vk], writes=[mk_])
            for gg in range(4):
                p.op("dve", lambda e, gg=gg: e.scalar_tensor_tensor(out=yc[:, gg * 64:(gg + 1) * 64],
                                                                    in0=mps[:, gg * 64:(gg + 1) * 64], scalar=bsT[:, gg:gg + 1],
                                                                    in1=zz[:, gg * 64:(gg + 1) * 64], op0=ALU.add, op1=ALU.mult),
                     reads=[mk_, "bsT", zk], writes=["yc"])
            p.op("pool", lambda e: e.tensor_tensor(out=junk[:, :], in0=yc[:, :], in1=yc[:, :], op=ALU.mult),
                 reads=["yc"], writes=["vn"])
            p.op("dve", lambda e: e.reduce_sum(out=G["ss"][:, 12, ti:ti + 1], in_=junk[:, :], axis=AX.X),
                 reads=["vn"], writes=["ss"])
            p.op("pool", lambda e: e.tensor_copy(ycb_[:, :], yc[:, :]), reads=["yc"], writes=[yk_])

        def sgu_tr(j):
            ycb_, yk_ = ycbs[j], "ycb%d" % (j % 2)
            for c2 in range(2):
                tp = trps[c2]
                p.op("pe", lambda e, tp=tp, c2=c2: e.transpose(tp[:, 512:640], ycb_[:, c2 * 128:(c2 + 1) * 128], ident[:]),
                     reads=[yk_, "ident"], writes=["tr%d" % c2])
                p.op("act", lambda e, tp=tp, c2=c2: e.copy(ycT[:, c2, j * 128:(j + 1) * 128], tp[:, 512:640]),
                     reads=["tr%d" % c2], writes=["ycT"])

        def make_stage2(go_):
            tis = [go_ * 4 + j for j in range(4)]

            def stage2():
                for j in range(4):
                    sgu_ln(j, zzs[j], "zz%d" % j)
                for j in range(2):
                    sgu_mix(j, zzs[j], "zz%d" % j, tis[j])
                for j in range(2):
                    sgu_tr(j)
                for j in range(2, 4):
                    sgu_mix(j, zzs[j], "zz%d" % j, tis[j])
                for j in range(2, 4):
                    sgu_tr(j)
                p.op("sp", lambda e: e.dma_start(out=io["YT"][768:1024, go_ * 512:(go_ + 1) * 512].rearrange("(c p) t -> p c t", p=128),
                                                 in_=ycT[:, :, :]), reads=["ycT"], dma="ytc")
            return stage2
        sgu_pending.append(make_stage2(go))
    for f_ in sgu_pending:
        f_()
    sgu_pending = []
    dbg_dump(p, "xT", xTs[(ngo - 1) % 2][:, :, :], [128, 8, 512], BF16)
    dbg_dump(p, "KaT", KaT[:, :, :], [128, 3, OH], BF16)
    dbg_dump(p, "QaT", QaT[:, :, :], [128, 3, OWN], BF16)
    dbg_dump(p, "Va", Va[:, :, :, :], [128, OH // 128, 6, 65], BF16)
    dbg_dump(p, "CQ", CQ[:, :, :], [128, 2, OWN], BF16)
    dbg_dump(p, "w_in", w_in[:, :, :], [128, 8, PIN], BF16)
    p.pop()

    p.push()
    tab = p.sb("tab", [128, 6, TABW], BF16)
    NSA = 3
    Es = [p.sb("E%d" % i, [128, 1024], BF16) for i in range(NSA)]
    PTs = [p.sb("PT%d" % i, [128, 1024], BF16) for i in range(NSA)]
    Sps = [p.ps("S%d" % i, [128, 1024]) for i in range(NSA)]
    Ops = [p.ps("O%d" % i, [128, 512]) for i in range(2)]
    gic = [0]

    def alloc_s():
        i = gic[0] % NSA
        gic[0] += 1
        return Sps[i], "S%d" % i

    def peek_s():
        i = gic[0] % NSA
        return Sps[i], "S%d" % i
    bufs = dict(peek=peek_s, osb=[p.sb("osb%d" % i, [128, 512], F32) for i in range(2)], rden=p.sb("rden", [128, 512], F32),
                ysq=[p.sb("ysq%d" % i, [128, 512], F32) for i in range(2)], ybf=[p.sb("ybf%d" % i, [128, 512], BF16) for i in range(2)],
                alloc=alloc_s)
    for h in range(6):
        p.op("sp", lambda e, h=h: e.dma_start(out=tab[:, h, :], in_=io["mtab"][:, h, :]), writes=["tab"], dma="tab")
    it = 0
    pend = []
    pendb = []
    NKT2 = 20
    for c in range(3):
        hh = (2 * c, 2 * c + 1)
        for qb in range(NQB):
            kt0 = 4 * qb

            def qk(ii):
                S, sk = alloc_s()
                kt = kt0 + ii
                for u in range(2):
                    p.op("pe", lambda e, S=S, u=u, kt=kt: e.matmul(S[:, u * 512:(u + 1) * 512],
                                                                   lhsT=KaT[u * 64:u * 64 + 64, c, kt * 128:(kt + 1) * 128],
                                                                   rhs=QaT[u * 64:u * 64 + 64, c, qb * 512:(qb + 1) * 512],
                                                                   start=True, stop=True),
                         reads=["KaT", "QaT"], writes=[sk])
                return S, sk

            inflight = [qk(0), qk(1)]
            for ii in range(NKT2):
                if ii + 2 < NKT2:
                    inflight.append(qk(ii + 2))
                S, sk = inflight.pop(0)
                E, ek = Es[ii % NSA], "E%d" % (ii % NSA)
                PT, pk = PTs[ii % NSA], "PT%d" % (ii % NSA)
                kt = kt0 + ii
                start = 2432 - 128 * ii
                p.op("act", lambda e, S=S, E=E: e.activation(out=E[:, :], in_=S[:, :], func=AF.Exp, scale=A_SCALE),
                     reads=[sk], writes=[ek])
                for u in range(2):
                    p.op("dve", lambda e, E=E, PT=PT, u=u, start=start: e.tensor_tensor(
                        out=PT[:, u * 512:(u + 1) * 512], in0=E[:, u * 512:(u + 1) * 512],
                        in1=tab[:, hh[u], start:start + 512], op=ALU.mult), reads=[ek, "tab"], writes=[pk + "_%d" % u])
                for u in range(2):
                    p.op("pe", lambda e, PT=PT, u=u, kt=kt, ii=ii: e.matmul(
                        Ops[u][0:65, :], lhsT=Va[:, kt, hh[u], 0:65], rhs=PT[:, u * 512:(u + 1) * 512],
                        start=(ii == 0), stop=(ii == NKT2 - 1)),
                        reads=["Va", pk + "_%d" % u], writes=["O%d" % u])
                if ii == 3:
                    while pend:
                        pendb.append(pend.pop(0)())
                if ii == 9:
                    while pendb:
                        pendb.pop(0)()
            for u in range(2):
                pend.append(attn_post(p, G, io, Ops[u], "O%d" % u, hh[u] * 64, qb, hh[u], bufs, u))
            it += 1
    while pend:
        pendb.append(pend.pop(0)())
    while pendb:
        pendb.pop(0)()
    p.pop()
    p.pop()

    CKV = p.sb("CKV", [128, SF], BF16)
    KT = p.sb("KT", [128, SF], BF16)
    p.push()
    wB = p.sb("wB", [128, 8, 160], BF16)
    wk96 = p.sb("wk96", [128, 8, 96], BF16)
    wk96r = p.sb("wk96r", [128, 8, 96], BF16)
    xbs = [p.sb("bxb%d" % i, [128, 4, D], BF16) for i in range(2)]
    xTs = [p.sb("bxT%d" % i, [128, 8, 512], BF16) for i in range(2)]
    sqs = [p.sb("bsq%d" % i, [128, 512], F32) for i in range(2)]
    rrs = [p.sb("brr%d" % i, [128, 512], F32) for i in range(2)]
    cks = [p.sb("cks%d" % i, [128, 512], F32) for i in range(2)]
    sks = [p.sb("sks%d" % i, [128, 512], F32) for i in range(2)]
    t1s = [p.sb("bt1_%d" % i, [128, 512], F32) for i in range(2)]
    t2s_ = [p.sb("bt2_%d" % i, [128, 512], F32) for i in range(2)]
    trps = [p.ps("btrps%d" % i, [128, 1024], BF16) for i in range(2)]
    pps = [p.ps("bpps%d" % i, [128, 512]) for i in range(4)]
    ssqs = [p.ps("bssq%d" % i, [128, 512]) for i in range(2)]
    for kc in range(8):
        p.op("pool", lambda e, kc=kc: e.dma_start(out=wB[:, kc, :], in_=W["w_in"][kc * 128:(kc + 1) * 128, 1344:1504]),
             writes=["wB"], dma="wB")
    p.op("dve", lambda e: e.memset(wk96[:], 0.0), writes=["wk96"])
    p.op("dve", lambda e: e.memset(wk96r[:], 0.0), writes=["wk96r"])
    p.op("dve", lambda e: e.tensor_copy(wk96[:, :, 64:96], wB[:, :, 128:160]), reads=["wB", "wk96"], writes=["wk96"])
    p.op("dve", lambda e: e.tensor_copy(wk96r[:, :, 64:80], wB[:, :, 144:160]), reads=["wB", "wk96r"], writes=["wk96r"])
    p.op("dve", lambda e: e.tensor_copy(wk96r[:, :, 80:96], wB[:, :, 128:144]), reads=["wB", "wk96r"], writes=["wk96r"])
    ngf = cfg.NG_F
    load_x_group(p, src["full"](0), xbs[0], "bxb0", "bxb0")
    for g in range(ngf):
        xb, xbk = xbs[g % 2], "bxb%d" % (g % 2)
        xT, xTk = xTs[g % 2], "bxT%d" % (g % 2)
        ck, ckk = cks[g % 2], "cks%d" % (g % 2)
        sk_, skk = sks[g % 2], "sks%d" % (g % 2)
        if g + 1 < ngf:
            load_x_group(p, src["full"](g + 1), xbs[(g + 1) % 2], "bxb%d" % ((g + 1) % 2), "bxb%d" % ((g + 1) % 2))
        p.op("sp", lambda e, ck=ck, g=g: e.dma_start(out=ck[64:96, :], in_=io["ckt"][:, g * 512:(g + 1) * 512]),
             writes=[ckk], dma=ckk)
        p.op("sp", lambda e, sk_=sk_, g=g: e.dma_start(out=sk_[64:96, :], in_=io["skt"][:, g * 512:(g + 1) * 512]),
             writes=[skk], dma=skk)
        transpose_group(p, G, xb, xbk, xT, xTk, trps)
        sq, sqk = sqs[g % 2], "bsq%d" % (g % 2)
        rr, rrk = rrs[g % 2], "brr%d" % (g % 2)
        t1, t1k = t1s[g % 2], "bt1_%d" % (g % 2)
        t2, t2k = t2s_[g % 2], "bt2_%d" % (g % 2)
        ssq, ssqk = ssqs[g % 2], "bssq%d" % (g % 2)
        pa, pb, pc = pps[(3 * g) % 4], pps[(3 * g + 1) % 4], pps[(3 * g + 2) % 4]
        ka, kb, kc_ = "bpps%d" % ((3 * g) % 4), "bpps%d" % ((3 * g + 1) % 4), "bpps%d" % ((3 * g + 2) % 4)
        for kc in range(8):
            p.op("pe", lambda e, kc=kc, pa=pa: e.matmul(pa[:, :], lhsT=wB[:, kc, 0:128], rhs=xT[:, kc, :],
                                                        start=(kc == 0), stop=(kc == 7)), reads=["wB", xTk], writes=[ka])
        for kc in range(8):
            p.op("pe", lambda e, kc=kc, pb=pb: e.matmul(pb[0:96, :], lhsT=wk96[:, kc, :], rhs=xT[:, kc, :],
                                                        start=(kc == 0), stop=(kc == 7)), reads=["wk96", xTk], writes=[kb])
        for kc in range(8):
            p.op("pe", lambda e, kc=kc, pc=pc: e.matmul(pc[0:96, :], lhsT=wk96r[:, kc, :], rhs=xT[:, kc, :],
                                                        start=(kc == 0), stop=(kc == 7)), reads=["wk96r", xTk], writes=[kc_])
        p.op("act", lambda e, pa=pa: e.activation(out=sq[:, :], in_=pa[:, :], func=AF.Square), reads=[ka], writes=[sqk])
        p.op("pe", lambda e: e.matmul(ssq[:, :], lhsT=G["ones"][:, :], rhs=sq[:, :], start=True, stop=True),
             reads=[sqk, "ones"], writes=[ssqk])
        p.op("act", lambda e: e.activation(out=rr[:, :], in_=ssq[:, :], func=AF.Sqrt, scale=1.0 / 128, bias=EPS),
             reads=[ssqk], writes=[rrk])
        p.op("dve", lambda e: e.reciprocal(rr[:, :], rr[:, :]), reads=[rrk], writes=[rrk])
        p.op("dve", lambda e, pa=pa, g=g: e.tensor_tensor(out=CKV[:, g * 512:(g + 1) * 512], in0=pa[:, :], in1=rr[:, :],
                                                          op=ALU.mult), reads=[ka, rrk], writes=["CKV"])
        p.op("dve", lambda e, pb=pb, ck=ck: e.tensor_tensor(out=t1[64:96, :], in0=pb[64:96, :], in1=ck[64:96, :], op=ALU.mult),
             reads=[kb, ckk], writes=[t1k])
        p.op("dve", lambda e, pc=pc, sk_=sk_: e.tensor_tensor(out=t2[64:96, :], in0=pc[64:96, :], in1=sk_[64:96, :], op=ALU.mult),
             reads=[kc_, skk], writes=[t2k])
        p.op("pool", lambda e, g=g: e.tensor_tensor(out=KT[64:96, g * 512:(g + 1) * 512], in0=t1[64:96, :], in1=t2[64:96, :],
                                                    op=ALU.add), reads=[t1k, t2k], writes=["KT"])
    p.pop()

    p.push()
    Vh = p.sb("Vh", [128, SF // 128, 65], BF16)
    QT = p.sb("QT", [128, OWN], BF16)
    cqt = p.sb("cqt", [128, OWN], F32)
    sqt = p.sb("sqt", [128, OWN], F32)
    wkv = p.sb("wkv", [128, 768], BF16)
    wkvf = p.sb("wkvf", [128, 768], F32)
    wqf = p.sb("wqf", [128, 2, 576], F32)
    wq = p.sb("wq", [128, 2, 6, 96], BF16)
    wqr = p.sb("wqr", [128, 2, 6, 96], BF16)
    qn = p.sb("qn", [128, 2], F32)
    kvn = p.sb("kvn", [128, 1], F32)
    q1 = p.sb("q1", [128, 512], F32)
    q2 = p.sb("q2", [128, 512], F32)
    NSB = 3
    PTs = [p.sb("bPT%d" % i, [128, 1024], BF16) for i in range(NSB)]
    Sps = [p.ps("bS%d" % i, [128, 1024]) for i in range(NSB)]
    Ops = [p.ps("bO%d" % i, [128, 512]) for i in range(2)]
    pend = []
    pendb = []
    gib = [0]

    def alloc_sb():
        i = gib[0] % NSB
        gib[0] += 1
        return Sps[i], "bS%d" % i

    def peek_sb():
        i = gib[0] % NSB
        return Sps[i], "bS%d" % i
    bufs = dict(peek=peek_sb, osb=[p.sb("bosb%d" % i, [128, 512], F32) for i in range(2)], rden=p.sb("brden", [128, 512], F32),
                ysq=[p.sb("bysq%d" % i, [128, 512], F32) for i in range(2)], ybf=[p.sb("bybf%d" % i, [128, 512], BF16) for i in range(2)],
                alloc=alloc_sb)
    p.op("sp", lambda e: e.dma_start(out=wkvf[:], in_=W["w_kv_up"]), writes=["wkvf"], dma="bw")
    p.op("sp", lambda e: e.dma_start(out=wqf[:, 0, :], in_=W["w_q_up"][0:128, :]), writes=["wqf"], dma="bw")
    p.op("sp", lambda e: e.dma_start(out=wqf[0:64, 1, :], in_=W["w_q_up"][128:192, :]), writes=["wqf"], dma="bw")
    p.op("sp", lambda e: e.dma_start(out=qn[:], in_=W["qn"]), writes=["qn"], dma="bw")
    p.op("sp", lambda e: e.dma_start(out=kvn[:], in_=W["kvn"]), writes=["kvn"], dma="bw")
    p.op("sp", lambda e: e.dma_start(out=cqt[64:96, :], in_=io["cqt"]), writes=["cqt"], dma="bw")
    p.op("sp", lambda e: e.dma_start(out=sqt[64:96, :], in_=io["sqt"]), writes=["sqt"], dma="bw")
    p.op("dve", lambda e: e.tensor_scalar(wkv[:, :], wkvf[:, :], kvn[:, 0:1], None, op0=ALU.mult), reads=["wkvf", "kvn"],
         writes=["wkv"])
    p.op("dve", lambda e: e.memset(wq[:], 0.0), writes=["wq"])
    p.op("dve", lambda e: e.memset(wqr[:], 0.0), writes=["wqr"])
    for cc, np_ in ((0, 128), (1, 64)):
        wv = wqf[0:np_, cc, :].rearrange("p (h c) -> p h c", h=6)
        p.op("dve", lambda e, cc=cc, np_=np_, wv=wv: e.tensor_scalar(wq[0:np_, cc, :, :], wv, qn[0:np_, cc:cc + 1], None,
                                                                      op0=ALU.mult), reads=["wqf", "qn", "wq"], writes=["wq"])
        p.op("dve", lambda e, cc=cc, np_=np_, wv=wv: e.tensor_scalar(wqr[0:np_, cc, :, 64:80], wv[:, :, 80:96],
                                                                      qn[0:np_, cc:cc + 1], None, op0=ALU.mult),
             reads=["wqf", "qn", "wqr"], writes=["wqr"])
        p.op("dve", lambda e, cc=cc, np_=np_, wv=wv: e.tensor_scalar(wqr[0:np_, cc, :, 80:96], wv[:, :, 64:80],
                                                                      qn[0:np_, cc:cc + 1], None, op0=ALU.mult),
             reads=["wqf", "qn", "wqr"], writes=["wqr"])
    p.op("pool", lambda e: e.memset(Vh[:, :, 64:65], 1.0), writes=["Vh"])
    it = 0
    gi = 0
    nkt = SF // 128
    for h in range(6):
        for g2 in range(SF // 1024):
            S, sk = alloc_sb()
            for u in range(2):
                g = 2 * g2 + u
                p.op("pe", lambda e, S=S, u=u, g=g, h=h: e.matmul(S[0:64, u * 512:(u + 1) * 512], lhsT=wkv[:, h * 128:h * 128 + 64],
                                                                  rhs=CKV[:, g * 512:(g + 1) * 512], start=True, stop=True),
                     reads=["wkv", "CKV"], writes=[sk])
            p.op("dve", lambda e, S=S, g2=g2: e.tensor_copy(KT[0:64, g2 * 1024:(g2 + 1) * 1024], S[0:64, :]),
                 reads=[sk], writes=["KT"])
        for t16 in range(nkt // 16):
            S, sk = alloc_sb()
            for u in range(16):
                t = t16 * 16 + u
                p.op("pe", lambda e, S=S, u=u, t=t, h=h: e.matmul(S[:, u * 64:(u + 1) * 64], lhsT=CKV[:, t * 128:(t + 1) * 128],
                                                                  rhs=wkv[:, h * 128 + 64:h * 128 + 128], start=True, stop=True),
                     reads=["wkv", "CKV"], writes=[sk])
            p.op("dve", lambda e, S=S, t16=t16: e.tensor_copy(Vh[:, t16 * 16:(t16 + 1) * 16, 0:64],
                                                               S[:, :].rearrange("p (t c) -> p t c", c=64)),
                 reads=[sk], writes=["Vh"])
        for qb in range(NQB):
            S, sk = alloc_sb()
            for u, wsrc in ((0, wq), (1, wqr)):
                p.op("pe", lambda e, S=S, u=u, wsrc=wsrc, h=h, qb=qb: e.matmul(S[0:96, u * 512:(u + 1) * 512], lhsT=wsrc[:, 0, h, :],
                                                                               rhs=CQ[:, 0, qb * 512:(qb + 1) * 512],
                                                                               start=True, stop=False),
                     reads=["wq", "wqr", "CQ"], writes=[sk])
                p.op("pe", lambda e, S=S, u=u, wsrc=wsrc, h=h, qb=qb: e.matmul(S[0:96, u * 512:(u + 1) * 512], lhsT=wsrc[0:64, 1, h, :],
                                                                               rhs=CQ[0:64, 1, qb * 512:(qb + 1) * 512],
                                                                               start=False, stop=True),
                     reads=["wq", "wqr", "CQ"], writes=[sk])
            p.op("dve", lambda e, S=S, qb=qb: e.tensor_copy(QT[0:64, qb * 512:(qb + 1) * 512], S[0:64, 0:512]),
                 reads=[sk], writes=["QT"])
            p.op("dve", lambda e, S=S, qb=qb: e.tensor_tensor(out=q1[64:96, :], in0=S[64:96, 0:512],
                                                              in1=cqt[64:96, qb * 512:(qb + 1) * 512], op=ALU.mult),
                 reads=[sk, "cqt"], writes=["q1"])
            p.op("dve", lambda e, S=S, qb=qb: e.tensor_tensor(out=q2[64:96, :], in0=S[64:96, 512:1024],
                                                              in1=sqt[64:96, qb * 512:(qb + 1) * 512], op=ALU.mult),
                 reads=[sk, "sqt"], writes=["q2"])
            p.op("pool", lambda e, qb=qb: e.tensor_tensor(out=QT[64:96, qb * 512:(qb + 1) * 512], in0=q1[64:96, :],
                                                          in1=q2[64:96, :], op=ALU.add), reads=["q1", "q2"], writes=["QT"])
        for qb in range(NQB):
            ops, okey = Ops[it % 2], "bO%d" % (it % 2)
            ngrp = nkt // 2

            def qk(i):
                S, sk = alloc_sb()
                for u in range(2):
                    kt = 2 * i + u
                    p.op("pe", lambda e, S=S, u=u, kt=kt: e.matmul(S[:, u * 512:(u + 1) * 512],
                                                                   lhsT=KT[0:96, kt * 128:(kt + 1) * 128],
                                                                   rhs=QT[0:96, qb * 512:(qb + 1) * 512], start=True, stop=True),
                         reads=["KT", "QT"], writes=[sk])
                return S, sk

            inflight = [qk(0), qk(1)]
            for i in range(ngrp):
                if i + 2 < ngrp:
                    inflight.append(qk(i + 2))
                S, sk = inflight.pop(0)
                PT, pk = PTs[i % NSB], "bPT%d" % (i % NSB)
                p.op("act", lambda e, S=S, PT=PT: e.activation(out=PT[:, :], in_=S[:, :], func=AF.Exp, scale=B_SCALE),
                     reads=[sk], writes=[pk])
                for u in range(2):
                    kt = 2 * i + u
                    p.op("pe", lambda e, PT=PT, u=u, kt=kt, ops=ops, i=i: e.matmul(
                        ops[0:65, :], lhsT=Vh[:, kt, 0:65], rhs=PT[:, u * 512:(u + 1) * 512],
                        start=(i == 0 and u == 0), stop=(i == ngrp - 1 and u == 1)),
                        reads=["Vh", pk], writes=[okey])
                if i == 3 and pend:
                    pendb.append(pend.pop()())
                if i == 9 and pendb:
                    pendb.pop()()
            pend.append(attn_post(p, G, io, ops, okey, 384 + h * 64, qb, 6 + h, bufs, it))
            it += 1
        while pend:
            pendb.append(pend.pop()())
        while pendb:
            pendb.pop()()
    p.pop()
    p.pop()

    p.push()
    wf1 = p.sb("wf1", [128, 8, DFF], BF16)
    wf2 = p.sb("wf2", [128, 32, D], BF16)
    p.push()
    wo = p.sb("wo", [128, 8, D], BF16)
    wofs = [p.sb("wof%d" % i, [128, D], F32) for i in range(2)]
    mixn = p.sb("mixn", [128, 8], F32)
    g1 = p.sb("g1", [128, D], F32)
    b1 = p.sb("b1", [128, D], F32)
    yTs = [p.sb("yT%d" % i, [128, 8, 512], BF16) for i in range(2)]
    xrs = [p.sb("xr%d" % i, [128, D], F32) for i in range(2)]
    accs = [p.sb("dacc%d" % i, [128, D], F32) for i in range(2)]
    x1s = [p.sb("x1s%d" % i, [128, D], F32) for i in range(2)]
    rst = p.sb("rst", [128, 3, NT], F32)
    sst = p.sb("sst", [128, 3, NT], F32)
    st12s = [p.sb("st12_%d" % i, [128, 12], F32) for i in range(2)]
    mvs = [p.sb("dmv%d" % i, [128, 2], F32) for i in range(2)]
    rss = [p.sb("drs%d" % i, [128, 1], F32) for i in range(2)]
    accps = [p.ps("accps%d" % i, [128, 1024]) for i in range(2)]
    p.op("sp", lambda e: e.dma_start(out=mixn[:], in_=W["mixn"]), writes=["mixn"], dma="dw")
    p.op("sp", lambda e: e.dma_start(out=g1[:], in_=W["ln1_g"].partition_broadcast(128)), writes=["g1"], dma="dw")
    p.op("sp", lambda e: e.dma_start(out=b1[:], in_=W["ln1_b"].partition_broadcast(128)), writes=["b1"], dma="dw")
    for kc in range(8):
        wf_, wfk = wofs[kc % 2], "wof%d" % (kc % 2)
        p.op("sp", lambda e, kc=kc: e.dma_start(out=wf_[:], in_=W["w_out"][kc * 128:(kc + 1) * 128, :]), writes=[wfk], dma=wfk)
        p.op("dve", lambda e, kc=kc: e.tensor_scalar(wo[:, kc, :], wf_[:, :], mixn[:, kc:kc + 1], None, op0=ALU.mult),
             reads=[wfk, "mixn"], writes=["wo"])

    def ffn_weight_prefetch():
        for kc in range(8):
            p.op("pool", lambda e, kc=kc: e.dma_start(out=wf1[:, kc, :], in_=W["w_ff1"][kc * 128:(kc + 1) * 128, :]),
                 writes=["wf1"], dma="wf1")
        for c4 in range(8):
            p.op("pool", lambda e, c4=c4: e.dma_start(out=wf2[:, c4 * 4:(c4 + 1) * 4, :],
                                                      in_=W["w_ff2"][c4 * 512:(c4 + 1) * 512, :].rearrange("(c p) d -> p c d", p=128)),
                 writes=["wf2"], dma="wf2")
    ssv = G["ss"]
    p.op("dve", lambda e: e.tensor_copy(sst[:, 0, :], ssv[:, 0, :]), reads=["ss"], writes=["sst"])
    p.op("dve", lambda e: e.tensor_copy(sst[:, 1, :], ssv[:, 6, :]), reads=["ss"], writes=["sst"])
    p.op("dve", lambda e: e.tensor_copy(sst[:, 2, :], ssv[:, 12, :]), reads=["ss"], writes=["sst"])
    for k in range(1, 6):
        p.op("dve", lambda e, k=k: e.tensor_tensor(out=sst[:, 0, :], in0=sst[:, 0, :], in1=ssv[:, k, :], op=ALU.add),
             reads=["ss", "sst"], writes=["sst"])
        p.op("dve", lambda e, k=k: e.tensor_tensor(out=sst[:, 1, :], in0=sst[:, 1, :], in1=ssv[:, 6 + k, :], op=ALU.add),
             reads=["ss", "sst"], writes=["sst"])
    for gidx, wdt in ((0, 384.0), (1, 384.0), (2, 256.0)):
        p.op("act", lambda e, gidx=gidx, wdt=wdt: e.activation(out=rst[:, gidx, :], in_=sst[:, gidx, :], func=AF.Sqrt,
                                                               scale=1.0 / wdt, bias=EPS), reads=["sst"], writes=["rst"])
    p.op("dve", lambda e: e.reciprocal(rst[:, :, :], rst[:, :, :]), reads=["rst"], writes=["rst"])
    KCG = ((0, 3), (3, 6), (6, 8))
    ai = 0
    for g in range(NQB):
        yT, yk = yTs[g % 2], "yT%d" % (g % 2)
        p.op("sp", lambda e, yT=yT, g=g: e.dma_start(out=yT[:, :, :], in_=io["YT"][:, g * 512:(g + 1) * 512].rearrange("(c p) t -> p c t", p=128)),
             writes=[yk], dma=yk)
        if g == 1:
            ffn_weight_prefetch()
        for j in range(4):
            ti = g * 4 + j
            xr, xk = xrs[ti % 2], "xr%d" % (ti % 2)
            x1, x1k = x1s[ti % 2], "x1s%d" % (ti % 2)
            acc, acck = accs[ti % 2], "dacc%d" % (ti % 2)
            p.op("sp", lambda e, xr=xr, ti=ti: e.dma_start(out=xr[:, :], in_=src["res"](ti)),
                 writes=[xk], dma=xk)
            for gidx, (k0, k1) in enumerate(KCG):
                ap_, ak = accps[ai % 2], "accps%d" % (ai % 2)
                ai += 1
                for half in range(2):
                    for kc in range(k0, k1):
                        p.op("pe", lambda e, ap_=ap_, half=half, kc=kc, k0=k0, k1=k1, j=j: e.matmul(
                            ap_[:, half * 512:(half + 1) * 512], lhsT=yT[:, kc, j * 128:(j + 1) * 128],
                            rhs=wo[:, kc, half * 512:(half + 1) * 512], start=(kc == k0), stop=(kc == k1 - 1)),
                            reads=[yk, "wo"], writes=[ak])
                if gidx == 0:
                    p.op("dve", lambda e, ap_=ap_, ti=ti: e.tensor_scalar(acc[:, :], ap_[:, :], rst[:, 0, ti:ti + 1], None, op0=ALU.mult),
                         reads=[ak, "rst"], writes=[acck])
                else:
                    p.op("dve", lambda e, ap_=ap_, ti=ti, gidx=gidx: e.scalar_tensor_tensor(
                        out=acc[:, :], in0=ap_[:, :], scalar=rst[:, gidx, ti:ti + 1], in1=acc[:, :], op0=ALU.mult, op1=ALU.add),
                        reads=[ak, "rst", acck], writes=[acck])
            p.op("dve", lambda e, xr=xr: e.scalar_tensor_tensor(out=acc[:, :], in0=xr[:, :], scalar=ALPHA, in1=acc[:, :],
                                                                op0=ALU.mult, op1=ALU.add), reads=[xk, acck], writes=[acck])
            layer_norm(p, acc, acck, x1, x1k, g1, "g1", b1, "b1", st12s[ti % 2], mvs[ti % 2], rss[ti % 2], "d1_%d" % (ti % 2))
            p.op("sp", lambda e, x1=x1, ti=ti: e.dma_start(out=io["X1"][ti * 128:(ti + 1) * 128, :], in_=x1[:, :]),
                 reads=[x1k], dma="x1o%d" % (ti % 2))
    p.pop()

    p.push()
    b1T = p.sb("b1T", [128, 32], F32)
    g2 = p.sb("g2", [128, D], F32)
    b2 = p.sb("b2", [128, D], F32)
    bf2 = p.sb("bf2", [128, D], F32)
    x1f = [p.sb("x1f%d" % i, [128, 2, D], F32) for i in range(2)]
    x1b = p.sb("x1b", [128, 2, D], BF16)
    x1Ts = [p.sb("x1T%d" % i, [128, 8, 256], BF16) for i in range(2)]
    hidT = p.sb("hidT", [128, 32, 256], BF16)
    rl = [p.sb("rl%d" % i, [128, 256], F32) for i in range(2)]
    t2ss = [p.sb("t2s%d" % i, [128, D], F32) for i in range(1)]
    outs = [p.sb("outs%d" % i, [128, D], F32) for i in range(2)]
    outb = [p.sb("outb%d" % i, [128, D], BF16) for i in range(2)]
    st12s = [p.sb("st12b%d" % i, [128, 12], F32) for i in range(2)]
    mvs = [p.sb("dmv2_%d" % i, [128, 2], F32) for i in range(2)]
    rss = [p.sb("drs2_%d" % i, [128, 1], F32) for i in range(2)]
    trps = [p.ps("dtrps%d" % i, [128, 1024], BF16) for i in range(2)]
    hps = [p.ps("hps%d" % i, [128, 512]) for i in range(2)]
    fps = [p.ps("fps%d" % i, [128, 1024]) for i in range(2)]
    p.op("sp", lambda e: e.dma_start(out=b1T[:], in_=W["b1T"]), writes=["b1T"], dma="dw2")
    p.op("sp", lambda e: e.dma_start(out=g2[:], in_=W["ln2_g"].partition_broadcast(128)), writes=["g2"], dma="dw2")
    p.op("sp", lambda e: e.dma_start(out=b2[:], in_=W["ln2_b"].partition_broadcast(128)), writes=["b2"], dma="dw2")
    p.op("sp", lambda e: e.dma_start(out=bf2[:], in_=W["b_ff2"].partition_broadcast(128)), writes=["bf2"], dma="dw2")
    ng2 = OWN // 256

    def prep(g):
        xf_, xfk = x1f[g % 2], "x1f%d" % (g % 2)
        p.op("sp", lambda e: e.dma_start(out=xf_[:, :, :], in_=io["X1"][g * 256:(g + 1) * 256, :].rearrange("(j p) d -> p j d", p=128)),
             writes=[xfk], dma=xfk)
        p.op("pool", lambda e: e.tensor_copy(x1b[:, :, :], xf_[:, :, :]), reads=[xfk], writes=["x1b"])
        transpose_group(p, G, x1b, "x1b", x1Ts[g % 2], "x1T%d" % (g % 2), trps, n_tok_tiles=2)

    prep(0)
    hi = 0
    fi = 0
    for g in range(ng2):
        xf_, xfk = x1f[g % 2], "x1f%d" % (g % 2)
        x1T, x1Tk = x1Ts[g % 2], "x1T%d" % (g % 2)
        for fc in range(32):
            hp_, hk = hps[hi % 2], "hps%d" % (hi % 2)
            r_, rk = rl[hi % 2], "rl%d" % (hi % 2)
            hi += 1
            for kc in range(8):
                p.op("pe", lambda e, hp_=hp_, kc=kc, fc=fc: e.matmul(hp_[:, 0:256], lhsT=wf1[:, kc, fc * 128:(fc + 1) * 128],
                                                                     rhs=x1T[:, kc, :], start=(kc == 0), stop=(kc == 7)),
                     reads=["wf1", x1Tk], writes=[hk])
            p.op("act", lambda e, hp_=hp_, r_=r_, fc=fc: e.activation(out=r_[:, :], in_=hp_[:, 0:256], func=AF.Relu,
                                                                      bias=b1T[:, fc:fc + 1], scale=1.0),
                 reads=[hk, "b1T"], writes=[rk])
            p.op("dve", lambda e, r_=r_, fc=fc: e.tensor_tensor(out=hidT[:, fc, :], in0=r_[:, :], in1=r_[:, :], op=ALU.mult),
                 reads=[rk], writes=["hidT"])
        if g + 1 < ng2:
            prep(g + 1)
        for j in range(2):
            ti = g * 2 + j
            fp_, fk = fps[fi % 2], "fps%d" % (fi % 2)
            o_, ok_ = outs[fi % 2], "outs%d" % (fi % 2)
            t2s, t2k = t2ss[0], "t2s0"
            par = fi % 2
            fi += 1
            for half in range(2):
                for fc in range(32):
                    p.op("pe", lambda e, fp_=fp_, half=half, fc=fc, j=j: e.matmul(
                        fp_[:, half * 512:(half + 1) * 512], lhsT=hidT[:, fc, j * 128:(j + 1) * 128],
                        rhs=wf2[:, fc, half * 512:(half + 1) * 512], start=(fc == 0), stop=(fc == 31)),
                        reads=["hidT", "wf2"], writes=[fk])
            p.op("dve", lambda e, fp_=fp_: e.tensor_tensor(out=t2s[:, :], in0=fp_[:, :], in1=bf2[:, :], op=ALU.add),
                 reads=[fk, "bf2"], writes=[t2k])
            p.op("dve", lambda e, xf_=xf_, j=j: e.scalar_tensor_tensor(out=t2s[:, :], in0=xf_[:, j, :], scalar=ALPHA, in1=t2s[:, :],
                                                                       op0=ALU.mult, op1=ALU.add), reads=[xfk, t2k], writes=[t2k])
            layer_norm(p, t2s, t2k, o_, ok_, g2, "g2", b2, "b2", st12s[par], mvs[par], rss[par], "d2_%d" % par)
            p.op("sp", lambda e, o_=o_, ti=ti: e.dma_start(out=dst["f32"][ti * 128:(ti + 1) * 128, :], in_=o_[:, :]),
                 reads=[ok_], dma=dst["tag"])
            if dst["bf16"] is not None:
                ob_, obk = outb[par], "outb%d" % par
                p.op("pool", lambda e, o_=o_, ob_=ob_: e.tensor_copy(ob_[:, :], o_[:, :]), reads=[ok_], writes=[obk])
                ch = ti // 4
                p.op("sp", lambda e, ob_=ob_, ti=ti: e.dma_start(out=dst["bf16"][ti * 128:(ti + 1) * 128, :], in_=ob_[:, :]),
                     reads=[obk], writes=["XBc%d" % ch], dma="xbc%d" % (ch % 2))
                if ti % 4 == 3:
                    ngrp_ = cfg.SF // OWN
                    groups_ = [list(range(b_ * ngrp_, (b_ + 1) * ngrp_)) for b_ in range(cfg.ncores // ngrp_)]
                    p.op("pool", lambda e, ch=ch: e.collective_compute("AllGather", ALU.bypass, replica_groups=groups_,
                                                                       ins=[io["XBc"][ch * 512:(ch + 1) * 512, :].opt()],
                                                                       outs=[io["XG"][ch].opt()]),
                         reads=["XBc%d" % ch], dma="cc", inc=1)
    p.pop()
    p.pop()


def layer_norm(p, src, skey, dstt, dkey, g, gk, b, bk, st12, mv, rs, tg):
    k6, kmv, krs = "st12" + tg, "mv" + tg, "rs" + tg
    p.op("dve", lambda e: e.bn_stats(st12[:, 0:6], src[:, 0:512]), reads=[skey], writes=[k6])
    p.op("dve", lambda e: e.bn_stats(st12[:, 6:12], src[:, 512:1024]), reads=[skey], writes=[k6])
    p.op("dve", lambda e: e.bn_aggr(mv[:, :], st12[:, :]), reads=[k6], writes=[kmv])
    p.op("act", lambda e: e.activation(out=rs[:, :], in_=mv[:, 1:2], func=AF.Sqrt, scale=1.0, bias=EPS), reads=[kmv], writes=[krs])
    p.op("dve", lambda e: e.reciprocal(rs[:, :], rs[:, :]), reads=[krs], writes=[krs])
    p.op("dve", lambda e: e.tensor_scalar(src[:, :], src[:, :], mv[:, 0:1], rs[:, 0:1], op0=ALU.subtract, op1=ALU.mult),
         reads=[skey, kmv, krs], writes=[skey])
    p.op("pool", lambda e: e.tensor_tensor(out=src[:, :], in0=src[:, :], in1=g[:, :], op=ALU.mult), reads=[skey, gk], writes=[skey])
    p.op("pool", lambda e: e.tensor_tensor(out=dstt[:, :], in0=src[:, :], in1=b[:, :], op=ALU.add), reads=[skey, bk], writes=[dkey])


_CACHE = {}


def host_weights(inp, layers):
    f = lambda a: np.ascontiguousarray(a, dtype=np.float32)
    L = list(layers)
    w = {}
    w["w_in"] = f(inp["w_in"][L])
    w["w_q_up"] = f(inp["w_q_up"][L])
    w["w_kv_up"] = f(inp["w_kv_up"][L])
    qn = np.zeros((len(L), 256), np.float32)
    qn[:, :192] = inp["q_norm"][L]
    w["qn"] = f(qn.reshape(len(L), 2, 128).transpose(0, 2, 1))
    w["kvn"] = f(inp["kv_norm"][L].reshape(len(L), 128, 1))
    w["sg_g"] = f(inp["sgu_ln_g"][L])
    w["sg_b"] = f(inp["sgu_ln_b"][L])
    w["sg_wT"] = f(np.transpose(inp["sgu_w"][L], (0, 1, 3, 2)))
    w["sg_bT"] = f(np.transpose(inp["sgu_b"][L], (0, 2, 1)))
    w["mixn"] = f(inp["mix_norm"][L].reshape(len(L), 8, 128).transpose(0, 2, 1))
    w["w_out"] = f(inp["w_out"][L])
    w["ln1_g"] = f(inp["ln1_g"][L])
    w["ln1_b"] = f(inp["ln1_b"][L])
    w["w_ff1"] = f(inp["w_ff1"][L])
    w["b1T"] = f(inp["b_ff1"][L].reshape(len(L), 32, 128).transpose(0, 2, 1))
    w["w_ff2"] = f(inp["w_ff2"][L])
    w["b_ff2"] = f(inp["b_ff2"][L])
    w["ln2_g"] = f(inp["ln2_g"][L])
    w["ln2_b"] = f(inp["ln2_b"][L])
    return w


def core_inputs(x_b, r, own, consts):
    S = x_b.shape[0]
    lo = r * own - HALO
    xo = np.zeros((own + 2 * HALO, D), np.float32)
    vm = np.zeros((own + 2 * HALO,), np.float32)
    a, b = max(lo, 0), min(lo + own + 2 * HALO, S)
    xo[a - lo:b - lo] = x_b[a:b]
    vm[a - lo:b - lo] = 1.0
    ct, st = consts["rope"]
    hs = np.zeros((128, 8), np.float32)
    if r - 1 >= 0:
        hs[:, r - 1] = 1.0
    if r + 1 < S // own:
        hs[:, 4 + r + 1] = 1.0
    d = dict(hsel=hs, xo=xo, xf=np.ascontiguousarray(x_b, dtype=np.float32),
             vmask=np.ascontiguousarray(vm.reshape(-1, 128).T), mtab=consts["mtab"],
             ckt=ct, skt=st, cqt=np.ascontiguousarray(ct[:, r * own:(r + 1) * own]),
             sqt=np.ascontiguousarray(st[:, r * own:(r + 1) * own]))
    return d


def run_layers(x, inp, own, n_groups_per_batch, fused_depth=1, dbg=False):
    B, S, _ = x.shape
    key = (own, S, fused_depth, dbg, B)
    if key not in _CACHE:
        _CACHE[key] = build_program(Cfg(own, S, depth=fused_depth, dbg=dbg, ncores=B * n_groups_per_batch))
    nc, stats = _CACHE[key]
    consts = dict(mtab=mask_table(), rope=rope_tables(S))
    depth = inp["w_in"].shape[0]
    cur = np.asarray(x, dtype=np.float32)
    extra = None
    for l0 in range(0, depth, fused_depth):
        w = host_weights(inp, range(l0, l0 + fused_depth))
        in_maps = []
        for c in range(B * n_groups_per_batch):
            b, r = c // n_groups_per_batch, c % n_groups_per_batch
            d = core_inputs(cur[b], r, own, consts)
            if fused_depth == 1:
                d.pop("hsel")
            d.update(w)
            in_maps.append(d)
        res = run_bass_kernel_spmd(nc, in_maps, core_ids=list(range(len(in_maps))))
        outs = [r_["out"] for r_ in res.results]
        cur = np.stack([np.concatenate(outs[b * n_groups_per_batch:(b + 1) * n_groups_per_batch], 0) for b in range(B)], 0)
        extra = res.results
    return cur, extra


def kernel(**inputs):
    x = np.asarray(inputs["x"], dtype=np.float32)
    inp = {k: np.asarray(v, dtype=np.float32) for k, v in inputs.items() if k != "x"}
    out, _ = run_layers(x, inp, own=x.shape[1] // 4, n_groups_per_batch=4, fused_depth=inp["w_in"].shape[0])
    return out.astype(np.float32)
```

```python
import types
import numpy as np
import ml_dtypes
import concourse.bass as bass
import concourse.mybir as mybir
from concourse.bass_utils import run_bass_kernel_spmd

F32 = mybir.dt.float32
BF16 = mybir.dt.bfloat16
I32 = mybir.dt.int32
AF = mybir.ActivationFunctionType
ALU = mybir.AluOpType
AX = mybir.AxisListType

ENGS = ("pe", "act", "dve", "pool", "sp")
SEM_LIMIT = 20000

D = 1024
PIN = 2016
HALO = 1024
DFF = 4096
EPS = 1e-5
ALPHA = float((2 * 2) ** 0.25)
TABW = 2944
A_SCALE = 0.125
B_SCALE = float(96 ** -0.5)
C_GELU = 0.044715
K_GELU = float(np.sqrt(2.0 / np.pi))


def _freeze(fn):
    if fn is None or fn.__closure__ is None:
        return fn
    cells = []
    for c in fn.__closure__:
        try:
            cells.append(types.CellType(c.cell_contents))
        except ValueError:
            cells.append(c)
    return types.FunctionType(fn.__code__, fn.__globals__, fn.__name__, fn.__defaults__, tuple(cells))


class Prog:
    def __init__(self, nc):
        self.nc = nc
        self.ops = {e: [] for e in ENGS}
        self.state = {}
        self.dma_sems = {}
        self.pending = {e: [] for e in ENGS}
        self._cms = []
        self._scopes = []

    def _reg(self, cm):
        t = cm.__enter__()
        (self._scopes[-1] if self._scopes else self._cms).append(cm)
        return t

    def sem(self, name):
        self._n = getattr(self, "_n", 0) + 1
        cm = self.nc.semaphore("m%d_%s" % (self._n, name))
        s = cm.__enter__()
        self._cms.append(cm)
        return s

    def sb(self, name, shape, dt):
        self._n = getattr(self, "_n", 0) + 1
        return self._reg(self.nc.sbuf_tensor("sb%d_%s" % (self._n, name), list(shape), dt))

    def ps(self, name, shape, dt=F32):
        self._n = getattr(self, "_n", 0) + 1
        return self._reg(self.nc.psum_tensor("ps%d_%s" % (self._n, name), list(shape), dt))

    def push(self):
        self._scopes.append([])

    def pop(self):
        self.barrier()
        for cm in reversed(self._scopes.pop()):
            cm.__exit__(None, None, None)

    def close(self):
        for cm in reversed(self._cms):
            cm.__exit__(None, None, None)
        self._cms = []

    def barrier(self):
        evs = []
        for e in ENGS:
            lst = self.ops[e]
            for i in range(len(lst) - 1, -1, -1):
                if lst[i]["dma"] is None and lst[i]["fn"] is not None:
                    evs.append(("eng", e, i))
                    break
        for tag, ds in self.dma_sems.items():
            evs.append(("dma", ds[0], ds[1]))
        for e in ENGS:
            self.pending[e].extend(evs)
        self.state = {}

    def op(self, eng, fn, reads=(), writes=(), dma=None, inc=16):
        fn = _freeze(fn)
        deps = []
        for k in reads:
            st = self.state.get(k)
            if st and st[0] is not None:
                deps.append((st[0], True))
        for k in writes:
            st = self.state.get(k)
            if st:
                if st[0] is not None:
                    deps.append((st[0], False))
                for r in st[1]:
                    deps.append((r, False))
        lst = self.ops[eng]
        idx = len(lst)
        if dma is not None:
            if dma not in self.dma_sems:
                self.dma_sems[dma] = [self.sem("d_" + dma), 0]
            ds = self.dma_sems[dma]
            ds[1] += inc
            ev = ("dma", ds[0], ds[1])
        else:
            ev = ("eng", eng, idx)
        fdeps = []
        for d, raw in deps:
            if d[0] == "eng" and d[1] == eng and dma is None:
                if eng == "pe" or (not raw and eng != "pool"):
                    continue
            if d[0] == "dma":
                for ds_ in self.dma_sems.values():
                    if ds_[0] is d[1]:
                        cur = ds_[1] - (inc if (dma is not None and self.dma_sems[dma][0] is d[1]) else 0)
                        d = ("dma", d[1], max(d[2], cur))
            fdeps.append(d)
        for d in self.pending[eng]:
            if d[0] == "eng" and d[1] == eng and eng == "pe":
                continue
            fdeps.append(d)
        self.pending[eng] = []
        lst.append(dict(fn=fn, deps=fdeps, dma=dma, ev=ev, ms=False, inc=inc))
        for k in reads:
            st = self.state.setdefault(k, [None, []])
            st[1].append(ev)
        for k in writes:
            self.state[k] = [ev, []]
        return ev

    def wait_all_dma(self, eng, tags):
        deps = []
        for t in tags:
            ds = self.dma_sems[t]
            deps.append(("dma", ds[0], ds[1]))
        self.ops[eng].append(dict(fn=None, deps=deps, dma=None, ev=None, ms=False))

    def emit(self):
        nc = self.nc
        for e in ENGS:
            for o in self.ops[e]:
                for d in o["deps"]:
                    if d[0] == "eng":
                        self.ops[d[1]][d[2]]["ms"] = True
        msmap = {}
        for e in ENGS:
            cur = None
            cnt = 0
            for i, o in enumerate(self.ops[e]):
                if o["ms"]:
                    if cur is None or cnt >= SEM_LIMIT:
                        cur = self.sem("s_%s_%d" % (e, i))
                        cnt = 0
                    cnt += 1
                    msmap[(e, i)] = (cur, cnt)
                    o["inc"] = cur
        stats = {}

        def run(e, eng):
            waited = {}
            nw = 0
            for i, o in enumerate(self.ops[e]):
                for d in o["deps"]:
                    if d[0] == "eng":
                        sem, val = msmap[(d[1], d[2])]
                    else:
                        sem, val = d[1], d[2]
                    key = id(sem)
                    if waited.get(key, 0) >= val:
                        continue
                    waited[key] = val
                    eng.wait_ge(sem, val)
                    nw += 1
                if o["fn"] is None:
                    continue
                ins = o["fn"](eng)
                if o["dma"] is not None:
                    ins.then_inc(o["ev"][1], o.get("inc", 16))
                elif o["ms"]:
                    ins.then_inc(o["inc"], 1)
            stats[e] = (len(self.ops[e]), nw)

        with nc.Block() as block:
            @block.tensor
            def _(eng):
                run("pe", eng)

            @block.scalar
            def _(eng):
                run("act", eng)

            @block.vector
            def _(eng):
                run("dve", eng)

            @block.gpsimd
            def _(eng):
                run("pool", eng)

            @block.sync
            def _(eng):
                run("sp", eng)
        self.stats = stats
        return stats


def mask_table():
    slopes = (2.0 ** (-8.0 * np.arange(1, 7) / 6)).astype(np.float32)
    pp = np.arange(128)[:, None]
    col = np.arange(TABW)[None, :]
    delta = pp - col + 1408
    ad = np.abs(delta)
    c = (ad <= 64).astype(np.float32) + ((delta % 4 == 0) & (ad <= 256)).astype(np.float32) \
        + ((delta % 16 == 0) & (ad <= 1024)).astype(np.float32)
    tab = np.zeros((128, 6, TABW), np.float32)
    for h in range(6):
        tab[:, h, :] = c * np.exp(-(slopes[h] * ad.astype(np.float32)).astype(np.float32))
    return tab.astype(ml_dtypes.bfloat16)


def rope_tables(S):
    inv_freq = (10000.0 ** (-np.arange(0, 32, 2, dtype=np.float32) / 32)).astype(np.float32)
    ang = (np.arange(S, dtype=np.float32)[:, None] * inv_freq[None, :]).astype(np.float32)
    cos = np.cos(ang).astype(np.float32).T
    sin = np.sin(ang).astype(np.float32).T
    ct = np.concatenate([cos, cos], 0)
    st = np.concatenate([-sin, sin], 0)
    return np.ascontiguousarray(ct), np.ascontiguousarray(st)


class Cfg:
    def __init__(self, own, sf, depth=1, dbg=False, ncores=8):
        self.ncores = ncores
        self.OWN = own
        self.SF = sf
        self.OH = own + 2 * HALO
        self.NT = own // 128
        self.NQB = own // 512
        self.NG_OH = self.OH // 512
        self.NG_F = sf // 512
        self.NKT_OH = self.OH // 128
        self.NKT_F = sf // 128
        self.depth = depth
        self.dbg = dbg


W_NAMES = [
    ("w_in", [D, PIN]), ("w_q_up", [192, 576]), ("w_kv_up", [128, 768]), ("qn", [128, 2]), ("kvn", [128, 1]),
    ("sg_g", [256]), ("sg_b", [256]), ("sg_wT", [4, 128, 128]), ("sg_bT", [128, 4]), ("mixn", [128, 8]),
    ("w_out", [D, D]), ("ln1_g", [D]), ("ln1_b", [D]), ("w_ff1", [D, DFF]), ("b1T", [128, 32]),
    ("w_ff2", [DFF, D]), ("b_ff2", [D]), ("ln2_g", [D]), ("ln2_b", [D]),
]


def build_program(cfg):
    nc = bass.Bass("TRN2", target_bir_lowering=False)
    OWN, SF, OH, NT = cfg.OWN, cfg.SF, cfg.OH, cfg.NT
    io = {}
    io["xo"] = nc.dram_tensor("xo", [OH, D], F32, kind="ExternalInput").ap()
    io["xf"] = nc.dram_tensor("xf", [SF, D], F32, kind="ExternalInput").ap()
    io["vmask"] = nc.dram_tensor("vmask", [128, OH // 128], F32, kind="ExternalInput").ap()
    io["mtab"] = nc.dram_tensor("mtab", [128, 6, TABW], BF16, kind="ExternalInput").ap()
    io["ckt"] = nc.dram_tensor("ckt", [32, SF], F32, kind="ExternalInput").ap()
    io["skt"] = nc.dram_tensor("skt", [32, SF], F32, kind="ExternalInput").ap()
    io["cqt"] = nc.dram_tensor("cqt", [32, OWN], F32, kind="ExternalInput").ap()
    io["sqt"] = nc.dram_tensor("sqt", [32, OWN], F32, kind="ExternalInput").ap()
    for nm, shp in W_NAMES:
        io[nm] = nc.dram_tensor(nm, [cfg.depth] + shp, F32, kind="ExternalInput").ap()
    io["out"] = nc.dram_tensor("out", [OWN, D], F32, kind="ExternalOutput").ap()
    io["YT"] = nc.dram_tensor("YT", [D, OWN], BF16, kind="Internal").ap()
    io["X1"] = nc.dram_tensor("X1", [OWN, D], F32, kind="Internal").ap()
    if cfg.depth > 1:
        io["hsel"] = nc.dram_tensor("hsel", [128, 8], F32, kind="ExternalInput").ap()
        io["XL"] = nc.dram_tensor("XL", [OWN, D], F32, kind="Internal").ap()
        io["XBc"] = nc.dram_tensor("XBc", [OWN, D], BF16, kind="Internal").ap()
        io["XG"] = nc.dram_tensor("XG", [OWN // 512, (SF // OWN) * 512, D], BF16, kind="Internal").ap()
        io["XH"] = nc.dram_tensor("XH", [2 * HALO, D], BF16, kind="Internal").ap()
    if cfg.dbg:
        io["dbg_yt"] = nc.dram_tensor("dbg_yt", [D, OWN], BF16, kind="ExternalOutput").ap()
        io["dbg_ss"] = nc.dram_tensor("dbg_ss", [128, 13 * NT], F32, kind="ExternalOutput").ap()
        io["dbg_x1"] = nc.dram_tensor("dbg_x1", [OWN, D], F32, kind="ExternalOutput").ap()

    p = Prog(nc)
    DBG["on"] = cfg.dbg
    G = {}
    G["ident"] = p.sb("ident", [128, 128], BF16)
    identf = p.sb("identf", [128, 128], F32)
    G["e65"] = p.sb("e65", [128, 64], F32)
    G["ones"] = p.sb("onesf", [128, 128], F32)
    G["ss"] = p.sb("ss", [128, 13, NT], F32)
    p.op("pool", lambda e: e.memset(identf[:], 0.0), writes=["identf"])
    p.op("pool", lambda e: e.affine_select(out=identf[:], in_=identf[:], compare_op=ALU.not_equal, fill=1.0,
                                           base=0, pattern=[[-1, 128]], channel_multiplier=1),
         reads=["identf"], writes=["identf"])
    p.op("pool", lambda e: e.tensor_copy(G["ident"][:], identf[:]), reads=["identf"], writes=["ident"])
    p.op("pool", lambda e: e.memset(G["e65"][:], 0.0), writes=["e65"])
    p.op("pool", lambda e: e.memset(G["e65"][64:65, :], 1.0), reads=["e65"], writes=["e65"])
    p.op("pool", lambda e: e.memset(G["ones"][:], 1.0), writes=["ones"])
    p.barrier()

    ngo = cfg.NG_OH
    for l in range(cfg.depth):
        W = {nm: io[nm][l] for nm, _ in W_NAMES}
        last = (l == cfg.depth - 1)
        if l == 0:
            src = dict(oh=lambda g: io["xo"][g * 512:(g + 1) * 512, :],
                       full=lambda g: io["xf"][g * 512:(g + 1) * 512, :],
                       res=lambda ti: io["xo"][HALO + ti * 128:HALO + (ti + 1) * 128, :])
        else:
            def oh2(g):
                if g < 2:
                    return io["XH"][g * 512:(g + 1) * 512, :]
                if g >= ngo - 2:
                    return io["XH"][1024 + (g - (ngo - 2)) * 512:1024 + (g - (ngo - 2) + 1) * 512, :]
                return io["XL"][(g - 2) * 512:(g - 1) * 512, :]
            src = dict(oh=oh2, full=lambda g: xg_rows(io, cfg, g * 512, 512),
                       res=lambda ti: io["XL"][ti * 128:(ti + 1) * 128, :])
        if last:
            dst = dict(f32=io["out"], bf16=None, tag="out")
        else:
            dst = dict(f32=io["XL"], bf16=io["XBc"], tag="xl")
        layer(p, cfg, G, io, W, src, dst)
        if not last:
            exchange(p, cfg, G, io)

    if cfg.dbg:
        p.op("sp", lambda e: e.dma_start(out=io["dbg_yt"], in_=io["YT"]), reads=[], dma="out")
        p.op("sp", lambda e: e.dma_start(out=io["dbg_ss"], in_=G["ss"][:].rearrange("p a b -> p (a b)")), dma="out")
        p.op("sp", lambda e: e.dma_start(out=io["dbg_x1"], in_=io["X1"]), dma="out")
    p.wait_all_dma("sp", ["out"])
    stats = p.emit()
    p.close()
    return nc, stats


DBG = {}


def dbg_dump(p, name, ap, shape, dt):
    if not DBG.get("on"):
        return
    t = p.nc.dram_tensor("dd_" + name, list(shape), dt, kind="ExternalOutput").ap()
    p.barrier()
    p.op("sp", lambda e: e.dma_start(out=t, in_=ap), dma="out")
    p.barrier()


def load_x_group(p, src_ap, xb, key, tag):
    p.op("pool", lambda e: e.dma_start(out=xb[:], in_=src_ap.rearrange("(j p) d -> p j d", p=128)),
         writes=[key], dma=tag)


def transpose_group(p, G, xb, xbkey, xT, xTkey, trps, n_tok_tiles=4):
    for kc in range(8):
        tp = trps[kc % 2]
        tkey = "tr%d" % (kc % 2)
        for j in range(n_tok_tiles):
            p.op("pe", lambda e, tp=tp, j=j, kc=kc: e.transpose(tp[:, j * 128:(j + 1) * 128],
                                                                 xb[:, j, kc * 128:(kc + 1) * 128], G["ident"][:]),
                 reads=[xbkey, "ident"], writes=[tkey])
        w = n_tok_tiles * 128
        p.op("act", lambda e, tp=tp, kc=kc, w=w: e.copy(xT[:, kc, 0:w], tp[:, 0:w]), reads=[tkey], writes=[xTkey])


def attn_post(p, G, io, ops, okey, h_row, qb, ss_idx, bufs, it):
    r = it % 2
    osb, rden, ysq, ybf = bufs["osb"][r], bufs["rden"], bufs["ysq"][r], bufs["ybf"][r]
    ko, kr, ks, kb = "osb%d" % r, "rden", "ysq%d" % r, "ybf%d" % r
    p.op("dve", lambda e: e.tensor_copy(osb[0:65, :], ops[0:65, :]), reads=[okey], writes=[ko])

    def deferred():
        mp, mk = bufs["peek"]()
        denps, kd = mp[:, 0:512], mk
        p.op("pe", lambda e: e.matmul(denps[0:64, :], lhsT=G["e65"][0:65, :], rhs=osb[0:65, :], start=True, stop=True),
             reads=[ko, "e65"], writes=[kd])
        p.op("dve", lambda e: e.tensor_copy(rden[0:64, :], denps[0:64, :]), reads=[kd], writes=[kr])
        p.op("dve", lambda e: e.reciprocal(rden[0:64, :], rden[0:64, :]), reads=[kr], writes=[kr])
        p.op("pool", lambda e: e.tensor_tensor(out=osb[0:64, :], in0=osb[0:64, :], in1=rden[0:64, :], op=ALU.mult),
             reads=[ko, kr], writes=[ko])
        p.op("pool", lambda e: e.tensor_copy(ybf[0:64, :], osb[0:64, :]), reads=[ko], writes=[kb])
        p.op("sp", lambda e: e.dma_start(out=io["YT"][h_row:h_row + 64, qb * 512:(qb + 1) * 512], in_=ybf[0:64, :]),
             reads=[kb], dma="yt%d" % r)
        p.op("pool", lambda e: e.tensor_tensor(out=ysq[0:64, :], in0=osb[0:64, :], in1=osb[0:64, :], op=ALU.mult),
             reads=[ko], writes=[ks])

        def stage_b():
            mp2, mk2 = bufs["peek"]()
            ssps = mp2[:, 512:1024]
            for j in range(4):
                p.op("pe", lambda e, j=j: e.matmul(ssps[:, j:j + 1], lhsT=ysq[0:64, j * 128:(j + 1) * 128],
                                                   rhs=G["ones"][0:64, 0:1], start=True, stop=True),
                     reads=[ks, "ones"], writes=[mk2])
            p.op("dve", lambda e: e.tensor_copy(G["ss"][:, ss_idx, qb * 4:(qb + 1) * 4], ssps[:, 0:4]),
                 reads=[mk2], writes=["ss"])
        return stage_b
    return deferred


def xg_rows(io, cfg, R0, n):
    j, q = R0 // cfg.OWN, R0 % cfg.OWN
    i, t = q // 512, q % 512
    assert t + n <= 512
    return io["XG"][i, j * 512 + t:j * 512 + t + n, :]


def exchange(p, cfg, G, io):
    OWN = cfg.OWN
    ngrp = cfg.SF // OWN
    groups = [list(range(b * ngrp, (b + 1) * ngrp)) for b in range(cfg.ncores // ngrp)]
    p.barrier()
    p.push()
    hsel = p.sb("hsel", [128, 8], F32)
    cands = [p.sb("cand%d" % i, [128, ngrp, D], BF16) for i in range(2)]
    hacc = p.sb("hacc", [128, D], F32)
    houts = [p.sb("hout%d" % i, [128, D], BF16) for i in range(2)]
    p.op("sp", lambda e: e.dma_start(out=hsel[:], in_=io["hsel"]), writes=["hsel"], dma="hsel")
    it = 0
    for side in range(2):
        for t in range(HALO // 128):
            cand, ck = cands[it % 2], "cand%d" % (it % 2)
            hout, hk = houts[it % 2], "hout%d" % (it % 2)
            for j in range(ngrp):
                r0 = j * OWN + (OWN - HALO if side == 0 else 0) + t * 128
                p.op("sp", lambda e, j=j, r0=r0: e.dma_start(out=cand[:, j, :], in_=xg_rows(io, cfg, r0, 128)),
                     writes=[ck], dma=ck)
            p.op("dve", lambda e: e.tensor_scalar(hacc[:, :], cand[:, 0, :], hsel[:, side * 4:side * 4 + 1], None, op0=ALU.mult),
                 reads=[ck, "hsel"], writes=["hacc"])
            for j in range(1, ngrp):
                p.op("dve", lambda e, j=j: e.scalar_tensor_tensor(out=hacc[:, :], in0=cand[:, j, :],
                                                                  scalar=hsel[:, side * 4 + j:side * 4 + j + 1], in1=hacc[:, :],
                                                                  op0=ALU.mult, op1=ALU.add), reads=[ck, "hsel", "hacc"], writes=["hacc"])
            p.op("pool", lambda e: e.tensor_copy(hout[:, :], hacc[:, :]), reads=["hacc"], writes=[hk])
            r1 = side * HALO + t * 128
            p.op("sp", lambda e, r1=r1: e.dma_start(out=io["XH"][r1:r1 + 128, :], in_=hout[:, :]), reads=[hk], dma="xh%d" % (it % 2))
            it += 1
    p.pop()


def layer(p, cfg, G, io, W, src, dst):
    OWN, SF, OH, NT, NQB = cfg.OWN, cfg.SF, cfg.OH, cfg.NT, cfg.NQB
    ident = G["ident"]

    p.push()
    CQ = p.sb("CQ", [128, 2, OWN], BF16)
    p.push()
    KaT = p.sb("KaT", [128, 3, OH], BF16)
    QaT = p.sb("QaT", [128, 3, OWN], BF16)
    Va = p.sb("Va", [128, OH // 128, 6, 65], BF16)

    p.push()
    w_in = p.sb("w_in", [128, 8, PIN], BF16)
    wsT = p.sb("wsT", [128, 4, 128], BF16)
    sg_g = p.sb("sg_g", [128, 256], F32)
    sg_b = p.sb("sg_b", [128, 256], F32)
    bsT = p.sb("bsT", [128, 4], F32)
    vm = p.sb("vm", [128, OH // 128], F32)
    xbs = [p.sb("xb%d" % i, [128, 4, D], BF16) for i in range(2)]
    xTs = [p.sb("xT%d" % i, [128, 8, 512], BF16) for i in range(2)]
    zh = p.sb("zh", [128, 512], F32)
    w1 = p.sb("w1", [128, 512], F32)
    w2 = p.sb("w2", [128, 512], F32)
    zzs = [p.sb("zz%d" % i, [128, 512], F32) for i in range(4)]
    vn = p.sb("vn", [128, 256], F32)
    vnbs = [p.sb("vnb%d" % i, [128, 256], BF16) for i in range(4)]
    yc = p.sb("yc", [128, 256], F32)
    ycbs = [p.sb("ycb%d" % i, [128, 256], BF16) for i in range(2)] * 2
    sgu_pending = []
    ycT = p.sb("ycT", [128, 2, 512], BF16)
    junk = vn
    st6 = p.sb("st6", [128, 6], F32)
    mv = p.sb("mv", [128, 2], F32)
    rs = p.sb("rs", [128, 1], F32)
    cqs = [zzs[0], zzs[1]]
    cqq = [zh, w1]
    cqr = w2
    trps = [p.ps("trps%d" % i, [128, 1024], BF16) for i in range(2)]
    pps = [p.ps("pps%d" % i, [128, 512]) for i in range(3)]
    ssq = p.ps("ssq", [128, 512])

    for kc in range(8):
        p.op("pool", lambda e, kc=kc: e.dma_start(out=w_in[:, kc, :], in_=W["w_in"][kc * 128:(kc + 1) * 128, :]),
             writes=["w_in"], dma="w_in")
    p.op("pool", lambda e: e.dma_start(out=wsT[:], in_=W["sg_wT"].rearrange("g s t -> s g t")), writes=["wsT"], dma="wsm")
    p.op("sp", lambda e: e.dma_start(out=sg_g[:], in_=W["sg_g"].partition_broadcast(128)), writes=["sg_g"], dma="wsm2")
    p.op("sp", lambda e: e.dma_start(out=sg_b[:], in_=W["sg_b"].partition_broadcast(128)), writes=["sg_b"], dma="wsm2")
    p.op("sp", lambda e: e.dma_start(out=bsT[:], in_=W["sg_bT"]), writes=["bsT"], dma="wsm2")
    p.op("sp", lambda e: e.dma_start(out=vm[:], in_=io["vmask"]), writes=["vm"], dma="wsm2")
    for h in range(6):
        p.op("dve", lambda e, h=h: e.tensor_copy(Va[:, :, h, 64:65], vm[:].rearrange("p (t o) -> p t o", o=1)),
             reads=["vm"], writes=["Va"])

    pp_i = [0]

    def next_pps():
        i = pp_i[0] % 3
        pp_i[0] += 1
        return pps[i], "pps%d" % i

    ngo = cfg.NG_OH
    load_x_group(p, src["oh"](0), xbs[0], "xb0", "xb0")
    for g in range(ngo):
        xb, xbk = xbs[g % 2], "xb%d" % (g % 2)
        xT, xTk = xTs[g % 2], "xT%d" % (g % 2)
        if g + 1 < ngo:
            load_x_group(p, src["oh"](g + 1), xbs[(g + 1) % 2], "xb%d" % ((g + 1) % 2), "xb%d" % ((g + 1) % 2))
        transpose_group(p, G, xb, xbk, xT, xTk, trps)
        own = (g * 512 >= HALO) and (g * 512 < HALO + OWN)
        go = g - HALO // 512
        for c in range(3):
            ps_, pk = next_pps()
            for kc in range(8):
                p.op("pe", lambda e, ps_=ps_, kc=kc, c=c: e.matmul(ps_[:, :], lhsT=w_in[:, kc, 384 + c * 128:384 + (c + 1) * 128],
                                                                   rhs=xT[:, kc, :], start=(kc == 0), stop=(kc == 7)),
                     reads=["w_in", xTk], writes=[pk])
            p.op("dve", lambda e, ps_=ps_, c=c, g=g: e.tensor_copy(KaT[:, c, g * 512:(g + 1) * 512], ps_[:, :]),
                 reads=[pk], writes=["KaT"])
        if own:
            for c in range(3):
                ps_, pk = next_pps()
                for kc in range(8):
                    p.op("pe", lambda e, ps_=ps_, kc=kc, c=c: e.matmul(ps_[:, :], lhsT=w_in[:, kc, c * 128:(c + 1) * 128],
                                                                       rhs=xT[:, kc, :], start=(kc == 0), stop=(kc == 7)),
                         reads=["w_in", xTk], writes=[pk])
                p.op("dve", lambda e, ps_=ps_, c=c, go=go: e.tensor_copy(QaT[:, c, go * 512:(go + 1) * 512], ps_[:, :]),
                     reads=[pk], writes=["QaT"])
        for j in range(4):
            ps_, pk = next_pps()
            for kc in range(8):
                p.op("pe", lambda e, ps_=ps_, kc=kc, j=j: e.matmul(ps_[:, 0:384], lhsT=xT[:, kc, j * 128:(j + 1) * 128],
                                                                   rhs=w_in[:, kc, 768:1152], start=(kc == 0), stop=(kc == 7)),
                     reads=["w_in", xTk], writes=[pk])
            p.op("dve", lambda e, ps_=ps_, j=j, g=g: e.tensor_copy(Va[:, g * 4 + j, :, 0:64],
                                                                   ps_[:, 0:384].rearrange("p (h c) -> p h c", h=6)),
                 reads=[pk], writes=["Va"])
        if not own:
            continue
        for f_ in sgu_pending:
            f_()
        sgu_pending = []
        psa, pka = next_pps()
        psb, pkb = next_pps()
        for kc in range(8):
            p.op("pe", lambda e, kc=kc: e.matmul(psa[:, :], lhsT=w_in[:, kc, 1152:1280], rhs=xT[:, kc, :],
                                                 start=(kc == 0), stop=(kc == 7)), reads=["w_in", xTk], writes=[pka])
        for kc in range(8):
            p.op("pe", lambda e, kc=kc: e.matmul(psb[0:64, :], lhsT=w_in[:, kc, 1280:1344], rhs=xT[:, kc, :],
                                                 start=(kc == 0), stop=(kc == 7)), reads=["w_in", xTk], writes=[pkb])
        p.op("dve", lambda e: e.tensor_copy(cqs[0][:, :], psa[:, :]), reads=[pka], writes=["zz0"])
        p.op("dve", lambda e: e.tensor_copy(cqs[1][0:64, :], psb[0:64, :]), reads=[pkb], writes=["zz1"])
        p.op("pool", lambda e: e.tensor_tensor(out=cqq[0][:, :], in0=cqs[0][:, :], in1=cqs[0][:, :], op=ALU.mult),
             reads=["zz0"], writes=["zh"])
        p.op("pool", lambda e: e.tensor_tensor(out=cqq[1][0:64, :], in0=cqs[1][0:64, :], in1=cqs[1][0:64, :], op=ALU.mult),
             reads=["zz1"], writes=["w1"])
        p.op("pe", lambda e: e.matmul(ssq[:, :], lhsT=G["ones"][:, :], rhs=cqq[0][:, :], start=True, stop=False),
             reads=["zh", "ones"], writes=["ssq"])
        p.op("pe", lambda e: e.matmul(ssq[:, :], lhsT=G["ones"][0:64, :], rhs=cqq[1][0:64, :], start=False, stop=True),
             reads=["w1", "ones"], writes=["ssq"])
        p.op("act", lambda e: e.activation(out=cqr[:, :], in_=ssq[:, :], func=AF.Sqrt, scale=1.0 / 192, bias=EPS),
             reads=["ssq"], writes=["w2"])
        p.op("dve", lambda e: e.reciprocal(cqr[:, :], cqr[:, :]), reads=["w2"], writes=["w2"])
        p.op("dve", lambda e, go=go: e.tensor_tensor(out=CQ[:, 0, go * 512:(go + 1) * 512], in0=cqs[0][:, :], in1=cqr[:, :],
                                                     op=ALU.mult), reads=["zz0", "w2"], writes=["CQ"])
        p.op("dve", lambda e, go=go: e.tensor_tensor(out=CQ[0:64, 1, go * 512:(go + 1) * 512], in0=cqs[1][0:64, :],
                                                     in1=cqr[0:64, :], op=ALU.mult), reads=["zz1", "w2"], writes=["CQ"])
        for f_ in sgu_pending:
            f_()
        sgu_pending = []
        for j in range(4):
            ti = go * 4 + j
            zz = zzs[j]
            zk = "zz%d" % j
            ps_, pk = next_pps()
            for kc in range(8):
                p.op("pe", lambda e, ps_=ps_, kc=kc, j=j: e.matmul(ps_[:, :], lhsT=xT[:, kc, j * 128:(j + 1) * 128],
                                                                   rhs=w_in[:, kc, 1504:2016], start=(kc == 0), stop=(kc == 7)),
                     reads=["w_in", xTk], writes=[pk])
            p.op("act", lambda e, ps_=ps_: e.activation(out=zh[:, :], in_=ps_[:, :], func=AF.Copy, scale=0.5),
                 reads=[pk], writes=["zh"])
            p.op("pool", lambda e: e.tensor_tensor(out=w1[:, :], in0=zh[:, :], in1=zh[:, :], op=ALU.mult),
                 reads=["zh"], writes=["w1"])
            p.op("dve", lambda e: e.tensor_scalar(w1[:, :], w1[:, :], 4.0 * C_GELU, 1.0, op0=ALU.mult, op1=ALU.add),
                 reads=["w1"], writes=["w1"])
            p.op("pool", lambda e: e.tensor_tensor(out=w2[:, :], in0=w1[:, :], in1=zh[:, :], op=ALU.mult),
                 reads=["w1", "zh"], writes=["w2"])
            p.op("act", lambda e: e.activation(out=w2[:, :], in_=w2[:, :], func=AF.Tanh, scale=2.0 * K_GELU),
                 reads=["w2"], writes=["w2"])
            p.op("dve", lambda e, zz=zz: e.scalar_tensor_tensor(out=zz[:, :], in0=w2[:, :], scalar=1.0, in1=zh[:, :],
                                                                op0=ALU.add, op1=ALU.mult), reads=["w2", "zh"], writes=[zk])

        def sgu_ln(j, zz, zk):
            vnb_, vk = vnbs[j], "vnb%d" % j
            p.op("dve", lambda e: e.bn_stats(st6[:, :], zz[:, 256:512]), reads=[zk], writes=["st6"])
            p.op("dve", lambda e: e.bn_aggr(mv[:, :], st6[:, :]), reads=["st6"], writes=["mv"])
            p.op("act", lambda e: e.activation(out=rs[:, :], in_=mv[:, 1:2], func=AF.Sqrt, scale=1.0, bias=EPS),
                 reads=["mv"], writes=["rs"])
            p.op("dve", lambda e: e.reciprocal(rs[:, :], rs[:, :]), reads=["rs"], writes=["rs"])
            p.op("dve", lambda e: e.tensor_scalar(vn[:, :], zz[:, 256:512], mv[:, 0:1], rs[:, 0:1], op0=ALU.subtract,
                                                  op1=ALU.mult), reads=[zk, "mv", "rs"], writes=["vn"])
            p.op("pool", lambda e: e.tensor_tensor(out=vn[:, :], in0=vn[:, :], in1=sg_g[:, :], op=ALU.mult),
                 reads=["vn", "sg_g"], writes=["vn"])
            p.op("pool", lambda e: e.tensor_tensor(out=vnb_[:, :], in0=vn[:, :], in1=sg_b[:, :], op=ALU.add),
                 reads=["vn", "sg_b"], writes=[vk])

        def sgu_mix(j, zz, zk, ti):
            vnb_, vk = vnbs[j], "vnb%d" % j
            ycb_, yk_ = ycbs[j], "ycb%d" % (j % 2)
            mps, mk_ = next_pps()
            for gg in range(4):
                p.op("pe", lambda e, gg=gg: e.matmul(mps[:, gg * 64:(gg + 1) * 64], lhsT=wsT[:, gg, :],
                                                     rhs=vnb_[:, gg * 64:(gg + 1) * 64], start=True, stop=True),
                     reads=["wsT", vk], writes=[mk_])
            for gg in range(4):
                p.op("dve", lambda e, gg=gg: e.scalar_tensor_tensor(out=yc[:, gg * 64:(gg + 1) * 64],
                                                                    in0=mps[:, gg * 64:(gg + 1) * 64], scalar=bsT[:, gg:gg + 1],
                                                                    in1=zz[:, gg * 64:(gg + 1) * 64], op0=ALU.add, op1=ALU.mult),
                     reads=[mk_, "bsT", zk], writes=["yc"])
            p.op("pool", lambda e: e.tensor_tensor(out=junk[:, :], in0=yc[:, :], in1=yc[:, :], op=ALU.mult),
                 reads=["yc"], writes=["vn"])
            p.op("dve", lambda e: e.reduce_sum(out=G["ss"][:, 12, ti:ti + 1], in_=junk[:, :], axis=AX.X),
                 reads=["vn"], writes=["ss"])
            p.op("pool", lambda e: e.tensor_copy(ycb_[:, :], yc[:, :]), reads=["yc"], writes=[yk_])

        def sgu_tr(j):
            ycb_, yk_ = ycbs[j], "ycb%d" % (j % 2)
            for c2 in range(2):
                tp = trps[c2]
                p.op("pe", lambda e, tp=tp, c2=c2: e.transpose(tp[:, 512:640], ycb_[:, c2 * 128:(c2 + 1) * 128], ident[:]),
                     reads=[yk_, "ident"], writes=["tr%d" % c2])
                p.op("act", lambda e, tp=tp, c2=c2: e.copy(ycT[:, c2, j * 128:(j + 1) * 128], tp[:, 512:640]),
                     reads=["tr%d" % c2], writes=["ycT"])

        def make_stage2(go_):
            tis = [go_ * 4 + j for j in range(4)]

            def stage2():
                for j in range(4):
                    sgu_ln(j, zzs[j], "zz%d" % j)
                for j in range(2):
                    sgu_mix(j, zzs[j], "zz%d" % j, tis[j])
                for j in range(2):
                    sgu_tr(j)
                for j in range(2, 4):
                    sgu_mix(j, zzs[j], "zz%d" % j, tis[j])
                for j in range(2, 4):
                    sgu_tr(j)
                p.op("sp", lambda e: e.dma_start(out=io["YT"][768:1024, go_ * 512:(go_ + 1) * 512].rearrange("(c p) t -> p c t", p=128),
                                                 in_=ycT[:, :, :]), reads=["ycT"], dma="ytc")
            return stage2
        sgu_pending.append(make_stage2(go))
    for f_ in sgu_pending:
        f_()
    sgu_pending = []
    dbg_dump(p, "xT", xTs[(ngo - 1) % 2][:, :, :], [128, 8, 512], BF16)
    dbg_dump(p, "KaT", KaT[:, :, :], [128, 3, OH], BF16)
    dbg_dump(p, "QaT", QaT[:, :, :], [128, 3, OWN], BF16)
    dbg_dump(p, "Va", Va[:, :, :, :], [128, OH // 128, 6, 65], BF16)
    dbg_dump(p, "CQ", CQ[:, :, :], [128, 2, OWN], BF16)
    dbg_dump(p, "w_in", w_in[:, :, :], [128, 8, PIN], BF16)
    p.pop()

    p.push()
    tab = p.sb("tab", [128, 6, TABW], BF16)
    NSA = 3
    Es = [p.sb("E%d" % i, [128, 1024], BF16) for i in range(NSA)]
    PTs = [p.sb("PT%d" % i, [128, 1024], BF16) for i in range(NSA)]
    Sps = [p.ps("S%d" % i, [128, 1024]) for i in range(NSA)]
    Ops = [p.ps("O%d" % i, [128, 512]) for i in range(2)]
    gic = [0]

    def alloc_s():
        i = gic[0] % NSA
        gic[0] += 1
        return Sps[i], "S%d" % i

    def peek_s():
        i = gic[0] % NSA
        return Sps[i], "S%d" % i
    bufs = dict(peek=peek_s, osb=[p.sb("osb%d" % i, [128, 512], F32) for i in range(2)], rden=p.sb("rden", [128, 512], F32),
                ysq=[p.sb("ysq%d" % i, [128, 512], F32) for i in range(2)], ybf=[p.sb("ybf%d" % i, [128, 512], BF16) for i in range(2)],
                alloc=alloc_s)
    for h in range(6):
        p.op("sp", lambda e, h=h: e.dma_start(out=tab[:, h, :], in_=io["mtab"][:, h, :]), writes=["tab"], dma="tab")
    it = 0
    pend = []
    pendb = []
    NKT2 = 20
    for c in range(3):
        hh = (2 * c, 2 * c + 1)
        for qb in range(NQB):
            kt0 = 4 * qb

            def qk(ii):
                S, sk = alloc_s()
                kt = kt0 + ii
                for u in range(2):
                    p.op("pe", lambda e, S=S, u=u, kt=kt: e.matmul(S[:, u * 512:(u + 1) * 512],
                                                                   lhsT=KaT[u * 64:u * 64 + 64, c, kt * 128:(kt + 1) * 128],
                                                                   rhs=QaT[u * 64:u * 64 + 64, c, qb * 512:(qb + 1) * 512],
                                                                   start=True, stop=True),
                         reads=["KaT", "QaT"], writes=[sk])
                return S, sk

            inflight = [qk(0), qk(1)]
            for ii in range(NKT2):
                if ii + 2 < NKT2:
                    inflight.append(qk(ii + 2))
                S, sk = inflight.pop(0)
                E, ek = Es[ii % NSA], "E%d" % (ii % NSA)
                PT, pk = PTs[ii % NSA], "PT%d" % (ii % NSA)
                kt = kt0 + ii
                start = 2432 - 128 * ii
                p.op("act", lambda e, S=S, E=E: e.activation(out=E[:, :], in_=S[:, :], func=AF.Exp, scale=A_SCALE),
                     reads=[sk], writes=[ek])
                for u in range(2):
                    p.op("dve", lambda e, E=E, PT=PT, u=u, start=start: e.tensor_tensor(
                        out=PT[:, u * 512:(u + 1) * 512], in0=E[:, u * 512:(u + 1) * 512],
                        in1=tab[:, hh[u], start:start + 512], op=ALU.mult), reads=[ek, "tab"], writes=[pk + "_%d" % u])
                for u in range(2):
                    p.op("pe", lambda e, PT=PT, u=u, kt=kt, ii=ii: e.matmul(
                        Ops[u][0:65, :], lhsT=Va[:, kt, hh[u], 0:65], rhs=PT[:, u * 512:(u + 1) * 512],
                        start=(ii == 0), stop=(ii == NKT2 - 1)),
                        reads=["Va", pk + "_%d" % u], writes=["O%d" % u])
                if ii == 3:
                    while pend:
                        pendb.append(pend.pop(0)())
                if ii == 9:
                    while pendb:
                        pendb.pop(0)()
            for u in range(2):
                pend.append(attn_post(p, G, io, Ops[u], "O%d" % u, hh[u] * 64, qb, hh[u], bufs, u))
            it += 1
    while pend:
        pendb.append(pend.pop(0)())
    while pendb:
        pendb.pop(0)()
    p.pop()
    p.pop()

    CKV = p.sb("CKV", [128, SF], BF16)
    KT = p.sb("KT", [128, SF], BF16)
    p.push()
    wB = p.sb("wB", [128, 8, 160], BF16)
    wk96 = p.sb("wk96", [128, 8, 96], BF16)
    wk96r = p.sb("wk96r", [128, 8, 96], BF16)
    xbs = [p.sb("bxb%d" % i, [128, 4, D], BF16) for i in range(2)]
    xTs = [p.sb("bxT%d" % i, [128, 8, 512], BF16) for i in range(2)]
    sqs = [p.sb("bsq%d" % i, [128, 512], F32) for i in range(2)]
    rrs = [p.sb("brr%d" % i, [128, 512], F32) for i in range(2)]
    cks = [p.sb("cks%d" % i, [128, 512], F32) for i in range(2)]
    sks = [p.sb("sks%d" % i, [128, 512], F32) for i in range(2)]
    t1s = [p.sb("bt1_%d" % i, [128, 512], F32) for i in range(2)]
    t2s_ = [p.sb("bt2_%d" % i, [128, 512], F32) for i in range(2)]
    trps = [p.ps("btrps%d" % i, [128, 1024], BF16) for i in range(2)]
    pps = [p.ps("bpps%d" % i, [128, 512]) for i in range(4)]
    ssqs = [p.ps("bssq%d" % i, [128, 512]) for i in range(2)]
    for kc in range(8):
        p.op("pool", lambda e, kc=kc: e.dma_start(out=wB[:, kc, :], in_=W["w_in"][kc * 128:(kc + 1) * 128, 1344:1504]),
             writes=["wB"], dma="wB")
    p.op("dve", lambda e: e.memset(wk96[:], 0.0), writes=["wk96"])
    p.op("dve", lambda e: e.memset(wk96r[:], 0.0), writes=["wk96r"])
    p.op("dve", lambda e: e.tensor_copy(wk96[:, :, 64:96], wB[:, :, 128:160]), reads=["wB", "wk96"], writes=["wk96"])
    p.op("dve", lambda e: e.tensor_copy(wk96r[:, :, 64:80], wB[:, :, 144:160]), reads=["wB", "wk96r"], writes=["wk96r"])
    p.op("dve", lambda e: e.tensor_copy(wk96r[:, :, 80:96], wB[:, :, 128:144]), reads=["wB", "wk96r"], writes=["wk96r"])
    ngf = cfg.NG_F
    load_x_group(p, src["full"](0), xbs[0], "bxb0", "bxb0")
    for g in range(ngf):
        xb, xbk = xbs[g % 2], "bxb%d" % (g % 2)
        xT, xTk = xTs[g % 2], "bxT%d" % (g % 2)
        ck, ckk = cks[g % 2], "cks%d" % (g % 2)
        sk_, skk = sks[g % 2], "sks%d" % (g % 2)
        if g + 1 < ngf:
            load_x_group(p, src["full"](g + 1), xbs[(g + 1) % 2], "bxb%d" % ((g + 1) % 2), "bxb%d" % ((g + 1) % 2))
        p.op("sp", lambda e, ck=ck, g=g: e.dma_start(out=ck[64:96, :], in_=io["ckt"][:, g * 512:(g + 1) * 512]),
             writes=[ckk], dma=ckk)
        p.op("sp", lambda e, sk_=sk_, g=g: e.dma_start(out=sk_[64:96, :], in_=io["skt"][:, g * 512:(g + 1) * 512]),
             writes=[skk], dma=skk)
        transpose_group(p, G, xb, xbk, xT, xTk, trps)
        sq, sqk = sqs[g % 2], "bsq%d" % (g % 2)
        rr, rrk = rrs[g % 2], "brr%d" % (g % 2)
        t1, t1k = t1s[g % 2], "bt1_%d" % (g % 2)
        t2, t2k = t2s_[g % 2], "bt2_%d" % (g % 2)
        ssq, ssqk = ssqs[g % 2], "bssq%d" % (g % 2)
        pa, pb, pc = pps[(3 * g) % 4], pps[(3 * g + 1) % 4], pps[(3 * g + 2) % 4]
        ka, kb, kc_ = "bpps%d" % ((3 * g) % 4), "bpps%d" % ((3 * g + 1) % 4), "bpps%d" % ((3 * g + 2) % 4)
        for kc in range(8):
            p.op("pe", lambda e, kc=kc, pa=pa: e.matmul(pa[:, :], lhsT=wB[:, kc, 0:128], rhs=xT[:, kc, :],
                                                        start=(kc == 0), stop=(kc == 7)), reads=["wB", xTk], writes=[ka])
        for kc in range(8):
            p.op("pe", lambda e, kc=kc, pb=pb: e.matmul(pb[0:96, :], lhsT=wk96[:, kc, :], rhs=xT[:, kc, :],
                                                        start=(kc == 0), stop=(kc == 7)), reads=["wk96", xTk], writes=[kb])
        for kc in range(8):
            p.op("pe", lambda e, kc=kc, pc=pc: e.matmul(pc[0:96, :], lhsT=wk96r[:, kc, :], rhs=xT[:, kc, :],
                                                        start=(kc == 0), stop=(kc == 7)), reads=["wk96r", xTk], writes=[kc_])
        p.op("act", lambda e, pa=pa: e.activation(out=sq[:, :], in_=pa[:, :], func=AF.Square), reads=[ka], writes=[sqk])
        p.op("pe", lambda e: e.matmul(ssq[:, :], lhsT=G["ones"][:, :], rhs=sq[:, :], start=True, stop=True),
             reads=[sqk, "ones"], writes=[ssqk])
        p.op("act", lambda e: e.activation(out=rr[:, :], in_=ssq[:, :], func=AF.Sqrt, scale=1.0 / 128, bias=EPS),
             reads=[ssqk], writes=[rrk])
        p.op("dve", lambda e: e.reciprocal(rr[:, :], rr[:, :]), reads=[rrk], writes=[rrk])
        p.op("dve", lambda e, pa=pa, g=g: e.tensor_tensor(out=CKV[:, g * 512:(g + 1) * 512], in0=pa[:, :], in1=rr[:, :],
                                                          op=ALU.mult), reads=[ka, rrk], writes=["CKV"])
        p.op("dve", lambda e, pb=pb, ck=ck: e.tensor_tensor(out=t1[64:96, :], in0=pb[64:96, :], in1=ck[64:96, :], op=ALU.mult),
             reads=[kb, ckk], writes=[t1k])
        p.op("dve", lambda e, pc=pc, sk_=sk_: e.tensor_tensor(out=t2[64:96, :], in0=pc[64:96, :], in1=sk_[64:96, :], op=ALU.mult),
             reads=[kc_, skk], writes=[t2k])
        p.op("pool", lambda e, g=g: e.tensor_tensor(out=KT[64:96, g * 512:(g + 1) * 512], in0=t1[64:96, :], in1=t2[64:96, :],
                                                    op=ALU.add), reads=[t1k, t2k], writes=["KT"])
    p.pop()

    p.push()
    Vh = p.sb("Vh", [128, SF // 128, 65], BF16)
    QT = p.sb("QT", [128, OWN], BF16)
    cqt = p.sb("cqt", [128, OWN], F32)
    sqt = p.sb("sqt", [128, OWN], F32)
    wkv = p.sb("wkv", [128, 768], BF16)
    wkvf = p.sb("wkvf", [128, 768], F32)
    wqf = p.sb("wqf", [128, 2, 576], F32)
    wq = p.sb("wq", [128, 2, 6, 96], BF16)
    wqr = p.sb("wqr", [128, 2, 6, 96], BF16)
    qn = p.sb("qn", [128, 2], F32)
    kvn = p.sb("kvn", [128, 1], F32)
    q1 = p.sb("q1", [128, 512], F32)
    q2 = p.sb("q2", [128, 512], F32)
    NSB = 3
    PTs = [p.sb("bPT%d" % i, [128, 1024], BF16) for i in range(NSB)]
    Sps = [p.ps("bS%d" % i, [128, 1024]) for i in range(NSB)]
    Ops = [p.ps("bO%d" % i, [128, 512]) for i in range(2)]
    pend = []
    pendb = []
    gib = [0]

    def alloc_sb():
        i = gib[0] % NSB
        gib[0] += 1
        return Sps[i], "bS%d" % i

    def peek_sb():
        i = gib[0] % NSB
        return Sps[i], "bS%d" % i
    bufs = dict(peek=peek_sb, osb=[p.sb("bosb%d" % i, [128, 512], F32) for i in range(2)], rden=p.sb("brden", [128, 512], F32),
                ysq=[p.sb("bysq%d" % i, [128, 512], F32) for i in range(2)], ybf=[p.sb("bybf%d" % i, [128, 512], BF16) for i in range(2)],
                alloc=alloc_sb)
    p.op("sp", lambda e: e.dma_start(out=wkvf[:], in_=W["w_kv_up"]), writes=["wkvf"], dma="bw")
    p.op("sp", lambda e: e.dma_start(out=wqf[:, 0, :], in_=W["w_q_up"][0:128, :]), writes=["wqf"], dma="bw")
    p.op("sp", lambda e: e.dma_start(out=wqf[0:64, 1, :], in_=W["w_q_up"][128:192, :]), writes=["wqf"], dma="bw")
    p.op("sp", lambda e: e.dma_start(out=qn[:], in_=W["qn"]), writes=["qn"], dma="bw")
    p.op("sp", lambda e: e.dma_start(out=kvn[:], in_=W["kvn"]), writes=["kvn"], dma="bw")
    p.op("sp", lambda e: e.dma_start(out=cqt[64:96, :], in_=io["cqt"]), writes=["cqt"], dma="bw")
    p.op("sp", lambda e: e.dma_start(out=sqt[64:96, :], in_=io["sqt"]), writes=["sqt"], dma="bw")
    p.op("dve", lambda e: e.tensor_scalar(wkv[:, :], wkvf[:, :], kvn[:, 0:1], None, op0=ALU.mult), reads=["wkvf", "kvn"],
         writes=["wkv"])
    p.op("dve", lambda e: e.memset(wq[:], 0.0), writes=["wq"])
    p.op("dve", lambda e: e.memset(wqr[:], 0.0), writes=["wqr"])
    for cc, np_ in ((0, 128), (1, 64)):
        wv = wqf[0:np_, cc, :].rearrange("p (h c) -> p h c", h=6)
        p.op("dve", lambda e, cc=cc, np_=np_, wv=wv: e.tensor_scalar(wq[0:np_, cc, :, :], wv, qn[0:np_, cc:cc + 1], None,
                                                                      op0=ALU.mult), reads=["wqf", "qn", "wq"], writes=["wq"])
        p.op("dve", lambda e, cc=cc, np_=np_, wv=wv: e.tensor_scalar(wqr[0:np_, cc, :, 64:80], wv[:, :, 80:96],
                                                                      qn[0:np_, cc:cc + 1], None, op0=ALU.mult),
             reads=["wqf", "qn", "wqr"], writes=["wqr"])
        p.op("dve", lambda e, cc=cc, np_=np_, wv=wv: e.tensor_scalar(wqr[0:np_, cc, :, 80:96], wv[:, :, 64:80],
                                                                      qn[0:np_, cc:cc + 1], None, op0=ALU.mult),
             reads=["wqf", "qn", "wqr"], writes=["wqr"])
    p.op("pool", lambda e: e.memset(Vh[:, :, 64:65], 1.0), writes=["Vh"])
    it = 0
    gi = 0
    nkt = SF // 128
    for h in range(6):
        for g2 in range(SF // 1024):
            S, sk = alloc_sb()
            for u in range(2):
                g = 2 * g2 + u
                p.op("pe", lambda e, S=S, u=u, g=g, h=h: e.matmul(S[0:64, u * 512:(u + 1) * 512], lhsT=wkv[:, h * 128:h * 128 + 64],
                                                                  rhs=CKV[:, g * 512:(g + 1) * 512], start=True, stop=True),
                     reads=["wkv", "CKV"], writes=[sk])
            p.op("dve", lambda e, S=S, g2=g2: e.tensor_copy(KT[0:64, g2 * 1024:(g2 + 1) * 1024], S[0:64, :]),
                 reads=[sk], writes=["KT"])
        for t16 in range(nkt // 16):
            S, sk = alloc_sb()
            for u in range(16):
                t = t16 * 16 + u
                p.op("pe", lambda e, S=S, u=u, t=t, h=h: e.matmul(S[:, u * 64:(u + 1) * 64], lhsT=CKV[:, t * 128:(t + 1) * 128],
                                                                  rhs=wkv[:, h * 128 + 64:h * 128 + 128], start=True, stop=True),
                     reads=["wkv", "CKV"], writes=[sk])
            p.op("dve", lambda e, S=S, t16=t16: e.tensor_copy(Vh[:, t16 * 16:(t16 + 1) * 16, 0:64],
                                                               S[:, :].rearrange("p (t c) -> p t c", c=64)),
                 reads=[sk], writes=["Vh"])
        for qb in range(NQB):
            S, sk = alloc_sb()
            for u, wsrc in ((0, wq), (1, wqr)):
                p.op("pe", lambda e, S=S, u=u, wsrc=wsrc, h=h, qb=qb: e.matmul(S[0:96, u * 512:(u + 1) * 512], lhsT=wsrc[:, 0, h, :],
                                                                               rhs=CQ[:, 0, qb * 512:(qb + 1) * 512],
                                                                               start=True, stop=False),
                     reads=["wq", "wqr", "CQ"], writes=[sk])
                p.op("pe", lambda e, S=S, u=u, wsrc=wsrc, h=h, qb=qb: e.matmul(S[0:96, u * 512:(u + 1) * 512], lhsT=wsrc[0:64, 1, h, :],
                                                                               rhs=CQ[0:64, 1, qb * 512:(qb + 1) * 512],
                                                                               start=False, stop=True),
                     reads=["wq", "wqr", "CQ"], writes=[sk])
            p.op("dve", lambda e, S=S, qb=qb: e.tensor_copy(QT[0:64, qb * 512:(qb + 1) * 512], S[0:64, 0:512]),
                 reads=[sk], writes=["QT"])
            p.op("dve", lambda e, S=S, qb=qb: e.tensor_tensor(out=q1[64:96, :], in0=S[64:96, 0:512],
                                                              in1=cqt[64:96, qb * 512:(qb + 1) * 512], op=ALU.mult),
                 reads=[sk, "cqt"], writes=["q1"])
            p.op("dve", lambda e, S=S, qb=qb: e.tensor_tensor(out=q2[64:96, :], in0=S[64:96, 512:1024],
                                                              in1=sqt[64:96, qb * 512:(qb + 1) * 512], op=ALU.mult),
                 reads=[sk, "sqt"], writes=["q2"])
            p.op("pool", lambda e, qb=qb: e.tensor_tensor(out=QT[64:96, qb * 512:(qb + 1) * 512], in0=q1[64:96, :],
                                                          in1=q2[64:96, :], op=ALU.add), reads=["q1", "q2"], writes=["QT"])
        for qb in range(NQB):
            ops, okey = Ops[it % 2], "bO%d" % (it % 2)
            ngrp = nkt // 2

            def qk(i):
                S, sk = alloc_sb()
                for u in range(2):
                    kt = 2 * i + u
                    p.op("pe", lambda e, S=S, u=u, kt=kt: e.matmul(S[:, u * 512:(u + 1) * 512],
                                                                   lhsT=KT[0:96, kt * 128:(kt + 1) * 128],
                                                                   rhs=QT[0:96, qb * 512:(qb + 1) * 512], start=True, stop=True),
                         reads=["KT", "QT"], writes=[sk])
                return S, sk

            inflight = [qk(0), qk(1)]
            for i in range(ngrp):
                if i + 2 < ngrp:
                    inflight.append(qk(i + 2))
                S, sk = inflight.pop(0)
                PT, pk = PTs[i % NSB], "bPT%d" % (i % NSB)
                p.op("act", lambda e, S=S, PT=PT: e.activation(out=PT[:, :], in_=S[:, :], func=AF.Exp, scale=B_SCALE),
                     reads=[sk], writes=[pk])
                for u in range(2):
                    kt = 2 * i + u
                    p.op("pe", lambda e, PT=PT, u=u, kt=kt, ops=ops, i=i: e.matmul(
                        ops[0:65, :], lhsT=Vh[:, kt, 0:65], rhs=PT[:, u * 512:(u + 1) * 512],
                        start=(i == 0 and u == 0), stop=(i == ngrp - 1 and u == 1)),
                        reads=["Vh", pk], writes=[okey])
                if i == 3 and pend:
                    pendb.append(pend.pop()())
                if i == 9 and pendb:
                    pendb.pop()()
            pend.append(attn_post(p, G, io, ops, okey, 384 + h * 64, qb, 6 + h, bufs, it))
            it += 1
        while pend:
            pendb.append(pend.pop()())
        while pendb:
            pendb.pop()()
    p.pop()
    p.pop()

    p.push()
    wf1 = p.sb("wf1", [128, 8, DFF], BF16)
    wf2 = p.sb("wf2", [128, 32, D], BF16)
    p.push()
    wo = p.sb("wo", [128, 8, D], BF16)
    wofs = [p.sb("wof%d" % i, [128, D], F32) for i in range(2)]
    mixn = p.sb("mixn", [128, 8], F32)
    g1 = p.sb("g1", [128, D], F32)
    b1 = p.sb("b1", [128, D], F32)
    yTs = [p.sb("yT%d" % i, [128, 8, 512], BF16) for i in range(2)]
    xrs = [p.sb("xr%d" % i, [128, D], F32) for i in range(2)]
    accs = [p.sb("dacc%d" % i, [128, D], F32) for i in range(2)]
    x1s = [p.sb("x1s%d" % i, [128, D], F32) for i in range(2)]
    rst = p.sb("rst", [128, 3, NT], F32)
    sst = p.sb("sst", [128, 3, NT], F32)
    st12s = [p.sb("st12_%d" % i, [128, 12], F32) for i in range(2)]
    mvs = [p.sb("dmv%d" % i, [128, 2], F32) for i in range(2)]
    rss = [p.sb("drs%d" % i, [128, 1], F32) for i in range(2)]
    accps = [p.ps("accps%d" % i, [128, 1024]) for i in range(2)]
    p.op("sp", lambda e: e.dma_start(out=mixn[:], in_=W["mixn"]), writes=["mixn"], dma="dw")
    p.op("sp", lambda e: e.dma_start(out=g1[:], in_=W["ln1_g"].partition_broadcast(128)), writes=["g1"], dma="dw")
    p.op("sp", lambda e: e.dma_start(out=b1[:], in_=W["ln1_b"].partition_broadcast(128)), writes=["b1"], dma="dw")
    for kc in range(8):
        wf_, wfk = wofs[kc % 2], "wof%d" % (kc % 2)
        p.op("sp", lambda e, kc=kc: e.dma_start(out=wf_[:], in_=W["w_out"][kc * 128:(kc + 1) * 128, :]), writes=[wfk], dma=wfk)
        p.op("dve", lambda e, kc=kc: e.tensor_scalar(wo[:, kc, :], wf_[:, :], mixn[:, kc:kc + 1], None, op0=ALU.mult),
             reads=[wfk, "mixn"], writes=["wo"])

    def ffn_weight_prefetch():
        for kc in range(8):
            p.op("pool", lambda e, kc=kc: e.dma_start(out=wf1[:, kc, :], in_=W["w_ff1"][kc * 128:(kc + 1) * 128, :]),
                 writes=["wf1"], dma="wf1")
        for c4 in range(8):
            p.op("pool", lambda e, c4=c4: e.dma_start(out=wf2[:, c4 * 4:(c4 + 1) * 4, :],
                                                      in_=W["w_ff2"][c4 * 512:(c4 + 1) * 512, :].rearrange("(c p) d -> p c d", p=128)),
                 writes=["wf2"], dma="wf2")
    ssv = G["ss"]
    p.op("dve", lambda e: e.tensor_copy(sst[:, 0, :], ssv[:, 0, :]), reads=["ss"], writes=["sst"])
    p.op("dve", lambda e: e.tensor_copy(sst[:, 1, :], ssv[:, 6, :]), reads=["ss"], writes=["sst"])
    p.op("dve", lambda e: e.tensor_copy(sst[:, 2, :], ssv[:, 12, :]), reads=["ss"], writes=["sst"])
    for k in range(1, 6):
        p.op("dve", lambda e, k=k: e.tensor_tensor(out=sst[:, 0, :], in0=sst[:, 0, :], in1=ssv[:, k, :], op=ALU.add),
             reads=["ss", "sst"], writes=["sst"])
        p.op("dve", lambda e, k=k: e.tensor_tensor(out=sst[:, 1, :], in0=sst[:, 1, :], in1=ssv[:, 6 + k, :], op=ALU.add),
             reads=["ss", "sst"], writes=["sst"])
    for gidx, wdt in ((0, 384.0), (1, 384.0), (2, 256.0)):
        p.op("act", lambda e, gidx=gidx, wdt=wdt: e.activation(out=rst[:, gidx, :], in_=sst[:, gidx, :], func=AF.Sqrt,
                                                               scale=1.0 / wdt, bias=EPS), reads=["sst"], writes=["rst"])
    p.op("dve", lambda e: e.reciprocal(rst[:, :, :], rst[:, :, :]), reads=["rst"], writes=["rst"])
    KCG = ((0, 3), (3, 6), (6, 8))
    ai = 0
    for g in range(NQB):
        yT, yk = yTs[g % 2], "yT%d" % (g % 2)
        p.op("sp", lambda e, yT=yT, g=g: e.dma_start(out=yT[:, :, :], in_=io["YT"][:, g * 512:(g + 1) * 512].rearrange("(c p) t -> p c t", p=128)),
             writes=[yk], dma=yk)
        if g == 1:
            ffn_weight_prefetch()
        for j in range(4):
            ti = g * 4 + j
            xr, xk = xrs[ti % 2], "xr%d" % (ti % 2)
            x1, x1k = x1s[ti % 2], "x1s%d" % (ti % 2)
            acc, acck = accs[ti % 2], "dacc%d" % (ti % 2)
            p.op("sp", lambda e, xr=xr, ti=ti: e.dma_start(out=xr[:, :], in_=src["res"](ti)),
                 writes=[xk], dma=xk)
            for gidx, (k0, k1) in enumerate(KCG):
                ap_, ak = accps[ai % 2], "accps%d" % (ai % 2)
                ai += 1
                for half in range(2):
                    for kc in range(k0, k1):
                        p.op("pe", lambda e, ap_=ap_, half=half, kc=kc, k0=k0, k1=k1, j=j: e.matmul(
                            ap_[:, half * 512:(half + 1) * 512], lhsT=yT[:, kc, j * 128:(j + 1) * 128],
                            rhs=wo[:, kc, half * 512:(half + 1) * 512], start=(kc == k0), stop=(kc == k1 - 1)),
                            reads=[yk, "wo"], writes=[ak])
                if gidx == 0:
                    p.op("dve", lambda e, ap_=ap_, ti=ti: e.tensor_scalar(acc[:, :], ap_[:, :], rst[:, 0, ti:ti + 1], None, op0=ALU.mult),
                         reads=[ak, "rst"], writes=[acck])
                else:
                    p.op("dve", lambda e, ap_=ap_, ti=ti, gidx=gidx: e.scalar_tensor_tensor(
                        out=acc[:, :], in0=ap_[:, :], scalar=rst[:, gidx, ti:ti + 1], in1=acc[:, :], op0=ALU.mult, op1=ALU.add),
                        reads=[ak, "rst", acck], writes=[acck])
            p.op("dve", lambda e, xr=xr: e.scalar_tensor_tensor(out=acc[:, :], in0=xr[:, :], scalar=ALPHA, in1=acc[:, :],
                                                                op0=ALU.mult, op1=ALU.add), reads=[xk, acck], writes=[acck])
            layer_norm(p, acc, acck, x1, x1k, g1, "g1", b1, "b1", st12s[ti % 2], mvs[ti % 2], rss[ti % 2], "d1_%d" % (ti % 2))
            p.op("sp", lambda e, x1=x1, ti=ti: e.dma_start(out=io["X1"][ti * 128:(ti + 1) * 128, :], in_=x1[:, :]),
                 reads=[x1k], dma="x1o%d" % (ti % 2))
    p.pop()

    p.push()
    b1T = p.sb("b1T", [128, 32], F32)
    g2 = p.sb("g2", [128, D], F32)
    b2 = p.sb("b2", [128, D], F32)
    bf2 = p.sb("bf2", [128, D], F32)
    x1f = [p.sb("x1f%d" % i, [128, 2, D], F32) for i in range(2)]
    x1b = p.sb("x1b", [128, 2, D], BF16)
    x1Ts = [p.sb("x1T%d" % i, [128, 8, 256], BF16) for i in range(2)]
    hidT = p.sb("hidT", [128, 32, 256], BF16)
    rl = [p.sb("rl%d" % i, [128, 256], F32) for i in range(2)]
    t2ss = [p.sb("t2s%d" % i, [128, D], F32) for i in range(1)]
    outs = [p.sb("outs%d" % i, [128, D], F32) for i in range(2)]
    outb = [p.sb("outb%d" % i, [128, D], BF16) for i in range(2)]
    st12s = [p.sb("st12b%d" % i, [128, 12], F32) for i in range(2)]
    mvs = [p.sb("dmv2_%d" % i, [128, 2], F32) for i in range(2)]
    rss = [p.sb("drs2_%d" % i, [128, 1], F32) for i in range(2)]
    trps = [p.ps("dtrps%d" % i, [128, 1024], BF16) for i in range(2)]
    hps = [p.ps("hps%d" % i, [128, 512]) for i in range(2)]
    fps = [p.ps("fps%d" % i, [128, 1024]) for i in range(2)]
    p.op("sp", lambda e: e.dma_start(out=b1T[:], in_=W["b1T"]), writes=["b1T"], dma="dw2")
    p.op("sp", lambda e: e.dma_start(out=g2[:], in_=W["ln2_g"].partition_broadcast(128)), writes=["g2"], dma="dw2")
    p.op("sp", lambda e: e.dma_start(out=b2[:], in_=W["ln2_b"].partition_broadcast(128)), writes=["b2"], dma="dw2")
    p.op("sp", lambda e: e.dma_start(out=bf2[:], in_=W["b_ff2"].partition_broadcast(128)), writes=["bf2"], dma="dw2")
    ng2 = OWN // 256

    def prep(g):
        xf_, xfk = x1f[g % 2], "x1f%d" % (g % 2)
        p.op("sp", lambda e: e.dma_start(out=xf_[:, :, :], in_=io["X1"][g * 256:(g + 1) * 256, :].rearrange("(j p) d -> p j d", p=128)),
             writes=[xfk], dma=xfk)
        p.op("pool", lambda e: e.tensor_copy(x1b[:, :, :], xf_[:, :, :]), reads=[xfk], writes=["x1b"])
        transpose_group(p, G, x1b, "x1b", x1Ts[g % 2], "x1T%d" % (g % 2), trps, n_tok_tiles=2)

    prep(0)
    hi = 0
    fi = 0
    for g in range(ng2):
        xf_, xfk = x1f[g % 2], "x1f%d" % (g % 2)
        x1T, x1Tk = x1Ts[g % 2], "x1T%d" % (g % 2)
        for fc in range(32):
            hp_, hk = hps[hi % 2], "hps%d" % (hi % 2)
            r_, rk = rl[hi % 2], "rl%d" % (hi % 2)
            hi += 1
            for kc in range(8):
                p.op("pe", lambda e, hp_=hp_, kc=kc, fc=fc: e.matmul(hp_[:, 0:256], lhsT=wf1[:, kc, fc * 128:(fc + 1) * 128],
                                                                     rhs=x1T[:, kc, :], start=(kc == 0), stop=(kc == 7)),
                     reads=["wf1", x1Tk], writes=[hk])
            p.op("act", lambda e, hp_=hp_, r_=r_, fc=fc: e.activation(out=r_[:, :], in_=hp_[:, 0:256], func=AF.Relu,
                                                                      bias=b1T[:, fc:fc + 1], scale=1.0),
                 reads=[hk, "b1T"], writes=[rk])
            p.op("dve", lambda e, r_=r_, fc=fc: e.tensor_tensor(out=hidT[:, fc, :], in0=r_[:, :], in1=r_[:, :], op=ALU.mult),
                 reads=[rk], writes=["hidT"])
        if g + 1 < ng2:
            prep(g + 1)
        for j in range(2):
            ti = g * 2 + j
            fp_, fk = fps[fi % 2], "fps%d" % (fi % 2)
            o_, ok_ = outs[fi % 2], "outs%d" % (fi % 2)
            t2s, t2k = t2ss[0], "t2s0"
            par = fi % 2
            fi += 1
            for half in range(2):
                for fc in range(32):
                    p.op("pe", lambda e, fp_=fp_, half=half, fc=fc, j=j: e.matmul(
                        fp_[:, half * 512:(half + 1) * 512], lhsT=hidT[:, fc, j * 128:(j + 1) * 128],
                        rhs=wf2[:, fc, half * 512:(half + 1) * 512], start=(fc == 0), stop=(fc == 31)),
                        reads=["hidT", "wf2"], writes=[fk])
            p.op("dve", lambda e, fp_=fp_: e.tensor_tensor(out=t2s[:, :], in0=fp_[:, :], in1=bf2[:, :], op=ALU.add),
                 reads=[fk, "bf2"], writes=[t2k])
            p.op("dve", lambda e, xf_=xf_, j=j: e.scalar_tensor_tensor(out=t2s[:, :], in0=xf_[:, j, :], scalar=ALPHA, in1=t2s[:, :],
                                                                       op0=ALU.mult, op1=ALU.add), reads=[xfk, t2k], writes=[t2k])
            layer_norm(p, t2s, t2k, o_, ok_, g2, "g2", b2, "b2", st12s[par], mvs[par], rss[par], "d2_%d" % par)
            p.op("sp", lambda e, o_=o_, ti=ti: e.dma_start(out=dst["f32"][ti * 128:(ti + 1) * 128, :], in_=o_[:, :]),
                 reads=[ok_], dma=dst["tag"])
            if dst["bf16"] is not None:
                ob_, obk = outb[par], "outb%d" % par
                p.op("pool", lambda e, o_=o_, ob_=ob_: e.tensor_copy(ob_[:, :], o_[:, :]), reads=[ok_], writes=[obk])
                ch = ti // 4
                p.op("sp", lambda e, ob_=ob_, ti=ti: e.dma_start(out=dst["bf16"][ti * 128:(ti + 1) * 128, :], in_=ob_[:, :]),
                     reads=[obk], writes=["XBc%d" % ch], dma="xbc%d" % (ch % 2))
                if ti % 4 == 3:
                    ngrp_ = cfg.SF // OWN
                    groups_ = [list(range(b_ * ngrp_, (b_ + 1) * ngrp_)) for b_ in range(cfg.ncores // ngrp_)]
                    p.op("pool", lambda e, ch=ch: e.collective_compute("AllGather", ALU.bypass, replica_groups=groups_,
                                                                       ins=[io["XBc"][ch * 512:(ch + 1) * 512, :].opt()],
                                                                       outs=[io["XG"][ch].opt()]),
                         reads=["XBc%d" % ch], dma="cc", inc=1)
    p.pop()
    p.pop()


def layer_norm(p, src, skey, dstt, dkey, g, gk, b, bk, st12, mv, rs, tg):
    k6, kmv, krs = "st12" + tg, "mv" + tg, "rs" + tg
    p.op("dve", lambda e: e.bn_stats(st12[:, 0:6], src[:, 0:512]), reads=[skey], writes=[k6])
    p.op("dve", lambda e: e.bn_stats(st12[:, 6:12], src[:, 512:1024]), reads=[skey], writes=[k6])
    p.op("dve", lambda e: e.bn_aggr(mv[:, :], st12[:, :]), reads=[k6], writes=[kmv])
    p.op("act", lambda e: e.activation(out=rs[:, :], in_=mv[:, 1:2], func=AF.Sqrt, scale=1.0, bias=EPS), reads=[kmv], writes=[krs])
    p.op("dve", lambda e: e.reciprocal(rs[:, :], rs[:, :]), reads=[krs], writes=[krs])
    p.op("dve", lambda e: e.tensor_scalar(src[:, :], src[:, :], mv[:, 0:1], rs[:, 0:1], op0=ALU.subtract, op1=ALU.mult),
         reads=[skey, kmv, krs], writes=[skey])
    p.op("pool", lambda e: e.tensor_tensor(out=src[:, :], in0=src[:, :], in1=g[:, :], op=ALU.mult), reads=[skey, gk], writes=[skey])
    p.op("pool", lambda e: e.tensor_tensor(out=dstt[:, :], in0=src[:, :], in1=b[:, :], op=ALU.add), reads=[skey, bk], writes=[dkey])


_CACHE = {}


def host_weights(inp, layers):
    f = lambda a: np.ascontiguousarray(a, dtype=np.float32)
    L = list(layers)
    w = {}
    w["w_in"] = f(inp["w_in"][L])
    w["w_q_up"] = f(inp["w_q_up"][L])
    w["w_kv_up"] = f(inp["w_kv_up"][L])
    qn = np.zeros((len(L), 256), np.float32)
    qn[:, :192] = inp["q_norm"][L]
    w["qn"] = f(qn.reshape(len(L), 2, 128).transpose(0, 2, 1))
    w["kvn"] = f(inp["kv_norm"][L].reshape(len(L), 128, 1))
    w["sg_g"] = f(inp["sgu_ln_g"][L])
    w["sg_b"] = f(inp["sgu_ln_b"][L])
    w["sg_wT"] = f(np.transpose(inp["sgu_w"][L], (0, 1, 3, 2)))
    w["sg_bT"] = f(np.transpose(inp["sgu_b"][L], (0, 2, 1)))
    w["mixn"] = f(inp["mix_norm"][L].reshape(len(L), 8, 128).transpose(0, 2, 1))
    w["w_out"] = f(inp["w_out"][L])
    w["ln1_g"] = f(inp["ln1_g"][L])
    w["ln1_b"] = f(inp["ln1_b"][L])
    w["w_ff1"] = f(inp["w_ff1"][L])
    w["b1T"] = f(inp["b_ff1"][L].reshape(len(L), 32, 128).transpose(0, 2, 1))
    w["w_ff2"] = f(inp["w_ff2"][L])
    w["b_ff2"] = f(inp["b_ff2"][L])
    w["ln2_g"] = f(inp["ln2_g"][L])
    w["ln2_b"] = f(inp["ln2_b"][L])
    return w


def core_inputs(x_b, r, own, consts):
    S = x_b.shape[0]
    lo = r * own - HALO
    xo = np.zeros((own + 2 * HALO, D), np.float32)
    vm = np.zeros((own + 2 * HALO,), np.float32)
    a, b = max(lo, 0), min(lo + own + 2 * HALO, S)
    xo[a - lo:b - lo] = x_b[a:b]
    vm[a - lo:b - lo] = 1.0
    ct, st = consts["rope"]
    hs = np.zeros((128, 8), np.float32)
    if r - 1 >= 0:
        hs[:, r - 1] = 1.0
    if r + 1 < S // own:
        hs[:, 4 + r + 1] = 1.0
    d = dict(hsel=hs, xo=xo, xf=np.ascontiguousarray(x_b, dtype=np.float32),
             vmask=np.ascontiguousarray(vm.reshape(-1, 128).T), mtab=consts["mtab"],
             ckt=ct, skt=st, cqt=np.ascontiguousarray(ct[:, r * own:(r + 1) * own]),
             sqt=np.ascontiguousarray(st[:, r * own:(r + 1) * own]))
    return d


def run_layers(x, inp, own, n_groups_per_batch, fused_depth=1, dbg=False):
    B, S, _ = x.shape
    key = (own, S, fused_depth, dbg, B)
    if key not in _CACHE:
        _CACHE[key] = build_program(Cfg(own, S, depth=fused_depth, dbg=dbg, ncores=B * n_groups_per_batch))
    nc, stats = _CACHE[key]
    consts = dict(mtab=mask_table(), rope=rope_tables(S))
    depth = inp["w_in"].shape[0]
    cur = np.asarray(x, dtype=np.float32)
    extra = None
    for l0 in range(0, depth, fused_depth):
        w = host_weights(inp, range(l0, l0 + fused_depth))
        in_maps = []
        for c in range(B * n_groups_per_batch):
            b, r = c // n_groups_per_batch, c % n_groups_per_batch
            d = core_inputs(cur[b], r, own, consts)
            if fused_depth == 1:
                d.pop("hsel")
            d.update(w)
            in_maps.append(d)
        res = run_bass_kernel_spmd(nc, in_maps, core_ids=list(range(len(in_maps))))
        outs = [r_["out"] for r_ in res.results]
        cur = np.stack([np.concatenate(outs[b * n_groups_per_batch:(b + 1) * n_groups_per_batch], 0) for b in range(B)], 0)
        extra = res.results
    return cur, extra


def kernel(**inputs):
    x = np.asarray(inputs["x"], dtype=np.float32)
    inp = {k: np.asarray(v, dtype=np.float32) for k, v in inputs.items() if k != "x"}
    out, _ = run_layers(x, inp, own=x.shape[1] // 4, n_groups_per_batch=4, fused_depth=inp["w_in"].shape[0])
    return out.astype(np.float32)
```
